# Optimizing a Trainium2 kernel written in Bass

```python
import jax
import jax.numpy as jnp
from jax import lax
import numpy as np

D_MODEL = 1024
BATCH = 4
SEQ = 4096
DEPTH = 1
DEC_BATCH = 32
DEC_SEQ = 16
PAST_LEN = 1024

CHUNK = 64
EPS = 1e-6
MLA_HEADS = 16
QK_NOPE = 64
QK_ROPE = 32
V_HEAD = 64
Q_LORA = 384
KV_LORA = 256
ROPE_THETA = 10000.0
MLA_SCALE = (QK_NOPE + QK_ROPE) ** -0.5
Q_BLOCK = 128
GM_CHUNK = 128
GM_GROUPS = 8
GM_WIDTH = 1024
GM_GROUP_DIM = GM_WIDTH // GM_GROUPS
N_MEM = 256
MEM_HEADS = 4
MEM_HEAD_DIM = 128
MEM_SCALE = MEM_HEAD_DIM ** -0.5
PEER_HEADS = 8
N_KEYS = 128
N_EXPERTS = N_KEYS * N_KEYS
PEER_TOPK = 16
PEER_QDIM = 256
PEER_HALF = PEER_QDIM // 2
PEER_BLOCK = 128
IN_TOTAL = Q_LORA + KV_LORA + QK_ROPE + 2 * GM_WIDTH + 2 * D_MODEL

kernel_name = 'hybrid_mla_gmlp_peer_stream_step'


def rmsnorm(x, g):
    xf = x.astype(jnp.float32)
    y = xf * lax.rsqrt(jnp.mean(xf * xf, axis=-1, keepdims=True) + EPS)
    return (y * g.astype(jnp.float32)).astype(x.dtype)


def rope(x, pos):
    d = x.shape[-1]
    half = d // 2
    inv = ROPE_THETA ** (-jnp.arange(half, dtype=jnp.float32) / half)
    ang = pos.astype(jnp.float32)[:, None] * inv[None, :]
    ang = ang.reshape((1, pos.shape[0]) + (1,) * (x.ndim - 3) + (half,))
    cos, sin = jnp.cos(ang), jnp.sin(ang)
    xf = x.astype(jnp.float32)
    x1, x2 = xf[..., :half], xf[..., half:]
    return jnp.concatenate([x1 * cos - x2 * sin, x1 * sin + x2 * cos], axis=-1).astype(x.dtype)


def mla_scores(q_nope, q_rope, k_nope, k_rope):
    s = jnp.einsum('bqhd,bkhd->bhqk', q_nope, k_nope) + jnp.einsum('bqhd,bkd->bhqk', q_rope, k_rope)
    return s.astype(jnp.float32) * MLA_SCALE


def mla_attend_prompt(q_nope, q_rope, k_nope, k_rope, v):
    B, S, H, _ = q_nope.shape
    nb = S // Q_BLOCK
    qn = q_nope.reshape(B, nb, Q_BLOCK, H, QK_NOPE).transpose(1, 0, 2, 3, 4)
    qr = q_rope.reshape(B, nb, Q_BLOCK, H, QK_ROPE).transpose(1, 0, 2, 3, 4)
    key_chunk = jnp.arange(S) // CHUNK

    def block(args):
        qn_b, qr_b, bi = args
        s = mla_scores(qn_b, qr_b, k_nope, k_rope)
        q_chunk = (bi * Q_BLOCK + jnp.arange(Q_BLOCK)) // CHUNK
        mask = key_chunk[None, :] <= q_chunk[:, None]
        s = jnp.where(mask[None, None], s, jnp.float32(-1e30))
        pr = jax.nn.softmax(s, axis=-1).astype(v.dtype)
        return jnp.einsum('bhqk,bkhd->bqhd', pr, v)

    out = lax.map(block, (qn, qr, jnp.arange(nb)))
    return out.transpose(1, 0, 2, 3, 4).reshape(B, S, H, V_HEAD)


def mla_attend_all(q_nope, q_rope, k_nope, k_rope, v):
    pr = jax.nn.softmax(mla_scores(q_nope, q_rope, k_nope, k_rope), axis=-1).astype(v.dtype)
    return jnp.einsum('bhqk,bkhd->bqhd', pr, v)


def mixer_block(x, p, pos, past):
    B, S, _ = x.shape
    h = rmsnorm(x, p['g_mix'])
    z = h @ p['w_in']
    splits = [Q_LORA, Q_LORA + KV_LORA, Q_LORA + KV_LORA + QK_ROPE,
              Q_LORA + KV_LORA + QK_ROPE + GM_WIDTH,
              Q_LORA + KV_LORA + QK_ROPE + 2 * GM_WIDTH,
              Q_LORA + KV_LORA + QK_ROPE + 2 * GM_WIDTH + D_MODEL]
    c_q, c_kv, k_r, z_u, z_v, z_ga, z_gb = jnp.split(z, splits, axis=-1)
    q = (rmsnorm(c_q, p['g_q_lat']) @ p['w_uq']).reshape(B, S, MLA_HEADS, QK_NOPE + QK_ROPE)
    q_nope = rmsnorm(q[..., :QK_NOPE], p['g_qn'])
    q_rope = rope(rmsnorm(q[..., QK_NOPE:], p['g_qr']), pos)
    c_kv = rmsnorm(c_kv, p['g_kv_lat'])
    k_r = rope(rmsnorm(k_r, p['g_kr']), pos)
    if past is None:
        ckv_all, kr_all = c_kv, k_r
    else:
        ckv_all = jnp.concatenate([past[0], c_kv], axis=1)
        kr_all = jnp.concatenate([past[1], k_r], axis=1)
    Sk = ckv_all.shape[1]
    k_nope = rmsnorm((ckv_all @ p['w_uk']).reshape(B, Sk, MLA_HEADS, QK_NOPE), p['g_kn'])
    v = (ckv_all @ p['w_uv']).reshape(B, Sk, MLA_HEADS, V_HEAD)
    if past is None:
        o = mla_attend_prompt(q_nope, q_rope, k_nope, kr_all, v)
    else:
        o = mla_attend_all(q_nope, q_rope, k_nope, kr_all, v)
    o_a = o.reshape(B, S, MLA_HEADS * V_HEAD) @ p['w_oa']
    u = jax.nn.gelu(z_u, approximate=False)
    v_g = rmsnorm(jax.nn.gelu(z_v, approximate=False), p['g_gm'])
    L = min(S, GM_CHUNK)
    C = S // L
    w_mask = jnp.tril(p['w_s'][:, :L, :L])
    v5 = v_g.reshape(B, C, L, GM_GROUPS, GM_GROUP_DIM)
    mixed = jnp.einsum('gts,bcsgd->bctgd', w_mask, v5) + p['b_s'][:, :L].T[None, None, :, :, None]
    o_b = (u * mixed.reshape(B, S, GM_WIDTH)) @ p['w_ob']
    y = (jax.nn.sigmoid(z_ga) * o_a + jax.nn.sigmoid(z_gb) * o_b) @ p['w_o']
    return x + y, c_kv, k_r, v_g


def memory_kv(mem, g_mem, w_ck, g_ck, w_cv):
    B, N, _ = mem.shape
    m = rmsnorm(mem, g_mem)
    k = rmsnorm((m @ w_ck).reshape(B, N, MEM_HEADS, MEM_HEAD_DIM), g_ck)
    v = (m @ w_cv).reshape(B, N, MEM_HEADS, MEM_HEAD_DIM)
    return k, v


def cross_attend(h, w_cq, g_cq, mem_k, mem_v, w_co):
    B, S, _ = h.shape
    q = rmsnorm((h @ w_cq).reshape(B, S, MEM_HEADS, MEM_HEAD_DIM), g_cq)
    s = jnp.einsum('bqhd,bkhd->bhqk', q, mem_k).astype(jnp.float32) * MEM_SCALE
    pr = jax.nn.softmax(s, axis=-1).astype(mem_v.dtype)
    o = jnp.einsum('bhqk,bkhd->bqhd', pr, mem_v).reshape(B, S, MEM_HEADS * MEM_HEAD_DIM)
    return o @ w_co


def peer_ffn(h, w_pq, sub_keys, peer_u, peer_v):
    B, S, D = h.shape
    T = B * S
    pad = (-T) % PEER_BLOCK
    xb = jnp.pad(h.reshape(T, D), ((0, pad), (0, 0))).reshape(-1, PEER_BLOCK, D)

    def block(xt):
        q = (xt @ w_pq).reshape(PEER_BLOCK, PEER_HEADS, 2, PEER_HALF)
        s = jnp.einsum('thpd,hpnd->thpn', q, sub_keys).astype(jnp.float32)
        sv, si = lax.top_k(s, PEER_TOPK)
        cand = sv[:, :, 0, :, None] + sv[:, :, 1, None, :]
        cidx = si[:, :, 0, :, None] * N_KEYS + si[:, :, 1, None, :]
        cand = cand.reshape(PEER_BLOCK, PEER_HEADS, PEER_TOPK * PEER_TOPK)
        cidx = cidx.reshape(PEER_BLOCK, PEER_HEADS, PEER_TOPK * PEER_TOPK)
        top_s, sel = lax.top_k(cand, PEER_TOPK)
        eidx = jnp.take_along_axis(cidx, sel, axis=-1)
        g = jax.nn.softmax(top_s, axis=-1).astype(xt.dtype)
        act = jax.nn.gelu(jnp.einsum('thkd,td->thk', peer_u[eidx], xt), approximate=False)
        return jnp.einsum('thk,thkd->td', g * act, peer_v[eidx])

    out = lax.map(block, xb).reshape(-1, D)[:T]
    return out.reshape(B, S, D)


def setup_inputs(seed: int = 0) -> dict:
    key = jax.random.key(seed)
    ks = iter(jax.random.split(key, 64))
    f32 = jnp.float32
    L = DEPTH

    def nrm(shape, scale):
        return jax.random.normal(next(ks), shape, f32) * scale

    def gain(n):
        return 1.0 + 0.02 * jax.random.normal(next(ks), (L, n), f32)

    return {
        'x_prompt': nrm((BATCH, SEQ, D_MODEL), 1.0),
        'x_sample': nrm((DEC_BATCH, DEC_SEQ, D_MODEL), 1.0),
        'cache_mla_ckv': nrm((L, DEC_BATCH, PAST_LEN, KV_LORA), 1.0),
        'cache_mla_krope': nrm((L, DEC_BATCH, PAST_LEN, QK_ROPE), 1.0),
        'cache_mem_k': nrm((L, DEC_BATCH, N_MEM, MEM_HEADS, MEM_HEAD_DIM), 1.0),
        'cache_mem_v': nrm((L, DEC_BATCH, N_MEM, MEM_HEADS, MEM_HEAD_DIM), 1.0),
        'mem_prompt': nrm((BATCH, N_MEM, D_MODEL), 1.0),
        'g_mix': gain(D_MODEL),
        'w_in': nrm((L, D_MODEL, IN_TOTAL), D_MODEL ** -0.5),
        'g_q_lat': gain(Q_LORA),
        'w_uq': nrm((L, Q_LORA, MLA_HEADS * (QK_NOPE + QK_ROPE)), Q_LORA ** -0.5),
        'g_qn': gain(QK_NOPE),
        'g_qr': gain(QK_ROPE),
        'g_kv_lat': gain(KV_LORA),
        'g_kr': gain(QK_ROPE),
        'w_uk': nrm((L, KV_LORA, MLA_HEADS * QK_NOPE), KV_LORA ** -0.5),
        'w_uv': nrm((L, KV_LORA, MLA_HEADS * V_HEAD), KV_LORA ** -0.5),
        'g_kn': gain(QK_NOPE),
        'w_oa': nrm((L, MLA_HEADS * V_HEAD, D_MODEL), (MLA_HEADS * V_HEAD) ** -0.5),
        'g_gm': gain(GM_WIDTH),
        'w_s': nrm((L, GM_GROUPS, GM_CHUNK, GM_CHUNK), 0.5 * GM_CHUNK ** -0.5),
        'b_s': 1.0 + 0.01 * jax.random.normal(next(ks), (L, GM_GROUPS, GM_CHUNK), f32),
        'w_ob': nrm((L, GM_WIDTH, D_MODEL), GM_WIDTH ** -0.5),
        'w_o': nrm((L, D_MODEL, D_MODEL), D_MODEL ** -0.5),
        'g_xattn': gain(D_MODEL),
        'g_mem': gain(D_MODEL),
        'w_cq': nrm((L, D_MODEL, MEM_HEADS * MEM_HEAD_DIM), D_MODEL ** -0.5),
        'g_cq': gain(MEM_HEAD_DIM),
        'w_ck': nrm((L, D_MODEL, MEM_HEADS * MEM_HEAD_DIM), D_MODEL ** -0.5),
        'g_ck': gain(MEM_HEAD_DIM),
        'w_cv': nrm((L, D_MODEL, MEM_HEADS * MEM_HEAD_DIM), D_MODEL ** -0.5),
        'w_co': nrm((L, MEM_HEADS * MEM_HEAD_DIM, D_MODEL), (MEM_HEADS * MEM_HEAD_DIM) ** -0.5),
        'g_ffn': gain(D_MODEL),
        'w_pq': nrm((L, D_MODEL, PEER_HEADS * PEER_QDIM), D_MODEL ** -0.5),
        'sub_keys': nrm((L, PEER_HEADS, 2, N_KEYS, PEER_HALF), PEER_HALF ** -0.5),
        'peer_u': nrm((L, N_EXPERTS, D_MODEL), D_MODEL ** -0.5),
        'peer_v': nrm((L, N_EXPERTS, D_MODEL), D_MODEL ** -0.5),
    }


def reference(x_prompt, x_sample, cache_mla_ckv, cache_mla_krope, cache_mem_k, cache_mem_v, mem_prompt,
              g_mix, w_in, g_q_lat, w_uq, g_qn, g_qr, g_kv_lat, g_kr, w_uk, w_uv, g_kn, w_oa,
              g_gm, w_s, b_s, w_ob, w_o,
              g_xattn, g_mem, w_cq, g_cq, w_ck, g_ck, w_cv, w_co,
              g_ffn, w_pq, sub_keys, peer_u, peer_v):
    S_p = x_prompt.shape[1]
    S_s = x_sample.shape[1]
    past_len = cache_mla_ckv.shape[2]
    pos_p = jnp.arange(S_p)
    pos_s = past_len + jnp.arange(S_s)
    xp, xs = x_prompt, x_sample
    ckv_p_l, kr_p_l, mk_p_l, mv_p_l, ckv_s_l, kr_s_l, vg_s_l = [], [], [], [], [], [], []
    for i in range(DEPTH):
        p = {'g_mix': g_mix[i], 'w_in': w_in[i], 'g_q_lat': g_q_lat[i], 'w_uq': w_uq[i],
             'g_qn': g_qn[i], 'g_qr': g_qr[i], 'g_kv_lat': g_kv_lat[i], 'g_kr': g_kr[i],
             'w_uk': w_uk[i], 'w_uv': w_uv[i], 'g_kn': g_kn[i], 'w_oa': w_oa[i],
             'g_gm': g_gm[i], 'w_s': w_s[i], 'b_s': b_s[i], 'w_ob': w_ob[i], 'w_o': w_o[i]}
        xp, ckv_p, kr_p, _ = mixer_block(xp, p, pos_p, None)
        xs, ckv_s, kr_s, vg_s = mixer_block(xs, p, pos_s, (cache_mla_ckv[i], cache_mla_krope[i]))
        mk_p, mv_p = memory_kv(mem_prompt, g_mem[i], w_ck[i], g_ck[i], w_cv[i])
        xp = xp + cross_attend(rmsnorm(xp, g_xattn[i]), w_cq[i], g_cq[i], mk_p, mv_p, w_co[i])
        xs = xs + cross_attend(rmsnorm(xs, g_xattn[i]), w_cq[i], g_cq[i], cache_mem_k[i], cache_mem_v[i], w_co[i])
        xp = xp + peer_ffn(rmsnorm(xp, g_ffn[i]), w_pq[i], sub_keys[i], peer_u[i], peer_v[i])
        xs = xs + peer_ffn(rmsnorm(xs, g_ffn[i]), w_pq[i], sub_keys[i], peer_u[i], peer_v[i])
        ckv_p_l.append(ckv_p)
        kr_p_l.append(kr_p)
        mk_p_l.append(mk_p)
        mv_p_l.append(mv_p)
        ckv_s_l.append(ckv_s)
        kr_s_l.append(kr_s)
        vg_s_l.append(vg_s)
    return (xp, xs, jnp.stack(ckv_p_l), jnp.stack(kr_p_l), jnp.stack(mk_p_l), jnp.stack(mv_p_l),
            jnp.stack(ckv_s_l), jnp.stack(kr_s_l), jnp.stack(vg_s_l))
```

```python
import numpy as np
from contextlib import ExitStack
import concourse.bass as bass
import concourse.mybir as mybir
from concourse.bass_utils import run_bass_kernel_spmd

F32 = mybir.dt.float32
BF16 = mybir.dt.bfloat16
I32 = mybir.dt.int32
U32 = mybir.dt.uint32
ALU = mybir.AluOpType
AF = mybir.ActivationFunctionType
AX = mybir.AxisListType

NCORES = 8
D = 1024
EPS = 1e-6
NOWN = 17
NTOK = NOWN * 128
NKP = 4096
NKS = 4 * 1040
NK = NKP + NKS
MLA_SCALE = 96 ** -0.5
MEM_SCALE = 128 ** -0.5
IN_Q, IN_KV, IN_Z = 0, 384, 672


class Buf:
    __slots__ = ("name", "lw", "rd")

    def __init__(self, name):
        self.name = name
        self.lw = None
        self.rd = {}


class T:
    __slots__ = ("t", "b")

    def __init__(self, t, b):
        self.t = t
        self.b = b

    def __getitem__(self, k):
        return self.t[k]


class _Eng:
    def __init__(self, name, selfsync):
        self.name = name
        self.sem = "e_" + name
        self.count = 0
        self.seen = {}
        self.prog = []
        self.selfsync = selfsync


class _Queue:
    def __init__(self, name, eng, nslots):
        self.name = name
        self.eng = eng
        self.slots = [["q_%s_%d" % (name, i), 0] for i in range(nslots)]
        self.next = 0


class Sched:
    def __init__(self, nc, st, selfsync=True):
        self.nc = nc
        self.eng = {
            "pe": _Eng("pe", False),
            "act": _Eng("act", selfsync),
            "dve": _Eng("dve", selfsync),
            "pool": _Eng("pool", selfsync),
            "sp": _Eng("sp", False),
        }
        self.queues = {
            "sp": _Queue("sp", "sp", 8),
            "pool": _Queue("pool", "pool", 8),
            "act": _Queue("act", "act", 4),
            "conv": _Queue("conv", "pool", 16),
        }
        self.final_tokens = []
        names = [E.sem for E in self.eng.values()]
        for Q in self.queues.values():
            names += [s[0] for s in Q.slots]
        self.sems = {n: st.enter_context(nc.semaphore(n)) for n in names}
        self.ninst = 0

    def _collect(self, E, reads, writes, extra=None):
        need = {}

        def add(tok):
            if tok is None:
                return
            s, v = tok
            if need.get(s, 0) < v:
                need[s] = v
        for b in reads:
            add(b.lw)
        for b in writes:
            add(b.lw)
            for s, v in b.rd.items():
                add((s, v))
        if extra:
            for t in extra:
                add(t)
        waits = []
        for s, v in need.items():
            if s == E.sem and not E.selfsync:
                continue
            if E.seen.get(s, 0) >= v:
                continue
            E.seen[s] = v
            waits.append((s, v))
        return waits

    def _commit(self, tok, reads, writes):
        for b in writes:
            b.lw = tok
            b.rd = {}
        s, v = tok
        for b in reads:
            if b in writes:
                continue
            if b.rd.get(s, 0) < v:
                b.rd[s] = v

    def op(self, eng, fn, R=(), W=()):
        E = self.eng[eng]
        reads = [x.b for x in R]
        writes = [x.b for x in W]
        waits = self._collect(E, reads, writes)
        E.count += 1
        tok = (E.sem, E.count)
        E.prog.append((waits, fn, (E.sem, 1)))
        self._commit(tok, reads, writes)
        return tok

    def dma(self, queue, fn, R=(), W=(), final=False):
        Q = self.queues[queue]
        E = self.eng[Q.eng]
        reads = [x.b for x in R]
        writes = [x.b for x in W]
        slot = Q.slots[Q.next]
        Q.next = (Q.next + 1) % len(Q.slots)
        extra = [(slot[0], slot[1] * 16)] if slot[1] > 0 else None
        waits = self._collect(E, reads, writes, extra)
        slot[1] += 1
        tok = (slot[0], slot[1] * 16)
        E.prog.append((waits, fn, (slot[0], 16)))
        self._commit(tok, reads, writes)
        if final:
            self.final_tokens.append(tok)
        return tok

    def barrier(self):
        toks = []
        for E in self.eng.values():
            if E.count > 0:
                toks.append((E.sem, E.count))
        for qn, Q in self.queues.items():
            if qn == "conv":
                continue
            for s in Q.slots:
                if s[1] > 0:
                    toks.append((s[0], s[1] * 16))
        for E in self.eng.values():
            waits = []
            for s, v in toks:
                if s == E.sem:
                    continue
                if E.seen.get(s, 0) >= v:
                    continue
                E.seen[s] = v
                waits.append((s, v))
            if waits:
                E.prog.append((waits, None, None))

    def flush(self, last=False):
        nc = self.nc
        sems = self.sems
        if last:
            fin = {}
            for s, v in self.final_tokens:
                if fin.get(s, 0) < v:
                    fin[s] = v
            self.eng["sp"].prog.append(([(s, v) for s, v in fin.items()], None, None))

        def run(E):
            prog = E.prog
            E.prog = []
            self.ninst += len(prog)

            def body(e):
                for waits, fn, inc in prog:
                    for s, v in waits:
                        e.wait_ge(sems[s], v)
                    if fn is not None:
                        ins = fn(e)
                        ins.then_inc(sems[inc[0]], inc[1])
            return body
        with nc.Block(no_gpsimd_drain=True) as block:
            block.sync(run(self.eng["sp"]))
            block.tensor(run(self.eng["pe"]))
            block.scalar(run(self.eng["act"]))
            block.vector(run(self.eng["dve"]))
            block.gpsimd(run(self.eng["pool"]))


class Ring:
    def __init__(self, tiles):
        self.tiles = tiles
        self.i = 0

    def next(self):
        t = self.tiles[self.i]
        self.i = (self.i + 1) % len(self.tiles)
        return t


class Kern:
    def __init__(self, dbg=None):
        self.nc = bass.Bass("TRN2", target_bir_lowering=False)
        self.st = ExitStack()
        self.S = Sched(self.nc, self.st)
        self.scopes = [self.st]
        self.dbg = dbg or {}
        self.uid = 0

    def din(self, name, shape, dt=F32):
        return self.nc.dram_tensor(name, list(shape), dt, kind="ExternalInput").ap()

    def dout(self, name, shape, dt=F32):
        return self.nc.dram_tensor(name, list(shape), dt, kind="ExternalOutput").ap()

    def sb(self, name, shape, dt):
        self.uid += 1
        nm = "%s_%d" % (name, self.uid)
        t = self.scopes[-1].enter_context(self.nc.sbuf_tensor(nm, list(shape), dt))
        return T(t, Buf(nm))

    def ring(self, name, shape, dt, n):
        return Ring([self.sb(name, shape, dt) for _ in range(n)])

    def push(self):
        s = ExitStack()
        self.scopes.append(s)
        return s

    def pop(self):
        self.S.barrier()
        self.S.flush()
        s = self.scopes.pop()
        s.close()

    def op(self, eng, fn, R=(), W=()):
        return self.S.op(eng, fn, R, W)

    def dma(self, fn, R=(), W=(), q="sp", final=False):
        return self.S.dma(q, fn, R, W, final)

    def load(self, dst, dst_ap, src_ap, q="sp", slow=False):
        if slow:
            self.dma(lambda e: e.dma_start(out=dst_ap, in_=src_ap, allow_slow_non_contiguous=True), W=[dst], q=q)
        else:
            self.dma(lambda e: e.dma_start(out=dst_ap, in_=src_ap), W=[dst], q=q)

    def store(self, dst_ap, src, src_ap):
        self.dma(lambda e: e.dma_start(out=dst_ap, in_=src_ap), R=[src], final=True)

    def wload(self, name, src, K, N, c0=0):
        kc = K // 128
        w = self.sb(name, [128, kc, N], BF16)
        for c in range(kc):
            self.load(w, w[:, c, :], src[c * 128:(c + 1) * 128, c0:c0 + N], q="pool")
        return w

    def gbload(self, name, src, n):
        g = self.sb(name, [128, n], F32)
        self.load(g, g[:, :], src[0:1, 0:n].partition_broadcast(128))
        return g

    def mm(self, ps, ps_ap, lhsT, lhsT_ap, rhs, rhs_ap, start, stop):
        self.op("pe", lambda e: e.matmul(ps_ap, lhsT=lhsT_ap, rhs=rhs_ap, start=start, stop=stop),
                R=[lhsT, rhs], W=[ps])

    def rstd(self, src, src_ap, n, Dn):
        sm = self.small.next()
        jk = self.junk
        self.op("act", lambda e: e.activation(out=jk[:, 0:n], in_=src_ap, func=AF.Square, accum_out=sm[:, 0:1]),
                R=[src], W=[jk, sm])
        self.op("act", lambda e: e.activation(out=sm[:, 1:2], in_=sm[:, 0:1], func=AF.Sqrt, bias=self.epsb[:, 0:1],
                                              scale=1.0 / Dn), R=[sm, self.epsb], W=[sm])
        self.op("dve", lambda e: e.reciprocal(out=sm[:, 2:3], in_=sm[:, 1:2]), R=[sm], W=[sm])
        return sm, sm[:, 2:3]

    def rms_tok(self, src, src_ap, n, gb, gb_ap, dst, dst_ap):
        sm, col = self.rstd(src, src_ap, n, n)
        self.op("dve", lambda e: e.scalar_tensor_tensor(out=dst_ap, in0=src_ap, scalar=col, in1=gb_ap,
                                                        op0=ALU.mult, op1=ALU.mult), R=[src, sm, gb], W=[dst])

    def transposes(self, src, src_ap_fn, n, dst, dst_ap, rows=128, eng="act"):
        pT = self.psT
        for j in range(n):
            ap = src_ap_fn(j)
            self.op("pe", lambda e, ap=ap, j=j: e.transpose(out=pT[:, j * 128:j * 128 + rows], in_=ap,
                                                             identity=self.ident[0:rows, 0:rows]),
                    R=[src, self.ident], W=[pT])
        view = pT[:, 0:n * 128].rearrange("p (c k) -> p c k", c=n)[:, :, 0:rows]
        if eng == "act":
            self.op("act", lambda e: e.copy(out=dst_ap, in_=view), R=[pT], W=[dst])
        else:
            self.op("dve", lambda e: e.tensor_copy(out=dst_ap, in_=view), R=[pT], W=[dst])

    def normT(self, xt, gb, hb, hT):
        self.rms_tok(xt, xt[:, :], D, gb, gb[:, :], hb, hb[:, :])
        self.transposes(hb, lambda j: hb[:, j * 128:(j + 1) * 128], 8, hT, hT[:, :, :])

    def proj(self, ps, ps_ap, hT, w, c0, n, kc=8):
        for c in range(kc):
            self.mm(ps, ps_ap, hT, hT[:, c, :], w, w[:, c, c0:c0 + n], c == 0, c == kc - 1)

    def build(self):
        nc = self.nc
        I = {}
        O = {}

        def di(name, shape, dt=F32):
            I[name] = self.din(name, shape, dt)

        def do(name, shape, dt=F32):
            O[name] = self.dout(name, shape, dt)
        di("x_all", [4096, D]); di("x_own", [NTOK, D])
        di("ckv_c", [4096, 256]); di("kr_c", [4096, 32])
        di("memk_c", [1024, 512]); di("memv_c", [1024, 512]); di("mem_p", [256, D])
        di("w_in", [D, 4768]); di("w_uq", [384, 1536]); di("w_uk", [256, 1024]); di("w_uv", [256, 1024])
        di("w_oa", [D, D]); di("w_ob", [D, D]); di("w_o", [D, D])
        di("w_cq", [D, 512]); di("w_ck", [D, 512]); di("w_cv", [D, 512]); di("w_co", [512, D])
        di("w_pq", [D, 2048]); di("sub_keys", [2048, 128]); di("peer_u", [16384, D]); di("peer_v", [16384, D])
        for g, n in [("g_mix", D), ("g_q_lat", 384), ("g_qn", 64), ("g_qr", 32), ("g_kv_lat", 256), ("g_kr", 32),
                     ("g_kn", 64), ("g_gm", D), ("g_xattn", D), ("g_mem", D), ("g_cq", 128), ("g_ck", 128),
                     ("g_ffn", D)]:
            di(g, [1, n])
        di("w_s", [1024, 128]); di("b_s", [8, 128])
        di("idn", [128, 128]); di("onesq", [96, 96]); di("ones128", [128, 128]); di("rotm", [96, 96])
        di("cmask", [128, 255]); di("trilm", [128, 128]); di("trilms", [128, 128])
        di("ropek", [128, 33, 32]); di("cosq", [32, NTOK]); di("sinq", [32, NTOK]); di("masks", [4, 128, 128])
        do("y_own", [NTOK, D]); do("ckv_p", [4096, 256]); do("kr_p", [4096, 32])
        do("mk_p", [256, 512]); do("mv_p", [256, 512])
        do("ckv_s", [64, 256]); do("kr_s", [64, 32]); do("vg_s", [64, D])
        self.I, self.O = I, O

        def ps(name, shape, dt):
            t = self.st.enter_context(nc.psum_tensor(name, shape, dt))
            return t
        self.psT = T(ps("psT", [128, 1024], BF16), Buf("psT"))
        tA = ps("psA", [128, 1024], F32); tB = ps("psB", [128, 1024], F32); tC = ps("psC", [128, 1024], F32)
        tD = ps("psD", [128, 512], F32)
        self.A = [T(tA, Buf("A0")), T(tA, Buf("A1"))]
        self.B = [T(tB, Buf("B0")), T(tB, Buf("B1"))]
        self.C = [T(tC, Buf("C0")), T(tC, Buf("C1"))]
        self.Dp = T(tD, Buf("D"))

        self.ident = self.sb("ident", [128, 128], BF16)
        self.load(self.ident, self.ident[:, :], I["idn"], q="pool")
        self.epsb = self.sb("epsb", [128, 1], F32)
        self.op("dve", lambda e: e.memset(self.epsb[:, :], EPS), W=[self.epsb])
        self.mscr = T(nc.dram_tensor("m_scr", [NTOK, D], BF16).ap(), Buf("mscr"))
        self.junk = self.sb("junk", [128, D], BF16)
        self.small = self.ring("small", [128, 8], F32, 4)
        self.X = self.ring("X", [128, D], F32, 2)
        self.HB = self.sb("HB", [128, D], BF16)
        self.HT = self.ring("HT", [128, 8, 128], BF16, 1)

        self.uvtab = T(nc.dram_tensor("uv_bf", [16384, 2 * D], BF16).ap(), Buf("uvtab"))
        self.hscr = T(nc.dram_tensor("h_scr", [NTOK, D], BF16).ap(), Buf("hscr"))
        self._conv_pending = True
        ph = self.dbg.get("phases", "1234")
        self.push()
        self.o_all = self.sb("o_all", [128, NOWN, D], BF16)
        self.op("pool", lambda e: e.memset(self.o_all[:, 16, :], 0.0), W=[self.o_all])
        self.gb_mix = self.gbload("gb_mix", I["g_mix"], D)
        self.push()
        ckvT = self.sb("ckvT", [128, 2, NK], BF16)
        KT = self.sb("KT", [96, NK], BF16)
        cqnT = self.sb("cqnT", [128, 3, NTOK], BF16)
        if "1" in ph:
            self.phase1(ckvT, KT, cqnT)
        if "2" in ph:
            self.phase2(ckvT, KT, cqnT)
        self.pop()
        if "3" in ph:
            self.phase3a()
        self.pop()
        self.emit_conv()
        if "4" in ph:
            self.phase3b()
        self.S.barrier()
        self.S.flush(last=True)
        self.st.close()
        return nc

    def emit_conv(self):
        if not self._conv_pending:
            return
        self._conv_pending = False
        I = self.I
        RCH = 1024
        for r in range(0, 16384, RCH):
            for which, nm in ((0, "peer_u"), (1, "peer_v")):
                self.dma(lambda e, r=r, which=which, nm=nm: e.dma_start(out=self.uvtab.t[r:r + RCH, which * D:(which + 1) * D],
                                                                         in_=I[nm][r:r + RCH, :]), W=[self.uvtab], q="conv")

    def phase1(self, ckvT, KT, cqnT):
        I, O = self.I, self.O
        self.push()
        Wkv = self.wload("Wkv", I["w_in"], D, 288, IN_KV)
        Wq = self.wload("Wq", I["w_in"], D, 384, IN_Q)
        gb_kv = self.sb("gb_kv", [128, 288], F32)
        self.load(gb_kv, gb_kv[:, 0:256], I["g_kv_lat"][0:1, 0:256].partition_broadcast(128))
        self.load(gb_kv, gb_kv[:, 256:288], I["g_kr"][0:1, 0:32].partition_broadcast(128))
        gb_ql = self.gbload("gb_ql", I["g_q_lat"], 384)
        ropek = self.sb("ropek", [128, 33, 32], F32)
        self.load(ropek, ropek[:, :, :], I["ropek"])
        kvo_r = self.ring("kvo", [128, 288], F32, 2)
        krn_r = self.ring("krn", [128, 64], F32, 2)
        kvb_r = self.ring("kvb", [128, 384], BF16, 2)
        for t in kvb_r.tiles:
            self.op("pool", lambda e, t=t: e.memset(t[:, :], 0.0), W=[t])
        cqb = self.sb("cqb", [128, 384], BF16)
        pT = self.psT

        def finish_kv(kvo, kind, idx):
            kvb = kvb_r.next()
            self.op("act", lambda e: e.copy(out=kvb[:, 0:256], in_=kvo[:, 0:256]), R=[kvo], W=[kvb])
            self.op("act", lambda e: e.copy(out=kvb[:, 320:352], in_=kvo[:, 256:288]), R=[kvo], W=[kvb])
            for j in range(3):
                self.op("pe", lambda e, j=j: e.transpose(out=pT[:, j * 128:(j + 1) * 128], in_=kvb[:, j * 128:(j + 1) * 128],
                                                       identity=self.ident[:, :]), R=[kvb, self.ident], W=[pT])
            if kind == "snew":
                dst = ckvT[:, :, NKP:NK].rearrange("p c (b k) -> p c b k", b=4)[:, :, :, 1024:1040]
                src = pT[:, 0:256].rearrange("p (c b k) -> p c b k", c=2, b=8)[:, :, 0:4, :]
                self.op("act", lambda e: e.copy(out=dst, in_=src), R=[pT], W=[ckvT])
                dstk = KT[64:96, NKP:NK].rearrange("p (b k) -> p b k", b=4)[:, :, 1024:1040]
                srck = pT[64:96, 256:320].rearrange("p (b k) -> p b k", b=4)
                self.op("act", lambda e: e.copy(out=dstk, in_=srck), R=[pT], W=[KT])
            else:
                c0 = idx * 128 if kind == "p" else NKP + (idx // 8) * 1040 + (idx % 8) * 128
                src = pT[:, 0:256].rearrange("p (c k) -> p c k", c=2)
                if "ckv" not in self.dbg.get("fk_skip", ""):
                    self.op("act", lambda e: e.copy(out=ckvT[:, :, c0:c0 + 128], in_=src), R=[pT], W=[ckvT])
                if "kt" not in self.dbg.get("fk_skip", ""):
                    self.op("act", lambda e: e.copy(out=KT[64:96, c0:c0 + 128], in_=pT[64:96, 256:384]), R=[pT], W=[KT])

        for i in self.dbg.get("p1_new", list(range(33))):
            xt = self.X.next()
            src = I["x_all"][i * 128:(i + 1) * 128, :] if i < 32 else I["x_own"][16 * 128:17 * 128, :]
            self.load(xt, xt[:, :], src)
            cut = self.dbg.get("p1_cut", 9)
            if cut < 2:
                continue
            hT = self.HT.next()
            self.normT(xt, self.gb_mix, self.HB, hT)
            if cut < 3:
                continue
            zp = self.A[i % 2]
            zoff = (i % 2) * 512
            z = zp.t[:, zoff:zoff + 288]
            self.proj(zp, z, hT, Wkv, 0, 288)
            if cut < 4:
                continue
            kvo = kvo_r.next()
            krn = krn_r.next()
            self.rms_tok(zp, zp.t[:, zoff:zoff + 256], 256, gb_kv, gb_kv[:, 0:256], kvo, kvo[:, 0:256])
            self.rms_tok(zp, zp.t[:, zoff + 256:zoff + 288], 32, gb_kv, gb_kv[:, 256:288], krn, krn[:, 0:32])
            cs = ropek[:, i, 0:16]
            sn = ropek[:, i, 16:32]
            self.op("dve", lambda e, krn=krn, cs=cs: e.tensor_tensor(out=krn[:, 32:48], in0=krn[:, 0:16], in1=cs, op=ALU.mult), R=[krn, ropek], W=[krn])
            self.op("dve", lambda e, krn=krn, sn=sn: e.tensor_tensor(out=krn[:, 48:64], in0=krn[:, 16:32], in1=sn, op=ALU.mult), R=[krn, ropek], W=[krn])
            self.op("dve", lambda e, krn=krn, kvo=kvo: e.tensor_tensor(out=kvo[:, 256:272], in0=krn[:, 32:48], in1=krn[:, 48:64], op=ALU.subtract), R=[krn], W=[kvo])
            self.op("dve", lambda e, krn=krn, sn=sn: e.tensor_tensor(out=krn[:, 32:48], in0=krn[:, 0:16], in1=sn, op=ALU.mult), R=[krn, ropek], W=[krn])
            self.op("dve", lambda e, krn=krn, cs=cs: e.tensor_tensor(out=krn[:, 48:64], in0=krn[:, 16:32], in1=cs, op=ALU.mult), R=[krn, ropek], W=[krn])
            self.op("dve", lambda e, krn=krn, kvo=kvo: e.tensor_tensor(out=kvo[:, 272:288], in0=krn[:, 32:48], in1=krn[:, 48:64], op=ALU.add), R=[krn], W=[kvo])
            if i < 32:
                self.store(O["ckv_p"][i * 128:(i + 1) * 128, :], kvo, kvo[:, 0:256])
                self.store(O["kr_p"][i * 128:(i + 1) * 128, :], kvo, kvo[:, 256:288])
                if cut < 5:
                    continue
                finish_kv(kvo, "p", i)
            else:
                self.store(O["ckv_s"][:, :], kvo, kvo[0:64, 0:256])
                self.store(O["kr_s"][:, :], kvo, kvo[0:64, 256:288])
                finish_kv(kvo, "snew", 0)
        for idx in range(self.dbg.get("p1_cache", 32)):
            kvo = kvo_r.next()
            self.load(kvo, kvo[:, 0:256], I["ckv_c"][idx * 128:(idx + 1) * 128, :])
            self.load(kvo, kvo[:, 256:288], I["kr_c"][idx * 128:(idx + 1) * 128, :])
            finish_kv(kvo, "scache", idx)
        for i in range(self.dbg.get("p1_q", NOWN)):
            xt = self.X.next()
            self.load(xt, xt[:, :], I["x_own"][i * 128:(i + 1) * 128, :])
            hT = self.HT.next()
            self.normT(xt, self.gb_mix, self.HB, hT)
            zp = self.A[i % 2]
            zoff = (i % 2) * 512
            self.proj(zp, zp.t[:, zoff:zoff + 384], hT, Wq, 0, 384)
            self.rms_tok(zp, zp.t[:, zoff:zoff + 384], 384, gb_ql, gb_ql[:, :], cqb, cqb[:, :])
            self.transposes(cqb, lambda j: cqb[:, j * 128:(j + 1) * 128], 3, cqnT, cqnT[:, :, i * 128:(i + 1) * 128])
        self.pop()

    def phase2(self, ckvT, KT, cqnT):
        I, O = self.I, self.O
        self.push()
        wuq = self.wload("wuq", I["w_uq"], 384, 1536)
        wuk = self.wload("wuk", I["w_uk"], 256, 1024)
        wuv = self.wload("wuv", I["w_uv"], 256, 1024)
        self.emit_conv()
        onesq = self.sb("onesq", [96, 96], BF16)
        self.load(onesq, onesq[:, :], I["onesq"], q="pool")
        rotm = self.sb("rotm", [96, 96], BF16)
        self.load(rotm, rotm[:, :], I["rotm"], q="pool")
        gq = self.sb("gq", [96, 2], F32)
        self.load(gq, gq[0:64, 0:1], I["g_qn"][0:1, 0:64].rearrange("o d -> d o"))
        self.load(gq, gq[64:96, 0:1], I["g_qr"][0:1, 0:32].rearrange("o d -> d o"))
        self.op("dve", lambda e: e.tensor_scalar(out=gq[:, 1:2], in0=gq[:, 0:1], scalar1=MLA_SCALE, scalar2=None, op0=ALU.mult),
                R=[gq], W=[gq])
        gkn = self.sb("gkn", [64, 1], F32)
        self.load(gkn, gkn[:, :], I["g_kn"][0:1, 0:64].rearrange("o d -> d o"))
        cosq = self.sb("cosq", [96, NTOK], F32)
        sinq = self.sb("sinq", [96, NTOK], F32)
        self.load(cosq, cosq[64:96, :], I["cosq"])
        self.load(sinq, sinq[64:96, :], I["sinq"])
        maskT = self.sb("maskT", [128, 4, 128], BF16)
        self.load(maskT, maskT[:, :, :], I["masks"].rearrange("m k q -> k m q"), q="pool")
        NVT = 32 + 36
        V = self.sb("V", [128, NVT, 72], BF16)
        self.op("pool", lambda e: e.memset(V[:, :, :], 0.0), W=[V])
        self.op("pool", lambda e: e.memset(V[:, :, 64:65], 1.0), W=[V])
        QT = self.sb("QT", [96, NTOK], BF16)
        sq_r = self.ring("sq", [96, 512], BF16, 2)
        rs_r = self.ring("rs", [96, 512], F32, 2)
        t1 = self.sb("t1", [96, 512], F32)
        t2 = self.sb("t2", [96, 512], F32)
        pt_r = self.ring("pt", [128, 4, 128], BF16, 3)
        ptS = [self.sb("ptS", [128, 9, 64], BF16) for _ in range(4)]
        for t in ptS:
            self.op("pool", lambda e, t=t: e.memset(t[:, :, :], 0.0), W=[t])
        A, B, C, Dp = self.A, self.B, self.C, self.Dp
        eps96 = self.epsb

        def fm_norm(ps, ps_ap, rows, n, ones_ap, gcol, gcol_ap, dst, dst_ap):
            sq = sq_r.next()
            rs = rs_r.next()
            self.op("act", lambda e: e.activation(out=sq[0:rows, 0:n], in_=ps_ap, func=AF.Square), R=[ps], W=[sq])
            self.mm(B[0], B[0].t[0:rows, 0:n], onesq, ones_ap, sq, sq[0:rows, 0:n], True, True)
            self.op("act", lambda e: e.activation(out=rs[0:rows, 0:n], in_=B[0].t[0:rows, 0:n], func=AF.Sqrt,
                                                  bias=eps96[0:rows, 0:1], scale=1.0), R=[B[0], eps96], W=[rs])
            self.op("dve", lambda e: e.reciprocal(out=rs[0:rows, 0:n], in_=rs[0:rows, 0:n]), R=[rs], W=[rs])
            self.op("dve", lambda e: e.scalar_tensor_tensor(out=dst_ap, in0=ps_ap, scalar=gcol_ap, in1=rs[0:rows, 0:n],
                                                            op0=ALU.mult, op1=ALU.mult), R=[ps, gcol, rs], W=[dst])

        kchunks = [(c0, min(512, NK - c0)) for c0 in range(0, NK, 512)]
        qchunks = [(c0, min(512, NTOK - c0)) for c0 in range(0, NTOK, 512)]
        vt = [(i, i * 128, 128) for i in range(32)]
        for bt in range(4):
            for j in range(9):
                vt.append((32 + bt * 9 + j, NKP + bt * 1040 + j * 128, 128 if j < 8 else 16))
        nh = self.dbg.get("nheads", 16)
        pi = 0
        for h in range(nh):
            for (c0, n) in kchunks:
                P = A[pi % 2]; po = (pi % 2) * 512; pi += 1
                for c in range(2):
                    self.mm(P, P.t[0:64, po:po + n], wuk, wuk[:, c, h * 64:(h + 1) * 64], ckvT, ckvT[:, c, c0:c0 + n], c == 0, c == 1)
                fm_norm(P, P.t[0:64, po:po + n], 64, n, onesq[0:64, 0:64], gkn, gkn[:, 0:1], KT, KT[0:64, c0:c0 + n])
            p2c = self.dbg.get("p2_cut", 9)
            if p2c < 2:
                continue
            for g0 in range(0, NVT, 8):
                P = A[pi % 2]; po = (pi % 2) * 512; pi += 1
                grp = vt[g0:g0 + 8]
                for (ti, c0, rows) in grp:
                    jj = ti - g0
                    for c in range(2):
                        self.mm(P, P.t[0:rows, po + jj * 64:po + jj * 64 + 64], ckvT, ckvT[:, c, c0:c0 + rows],
                                wuv, wuv[:, c, h * 64:(h + 1) * 64], c == 0, c == 1)
                ng = len(grp)
                src = P.t[:, po:po + ng * 64].rearrange("p (j d) -> p j d", j=ng)
                self.op("act", lambda e, src=src, g0=g0, ng=ng: e.copy(out=V[:, g0:g0 + ng, 0:64], in_=src), R=[P], W=[V])
            if p2c < 3:
                continue
            for (c0, n) in qchunks:
                P = A[pi % 2]; po = (pi % 2) * 512; pi += 1
                for c in range(3):
                    self.mm(P, P.t[0:96, po:po + n], wuq, wuq[:, c, h * 96:(h + 1) * 96], cqnT, cqnT[:, c, c0:c0 + n], c == 0, c == 2)
                fm_norm(P, P.t[0:96, po:po + n], 96, n, onesq[:, :], gq, gq[:, 1:2], QT, QT[0:96, c0:c0 + n])
                self.mm(B[1], B[1].t[0:96, 512:512 + n], rotm, rotm[:, :], QT, QT[0:96, c0:c0 + n], True, True)
                self.op("dve", lambda e, c0=c0, n=n: e.tensor_tensor(out=t1[64:96, 0:n], in0=QT[64:96, c0:c0 + n], in1=cosq[64:96, c0:c0 + n], op=ALU.mult),
                        R=[QT, cosq], W=[t1])
                self.op("dve", lambda e, c0=c0, n=n: e.tensor_tensor(out=t2[64:96, 0:n], in0=B[1].t[64:96, 512:512 + n], in1=sinq[64:96, c0:c0 + n], op=ALU.mult),
                        R=[B[1], sinq], W=[t2])
                self.op("dve", lambda e, c0=c0, n=n: e.tensor_tensor(out=QT[64:96, c0:c0 + n], in0=t1[64:96, 0:n], in1=t2[64:96, 0:n], op=ALU.add),
                        R=[t1, t2], W=[QT])
            if p2c < 4:
                continue
            si = 0
            groups = []
            for s in range(16):
                nkb = 4 * (s // 2) + (2 if s % 2 == 0 else 4)
                for g0 in range(0, nkb, 4):
                    groups.append((s, nkb, list(range(g0, min(g0 + 4, nkb)))))
            Obank = [(Dp, 0), (B[1], 512)]

            def qk(gi):
                s, nkb, blks = groups[gi]
                Sp = C[gi % 2]; so = (gi % 2) * 512
                for j, kb in enumerate(blks):
                    self.mm(Sp, Sp.t[:, so + j * 128:so + (j + 1) * 128], KT, KT[0:96, kb * 128:(kb + 1) * 128],
                            QT, QT[0:96, s * 128:(s + 1) * 128], True, True)
            qk(0)
            for gi, (s, nkb, blks) in enumerate(groups):
                if gi + 1 < len(groups):
                    qk(gi + 1)
                Sp = C[gi % 2]; so = (gi % 2) * 512
                Ops, oo = Obank[s % 2]
                pt = pt_r.next()
                nb = len(blks)
                self.op("act", lambda e, Sp=Sp, so=so, nb=nb, pt=pt: e.activation(
                    out=pt[:, 0:nb, :], in_=Sp.t[:, so:so + nb * 128].rearrange("p (j q) -> p j q", j=nb), func=AF.Exp),
                    R=[Sp], W=[pt])
                for j, kb in enumerate(blks):
                    if kb >= nkb - 2:
                        mi = (0 if s % 2 == 0 else 2) + (kb - (nkb - 2))
                        self.op("dve", lambda e, pt=pt, j=j, mi=mi: e.tensor_tensor(out=pt[:, j, :], in0=pt[:, j, :], in1=maskT[:, mi, :], op=ALU.mult),
                                R=[pt, maskT], W=[pt])
                for j, kb in enumerate(blks):
                    self.mm(Ops, Ops.t[:, oo:oo + 72], pt, pt[:, j, :], V, V[:, kb, :], kb == 0, kb == nkb - 1)
                if blks[-1] == nkb - 1:
                    sm = self.small.next()
                    self.op("dve", lambda e, sm=sm, Ops=Ops, oo=oo: e.reciprocal(out=sm[:, 0:1], in_=Ops.t[:, oo + 64:oo + 65]), R=[Ops], W=[sm])
                    self.op("dve", lambda e, sm=sm, Ops=Ops, oo=oo, s=s, h=h: e.tensor_scalar(out=self.o_all[:, s, h * 64:(h + 1) * 64], in0=Ops.t[:, oo:oo + 64],
                                                                                   scalar1=sm[:, 0:1], scalar2=None, op0=ALU.mult),
                            R=[Ops, sm], W=[self.o_all])
            si = len(groups)
            if p2c < 5:
                continue
            Ops = Dp
            for bt in range(4):
                base = NKP + bt * 1040
                Sp = C[si % 2]; so = (si % 2) * 512; si += 1
                qc = 2048 + bt * 16
                for j in range(8):
                    self.mm(Sp, Sp.t[:, so + j * 16:so + (j + 1) * 16], KT, KT[0:96, base + j * 128:base + (j + 1) * 128],
                            QT, QT[0:96, qc:qc + 16], True, True)
                self.mm(Sp, Sp.t[0:16, so + 128:so + 144], KT, KT[0:96, base + 1024:base + 1040], QT, QT[0:96, qc:qc + 16], True, True)
                p = ptS[bt]
                self.op("act", lambda e, Sp=Sp, so=so, p=p, bt=bt: e.activation(
                    out=p[:, 0:8, bt * 16:(bt + 1) * 16], in_=Sp.t[:, so:so + 128].rearrange("p (j q) -> p j q", j=8), func=AF.Exp),
                    R=[Sp], W=[p])
                self.op("act", lambda e, Sp=Sp, so=so, p=p, bt=bt: e.activation(
                    out=p[0:16, 8, bt * 16:(bt + 1) * 16], in_=Sp.t[0:16, so + 128:so + 144], func=AF.Exp), R=[Sp], W=[p])
                for j in range(9):
                    rows = 128 if j < 8 else 16
                    self.mm(Ops, Ops.t[0:64, 0:72], p, p[0:rows, j, :], V, V[0:rows, 32 + bt * 9 + j, :],
                            bt == 0 and j == 0, bt == 3 and j == 8)
            sm = self.small.next()
            self.op("dve", lambda e, sm=sm, Ops=Ops: e.reciprocal(out=sm[0:64, 0:1], in_=Ops.t[0:64, 64:65]), R=[Ops], W=[sm])
            self.op("dve", lambda e, sm=sm, Ops=Ops, h=h: e.tensor_scalar(out=self.o_all[0:64, 16, h * 64:(h + 1) * 64], in0=Ops.t[0:64, 0:64],
                                                                     scalar1=sm[0:64, 0:1], scalar2=None, op0=ALU.mult),
                    R=[Ops, sm], W=[self.o_all])
        self.pop()

    def phase3a(self):
        I, O = self.I, self.O
        self.push()
        Wz = self.wload("Wz", I["w_in"], D, 4096, IN_Z)
        woa = self.wload("woa", I["w_oa"], D, D)
        wob = self.wload("wob", I["w_ob"], D, D)
        gb_gm = self.gbload("gb_gm", I["g_gm"], D)
        trilm = self.sb("trilm", [128, 128], F32)
        trilms = self.sb("trilms", [128, 128], F32)
        self.load(trilm, trilm[:, :], I["trilm"])
        self.load(trilms, trilms[:, :], I["trilms"])
        wsf = self.sb("wsf", [128, 8, 128], F32)
        wsb = self.sb("wsb", [128, 8, 128], BF16)
        WmT = [self.sb("WmT", [128, 8, 128], BF16) for _ in range(2)]
        bT = [self.sb("bT", [128, 8], F32) for _ in range(2)]
        ws_tgs = I["w_s"].rearrange("(g t) s -> t g s", g=8)
        for k in range(2):
            if k == 0:
                self.load(wsf, wsf[:, :, :], ws_tgs)
                self.load(bT[0], bT[0][:, :], I["b_s"].rearrange("g t -> t g"), slow=True)
                tm = trilm
            else:
                self.op("pool", lambda e: e.memset(wsf[:, :, :], 0.0), W=[wsf])
                self.op("pool", lambda e: e.memset(bT[1][:, :], 0.0), W=[bT[1]])
                for bt in range(4):
                    self.load(wsf, wsf[bt * 16:(bt + 1) * 16, :, bt * 16:(bt + 1) * 16], ws_tgs[0:16, :, 0:16])
                    self.load(bT[1], bT[1][bt * 16:(bt + 1) * 16, :], I["b_s"][:, 0:16].rearrange("g t -> t g"), slow=True)
                tm = trilms
            self.op("dve", lambda e, tm=tm: e.tensor_tensor(out=wsb[:, :, :], in0=wsf[:, :, :],
                                                           in1=tm[:, :].unsqueeze(1).to_broadcast([128, 8, 128]), op=ALU.mult),
                    R=[wsf, tm], W=[wsb])
            self.transposes(wsb, lambda j: wsb[:, j, :], 8, WmT[k], WmT[k][:, :, :])
        u = self.sb("u", [128, D], F32)
        gv = self.sb("gv", [128, D], F32)
        vg = self.sb("vg", [128, D], F32)
        vgb = self.sb("vgb", [128, D], BF16)
        sig = self.sb("sig", [128, D], F32)
        um = self.sb("um", [128, D], BF16)
        umT = self.sb("umT", [128, 8, 128], BF16)
        oT = self.sb("oT", [128, 8, 128], BF16)
        tt = self.sb("tt", [128, D], F32)
        mo_r = self.ring("mo", [128, D], BF16, 2)
        A, B, C = self.A, self.B, self.C
        zi = 0
        for i in range(NOWN):
            k = 0 if i < 16 else 1
            xt = self.X.next()
            self.load(xt, xt[:, :], I["x_own"][i * 128:(i + 1) * 128, :])
            hT = self.HT.next()
            self.normT(xt, self.gb_mix, self.HB, hT)

            def zchunk(col, func, dst, dst_ap):
                nonlocal zi
                P = A[zi % 2]; po = (zi % 2) * 512; zi += 1
                self.proj(P, P.t[:, po:po + 512], hT, Wz, col, 512)
                self.op("act", lambda e: e.activation(out=dst_ap, in_=P.t[:, po:po + 512], func=func), R=[P], W=[dst])
            for n in range(2):
                zchunk(n * 512, AF.Gelu, u, u[:, n * 512:(n + 1) * 512])
            for n in range(2):
                zchunk(1024 + n * 512, AF.Gelu, gv, gv[:, n * 512:(n + 1) * 512])
            self.rms_tok(gv, gv[:, :], D, gb_gm, gb_gm[:, :], vg, vg[:, :])
            if i == 16:
                self.store(O["vg_s"][:, :], vg, vg[0:64, :])
            self.op("act", lambda e: e.copy(out=vgb[:, :], in_=vg[:, :]), R=[vg], W=[vgb])
            for g in range(8):
                Pm = B[g // 4]
                self.mm(Pm, Pm.t[:, g * 128:(g + 1) * 128], WmT[k], WmT[k][:, g, :], vgb, vgb[:, g * 128:(g + 1) * 128], True, True)
            for g in range(8):
                Pm = B[g // 4]
                self.op("dve", lambda e, g=g, Pm=Pm, k=k: e.scalar_tensor_tensor(
                    out=um[:, g * 128:(g + 1) * 128], in0=Pm.t[:, g * 128:(g + 1) * 128], scalar=bT[k][:, g:g + 1],
                    in1=u[:, g * 128:(g + 1) * 128], op0=ALU.add, op1=ALU.mult), R=[Pm, bT[k], u], W=[um])
            self.transposes(self.o_all, lambda j, i=i: self.o_all[:, i, j * 128:(j + 1) * 128], 8, oT, oT[:, :, :])
            for n in range(2):
                self.proj(C[n], C[n].t[:, n * 512:(n + 1) * 512], oT, woa, n * 512, 512)
            for n in range(2):
                zchunk(2048 + n * 512, AF.Sigmoid, sig, sig[:, n * 512:(n + 1) * 512])
            for n in range(2):
                self.op("dve", lambda e, n=n: e.tensor_tensor(out=tt[:, n * 512:(n + 1) * 512], in0=sig[:, n * 512:(n + 1) * 512],
                                                           in1=C[n].t[:, n * 512:(n + 1) * 512], op=ALU.mult), R=[sig, C[n]], W=[tt])
            self.transposes(um, lambda j: um[:, j * 128:(j + 1) * 128], 8, umT, umT[:, :, :])
            for n in range(2):
                self.proj(C[n], C[n].t[:, n * 512:(n + 1) * 512], umT, wob, n * 512, 512)
            for n in range(2):
                zchunk(3072 + n * 512, AF.Sigmoid, sig, sig[:, n * 512:(n + 1) * 512])
            for n in range(2):
                self.op("dve", lambda e, n=n: e.tensor_tensor(out=sig[:, n * 512:(n + 1) * 512], in0=sig[:, n * 512:(n + 1) * 512],
                                                           in1=C[n].t[:, n * 512:(n + 1) * 512], op=ALU.mult), R=[sig, C[n]], W=[sig])
            mo = mo_r.next()
            self.op("dve", lambda e, mo=mo: e.tensor_tensor(out=mo[:, :], in0=sig[:, :], in1=tt[:, :], op=ALU.add),
                    R=[sig, tt], W=[mo])
            self.dma(lambda e, mo=mo, i=i: e.dma_start(out=self.mscr.t[i * 128:(i + 1) * 128, :], in_=mo[:, :]), R=[mo], W=[self.mscr])
        self.pop()

    def phase3b(self):
        I, O = self.I, self.O
        A, B, C, Dp, pT = self.A, self.B, self.C, self.Dp, self.psT
        self.push()
        wo = self.wload("wo", I["w_o"], D, D)
        wcq = self.wload("wcq", I["w_cq"], D, 512)
        wco = self.wload("wco", I["w_co"], 512, D)
        wpq = self.wload("wpq", I["w_pq"], D, 2048)
        gb_xa = self.gbload("gb_xa", I["g_xattn"], D)
        gb_ff = self.gbload("gb_ff", I["g_ffn"], D)
        ones128 = self.sb("ones128", [128, 128], BF16)
        self.load(ones128, ones128[:, :], I["ones128"], q="pool")
        cmask = self.sb("cmask", [128, 255], BF16)
        self.load(cmask, cmask[:, :], I["cmask"], q="pool")
        gcq = self.sb("gcq", [128, 2], F32)
        self.load(gcq, gcq[:, 0:1], I["g_cq"][0:1, 0:128].rearrange("o d -> d o"))
        self.op("dve", lambda e: e.tensor_scalar(out=gcq[:, 1:2], in0=gcq[:, 0:1], scalar1=MEM_SCALE, scalar2=None, op0=ALU.mult),
                R=[gcq], W=[gcq])
        memKT = [self.sb("memKT", [128, 4, 256], BF16) for _ in range(4)]
        memV = [self.sb("memV", [128, 2, 4, 136], BF16) for _ in range(4)]
        for t in memV:
            self.op("pool", lambda e, t=t: e.memset(t[:, :, :, :], 0.0), W=[t])
            self.op("pool", lambda e, t=t: e.memset(t[:, :, :, 128:129], 1.0), W=[t])
        kf = self.sb("kf", [128, 512], F32)
        kb16 = self.sb("kb16", [128, 512], BF16)
        skT = self.sb("skT", [128, 16, 128], BF16)

        self.push()
        wck = self.wload("wck", I["w_ck"], D, 512)
        wcv = self.wload("wcv", I["w_cv"], D, 512)
        gb_mem = self.gbload("gb_mem", I["g_mem"], D)
        gb_ck = self.gbload("gb_ck", I["g_ck"], 128)
        sq4 = self.sb("sq4", [128, 512], F32)
        skb = self.sb("skb", [128, 16, 128], BF16)
        self.load(skb, skb[:, :, :], I["sub_keys"].rearrange("(a n) d -> n a d", a=16), q="pool")
        for half in range(2):
            self.transposes(skb, lambda j, half=half: skb[:, half * 8 + j, :], 8, skT, skT[:, half * 8:(half + 1) * 8, :])
        for mt in range(2):
            xt = self.X.next()
            self.load(xt, xt[:, :], I["mem_p"][mt * 128:(mt + 1) * 128, :])
            hT = self.HT.next()
            self.normT(xt, gb_mem, self.HB, hT)
            P = A[0]
            self.proj(P, P.t[:, 0:512], hT, wck, 0, 512)
            sm = self.small.next()
            self.op("act", lambda e: e.activation(out=sq4[:, :], in_=P.t[:, 0:512], func=AF.Square), R=[P], W=[sq4])
            self.op("dve", lambda e, sm=sm: e.tensor_reduce(out=sm[:, 0:4], in_=sq4[:, :].rearrange("p (h d) -> p h d", h=4),
                                                          axis=AX.X, op=ALU.add), R=[sq4], W=[sm])
            self.op("act", lambda e, sm=sm: e.activation(out=sm[:, 4:8], in_=sm[:, 0:4], func=AF.Sqrt, bias=self.epsb[:, 0:1],
                                                       scale=1.0 / 128), R=[sm, self.epsb], W=[sm])
            self.op("dve", lambda e, sm=sm: e.reciprocal(out=sm[:, 4:8], in_=sm[:, 4:8]), R=[sm], W=[sm])
            self.op("dve", lambda e, sm=sm: e.tensor_tensor(out=kf[:, :].rearrange("p (h d) -> p h d", h=4),
                                                          in0=P.t[:, 0:512].rearrange("p (h d) -> p h d", h=4),
                                                          in1=sm[:, 4:8].unsqueeze(2).to_broadcast([128, 4, 128]), op=ALU.mult),
                    R=[P, sm], W=[kf])
            self.op("dve", lambda e: e.tensor_tensor(out=kf[:, :].rearrange("p (h d) -> p h d", h=4),
                                                   in0=kf[:, :].rearrange("p (h d) -> p h d", h=4),
                                                   in1=gb_ck[:, :].unsqueeze(1).to_broadcast([128, 4, 128]), op=ALU.mult),
                    R=[kf, gb_ck], W=[kf])
            self.store(O["mk_p"][mt * 128:(mt + 1) * 128, :], kf, kf[:, :])
            self.op("act", lambda e: e.copy(out=kb16[:, :], in_=kf[:, :]), R=[kf], W=[kb16])
            self.transposes(kb16, lambda j: kb16[:, j * 128:(j + 1) * 128], 4, memKT[0], memKT[0][:, :, mt * 128:(mt + 1) * 128])
            P2 = A[1]
            self.proj(P2, P2.t[:, 512:1024], hT, wcv, 0, 512)
            self.op("act", lambda e: e.copy(out=sq4[:, :], in_=P2.t[:, 512:1024]), R=[P2], W=[sq4])
            self.store(O["mv_p"][mt * 128:(mt + 1) * 128, :], sq4, sq4[:, :])
            self.op("dve", lambda e, mt=mt: e.tensor_copy(out=memV[0][:, mt, :, 0:128], in_=sq4[:, :].rearrange("p (h d) -> p h d", h=4)),
                    R=[sq4], W=[memV[0]])
        self.pop()

        def load_sample_mem():
            for bt in range(4):
                for mt in range(2):
                    r0 = bt * 256 + mt * 128
                    self.load(kf, kf[:, :], I["memk_c"][r0:r0 + 128, :])
                    self.op("act", lambda e: e.copy(out=kb16[:, :], in_=kf[:, :]), R=[kf], W=[kb16])
                    self.transposes(kb16, lambda j: kb16[:, j * 128:(j + 1) * 128], 4, memKT[bt], memKT[bt][:, :, mt * 128:(mt + 1) * 128])
                    xt = self.X.next()
                    self.load(xt, xt[:, 0:512], I["memv_c"][r0:r0 + 128, :])
                    self.op("dve", lambda e, xt=xt, bt=bt, mt=mt: e.tensor_copy(out=memV[bt][:, mt, :, 0:128],
                                                                           in_=xt[:, 0:512].rearrange("p (h d) -> p h d", h=4)),
                            R=[xt], W=[memV[bt]])

        TS = [(self.sb("x2", [128, D], F32), self.sb("h3b", [128, D], BF16), self.sb("idxT", [128, 128], I32), self.sb("gT", [128, 128], F32))
              for _ in range(2)]
        mi_r = self.ring("mi", [128, D], BF16, 1)
        mT = self.sb("mT", [128, 8, 128], BF16)
        sqc = self.sb("sqc", [128, 512], BF16)
        rsc = self.sb("rsc", [128, 512], F32)
        qcn = self.sb("qcn", [128, 4, 128], BF16)
        ptc = self.sb("ptc", [128, 8, 128], BF16)
        ptcS = [self.sb("ptcS", [128, 8, 128], BF16) for _ in range(4)]
        for t in ptcS:
            self.op("pool", lambda e, t=t: e.memset(t[:, :, :], 0.0), W=[t])
        oc = self.sb("oc", [128, 4, 128], BF16)
        ocT = self.sb("ocT", [128, 4, 128], BF16)
        h3T = self.sb("h3T", [128, 8, 128], BF16)
        qTb = self.sb("qTb", [128, 16, 128], BF16)
        sc = self.sb("sc", [128, 2048], F32)
        wks = [self.sb("wk", [128, 256], F32) for _ in range(2)]
        sv = self.sb("sv", [128, 16, 16], F32)
        si = self.sb("si", [128, 16, 16], U32)
        sif = self.sb("sif", [128, 16, 16], F32)
        ts = self.sb("ts", [128, 8, 16], F32)
        sel = self.sb("sel", [128, 8, 16], U32)
        self_f = self.sb("self_f", [128, 8, 16], F32)
        k1i = self.sb("k1i", [128, 8, 16], I32)
        k1f = self.sb("k1f", [128, 8, 16], F32)
        k2f = self.sb("k2f", [128, 8, 16], F32)
        iota16 = self.sb("iota16", [128, 16], F32)
        for k in range(16):
            self.op("pool", lambda e, k=k: e.memset(iota16[:, k:k + 1], float(k)), W=[iota16])
        eq = self.sb("eq", [128, 4, 16, 16], BF16)
        i12 = self.sb("i12", [128, 2, 128], F32)
        tb = self.sb("tb", [128, 3, 128], BF16)
        ex = self.sb("ex", [128, 8, 16], F32)
        i1T = self.sb("i1T", [128, 128], F32)
        i2T = self.sb("i2T", [128, 128], F32)
        NB = self.dbg.get("NB", 8)
        GUV = self.ring("GUV", [128, 2 * D], BF16, NB)
        wsel_r = self.ring("wsel", [128, 128], BF16, 4)
        djunk_r = self.ring("djunk", [128, D], BF16, 1)
        act_r = self.ring("actc", [128, 1], F32, 6)
        gl_r = self.ring("glc", [128, 2], F32, 6)

        svb = [T(sv.t, Buf("svb")) for _ in range(16)]
        sib = [T(si.t, Buf("sib")) for _ in range(16)]
        tsb = [T(ts.t, Buf("tsb")) for _ in range(8)]
        selb = [T(sel.t, Buf("selb")) for _ in range(8)]

        def pre(i):
            x2, h3b, idxT, gT = TS[i % 2]
            x1 = x2
            if i == 16:
                load_sample_mem()
            xt = self.X.next()
            yield self.load(xt, xt[:, :], I["x_own"][i * 128:(i + 1) * 128, :])
            mi_ = mi_r.next()
            yield self.dma(lambda e, mi_=mi_, i=i: e.dma_start(out=mi_[:, :], in_=self.mscr.t[i * 128:(i + 1) * 128, :]), R=[self.mscr], W=[mi_])
            yield from (None for _ in range(2))
            yield self.transposes(mi_, lambda j, mi_=mi_: mi_[:, j * 128:(j + 1) * 128], 8, mT, mT[:, :, :])
            yield from (None for _ in range(2))
            for n in range(2):
                yield self.proj(Dp, Dp.t[:, 0:512], mT, wo, n * 512, 512)
                yield self.op("dve", lambda e, n=n, xt=xt: e.tensor_tensor(out=x1[:, n * 512:(n + 1) * 512], in0=xt[:, n * 512:(n + 1) * 512],
                                                                 in1=Dp.t[:, 0:512], op=ALU.add), R=[xt, Dp], W=[x1])
            hT = self.HT.next()
            yield self.rms_tok(x1, x1[:, :], D, gb_xa, gb_xa[:, :], self.HB, self.HB[:, :])
            yield from (None for _ in range(3))
            yield self.transposes(self.HB, lambda j: self.HB[:, j * 128:(j + 1) * 128], 8, hT, hT[:, :, :])
            yield from (None for _ in range(2))
            P = Dp
            for hd in range(4):
                for c in range(8):
                    yield self.mm(P, P.t[:, hd * 128:(hd + 1) * 128], wcq, wcq[:, c, hd * 128:(hd + 1) * 128], hT, hT[:, c, :], c == 0, c == 7)
            yield self.op("act", lambda e: e.activation(out=sqc[:, :], in_=P.t[:, 0:512], func=AF.Square), R=[P], W=[sqc])
            yield self.op("act", lambda e: e.copy(out=kf[:, :], in_=P.t[:, 0:512]), R=[P], W=[kf])
            yield from (None for _ in range(2))
            yield self.mm(P, P.t[:, 0:512], ones128, ones128[:, :], sqc, sqc[:, :], True, True)
            yield self.op("act", lambda e: e.activation(out=rsc[:, :], in_=P.t[:, 0:512], func=AF.Sqrt, bias=self.epsb[:, 0:1], scale=1.0),
                    R=[P, self.epsb], W=[rsc])
            yield self.op("dve", lambda e: e.reciprocal(out=rsc[:, :], in_=rsc[:, :]), R=[rsc], W=[rsc])
            yield self.op("dve", lambda e: e.scalar_tensor_tensor(out=qcn[:, :, :].rearrange("p h t -> p (h t)"), in0=kf[:, :], scalar=gcq[:, 1:2],
                                                            in1=rsc[:, :], op0=ALU.mult, op1=ALU.mult), R=[kf, gcq, rsc], W=[qcn])
            yield from (None for _ in range(3))
            if i < 16:
                batches = [(0, 0, 128, ptc)]
            else:
                batches = [(bt, bt * 16, 16, ptcS[bt]) for bt in range(4)]
            for (mb, c0, n, p) in batches:
                for pair in range(2):
                    for hd in (2 * pair, 2 * pair + 1):
                        for mt in range(2):
                            jj = (hd % 2) * 2 + mt
                            yield self.mm(P, P.t[:, jj * 128:jj * 128 + n], memKT[mb], memKT[mb][:, hd, mt * 128:(mt + 1) * 128],
                                          qcn, qcn[:, hd, c0:c0 + n], True, True)
                    yield self.op("act", lambda e, pair=pair, p=p, c0=c0, n=n: e.activation(
                        out=p[:, pair * 4:(pair + 1) * 4, c0:c0 + n],
                        in_=P.t[:, 0:512].rearrange("p (j q) -> p j q", j=4)[:, :, 0:n], func=AF.Exp),
                        R=[P], W=[p])
            yield from (None for _ in range(2))
            nq = 128
            for pair in range(2):
                for hd in (2 * pair, 2 * pair + 1):
                    col = (hd % 2) * 256
                    nacc = len(batches) * 2
                    k = 0
                    for (mb, c0, n, p) in batches:
                        for mt in range(2):
                            yield self.mm(P, P.t[0:nq, col:col + 136], p, p[:, hd * 2 + mt, 0:nq], memV[mb], memV[mb][:, mt, hd, :], k == 0, k == nacc - 1)
                            k += 1
                sm = self.small.next()
                ocv = P.t[:, 0:512].rearrange("p (h c) -> p h c", h=2)
                yield self.op("dve", lambda e, sm=sm, ocv=ocv: e.tensor_scalar(out=sm[:, 0:2], in0=ocv[:, :, 128:129].rearrange("p h o -> p (h o)"),
                                                                     scalar1=1e-30, scalar2=None, op0=ALU.max), R=[P], W=[sm])
                yield self.op("dve", lambda e, sm=sm: e.reciprocal(out=sm[:, 0:2], in_=sm[:, 0:2]), R=[sm], W=[sm])
                yield self.op("dve", lambda e, sm=sm, ocv=ocv, pair=pair: e.tensor_tensor(out=oc[:, 2 * pair:2 * pair + 2, :], in0=ocv[:, :, 0:128],
                                                                                in1=sm[:, 0:2].unsqueeze(2).to_broadcast([128, 2, 128]), op=ALU.mult),
                        R=[P, sm], W=[oc])
            yield from (None for _ in range(2))
            yield self.transposes(oc, lambda j: oc[:, j, :], 4, ocT, ocT[:, :, :])
            yield from (None for _ in range(2))
            for n in range(2):
                yield self.proj(P, P.t[:, 0:512], ocT, wco, n * 512, 512, kc=4)
                yield self.op("dve", lambda e, n=n: e.tensor_tensor(out=x2[:, n * 512:(n + 1) * 512], in0=x1[:, n * 512:(n + 1) * 512],
                                                           in1=P.t[:, 0:512], op=ALU.add), R=[x1, P], W=[x2])
            yield self.rms_tok(x2, x2[:, :], D, gb_ff, gb_ff[:, :], h3b, h3b[:, :])
            yield from (None for _ in range(3))
            yield self.transposes(h3b, lambda j: h3b[:, j * 128:(j + 1) * 128], 8, h3T, h3T[:, :, :])
            yield from (None for _ in range(2))
            for r in range(4):
                for hp in range(r * 4, r * 4 + 4):
                    off = (hp % 4) * 128
                    for c in range(8):
                        yield self.mm(P, P.t[:, off:off + 128], wpq, wpq[:, c, hp * 128:(hp + 1) * 128], h3T, h3T[:, c, :], c == 0, c == 7)
                yield self.op("act", lambda e, r=r: e.copy(out=qTb[:, r * 4:r * 4 + 4, :].rearrange("p a t -> p (a t)"), in_=P.t[:, 0:512]),
                              R=[P], W=[qTb])
            for r in range(4):
                for hp in range(r * 4, r * 4 + 4):
                    off = (hp % 4) * 128
                    yield self.mm(P, P.t[:, off:off + 128], qTb, qTb[:, hp, :], skT, skT[:, hp, :], True, True)
                yield self.op("act", lambda e, r=r: e.copy(out=sc[:, r * 512:(r + 1) * 512], in_=P.t[:, 0:512]), R=[P], W=[sc])

            def top16_pair(items):
                for stage in range(5):
                    for (src_ap, width, vout, iout, wkb, vb, ib) in items:
                        if stage == 0:
                            self.op("dve", lambda e, vout=vout, src_ap=src_ap: e.max(out=vout[:, 0:8], in_=src_ap), R=[sc], W=[vb])
                        elif stage == 1:
                            self.op("dve", lambda e, vout=vout, iout=iout, src_ap=src_ap: e.max_index(out=iout[:, 0:8], in_max=vout[:, 0:8], in_values=src_ap),
                                    R=[sc, vb], W=[ib])
                        elif stage == 2:
                            self.op("dve", lambda e, vout=vout, src_ap=src_ap, wkb=wkb, width=width: e.match_replace(
                                out=wkb[:, 0:width], in_to_replace=vout[:, 0:8], in_values=src_ap, imm_value=-1e30), R=[sc, vb], W=[wkb])
                        elif stage == 3:
                            self.op("dve", lambda e, vout=vout, wkb=wkb, width=width: e.max(out=vout[:, 8:16], in_=wkb[:, 0:width]), R=[wkb], W=[vb])
                        else:
                            self.op("dve", lambda e, vout=vout, iout=iout, wkb=wkb, width=width: e.max_index(
                                out=iout[:, 8:16], in_max=vout[:, 8:16], in_values=wkb[:, 0:width]), R=[wkb, vb], W=[ib])
                    yield None
            for hp in range(0, 16, 2):
                yield from top16_pair([(sc[:, q * 128:(q + 1) * 128], 128, sv[:, q, :], si[:, q, :], wks[q % 2], svb[q], sib[q]) for q in (hp, hp + 1)])
            yield self.op("dve", lambda e: e.tensor_copy(out=sif[:, :, :], in_=si[:, :, :]), R=sib, W=[sif])
            sv4 = sv[:, :, :].rearrange("p (h two) k -> p h two k", two=2)
            cand = sc[:, :].rearrange("p (h a b) -> p h a b", h=8, a=16)
            yield self.op("dve", lambda e: e.tensor_tensor(out=cand, in0=sv4[:, :, 0, :].unsqueeze(3).to_broadcast([128, 8, 16, 16]),
                                                   in1=sv4[:, :, 1, :].unsqueeze(2).to_broadcast([128, 8, 16, 16]), op=ALU.add),
                    R=svb, W=[sc])
            for hh in range(0, 8, 2):
                yield from top16_pair([(sc[:, q * 256:(q + 1) * 256], 256, ts[:, q, :], sel[:, q, :], wks[q % 2], tsb[q], selb[q]) for q in (hh, hh + 1)])
            yield self.op("dve", lambda e: e.tensor_copy(out=self_f[:, :, :], in_=sel[:, :, :]), R=selb, W=[self_f])
            yield self.op("dve", lambda e: e.tensor_scalar(out=k1i[:, :, :], in0=self_f[:, :, :], scalar1=-7.5, scalar2=0.0625, op0=ALU.add, op1=ALU.mult),
                    R=[self_f], W=[k1i])
            yield self.op("dve", lambda e: e.tensor_copy(out=k1f[:, :, :], in_=k1i[:, :, :]), R=[k1i], W=[k1f])
            yield self.op("dve", lambda e: e.scalar_tensor_tensor(out=k2f[:, :, :], in0=k1f[:, :, :], scalar=-16.0, in1=self_f[:, :, :],
                                                            op0=ALU.mult, op1=ALU.add), R=[k1f, self_f], W=[k2f])
            sif4 = sif[:, :, :].rearrange("p (h two) k -> p h two k", two=2)
            io4 = iota16[:, :].unsqueeze(1).unsqueeze(1).to_broadcast([128, 4, 16, 16])
            for which, kf_ in ((0, k1f), (1, k2f)):
                for h2 in range(2):
                    hs = slice(h2 * 4, h2 * 4 + 4)
                    yield self.op("dve", lambda e, kf_=kf_, hs=hs: e.tensor_tensor(out=eq[:, :, :, :], in0=io4,
                                                                          in1=kf_[:, hs, :].unsqueeze(3).to_broadcast([128, 4, 16, 16]), op=ALU.is_equal),
                            R=[iota16, kf_], W=[eq])
                    yield self.op("pool", lambda e, which=which, hs=hs: e.tensor_tensor(out=eq[:, :, :, :], in0=eq[:, :, :, :],
                                                                              in1=sif4[:, hs, which, :].unsqueeze(2).to_broadcast([128, 4, 16, 16]), op=ALU.mult),
                            R=[eq, sif], W=[eq])
                    yield self.op("dve", lambda e, which=which, h2=h2: e.tensor_reduce(out=i12[:, which, h2 * 64:(h2 + 1) * 64].rearrange("p (h k) -> p h k", h=4),
                                                                              in_=eq[:, :, :, :], axis=AX.X, op=ALU.add), R=[eq], W=[i12])
            yield self.op("act", lambda e: e.copy(out=tb[:, 0:2, :], in_=i12[:, :, :]), R=[i12], W=[tb])
            yield self.op("dve", lambda e: e.tensor_tensor(out=ex[:, :, :], in0=ts[:, :, :], in1=ts[:, :, 0:1].to_broadcast([128, 8, 16]), op=ALU.subtract),
                    R=tsb, W=[ex])
            yield self.op("act", lambda e: e.activation(out=ex[:, :, :], in_=ex[:, :, :], func=AF.Exp), R=[ex], W=[ex])
            sm = self.small.next()
            yield self.op("dve", lambda e, sm=sm: e.tensor_reduce(out=sm[:, 0:8], in_=ex[:, :, :], axis=AX.X, op=ALU.add), R=[ex], W=[sm])
            yield self.op("dve", lambda e, sm=sm: e.reciprocal(out=sm[:, 0:8], in_=sm[:, 0:8]), R=[sm], W=[sm])
            yield self.op("dve", lambda e, sm=sm: e.tensor_tensor(out=tb[:, 2, :].rearrange("p (h k) -> p h k", h=8), in0=ex[:, :, :],
                                                          in1=sm[:, 0:8].unsqueeze(2).to_broadcast([128, 8, 16]), op=ALU.mult),
                    R=[ex, sm], W=[tb])
            yield from (None for _ in range(3))
            for j in range(3):
                yield self.op("pe", lambda e, j=j: e.transpose(out=pT[:, j * 128:(j + 1) * 128], in_=tb[:, j, :], identity=self.ident[:, :]),
                        R=[tb, self.ident], W=[pT])
            yield self.op("act", lambda e: e.copy(out=i1T[:, :], in_=pT[:, 0:128]), R=[pT], W=[i1T])
            yield self.op("act", lambda e: e.copy(out=i2T[:, :], in_=pT[:, 128:256]), R=[pT], W=[i2T])
            yield self.op("dve", lambda e: e.scalar_tensor_tensor(out=idxT[:, :], in0=i1T[:, :], scalar=128.0, in1=i2T[:, :],
                                                            op0=ALU.mult, op1=ALU.add), R=[i1T, i2T], W=[idxT])
            yield self.op("act", lambda e: e.copy(out=gT[:, :], in_=pT[:, 256:384]), R=[pT], W=[gT])
        def loop(i, gen):
            x2, h3b, idxT, gT = TS[i % 2]
            ntok = 128 if i < 16 else 64
            Pacc = [B[0], B[1]]
            st_g, st_x, st_w = {}, {}, {}

            def stageA(t):
                guv = GUV.next()
                st_g[t] = guv
                self.dma(lambda e: e.indirect_dma_start(out=guv[:, :], out_offset=None, in_=self.uvtab.t,
                                                         in_offset=bass.IndirectOffsetOnAxis(ap=idxT[:, t:t + 1], axis=0)),
                         R=[idxT, self.uvtab], W=[guv], q="pool")

            st_a = {}

            def stageB1(t):
                guv = st_g[t]
                ac = act_r.next(); dj = djunk_r.next()
                st_a[t] = ac
                Px = A if order.index(t) % 2 == 0 else C
                tX = Px[0].t
                for n in range(2):
                    self.mm(Px[n], tX[:, n * 512:(n + 1) * 512], self.ident, self.ident[:, t:t + 1].to_broadcast([128, 128]),
                            h3b, h3b[:, n * 512:(n + 1) * 512], True, True)
                self.op("dve", lambda e: e.scalar_tensor_tensor(out=dj[:, :], in0=guv[:, 0:D], scalar=1.0, in1=tX[:, :], op0=ALU.mult, op1=ALU.mult,
                                                                accum_out=ac[:, 0:1]), R=[guv, Px[0], Px[1]], W=[dj, ac])

            def stageB2(t):
                ac = st_a.pop(t)
                g2 = gl_r.next()
                self.op("act", lambda e: e.activation(out=g2[:, 0:1], in_=ac[:, 0:1], func=AF.Gelu), R=[ac], W=[g2])
                self.op("act", lambda e: e.activation(out=g2[:, 1:2], in_=g2[:, 0:1], func=AF.Copy, scale=gT[:, t:t + 1]), R=[g2, gT], W=[g2])
                ws = wsel_r.next()
                st_w[t] = ws
                r32 = t % 32
                self.op("act", lambda e: e.activation(out=ws[:, 0:32], in_=cmask[:, 127 - r32:159 - r32], func=AF.Copy, scale=g2[:, 1:2]),
                        R=[cmask, g2], W=[ws])

            ngrp = ntok // 32
            order = [g * 32 + r for r in range(32) for g in range(ngrp)]

            def stageC2(ts_):
                items = [(t, st_w.pop(t), st_g.pop(t)) for t in ts_]
                for n in range(2):
                    for (t, ws, guv) in items:
                        j = t // 32
                        r = t % 32
                        self.op("pe", lambda e, n=n, ws=ws, guv=guv, j=j, r=r: e.matmul(
                            Pacc[n].t[32 * j:32 * j + 32, n * 512:(n + 1) * 512], lhsT=ws[:, 0:32], rhs=guv[:, D + n * 512:D + (n + 1) * 512],
                            start=(r == 0), stop=(r == 31), tile_position=(0, 32 * j)), R=[ws, guv], W=[Pacc[n]])
            LA = NB - 4
            nst = len(order)
            for step in range(nst + LA + 3):
                if step < nst:
                    stageA(order[step])
                if 0 <= step - LA < nst:
                    stageB1(order[step - LA])
                if 0 <= step - LA - 1 < nst:
                    stageB2(order[step - LA - 1])
                k = step - LA - 2
                if k >= 1 and k % 2 == 1 and k < nst:
                    stageC2([order[k - 1], order[k]])
                if gen is not None:
                    for _ in range(3):
                        next(gen, None)
            yo = self.X.next()
            for n in range(2):
                self.op("dve", lambda e, n=n, yo=yo: e.tensor_tensor(out=yo[:, n * 512:(n + 1) * 512], in0=x2[:, n * 512:(n + 1) * 512],
                                                                 in1=Pacc[n].t[:, n * 512:(n + 1) * 512], op=ALU.add), R=[x2, Pacc[n]], W=[yo])
            self.store(O["y_own"][i * 128:(i + 1) * 128, :], yo, yo[:, :])
            if gen is not None:
                for _ in gen:
                    pass

        NADV = self.dbg.get("NADV", 6)
        g0 = pre(0)
        self.npre = 0
        for _ in g0:
            self.npre += 1
        for i in range(NOWN):
            gen = pre(i + 1) if i + 1 < NOWN else None
            loop(i, gen)
        self.pop()


def _own_blocks(j):
    blks = []
    for s in range(16):
        m = s // 2
        if s % 2 == 0:
            blks.append(4 * m + (0 if j == 0 else 1))
        else:
            blks.append(4 * m + (3 if j == 0 else 2))
    return blks


def _consts(j):
    f = np.float32
    c = {}
    c["idn"] = np.eye(128, dtype=f)
    oq = np.zeros((96, 96), f)
    oq[0:64, 0:64] = 1.0 / 64
    oq[64:96, 64:96] = 1.0 / 32
    c["onesq"] = oq
    c["ones128"] = np.full((128, 128), 1.0 / 128, f)
    rm = np.zeros((96, 96), f)
    for d in range(64, 80):
        rm[d + 16, d] = -1.0
    for d in range(80, 96):
        rm[d - 16, d] = 1.0
    c["rotm"] = rm
    cm = np.zeros((128, 255), f)
    cm[:, 127] = 1.0
    c["cmask"] = cm
    c["trilm"] = np.tril(np.ones((128, 128), f))
    tms = np.zeros((128, 128), f)
    for bt in range(4):
        tms[bt * 16:(bt + 1) * 16, bt * 16:(bt + 1) * 16] = np.tril(np.ones((16, 16), f))
    c["trilms"] = tms
    inv = (np.float32(10000.0) ** (-np.arange(16, dtype=f) / np.float32(16))).astype(f)
    pos = np.zeros((128, 33), f)
    for i in range(32):
        pos[:, i] = i * 128 + np.arange(128)
    pos[0:64, 32] = 1024 + (np.arange(64) % 16)
    ang = (pos[:, :, None] * inv[None, None, :]).astype(f)
    c["ropek"] = np.concatenate([np.cos(ang), np.sin(ang)], axis=2).astype(f)
    blks = _own_blocks(j)
    posq = np.zeros((NTOK,), f)
    for s, b in enumerate(blks):
        posq[s * 128:(s + 1) * 128] = b * 128 + np.arange(128)
    posq[2048:2112] = 1024 + (np.arange(64) % 16)
    angq = (posq[None, :] * np.concatenate([inv, inv])[:, None]).astype(f)
    c["cosq"] = np.cos(angq).astype(f)
    c["sinq"] = np.sin(angq).astype(f)
    kk = np.arange(128)[:, None] // 64
    qq = np.arange(128)[None, :] // 64
    diag = (kk <= qq).astype(f)
    full = np.ones((128, 128), f)
    zero = np.zeros((128, 128), f)
    c["masks"] = np.stack([diag, zero, full, diag] if j == 0 else [full, diag, diag, zero]).astype(f)
    return c


_NC_CACHE = {}


def _get_nc(dbg=None):
    key = repr(sorted((dbg or {}).items()))
    if key not in _NC_CACHE:
        k = Kern(dbg)
        _NC_CACHE[key] = k.build()
    return _NC_CACHE[key]


def kernel(x_prompt, x_sample, cache_mla_ckv, cache_mla_krope, cache_mem_k, cache_mem_v, mem_prompt,
           g_mix, w_in, g_q_lat, w_uq, g_qn, g_qr, g_kv_lat, g_kr, w_uk, w_uv, g_kn, w_oa,
           g_gm, w_s, b_s, w_ob, w_o,
           g_xattn, g_mem, w_cq, g_cq, w_ck, g_ck, w_cv, w_co,
           g_ffn, w_pq, sub_keys, peer_u, peer_v, _dbg=None):
    f = np.float32
    A = lambda a: np.ascontiguousarray(np.asarray(a, dtype=f))
    x_prompt = A(x_prompt); x_sample = A(x_sample)
    shared = {
        "w_in": A(w_in)[0], "w_uq": A(w_uq)[0], "w_uk": A(w_uk)[0], "w_uv": A(w_uv)[0],
        "w_oa": A(w_oa)[0], "w_ob": A(w_ob)[0], "w_o": A(w_o)[0],
        "w_cq": A(w_cq)[0], "w_ck": A(w_ck)[0], "w_cv": A(w_cv)[0], "w_co": A(w_co)[0],
        "w_pq": A(w_pq)[0], "sub_keys": A(sub_keys)[0].reshape(2048, 128),
        "peer_u": A(peer_u)[0], "peer_v": A(peer_v)[0],
        "g_mix": A(g_mix), "g_q_lat": A(g_q_lat), "g_qn": A(g_qn), "g_qr": A(g_qr), "g_kv_lat": A(g_kv_lat),
        "g_kr": A(g_kr), "g_kn": A(g_kn), "g_gm": A(g_gm), "g_xattn": A(g_xattn), "g_mem": A(g_mem),
        "g_cq": A(g_cq), "g_ck": A(g_ck), "g_ffn": A(g_ffn),
        "w_s": A(w_s)[0].reshape(1024, 128), "b_s": A(b_s)[0],
    }
    ckv_c = A(cache_mla_ckv)[0]; kr_c = A(cache_mla_krope)[0]
    mk_c = A(cache_mem_k)[0]; mv_c = A(cache_mem_v)[0]; mem_p = A(mem_prompt)
    consts = [_consts(0), _consts(1)]
    in_maps = []
    for c in range(NCORES):
        b, j = c // 2, c % 2
        blks = _own_blocks(j)
        x_own = np.zeros((NTOK, D), f)
        for s, blk in enumerate(blks):
            x_own[s * 128:(s + 1) * 128] = x_prompt[b, blk * 128:(blk + 1) * 128]
        x_own[2048:2112] = x_sample[4 * c:4 * c + 4].reshape(64, D)
        m = dict(shared)
        m.update(consts[j])
        m["x_all"] = x_prompt[b]
        m["x_own"] = x_own
        m["ckv_c"] = np.ascontiguousarray(ckv_c[4 * c:4 * c + 4].reshape(4096, 256))
        m["kr_c"] = np.ascontiguousarray(kr_c[4 * c:4 * c + 4].reshape(4096, 32))
        m["memk_c"] = np.ascontiguousarray(mk_c[4 * c:4 * c + 4].reshape(1024, 512))
        m["memv_c"] = np.ascontiguousarray(mv_c[4 * c:4 * c + 4].reshape(1024, 512))
        m["mem_p"] = mem_p[b]
        in_maps.append(m)
    nc = _get_nc(_dbg)
    res = run_bass_kernel_spmd(nc, in_maps, core_ids=list(range(NCORES)))
    R = res.results
    B, S, DB, DS = 4, 4096, 32, 16
    y_p = np.zeros((B, S, D), f); y_s = np.zeros((DB, DS, D), f)
    ckv_p = np.zeros((1, B, S, 256), f); kr_p = np.zeros((1, B, S, 32), f)
    mk_p = np.zeros((1, B, 256, 4, 128), f); mv_p = np.zeros((1, B, 256, 4, 128), f)
    ckv_s = np.zeros((1, DB, DS, 256), f); kr_s = np.zeros((1, DB, DS, 32), f); vg_s = np.zeros((1, DB, DS, D), f)
    for c in range(NCORES):
        b, j = c // 2, c % 2
        r = R[c]
        for s, blk in enumerate(_own_blocks(j)):
            y_p[b, blk * 128:(blk + 1) * 128] = r["y_own"][s * 128:(s + 1) * 128]
        y_s[4 * c:4 * c + 4] = r["y_own"][2048:2112].reshape(4, 16, D)
        half = slice(j * 2048, (j + 1) * 2048)
        ckv_p[0, b, half] = r["ckv_p"][half]
        kr_p[0, b, half] = r["kr_p"][half]
        mk_p[0, b, j * 128:(j + 1) * 128] = r["mk_p"][j * 128:(j + 1) * 128].reshape(128, 4, 128)
        mv_p[0, b, j * 128:(j + 1) * 128] = r["mv_p"][j * 128:(j + 1) * 128].reshape(128, 4, 128)
        ckv_s[0, 4 * c:4 * c + 4] = r["ckv_s"].reshape(4, 16, 256)
        kr_s[0, 4 * c:4 * c + 4] = r["kr_s"].reshape(4, 16, 32)
        vg_s[0, 4 * c:4 * c + 4] = r["vg_s"].reshape(4, 16, D)
    return (y_p, y_s, ckv_p, kr_p, mk_p, mv_p, ckv_s, kr_s, vg_s)
```

```python
import numpy as np
from contextlib import ExitStack
import concourse.bass as bass
import concourse.mybir as mybir
from concourse.bass_utils import run_bass_kernel_spmd

F32 = mybir.dt.float32
BF16 = mybir.dt.bfloat16
I32 = mybir.dt.int32
U32 = mybir.dt.uint32
ALU = mybir.AluOpType
AF = mybir.ActivationFunctionType
AX = mybir.AxisListType

NCORES = 8
D = 1024
EPS = 1e-6
NOWN = 17
NTOK = NOWN * 128
NKP = 4096
NKS = 4 * 1040
NK = NKP + NKS
MLA_SCALE = 96 ** -0.5
MEM_SCALE = 128 ** -0.5
IN_Q, IN_KV, IN_Z = 0, 384, 672


class Buf:
    __slots__ = ("name", "lw", "rd")

    def __init__(self, name):
        self.name = name
        self.lw = None
        self.rd = {}


class T:
    __slots__ = ("t", "b")

    def __init__(self, t, b):
        self.t = t
        self.b = b

    def __getitem__(self, k):
        return self.t[k]


class _Eng:
    def __init__(self, name, selfsync):
        self.name = name
        self.sem = "e_" + name
        self.count = 0
        self.seen = {}
        self.prog = []
        self.selfsync = selfsync


class _Queue:
    def __init__(self, name, eng, nslots):
        self.name = name
        self.eng = eng
        self.slots = [["q_%s_%d" % (name, i), 0] for i in range(nslots)]
        self.next = 0


class Sched:
    def __init__(self, nc, st, selfsync=True):
        self.nc = nc
        self.eng = {
            "pe": _Eng("pe", False),
            "act": _Eng("act", selfsync),
            "dve": _Eng("dve", selfsync),
            "pool": _Eng("pool", selfsync),
            "sp": _Eng("sp", False),
        }
        self.queues = {
            "sp": _Queue("sp", "sp", 8),
            "pool": _Queue("pool", "pool", 8),
            "act": _Queue("act", "act", 4),
            "conv": _Queue("conv", "pool", 16),
        }
        self.final_tokens = []
        names = [E.sem for E in self.eng.values()]
        for Q in self.queues.values():
            names += [s[0] for s in Q.slots]
        self.sems = {n: st.enter_context(nc.semaphore(n)) for n in names}
        self.ninst = 0

    def _collect(self, E, reads, writes, extra=None):
        need = {}

        def add(tok):
            if tok is None:
                return
            s, v = tok
            if need.get(s, 0) < v:
                need[s] = v
        for b in reads:
            add(b.lw)
        for b in writes:
            add(b.lw)
            for s, v in b.rd.items():
                add((s, v))
        if extra:
            for t in extra:
                add(t)
        waits = []
        for s, v in need.items():
            if s == E.sem and not E.selfsync:
                continue
            if E.seen.get(s, 0) >= v:
                continue
            E.seen[s] = v
            waits.append((s, v))
        return waits

    def _commit(self, tok, reads, writes):
        for b in writes:
            b.lw = tok
            b.rd = {}
        s, v = tok
        for b in reads:
            if b in writes:
                continue
            if b.rd.get(s, 0) < v:
                b.rd[s] = v

    def op(self, eng, fn, R=(), W=()):
        E = self.eng[eng]
        reads = [x.b for x in R]
        writes = [x.b for x in W]
        waits = self._collect(E, reads, writes)
        E.count += 1
        tok = (E.sem, E.count)
        E.prog.append((waits, fn, (E.sem, 1)))
        self._commit(tok, reads, writes)
        return tok

    def dma(self, queue, fn, R=(), W=(), final=False):
        Q = self.queues[queue]
        E = self.eng[Q.eng]
        reads = [x.b for x in R]
        writes = [x.b for x in W]
        slot = Q.slots[Q.next]
        Q.next = (Q.next + 1) % len(Q.slots)
        extra = [(slot[0], slot[1] * 16)] if slot[1] > 0 else None
        waits = self._collect(E, reads, writes, extra)
        slot[1] += 1
        tok = (slot[0], slot[1] * 16)
        E.prog.append((waits, fn, (slot[0], 16)))
        self._commit(tok, reads, writes)
        if final:
            self.final_tokens.append(tok)
        return tok

    def barrier(self):
        toks = []
        for E in self.eng.values():
            if E.count > 0:
                toks.append((E.sem, E.count))
        for qn, Q in self.queues.items():
            if qn == "conv":
                continue
            for s in Q.slots:
                if s[1] > 0:
                    toks.append((s[0], s[1] * 16))
        for E in self.eng.values():
            waits = []
            for s, v in toks:
                if s == E.sem:
                    continue
                if E.seen.get(s, 0) >= v:
                    continue
                E.seen[s] = v
                waits.append((s, v))
            if waits:
                E.prog.append((waits, None, None))

    def flush(self, last=False):
        nc = self.nc
        sems = self.sems
        if last:
            fin = {}
            for s, v in self.final_tokens:
                if fin.get(s, 0) < v:
                    fin[s] = v
            self.eng["sp"].prog.append(([(s, v) for s, v in fin.items()], None, None))

        def run(E):
            prog = E.prog
            E.prog = []
            self.ninst += len(prog)

            def body(e):
                for waits, fn, inc in prog:
                    for s, v in waits:
                        e.wait_ge(sems[s], v)
                    if fn is not None:
                        ins = fn(e)
                        ins.then_inc(sems[inc[0]], inc[1])
            return body
        with nc.Block(no_gpsimd_drain=True) as block:
            block.sync(run(self.eng["sp"]))
            block.tensor(run(self.eng["pe"]))
            block.scalar(run(self.eng["act"]))
            block.vector(run(self.eng["dve"]))
            block.gpsimd(run(self.eng["pool"]))


class Ring:
    def __init__(self, tiles):
        self.tiles = tiles
        self.i = 0

    def next(self):
        t = self.tiles[self.i]
        self.i = (self.i + 1) % len(self.tiles)
        return t


class Kern:
    def __init__(self, dbg=None):
        self.nc = bass.Bass("TRN2", target_bir_lowering=False)
        self.st = ExitStack()
        self.S = Sched(self.nc, self.st)
        self.scopes = [self.st]
        self.dbg = dbg or {}
        self.uid = 0

    def din(self, name, shape, dt=F32):
        return self.nc.dram_tensor(name, list(shape), dt, kind="ExternalInput").ap()

    def dout(self, name, shape, dt=F32):
        return self.nc.dram_tensor(name, list(shape), dt, kind="ExternalOutput").ap()

    def sb(self, name, shape, dt):
        self.uid += 1
        nm = "%s_%d" % (name, self.uid)
        t = self.scopes[-1].enter_context(self.nc.sbuf_tensor(nm, list(shape), dt))
        return T(t, Buf(nm))

    def ring(self, name, shape, dt, n):
        return Ring([self.sb(name, shape, dt) for _ in range(n)])

    def push(self):
        s = ExitStack()
        self.scopes.append(s)
        return s

    def pop(self):
        self.S.barrier()
        self.S.flush()
        s = self.scopes.pop()
        s.close()

    def op(self, eng, fn, R=(), W=()):
        return self.S.op(eng, fn, R, W)

    def dma(self, fn, R=(), W=(), q="sp", final=False):
        return self.S.dma(q, fn, R, W, final)

    def load(self, dst, dst_ap, src_ap, q="sp", slow=False):
        if slow:
            self.dma(lambda e: e.dma_start(out=dst_ap, in_=src_ap, allow_slow_non_contiguous=True), W=[dst], q=q)
        else:
            self.dma(lambda e: e.dma_start(out=dst_ap, in_=src_ap), W=[dst], q=q)

    def store(self, dst_ap, src, src_ap):
        self.dma(lambda e: e.dma_start(out=dst_ap, in_=src_ap), R=[src], final=True)

    def wload(self, name, src, K, N, c0=0):
        kc = K // 128
        w = self.sb(name, [128, kc, N], BF16)
        for c in range(kc):
            self.load(w, w[:, c, :], src[c * 128:(c + 1) * 128, c0:c0 + N], q="pool")
        return w

    def gbload(self, name, src, n):
        g = self.sb(name, [128, n], F32)
        self.load(g, g[:, :], src[0:1, 0:n].partition_broadcast(128))
        return g

    def mm(self, ps, ps_ap, lhsT, lhsT_ap, rhs, rhs_ap, start, stop):
        self.op("pe", lambda e: e.matmul(ps_ap, lhsT=lhsT_ap, rhs=rhs_ap, start=start, stop=stop),
                R=[lhsT, rhs], W=[ps])

    def rstd(self, src, src_ap, n, Dn):
        sm = self.small.next()
        jk = self.junk
        self.op("act", lambda e: e.activation(out=jk[:, 0:n], in_=src_ap, func=AF.Square, accum_out=sm[:, 0:1]),
                R=[src], W=[jk, sm])
        self.op("act", lambda e: e.activation(out=sm[:, 1:2], in_=sm[:, 0:1], func=AF.Sqrt, bias=self.epsb[:, 0:1],
                                              scale=1.0 / Dn), R=[sm, self.epsb], W=[sm])
        self.op("dve", lambda e: e.reciprocal(out=sm[:, 2:3], in_=sm[:, 1:2]), R=[sm], W=[sm])
        return sm, sm[:, 2:3]

    def rms_tok(self, src, src_ap, n, gb, gb_ap, dst, dst_ap):
        sm, col = self.rstd(src, src_ap, n, n)
        self.op("dve", lambda e: e.scalar_tensor_tensor(out=dst_ap, in0=src_ap, scalar=col, in1=gb_ap,
                                                        op0=ALU.mult, op1=ALU.mult), R=[src, sm, gb], W=[dst])

    def transposes(self, src, src_ap_fn, n, dst, dst_ap, rows=128, eng="act"):
        pT = self.psT
        for j in range(n):
            ap = src_ap_fn(j)
            self.op("pe", lambda e, ap=ap, j=j: e.transpose(out=pT[:, j * 128:j * 128 + rows], in_=ap,
                                                             identity=self.ident[0:rows, 0:rows]),
                    R=[src, self.ident], W=[pT])
        view = pT[:, 0:n * 128].rearrange("p (c k) -> p c k", c=n)[:, :, 0:rows]
        if eng == "act":
            self.op("act", lambda e: e.copy(out=dst_ap, in_=view), R=[pT], W=[dst])
        else:
            self.op("dve", lambda e: e.tensor_copy(out=dst_ap, in_=view), R=[pT], W=[dst])

    def normT(self, xt, gb, hb, hT):
        self.rms_tok(xt, xt[:, :], D, gb, gb[:, :], hb, hb[:, :])
        self.transposes(hb, lambda j: hb[:, j * 128:(j + 1) * 128], 8, hT, hT[:, :, :])

    def proj(self, ps, ps_ap, hT, w, c0, n, kc=8):
        for c in range(kc):
            self.mm(ps, ps_ap, hT, hT[:, c, :], w, w[:, c, c0:c0 + n], c == 0, c == kc - 1)

    def build(self):
        nc = self.nc
        I = {}
        O = {}

        def di(name, shape, dt=F32):
            I[name] = self.din(name, shape, dt)

        def do(name, shape, dt=F32):
            O[name] = self.dout(name, shape, dt)
        di("x_all", [4096, D]); di("x_own", [NTOK, D])
        di("ckv_c", [4096, 256]); di("kr_c", [4096, 32])
        di("memk_c", [1024, 512]); di("memv_c", [1024, 512]); di("mem_p", [256, D])
        di("w_in", [D, 4768]); di("w_uq", [384, 1536]); di("w_uk", [256, 1024]); di("w_uv", [256, 1024])
        di("w_oa", [D, D]); di("w_ob", [D, D]); di("w_o", [D, D])
        di("w_cq", [D, 512]); di("w_ck", [D, 512]); di("w_cv", [D, 512]); di("w_co", [512, D])
        di("w_pq", [D, 2048]); di("sub_keys", [2048, 128]); di("peer_u", [16384, D]); di("peer_v", [16384, D])
        for g, n in [("g_mix", D), ("g_q_lat", 384), ("g_qn", 64), ("g_qr", 32), ("g_kv_lat", 256), ("g_kr", 32),
                     ("g_kn", 64), ("g_gm", D), ("g_xattn", D), ("g_mem", D), ("g_cq", 128), ("g_ck", 128),
                     ("g_ffn", D)]:
            di(g, [1, n])
        di("w_s", [1024, 128]); di("b_s", [8, 128])
        di("idn", [128, 128]); di("onesq", [96, 96]); di("ones128", [128, 128]); di("rotm", [96, 96])
        di("cmask", [128, 255]); di("trilm", [128, 128]); di("trilms", [128, 128])
        di("ropek", [128, 33, 32]); di("cosq", [32, NTOK]); di("sinq", [32, NTOK]); di("masks", [4, 128, 128])
        do("y_own", [NTOK, D]); do("ckv_p", [4096, 256]); do("kr_p", [4096, 32])
        do("mk_p", [256, 512]); do("mv_p", [256, 512])
        do("ckv_s", [64, 256]); do("kr_s", [64, 32]); do("vg_s", [64, D])
        self.I, self.O = I, O

        def ps(name, shape, dt):
            t = self.st.enter_context(nc.psum_tensor(name, shape, dt))
            return t
        self.psT = T(ps("psT", [128, 1024], BF16), Buf("psT"))
        tA = ps("psA", [128, 1024], F32); tB = ps("psB", [128, 1024], F32); tC = ps("psC", [128, 1024], F32)
        tD = ps("psD", [128, 512], F32)
        self.A = [T(tA, Buf("A0")), T(tA, Buf("A1"))]
        self.B = [T(tB, Buf("B0")), T(tB, Buf("B1"))]
        self.C = [T(tC, Buf("C0")), T(tC, Buf("C1"))]
        self.Dp = T(tD, Buf("D"))

        self.ident = self.sb("ident", [128, 128], BF16)
        self.load(self.ident, self.ident[:, :], I["idn"], q="pool")
        self.epsb = self.sb("epsb", [128, 1], F32)
        self.op("dve", lambda e: e.memset(self.epsb[:, :], EPS), W=[self.epsb])
        self.mscr = T(nc.dram_tensor("m_scr", [NTOK, D], BF16).ap(), Buf("mscr"))
        self.junk = self.sb("junk", [128, D], BF16)
        self.small = self.ring("small", [128, 8], F32, 4)
        self.X = self.ring("X", [128, D], F32, 2)
        self.HB = self.sb("HB", [128, D], BF16)
        self.HT = self.ring("HT", [128, 8, 128], BF16, 1)

        self.uvtab = T(nc.dram_tensor("uv_bf", [16384, 2 * D], BF16).ap(), Buf("uvtab"))
        self.hscr = T(nc.dram_tensor("h_scr", [NTOK, D], BF16).ap(), Buf("hscr"))
        self._conv_pending = True
        ph = self.dbg.get("phases", "1234")
        self.push()
        self.o_all = self.sb("o_all", [128, NOWN, D], BF16)
        self.op("pool", lambda e: e.memset(self.o_all[:, 16, :], 0.0), W=[self.o_all])
        self.gb_mix = self.gbload("gb_mix", I["g_mix"], D)
        self.push()
        ckvT = self.sb("ckvT", [128, 2, NK], BF16)
        KT = self.sb("KT", [96, NK], BF16)
        cqnT = self.sb("cqnT", [128, 3, NTOK], BF16)
        if "1" in ph:
            self.phase1(ckvT, KT, cqnT)
        if "2" in ph:
            self.phase2(ckvT, KT, cqnT)
        self.pop()
        if "3" in ph:
            self.phase3a()
        self.pop()
        self.emit_conv()
        if "4" in ph:
            self.phase3b()
        self.S.barrier()
        self.S.flush(last=True)
        self.st.close()
        return nc

    def emit_conv(self):
        if not self._conv_pending:
            return
        self._conv_pending = False
        I = self.I
        RCH = 1024
        for r in range(0, 16384, RCH):
            for which, nm in ((0, "peer_u"), (1, "peer_v")):
                self.dma(lambda e, r=r, which=which, nm=nm: e.dma_start(out=self.uvtab.t[r:r + RCH, which * D:(which + 1) * D],
                                                                         in_=I[nm][r:r + RCH, :]), W=[self.uvtab], q="conv")

    def phase1(self, ckvT, KT, cqnT):
        I, O = self.I, self.O
        self.push()
        Wkv = self.wload("Wkv", I["w_in"], D, 288, IN_KV)
        Wq = self.wload("Wq", I["w_in"], D, 384, IN_Q)
        gb_kv = self.sb("gb_kv", [128, 288], F32)
        self.load(gb_kv, gb_kv[:, 0:256], I["g_kv_lat"][0:1, 0:256].partition_broadcast(128))
        self.load(gb_kv, gb_kv[:, 256:288], I["g_kr"][0:1, 0:32].partition_broadcast(128))
        gb_ql = self.gbload("gb_ql", I["g_q_lat"], 384)
        ropek = self.sb("ropek", [128, 33, 32], F32)
        self.load(ropek, ropek[:, :, :], I["ropek"])
        kvo_r = self.ring("kvo", [128, 288], F32, 2)
        krn_r = self.ring("krn", [128, 64], F32, 2)
        kvb_r = self.ring("kvb", [128, 384], BF16, 2)
        for t in kvb_r.tiles:
            self.op("pool", lambda e, t=t: e.memset(t[:, :], 0.0), W=[t])
        cqb = self.sb("cqb", [128, 384], BF16)
        pT = self.psT

        def finish_kv(kvo, kind, idx):
            kvb = kvb_r.next()
            self.op("act", lambda e: e.copy(out=kvb[:, 0:256], in_=kvo[:, 0:256]), R=[kvo], W=[kvb])
            self.op("act", lambda e: e.copy(out=kvb[:, 320:352], in_=kvo[:, 256:288]), R=[kvo], W=[kvb])
            for j in range(3):
                self.op("pe", lambda e, j=j: e.transpose(out=pT[:, j * 128:(j + 1) * 128], in_=kvb[:, j * 128:(j + 1) * 128],
                                                       identity=self.ident[:, :]), R=[kvb, self.ident], W=[pT])
            if kind == "snew":
                dst = ckvT[:, :, NKP:NK].rearrange("p c (b k) -> p c b k", b=4)[:, :, :, 1024:1040]
                src = pT[:, 0:256].rearrange("p (c b k) -> p c b k", c=2, b=8)[:, :, 0:4, :]
                self.op("act", lambda e: e.copy(out=dst, in_=src), R=[pT], W=[ckvT])
                dstk = KT[64:96, NKP:NK].rearrange("p (b k) -> p b k", b=4)[:, :, 1024:1040]
                srck = pT[64:96, 256:320].rearrange("p (b k) -> p b k", b=4)
                self.op("act", lambda e: e.copy(out=dstk, in_=srck), R=[pT], W=[KT])
            else:
                c0 = idx * 128 if kind == "p" else NKP + (idx // 8) * 1040 + (idx % 8) * 128
                src = pT[:, 0:256].rearrange("p (c k) -> p c k", c=2)
                if "ckv" not in self.dbg.get("fk_skip", ""):
                    self.op("act", lambda e: e.copy(out=ckvT[:, :, c0:c0 + 128], in_=src), R=[pT], W=[ckvT])
                if "kt" not in self.dbg.get("fk_skip", ""):
                    self.op("act", lambda e: e.copy(out=KT[64:96, c0:c0 + 128], in_=pT[64:96, 256:384]), R=[pT], W=[KT])

        for i in self.dbg.get("p1_new", list(range(33))):
            xt = self.X.next()
            src = I["x_all"][i * 128:(i + 1) * 128, :] if i < 32 else I["x_own"][16 * 128:17 * 128, :]
            self.load(xt, xt[:, :], src)
            cut = self.dbg.get("p1_cut", 9)
            if cut < 2:
                continue
            hT = self.HT.next()
            self.normT(xt, self.gb_mix, self.HB, hT)
            if cut < 3:
                continue
            zp = self.A[i % 2]
            zoff = (i % 2) * 512
            z = zp.t[:, zoff:zoff + 288]
            self.proj(zp, z, hT, Wkv, 0, 288)
            if cut < 4:
                continue
            kvo = kvo_r.next()
            krn = krn_r.next()
            self.rms_tok(zp, zp.t[:, zoff:zoff + 256], 256, gb_kv, gb_kv[:, 0:256], kvo, kvo[:, 0:256])
            self.rms_tok(zp, zp.t[:, zoff + 256:zoff + 288], 32, gb_kv, gb_kv[:, 256:288], krn, krn[:, 0:32])
            cs = ropek[:, i, 0:16]
            sn = ropek[:, i, 16:32]
            self.op("dve", lambda e, krn=krn, cs=cs: e.tensor_tensor(out=krn[:, 32:48], in0=krn[:, 0:16], in1=cs, op=ALU.mult), R=[krn, ropek], W=[krn])
            self.op("dve", lambda e, krn=krn, sn=sn: e.tensor_tensor(out=krn[:, 48:64], in0=krn[:, 16:32], in1=sn, op=ALU.mult), R=[krn, ropek], W=[krn])
            self.op("dve", lambda e, krn=krn, kvo=kvo: e.tensor_tensor(out=kvo[:, 256:272], in0=krn[:, 32:48], in1=krn[:, 48:64], op=ALU.subtract), R=[krn], W=[kvo])
            self.op("dve", lambda e, krn=krn, sn=sn: e.tensor_tensor(out=krn[:, 32:48], in0=krn[:, 0:16], in1=sn, op=ALU.mult), R=[krn, ropek], W=[krn])
            self.op("dve", lambda e, krn=krn, cs=cs: e.tensor_tensor(out=krn[:, 48:64], in0=krn[:, 16:32], in1=cs, op=ALU.mult), R=[krn, ropek], W=[krn])
            self.op("dve", lambda e, krn=krn, kvo=kvo: e.tensor_tensor(out=kvo[:, 272:288], in0=krn[:, 32:48], in1=krn[:, 48:64], op=ALU.add), R=[krn], W=[kvo])
            if i < 32:
                self.store(O["ckv_p"][i * 128:(i + 1) * 128, :], kvo, kvo[:, 0:256])
                self.store(O["kr_p"][i * 128:(i + 1) * 128, :], kvo, kvo[:, 256:288])
                if cut < 5:
                    continue
                finish_kv(kvo, "p", i)
            else:
                self.store(O["ckv_s"][:, :], kvo, kvo[0:64, 0:256])
                self.store(O["kr_s"][:, :], kvo, kvo[0:64, 256:288])
                finish_kv(kvo, "snew", 0)
        for idx in range(self.dbg.get("p1_cache", 32)):
            kvo = kvo_r.next()
            self.load(kvo, kvo[:, 0:256], I["ckv_c"][idx * 128:(idx + 1) * 128, :])
            self.load(kvo, kvo[:, 256:288], I["kr_c"][idx * 128:(idx + 1) * 128, :])
            finish_kv(kvo, "scache", idx)
        for i in range(self.dbg.get("p1_q", NOWN)):
            xt = self.X.next()
            self.load(xt, xt[:, :], I["x_own"][i * 128:(i + 1) * 128, :])
            hT = self.HT.next()
            self.normT(xt, self.gb_mix, self.HB, hT)
            zp = self.A[i % 2]
            zoff = (i % 2) * 512
            self.proj(zp, zp.t[:, zoff:zoff + 384], hT, Wq, 0, 384)
            self.rms_tok(zp, zp.t[:, zoff:zoff + 384], 384, gb_ql, gb_ql[:, :], cqb, cqb[:, :])
            self.transposes(cqb, lambda j: cqb[:, j * 128:(j + 1) * 128], 3, cqnT, cqnT[:, :, i * 128:(i + 1) * 128])
        self.pop()

    def phase2(self, ckvT, KT, cqnT):
        I, O = self.I, self.O
        self.push()
        wuq = self.wload("wuq", I["w_uq"], 384, 1536)
        wuk = self.wload("wuk", I["w_uk"], 256, 1024)
        wuv = self.wload("wuv", I["w_uv"], 256, 1024)
        onesq = self.sb("onesq", [96, 96], BF16)
        self.load(onesq, onesq[:, :], I["onesq"], q="pool")
        rotm = self.sb("rotm", [96, 96], BF16)
        self.load(rotm, rotm[:, :], I["rotm"], q="pool")
        gq = self.sb("gq", [96, 2], F32)
        self.load(gq, gq[0:64, 0:1], I["g_qn"][0:1, 0:64].rearrange("o d -> d o"))
        self.load(gq, gq[64:96, 0:1], I["g_qr"][0:1, 0:32].rearrange("o d -> d o"))
        self.op("dve", lambda e: e.tensor_scalar(out=gq[:, 1:2], in0=gq[:, 0:1], scalar1=MLA_SCALE, scalar2=None, op0=ALU.mult),
                R=[gq], W=[gq])
        gkn = self.sb("gkn", [64, 1], F32)
        self.load(gkn, gkn[:, :], I["g_kn"][0:1, 0:64].rearrange("o d -> d o"))
        cosq = self.sb("cosq", [96, NTOK], F32)
        sinq = self.sb("sinq", [96, NTOK], F32)
        self.load(cosq, cosq[64:96, :], I["cosq"])
        self.load(sinq, sinq[64:96, :], I["sinq"])
        maskT = self.sb("maskT", [128, 4, 128], BF16)
        self.load(maskT, maskT[:, :, :], I["masks"].rearrange("m k q -> k m q"), q="pool")
        NVT = 32 + 36
        V = self.sb("V", [128, NVT, 72], BF16)
        self.op("pool", lambda e: e.memset(V[:, :, :], 0.0), W=[V])
        self.op("pool", lambda e: e.memset(V[:, :, 64:65], 1.0), W=[V])
        QT = self.sb("QT", [96, NTOK], BF16)
        sq_r = self.ring("sq", [96, 512], BF16, 2)
        rs_r = self.ring("rs", [96, 512], F32, 2)
        tq_r = self.ring("tq", [96, 512], F32, 2)
        t1 = self.sb("t1", [96, 512], F32)
        t2 = self.sb("t2", [96, 512], F32)
        pt_r = self.ring("pt", [128, 4, 128], BF16, 3)
        ptS = [self.sb("ptS", [128, 9, 64], BF16) for _ in range(4)]
        for t in ptS:
            self.op("pool", lambda e, t=t: e.memset(t[:, :, :], 0.0), W=[t])
        A, B, C, Dp = self.A, self.B, self.C, self.Dp
        eps96 = self.epsb

        def fm_norm(ps, ps_ap, rows, n, ones_ap, gcol, gcol_ap, dst, dst_ap):
            sq = sq_r.next()
            rs = rs_r.next()
            self.op("act", lambda e: e.activation(out=sq[0:rows, 0:n], in_=ps_ap, func=AF.Square), R=[ps], W=[sq])
            self.mm(B[0], B[0].t[0:rows, 0:n], onesq, ones_ap, sq, sq[0:rows, 0:n], True, True)
            self.op("act", lambda e: e.activation(out=rs[0:rows, 0:n], in_=B[0].t[0:rows, 0:n], func=AF.Sqrt,
                                                  bias=eps96[0:rows, 0:1], scale=1.0), R=[B[0], eps96], W=[rs])
            tq = tq_r.next()
            self.op("act", lambda e: e.activation(out=tq[0:rows, 0:n], in_=ps_ap, func=AF.Copy, scale=gcol_ap), R=[ps, gcol], W=[tq])
            self.op("dve", lambda e: e.reciprocal(out=rs[0:rows, 0:n], in_=rs[0:rows, 0:n]), R=[rs], W=[rs])
            self.op("pool", lambda e: e.tensor_tensor(out=dst_ap, in0=tq[0:rows, 0:n], in1=rs[0:rows, 0:n], op=ALU.mult), R=[tq, rs], W=[dst])

        kchunks = [(c0, min(512, NK - c0)) for c0 in range(0, NK, 512)]
        qchunks = [(c0, min(512, NTOK - c0)) for c0 in range(0, NTOK, 512)]
        vt = [(i, i * 128, 128) for i in range(32)]
        for bt in range(4):
            for j in range(9):
                vt.append((32 + bt * 9 + j, NKP + bt * 1040 + j * 128, 128 if j < 8 else 16))
        nh = self.dbg.get("nheads", 16)
        pi = 0
        for h in range(nh):
            for (c0, n) in kchunks:
                P = A[pi % 2]; po = (pi % 2) * 512; pi += 1
                for c in range(2):
                    self.mm(P, P.t[0:64, po:po + n], wuk, wuk[:, c, h * 64:(h + 1) * 64], ckvT, ckvT[:, c, c0:c0 + n], c == 0, c == 1)
                fm_norm(P, P.t[0:64, po:po + n], 64, n, onesq[0:64, 0:64], gkn, gkn[:, 0:1], KT, KT[0:64, c0:c0 + n])
            p2c = self.dbg.get("p2_cut", 9)
            if p2c < 2:
                continue
            for g0 in range(0, NVT, 8):
                P = A[pi % 2]; po = (pi % 2) * 512; pi += 1
                grp = vt[g0:g0 + 8]
                for (ti, c0, rows) in grp:
                    jj = ti - g0
                    for c in range(2):
                        self.mm(P, P.t[0:rows, po + jj * 64:po + jj * 64 + 64], ckvT, ckvT[:, c, c0:c0 + rows],
                                wuv, wuv[:, c, h * 64:(h + 1) * 64], c == 0, c == 1)
                ng = len(grp)
                src = P.t[:, po:po + ng * 64].rearrange("p (j d) -> p j d", j=ng)
                self.op("act", lambda e, src=src, g0=g0, ng=ng: e.copy(out=V[:, g0:g0 + ng, 0:64], in_=src), R=[P], W=[V])
            if p2c < 3:
                continue
            for (c0, n) in qchunks:
                P = A[pi % 2]; po = (pi % 2) * 512; pi += 1
                for c in range(3):
                    self.mm(P, P.t[0:96, po:po + n], wuq, wuq[:, c, h * 96:(h + 1) * 96], cqnT, cqnT[:, c, c0:c0 + n], c == 0, c == 2)
                fm_norm(P, P.t[0:96, po:po + n], 96, n, onesq[:, :], gq, gq[:, 1:2], QT, QT[0:96, c0:c0 + n])
                self.mm(B[1], B[1].t[0:96, 512:512 + n], rotm, rotm[:, :], QT, QT[0:96, c0:c0 + n], True, True)
                self.op("dve", lambda e, c0=c0, n=n: e.tensor_tensor(out=t1[64:96, 0:n], in0=QT[64:96, c0:c0 + n], in1=cosq[64:96, c0:c0 + n], op=ALU.mult),
                        R=[QT, cosq], W=[t1])
                self.op("dve", lambda e, c0=c0, n=n: e.tensor_tensor(out=t2[64:96, 0:n], in0=B[1].t[64:96, 512:512 + n], in1=sinq[64:96, c0:c0 + n], op=ALU.mult),
                        R=[B[1], sinq], W=[t2])
                self.op("dve", lambda e, c0=c0, n=n: e.tensor_tensor(out=QT[64:96, c0:c0 + n], in0=t1[64:96, 0:n], in1=t2[64:96, 0:n], op=ALU.add),
                        R=[t1, t2], W=[QT])
            if p2c < 4:
                continue
            si = 0
            groups = []
            for s in range(16):
                nkb = 4 * (s // 2) + (2 if s % 2 == 0 else 4)
                for g0 in range(0, nkb, 4):
                    groups.append((s, nkb, list(range(g0, min(g0 + 4, nkb)))))
            Obank = [(Dp, 0), (B[1], 512)]

            def qk(gi):
                s, nkb, blks = groups[gi]
                Sp = C[gi % 2]; so = (gi % 2) * 512
                for j, kb in enumerate(blks):
                    self.mm(Sp, Sp.t[:, so + j * 128:so + (j + 1) * 128], KT, KT[0:96, kb * 128:(kb + 1) * 128],
                            QT, QT[0:96, s * 128:(s + 1) * 128], True, True)
            qk(0)
            for gi, (s, nkb, blks) in enumerate(groups):
                if gi + 1 < len(groups):
                    qk(gi + 1)
                Sp = C[gi % 2]; so = (gi % 2) * 512
                Ops, oo = Obank[s % 2]
                pt = pt_r.next()
                nb = len(blks)
                self.op("act", lambda e, Sp=Sp, so=so, nb=nb, pt=pt: e.activation(
                    out=pt[:, 0:nb, :], in_=Sp.t[:, so:so + nb * 128].rearrange("p (j q) -> p j q", j=nb), func=AF.Exp),
                    R=[Sp], W=[pt])
                for j, kb in enumerate(blks):
                    if kb >= nkb - 2:
                        mi = (0 if s % 2 == 0 else 2) + (kb - (nkb - 2))
                        self.op("dve", lambda e, pt=pt, j=j, mi=mi: e.tensor_tensor(out=pt[:, j, :], in0=pt[:, j, :], in1=maskT[:, mi, :], op=ALU.mult),
                                R=[pt, maskT], W=[pt])
                for j, kb in enumerate(blks):
                    self.mm(Ops, Ops.t[:, oo:oo + 72], pt, pt[:, j, :], V, V[:, kb, :], kb == 0, kb == nkb - 1)
                if blks[-1] == nkb - 1:
                    sm = self.small.next()
                    self.op("dve", lambda e, sm=sm, Ops=Ops, oo=oo: e.reciprocal(out=sm[:, 0:1], in_=Ops.t[:, oo + 64:oo + 65]), R=[Ops], W=[sm])
                    self.op("dve", lambda e, sm=sm, Ops=Ops, oo=oo, s=s, h=h: e.tensor_scalar(out=self.o_all[:, s, h * 64:(h + 1) * 64], in0=Ops.t[:, oo:oo + 64],
                                                                                   scalar1=sm[:, 0:1], scalar2=None, op0=ALU.mult),
                            R=[Ops, sm], W=[self.o_all])
            si = len(groups)
            if p2c < 5:
                continue
            Ops = Dp
            for bt in range(4):
                base = NKP + bt * 1040
                Sp = C[si % 2]; so = (si % 2) * 512; si += 1
                qc = 2048 + bt * 16
                for j in range(8):
                    self.mm(Sp, Sp.t[:, so + j * 16:so + (j + 1) * 16], KT, KT[0:96, base + j * 128:base + (j + 1) * 128],
                            QT, QT[0:96, qc:qc + 16], True, True)
                self.mm(Sp, Sp.t[0:16, so + 128:so + 144], KT, KT[0:96, base + 1024:base + 1040], QT, QT[0:96, qc:qc + 16], True, True)
                p = ptS[bt]
                self.op("act", lambda e, Sp=Sp, so=so, p=p, bt=bt: e.activation(
                    out=p[:, 0:8, bt * 16:(bt + 1) * 16], in_=Sp.t[:, so:so + 128].rearrange("p (j q) -> p j q", j=8), func=AF.Exp),
                    R=[Sp], W=[p])
                self.op("act", lambda e, Sp=Sp, so=so, p=p, bt=bt: e.activation(
                    out=p[0:16, 8, bt * 16:(bt + 1) * 16], in_=Sp.t[0:16, so + 128:so + 144], func=AF.Exp), R=[Sp], W=[p])
                for j in range(9):
                    rows = 128 if j < 8 else 16
                    self.mm(Ops, Ops.t[0:64, 0:72], p, p[0:rows, j, :], V, V[0:rows, 32 + bt * 9 + j, :],
                            bt == 0 and j == 0, bt == 3 and j == 8)
            sm = self.small.next()
            self.op("dve", lambda e, sm=sm, Ops=Ops: e.reciprocal(out=sm[0:64, 0:1], in_=Ops.t[0:64, 64:65]), R=[Ops], W=[sm])
            self.op("dve", lambda e, sm=sm, Ops=Ops, h=h: e.tensor_scalar(out=self.o_all[0:64, 16, h * 64:(h + 1) * 64], in0=Ops.t[0:64, 0:64],
                                                                     scalar1=sm[0:64, 0:1], scalar2=None, op0=ALU.mult),
                    R=[Ops, sm], W=[self.o_all])
        self.pop()

    def phase3a(self):
        I, O = self.I, self.O
        self.push()
        Wz = self.wload("Wz", I["w_in"], D, 4096, IN_Z)
        woa = self.wload("woa", I["w_oa"], D, D)
        wob = self.wload("wob", I["w_ob"], D, D)
        gb_gm = self.gbload("gb_gm", I["g_gm"], D)
        self.emit_conv()
        trilm = self.sb("trilm", [128, 128], F32)
        trilms = self.sb("trilms", [128, 128], F32)
        self.load(trilm, trilm[:, :], I["trilm"])
        self.load(trilms, trilms[:, :], I["trilms"])
        wsf = self.sb("wsf", [128, 8, 128], F32)
        wsb = self.sb("wsb", [128, 8, 128], BF16)
        WmT = [self.sb("WmT", [128, 8, 128], BF16) for _ in range(2)]
        bT = [self.sb("bT", [128, 8], F32) for _ in range(2)]
        ws_tgs = I["w_s"].rearrange("(g t) s -> t g s", g=8)
        for k in range(2):
            if k == 0:
                self.load(wsf, wsf[:, :, :], ws_tgs)
                self.load(bT[0], bT[0][:, :], I["b_s"].rearrange("g t -> t g"), slow=True)
                tm = trilm
            else:
                self.op("pool", lambda e: e.memset(wsf[:, :, :], 0.0), W=[wsf])
                self.op("pool", lambda e: e.memset(bT[1][:, :], 0.0), W=[bT[1]])
                for bt in range(4):
                    self.load(wsf, wsf[bt * 16:(bt + 1) * 16, :, bt * 16:(bt + 1) * 16], ws_tgs[0:16, :, 0:16])
                    self.load(bT[1], bT[1][bt * 16:(bt + 1) * 16, :], I["b_s"][:, 0:16].rearrange("g t -> t g"), slow=True)
                tm = trilms
            self.op("dve", lambda e, tm=tm: e.tensor_tensor(out=wsb[:, :, :], in0=wsf[:, :, :],
                                                           in1=tm[:, :].unsqueeze(1).to_broadcast([128, 8, 128]), op=ALU.mult),
                    R=[wsf, tm], W=[wsb])
            self.transposes(wsb, lambda j: wsb[:, j, :], 8, WmT[k], WmT[k][:, :, :])
        u = self.sb("u", [128, D], F32)
        gv = self.sb("gv", [128, D], F32)
        vg = self.sb("vg", [128, D], F32)
        vgb = self.sb("vgb", [128, D], BF16)
        sig = self.sb("sig", [128, D], F32)
        um = self.sb("um", [128, D], BF16)
        umT = self.sb("umT", [128, 8, 128], BF16)
        oT = self.sb("oT", [128, 8, 128], BF16)
        tt = self.sb("tt", [128, D], F32)
        mo_r = self.ring("mo", [128, D], BF16, 2)
        A, B, C = self.A, self.B, self.C
        zi = 0
        for i in range(NOWN):
            k = 0 if i < 16 else 1
            xt = self.X.next()
            self.load(xt, xt[:, :], I["x_own"][i * 128:(i + 1) * 128, :])
            hT = self.HT.next()
            self.normT(xt, self.gb_mix, self.HB, hT)

            def zchunk(col, func, dst, dst_ap):
                nonlocal zi
                P = A[zi % 2]; po = (zi % 2) * 512; zi += 1
                self.proj(P, P.t[:, po:po + 512], hT, Wz, col, 512)
                self.op("act", lambda e: e.activation(out=dst_ap, in_=P.t[:, po:po + 512], func=func), R=[P], W=[dst])
            for n in range(2):
                zchunk(n * 512, AF.Gelu, u, u[:, n * 512:(n + 1) * 512])
            for n in range(2):
                zchunk(1024 + n * 512, AF.Gelu, gv, gv[:, n * 512:(n + 1) * 512])
            self.rms_tok(gv, gv[:, :], D, gb_gm, gb_gm[:, :], vg, vg[:, :])
            if i == 16:
                self.store(O["vg_s"][:, :], vg, vg[0:64, :])
            self.op("act", lambda e: e.copy(out=vgb[:, :], in_=vg[:, :]), R=[vg], W=[vgb])
            for g in range(8):
                Pm = B[g // 4]
                self.mm(Pm, Pm.t[:, g * 128:(g + 1) * 128], WmT[k], WmT[k][:, g, :], vgb, vgb[:, g * 128:(g + 1) * 128], True, True)
            for g in range(8):
                Pm = B[g // 4]
                self.op("dve", lambda e, g=g, Pm=Pm, k=k: e.scalar_tensor_tensor(
                    out=um[:, g * 128:(g + 1) * 128], in0=Pm.t[:, g * 128:(g + 1) * 128], scalar=bT[k][:, g:g + 1],
                    in1=u[:, g * 128:(g + 1) * 128], op0=ALU.add, op1=ALU.mult), R=[Pm, bT[k], u], W=[um])
            self.transposes(self.o_all, lambda j, i=i: self.o_all[:, i, j * 128:(j + 1) * 128], 8, oT, oT[:, :, :])
            for n in range(2):
                self.proj(C[n], C[n].t[:, n * 512:(n + 1) * 512], oT, woa, n * 512, 512)
            for n in range(2):
                zchunk(2048 + n * 512, AF.Sigmoid, sig, sig[:, n * 512:(n + 1) * 512])
            for n in range(2):
                self.op("dve", lambda e, n=n: e.tensor_tensor(out=tt[:, n * 512:(n + 1) * 512], in0=sig[:, n * 512:(n + 1) * 512],
                                                           in1=C[n].t[:, n * 512:(n + 1) * 512], op=ALU.mult), R=[sig, C[n]], W=[tt])
            self.transposes(um, lambda j: um[:, j * 128:(j + 1) * 128], 8, umT, umT[:, :, :])
            for n in range(2):
                self.proj(C[n], C[n].t[:, n * 512:(n + 1) * 512], umT, wob, n * 512, 512)
            for n in range(2):
                zchunk(3072 + n * 512, AF.Sigmoid, sig, sig[:, n * 512:(n + 1) * 512])
            for n in range(2):
                self.op("dve", lambda e, n=n: e.tensor_tensor(out=sig[:, n * 512:(n + 1) * 512], in0=sig[:, n * 512:(n + 1) * 512],
                                                           in1=C[n].t[:, n * 512:(n + 1) * 512], op=ALU.mult), R=[sig, C[n]], W=[sig])
            mo = mo_r.next()
            self.op("dve", lambda e, mo=mo: e.tensor_tensor(out=mo[:, :], in0=sig[:, :], in1=tt[:, :], op=ALU.add),
                    R=[sig, tt], W=[mo])
            self.dma(lambda e, mo=mo, i=i: e.dma_start(out=self.mscr.t[i * 128:(i + 1) * 128, :], in_=mo[:, :]), R=[mo], W=[self.mscr])
        self.pop()

    def phase3b(self):
        I, O = self.I, self.O
        A, B, C, Dp, pT = self.A, self.B, self.C, self.Dp, self.psT
        self.push()
        wo = self.wload("wo", I["w_o"], D, D)
        wcq = self.wload("wcq", I["w_cq"], D, 512)
        wco = self.wload("wco", I["w_co"], 512, D)
        wpq = self.wload("wpq", I["w_pq"], D, 2048)
        gb_xa = self.gbload("gb_xa", I["g_xattn"], D)
        gb_ff = self.gbload("gb_ff", I["g_ffn"], D)
        ones128 = self.sb("ones128", [128, 128], BF16)
        self.load(ones128, ones128[:, :], I["ones128"], q="pool")
        cmask = self.sb("cmask", [128, 255], BF16)
        self.load(cmask, cmask[:, :], I["cmask"], q="pool")
        gcq = self.sb("gcq", [128, 2], F32)
        self.load(gcq, gcq[:, 0:1], I["g_cq"][0:1, 0:128].rearrange("o d -> d o"))
        self.op("dve", lambda e: e.tensor_scalar(out=gcq[:, 1:2], in0=gcq[:, 0:1], scalar1=MEM_SCALE, scalar2=None, op0=ALU.mult),
                R=[gcq], W=[gcq])
        memKT = [self.sb("memKT", [128, 4, 256], BF16) for _ in range(4)]
        memV = [self.sb("memV", [128, 2, 4, 136], BF16) for _ in range(4)]
        for t in memV:
            self.op("pool", lambda e, t=t: e.memset(t[:, :, :, :], 0.0), W=[t])
            self.op("pool", lambda e, t=t: e.memset(t[:, :, :, 128:129], 1.0), W=[t])
        kf = self.sb("kf", [128, 512], F32)
        kb16 = self.sb("kb16", [128, 512], BF16)
        skT = self.sb("skT", [128, 16, 128], BF16)

        self.push()
        wck = self.wload("wck", I["w_ck"], D, 512)
        wcv = self.wload("wcv", I["w_cv"], D, 512)
        gb_mem = self.gbload("gb_mem", I["g_mem"], D)
        gb_ck = self.gbload("gb_ck", I["g_ck"], 128)
        sq4 = self.sb("sq4", [128, 512], F32)
        skb = self.sb("skb", [128, 16, 128], BF16)
        self.load(skb, skb[:, :, :], I["sub_keys"].rearrange("(a n) d -> n a d", a=16), q="pool")
        for half in range(2):
            self.transposes(skb, lambda j, half=half: skb[:, half * 8 + j, :], 8, skT, skT[:, half * 8:(half + 1) * 8, :])
        for mt in range(2):
            xt = self.X.next()
            self.load(xt, xt[:, :], I["mem_p"][mt * 128:(mt + 1) * 128, :])
            hT = self.HT.next()
            self.normT(xt, gb_mem, self.HB, hT)
            P = A[0]
            self.proj(P, P.t[:, 0:512], hT, wck, 0, 512)
            sm = self.small.next()
            self.op("act", lambda e: e.activation(out=sq4[:, :], in_=P.t[:, 0:512], func=AF.Square), R=[P], W=[sq4])
            self.op("dve", lambda e, sm=sm: e.tensor_reduce(out=sm[:, 0:4], in_=sq4[:, :].rearrange("p (h d) -> p h d", h=4),
                                                          axis=AX.X, op=ALU.add), R=[sq4], W=[sm])
            self.op("act", lambda e, sm=sm: e.activation(out=sm[:, 4:8], in_=sm[:, 0:4], func=AF.Sqrt, bias=self.epsb[:, 0:1],
                                                       scale=1.0 / 128), R=[sm, self.epsb], W=[sm])
            self.op("dve", lambda e, sm=sm: e.reciprocal(out=sm[:, 4:8], in_=sm[:, 4:8]), R=[sm], W=[sm])
            self.op("dve", lambda e, sm=sm: e.tensor_tensor(out=kf[:, :].rearrange("p (h d) -> p h d", h=4),
                                                          in0=P.t[:, 0:512].rearrange("p (h d) -> p h d", h=4),
                                                          in1=sm[:, 4:8].unsqueeze(2).to_broadcast([128, 4, 128]), op=ALU.mult),
                    R=[P, sm], W=[kf])
            self.op("dve", lambda e: e.tensor_tensor(out=kf[:, :].rearrange("p (h d) -> p h d", h=4),
                                                   in0=kf[:, :].rearrange("p (h d) -> p h d", h=4),
                                                   in1=gb_ck[:, :].unsqueeze(1).to_broadcast([128, 4, 128]), op=ALU.mult),
                    R=[kf, gb_ck], W=[kf])
            self.store(O["mk_p"][mt * 128:(mt + 1) * 128, :], kf, kf[:, :])
            self.op("act", lambda e: e.copy(out=kb16[:, :], in_=kf[:, :]), R=[kf], W=[kb16])
            self.transposes(kb16, lambda j: kb16[:, j * 128:(j + 1) * 128], 4, memKT[0], memKT[0][:, :, mt * 128:(mt + 1) * 128])
            P2 = A[1]
            self.proj(P2, P2.t[:, 512:1024], hT, wcv, 0, 512)
            self.op("act", lambda e: e.copy(out=sq4[:, :], in_=P2.t[:, 512:1024]), R=[P2], W=[sq4])
            self.store(O["mv_p"][mt * 128:(mt + 1) * 128, :], sq4, sq4[:, :])
            self.op("dve", lambda e, mt=mt: e.tensor_copy(out=memV[0][:, mt, :, 0:128], in_=sq4[:, :].rearrange("p (h d) -> p h d", h=4)),
                    R=[sq4], W=[memV[0]])
        self.pop()

        def load_sample_mem():
            for bt in range(4):
                for mt in range(2):
                    r0 = bt * 256 + mt * 128
                    self.load(kf, kf[:, :], I["memk_c"][r0:r0 + 128, :])
                    self.op("act", lambda e: e.copy(out=kb16[:, :], in_=kf[:, :]), R=[kf], W=[kb16])
                    self.transposes(kb16, lambda j: kb16[:, j * 128:(j + 1) * 128], 4, memKT[bt], memKT[bt][:, :, mt * 128:(mt + 1) * 128])
                    xt = self.X.next()
                    self.load(xt, xt[:, 0:512], I["memv_c"][r0:r0 + 128, :])
                    self.op("dve", lambda e, xt=xt, bt=bt, mt=mt: e.tensor_copy(out=memV[bt][:, mt, :, 0:128],
                                                                           in_=xt[:, 0:512].rearrange("p (h d) -> p h d", h=4)),
                            R=[xt], W=[memV[bt]])

        TS = [(self.sb("x2", [128, D], F32), self.sb("h3b", [128, D], BF16), self.sb("idxT", [128, 128], I32), self.sb("gT", [128, 128], F32))
              for _ in range(2)]
        mi_r = self.ring("mi", [128, D], BF16, 1)
        mT = self.sb("mT", [128, 8, 128], BF16)
        sqc = self.sb("sqc", [128, 512], BF16)
        rsc = self.sb("rsc", [128, 512], F32)
        qcn = self.sb("qcn", [128, 4, 128], BF16)
        ptc = self.sb("ptc", [128, 8, 128], BF16)
        ptcS = [self.sb("ptcS", [128, 8, 128], BF16) for _ in range(4)]
        for t in ptcS:
            self.op("pool", lambda e, t=t: e.memset(t[:, :, :], 0.0), W=[t])
        oc = self.sb("oc", [128, 4, 128], BF16)
        ocT = self.sb("ocT", [128, 4, 128], BF16)
        h3T = self.sb("h3T", [128, 8, 128], BF16)
        qTb = self.sb("qTb", [128, 16, 128], BF16)
        sc = self.sb("sc", [128, 2048], F32)
        wks = [self.sb("wk", [128, 256], F32) for _ in range(2)]
        sv = self.sb("sv", [128, 16, 16], F32)
        si = self.sb("si", [128, 16, 16], U32)
        sif = self.sb("sif", [128, 16, 16], F32)
        ts = self.sb("ts", [128, 8, 16], F32)
        sel = self.sb("sel", [128, 8, 16], U32)
        self_f = self.sb("self_f", [128, 8, 16], F32)
        k1i = self.sb("k1i", [128, 8, 16], I32)
        k1f = self.sb("k1f", [128, 8, 16], F32)
        k2f = self.sb("k2f", [128, 8, 16], F32)
        iota16 = self.sb("iota16", [128, 16], F32)
        for k in range(16):
            self.op("pool", lambda e, k=k: e.memset(iota16[:, k:k + 1], float(k)), W=[iota16])
        eq = self.sb("eq", [128, 4, 16, 16], BF16)
        i12 = self.sb("i12", [128, 2, 128], F32)
        tb = self.sb("tb", [128, 3, 128], BF16)
        ex = self.sb("ex", [128, 8, 16], F32)
        i1T = self.sb("i1T", [128, 128], F32)
        i2T = self.sb("i2T", [128, 128], F32)
        NB = self.dbg.get("NB", 8)
        GUV = self.ring("GUV", [128, 2 * D], BF16, NB)
        wsel_r = self.ring("wsel", [128, 128], BF16, 4)
        djunk_r = self.ring("djunk", [128, D], BF16, 1)
        act_r = self.ring("actc", [128, 1], F32, 6)
        gl_r = self.ring("glc", [128, 2], F32, 6)

        svb = [T(sv.t, Buf("svb")) for _ in range(16)]
        sib = [T(si.t, Buf("sib")) for _ in range(16)]
        tsb = [T(ts.t, Buf("tsb")) for _ in range(8)]
        selb = [T(sel.t, Buf("selb")) for _ in range(8)]

        def pre(i):
            x2, h3b, idxT, gT = TS[i % 2]
            x1 = x2
            if i == 16:
                load_sample_mem()
            xt = self.X.next()
            yield self.load(xt, xt[:, :], I["x_own"][i * 128:(i + 1) * 128, :])
            mi_ = mi_r.next()
            yield self.dma(lambda e, mi_=mi_, i=i: e.dma_start(out=mi_[:, :], in_=self.mscr.t[i * 128:(i + 1) * 128, :]), R=[self.mscr], W=[mi_])
            yield from (None for _ in range(2))
            yield self.transposes(mi_, lambda j, mi_=mi_: mi_[:, j * 128:(j + 1) * 128], 8, mT, mT[:, :, :])
            yield from (None for _ in range(2))
            for n in range(2):
                yield self.proj(Dp, Dp.t[:, 0:512], mT, wo, n * 512, 512)
                yield self.op("dve", lambda e, n=n, xt=xt: e.tensor_tensor(out=x1[:, n * 512:(n + 1) * 512], in0=xt[:, n * 512:(n + 1) * 512],
                                                                 in1=Dp.t[:, 0:512], op=ALU.add), R=[xt, Dp], W=[x1])
            hT = self.HT.next()
            yield self.rms_tok(x1, x1[:, :], D, gb_xa, gb_xa[:, :], self.HB, self.HB[:, :])
            yield from (None for _ in range(3))
            yield self.transposes(self.HB, lambda j: self.HB[:, j * 128:(j + 1) * 128], 8, hT, hT[:, :, :])
            yield from (None for _ in range(2))
            P = Dp
            for hd in range(4):
                for c in range(8):
                    yield self.mm(P, P.t[:, hd * 128:(hd + 1) * 128], wcq, wcq[:, c, hd * 128:(hd + 1) * 128], hT, hT[:, c, :], c == 0, c == 7)
            yield self.op("act", lambda e: e.activation(out=sqc[:, :], in_=P.t[:, 0:512], func=AF.Square), R=[P], W=[sqc])
            yield self.op("act", lambda e: e.copy(out=kf[:, :], in_=P.t[:, 0:512]), R=[P], W=[kf])
            yield from (None for _ in range(2))
            yield self.mm(P, P.t[:, 0:512], ones128, ones128[:, :], sqc, sqc[:, :], True, True)
            yield self.op("act", lambda e: e.activation(out=rsc[:, :], in_=P.t[:, 0:512], func=AF.Sqrt, bias=self.epsb[:, 0:1], scale=1.0),
                    R=[P, self.epsb], W=[rsc])
            yield self.op("dve", lambda e: e.reciprocal(out=rsc[:, :], in_=rsc[:, :]), R=[rsc], W=[rsc])
            yield self.op("dve", lambda e: e.scalar_tensor_tensor(out=qcn[:, :, :].rearrange("p h t -> p (h t)"), in0=kf[:, :], scalar=gcq[:, 1:2],
                                                            in1=rsc[:, :], op0=ALU.mult, op1=ALU.mult), R=[kf, gcq, rsc], W=[qcn])
            yield from (None for _ in range(3))
            if i < 16:
                batches = [(0, 0, 128, ptc)]
            else:
                batches = [(bt, bt * 16, 16, ptcS[bt]) for bt in range(4)]
            for (mb, c0, n, p) in batches:
                for pair in range(2):
                    for hd in (2 * pair, 2 * pair + 1):
                        for mt in range(2):
                            jj = (hd % 2) * 2 + mt
                            yield self.mm(P, P.t[:, jj * 128:jj * 128 + n], memKT[mb], memKT[mb][:, hd, mt * 128:(mt + 1) * 128],
                                          qcn, qcn[:, hd, c0:c0 + n], True, True)
                    yield self.op("act", lambda e, pair=pair, p=p, c0=c0, n=n: e.activation(
                        out=p[:, pair * 4:(pair + 1) * 4, c0:c0 + n],
                        in_=P.t[:, 0:512].rearrange("p (j q) -> p j q", j=4)[:, :, 0:n], func=AF.Exp),
                        R=[P], W=[p])
            yield from (None for _ in range(2))
            nq = 128
            for pair in range(2):
                for hd in (2 * pair, 2 * pair + 1):
                    col = (hd % 2) * 256
                    nacc = len(batches) * 2
                    k = 0
                    for (mb, c0, n, p) in batches:
                        for mt in range(2):
                            yield self.mm(P, P.t[0:nq, col:col + 136], p, p[:, hd * 2 + mt, 0:nq], memV[mb], memV[mb][:, mt, hd, :], k == 0, k == nacc - 1)
                            k += 1
                sm = self.small.next()
                ocv = P.t[:, 0:512].rearrange("p (h c) -> p h c", h=2)
                yield self.op("dve", lambda e, sm=sm, ocv=ocv: e.tensor_scalar(out=sm[:, 0:2], in0=ocv[:, :, 128:129].rearrange("p h o -> p (h o)"),
                                                                     scalar1=1e-30, scalar2=None, op0=ALU.max), R=[P], W=[sm])
                yield self.op("dve", lambda e, sm=sm: e.reciprocal(out=sm[:, 0:2], in_=sm[:, 0:2]), R=[sm], W=[sm])
                yield self.op("dve", lambda e, sm=sm, ocv=ocv, pair=pair: e.tensor_tensor(out=oc[:, 2 * pair:2 * pair + 2, :], in0=ocv[:, :, 0:128],
                                                                                in1=sm[:, 0:2].unsqueeze(2).to_broadcast([128, 2, 128]), op=ALU.mult),
                        R=[P, sm], W=[oc])
            yield from (None for _ in range(2))
            yield self.transposes(oc, lambda j: oc[:, j, :], 4, ocT, ocT[:, :, :])
            yield from (None for _ in range(2))
            for n in range(2):
                yield self.proj(P, P.t[:, 0:512], ocT, wco, n * 512, 512, kc=4)
                yield self.op("dve", lambda e, n=n: e.tensor_tensor(out=x2[:, n * 512:(n + 1) * 512], in0=x1[:, n * 512:(n + 1) * 512],
                                                           in1=P.t[:, 0:512], op=ALU.add), R=[x1, P], W=[x2])
            yield self.rms_tok(x2, x2[:, :], D, gb_ff, gb_ff[:, :], h3b, h3b[:, :])
            yield from (None for _ in range(3))
            yield self.transposes(h3b, lambda j: h3b[:, j * 128:(j + 1) * 128], 8, h3T, h3T[:, :, :])
            yield from (None for _ in range(2))
            for r in range(4):
                for hp in range(r * 4, r * 4 + 4):
                    off = (hp % 4) * 128
                    for c in range(8):
                        yield self.mm(P, P.t[:, off:off + 128], wpq, wpq[:, c, hp * 128:(hp + 1) * 128], h3T, h3T[:, c, :], c == 0, c == 7)
                yield self.op("act", lambda e, r=r: e.copy(out=qTb[:, r * 4:r * 4 + 4, :].rearrange("p a t -> p (a t)"), in_=P.t[:, 0:512]),
                              R=[P], W=[qTb])
            for r in range(4):
                for hp in range(r * 4, r * 4 + 4):
                    off = (hp % 4) * 128
                    yield self.mm(P, P.t[:, off:off + 128], qTb, qTb[:, hp, :], skT, skT[:, hp, :], True, True)
                yield self.op("act", lambda e, r=r: e.copy(out=sc[:, r * 512:(r + 1) * 512], in_=P.t[:, 0:512]), R=[P], W=[sc])

            def top16_pair(items):
                for stage in range(5):
                    for (src_ap, width, vout, iout, wkb, vb, ib) in items:
                        if stage == 0:
                            self.op("dve", lambda e, vout=vout, src_ap=src_ap: e.max(out=vout[:, 0:8], in_=src_ap), R=[sc], W=[vb])
                        elif stage == 1:
                            self.op("dve", lambda e, vout=vout, iout=iout, src_ap=src_ap: e.max_index(out=iout[:, 0:8], in_max=vout[:, 0:8], in_values=src_ap),
                                    R=[sc, vb], W=[ib])
                        elif stage == 2:
                            self.op("dve", lambda e, vout=vout, src_ap=src_ap, wkb=wkb, width=width: e.match_replace(
                                out=wkb[:, 0:width], in_to_replace=vout[:, 0:8], in_values=src_ap, imm_value=-1e30), R=[sc, vb], W=[wkb])
                        elif stage == 3:
                            self.op("dve", lambda e, vout=vout, wkb=wkb, width=width: e.max(out=vout[:, 8:16], in_=wkb[:, 0:width]), R=[wkb], W=[vb])
                        else:
                            self.op("dve", lambda e, vout=vout, iout=iout, wkb=wkb, width=width: e.max_index(
                                out=iout[:, 8:16], in_max=vout[:, 8:16], in_values=wkb[:, 0:width]), R=[wkb, vb], W=[ib])
                    yield None
            for hp in range(0, 16, 2):
                yield from top16_pair([(sc[:, q * 128:(q + 1) * 128], 128, sv[:, q, :], si[:, q, :], wks[q % 2], svb[q], sib[q]) for q in (hp, hp + 1)])
            yield self.op("dve", lambda e: e.tensor_copy(out=sif[:, :, :], in_=si[:, :, :]), R=sib, W=[sif])
            sv4 = sv[:, :, :].rearrange("p (h two) k -> p h two k", two=2)
            cand = sc[:, :].rearrange("p (h a b) -> p h a b", h=8, a=16)
            yield self.op("dve", lambda e: e.tensor_tensor(out=cand, in0=sv4[:, :, 0, :].unsqueeze(3).to_broadcast([128, 8, 16, 16]),
                                                   in1=sv4[:, :, 1, :].unsqueeze(2).to_broadcast([128, 8, 16, 16]), op=ALU.add),
                    R=svb, W=[sc])
            for hh in range(0, 8, 2):
                yield from top16_pair([(sc[:, q * 256:(q + 1) * 256], 256, ts[:, q, :], sel[:, q, :], wks[q % 2], tsb[q], selb[q]) for q in (hh, hh + 1)])
            yield self.op("dve", lambda e: e.tensor_copy(out=self_f[:, :, :], in_=sel[:, :, :]), R=selb, W=[self_f])
            yield self.op("dve", lambda e: e.tensor_scalar(out=k1i[:, :, :], in0=self_f[:, :, :], scalar1=-7.5, scalar2=0.0625, op0=ALU.add, op1=ALU.mult),
                    R=[self_f], W=[k1i])
            yield self.op("dve", lambda e: e.tensor_copy(out=k1f[:, :, :], in_=k1i[:, :, :]), R=[k1i], W=[k1f])
            yield self.op("dve", lambda e: e.scalar_tensor_tensor(out=k2f[:, :, :], in0=k1f[:, :, :], scalar=-16.0, in1=self_f[:, :, :],
                                                            op0=ALU.mult, op1=ALU.add), R=[k1f, self_f], W=[k2f])
            sif4 = sif[:, :, :].rearrange("p (h two) k -> p h two k", two=2)
            io4 = iota16[:, :].unsqueeze(1).unsqueeze(1).to_broadcast([128, 4, 16, 16])
            for which, kf_ in ((0, k1f), (1, k2f)):
                for h2 in range(2):
                    hs = slice(h2 * 4, h2 * 4 + 4)
                    yield self.op("dve", lambda e, kf_=kf_, hs=hs: e.tensor_tensor(out=eq[:, :, :, :], in0=io4,
                                                                          in1=kf_[:, hs, :].unsqueeze(3).to_broadcast([128, 4, 16, 16]), op=ALU.is_equal),
                            R=[iota16, kf_], W=[eq])
                    yield self.op("pool", lambda e, which=which, hs=hs: e.tensor_tensor(out=eq[:, :, :, :], in0=eq[:, :, :, :],
                                                                              in1=sif4[:, hs, which, :].unsqueeze(2).to_broadcast([128, 4, 16, 16]), op=ALU.mult),
                            R=[eq, sif], W=[eq])
                    yield self.op("dve", lambda e, which=which, h2=h2: e.tensor_reduce(out=i12[:, which, h2 * 64:(h2 + 1) * 64].rearrange("p (h k) -> p h k", h=4),
                                                                              in_=eq[:, :, :, :], axis=AX.X, op=ALU.add), R=[eq], W=[i12])
            yield self.op("act", lambda e: e.copy(out=tb[:, 0:2, :], in_=i12[:, :, :]), R=[i12], W=[tb])
            yield self.op("dve", lambda e: e.tensor_tensor(out=ex[:, :, :], in0=ts[:, :, :], in1=ts[:, :, 0:1].to_broadcast([128, 8, 16]), op=ALU.subtract),
                    R=tsb, W=[ex])
            yield self.op("act", lambda e: e.activation(out=ex[:, :, :], in_=ex[:, :, :], func=AF.Exp), R=[ex], W=[ex])
            sm = self.small.next()
            yield self.op("dve", lambda e, sm=sm: e.tensor_reduce(out=sm[:, 0:8], in_=ex[:, :, :], axis=AX.X, op=ALU.add), R=[ex], W=[sm])
            yield self.op("dve", lambda e, sm=sm: e.reciprocal(out=sm[:, 0:8], in_=sm[:, 0:8]), R=[sm], W=[sm])
            yield self.op("dve", lambda e, sm=sm: e.tensor_tensor(out=tb[:, 2, :].rearrange("p (h k) -> p h k", h=8), in0=ex[:, :, :],
                                                          in1=sm[:, 0:8].unsqueeze(2).to_broadcast([128, 8, 16]), op=ALU.mult),
                    R=[ex, sm], W=[tb])
            yield from (None for _ in range(3))
            for j in range(3):
                yield self.op("pe", lambda e, j=j: e.transpose(out=pT[:, j * 128:(j + 1) * 128], in_=tb[:, j, :], identity=self.ident[:, :]),
                        R=[tb, self.ident], W=[pT])
            yield self.op("act", lambda e: e.copy(out=i1T[:, :], in_=pT[:, 0:128]), R=[pT], W=[i1T])
            yield self.op("act", lambda e: e.copy(out=i2T[:, :], in_=pT[:, 128:256]), R=[pT], W=[i2T])
            yield self.op("dve", lambda e: e.scalar_tensor_tensor(out=idxT[:, :], in0=i1T[:, :], scalar=128.0, in1=i2T[:, :],
                                                            op0=ALU.mult, op1=ALU.add), R=[i1T, i2T], W=[idxT])
            yield self.op("act", lambda e: e.copy(out=gT[:, :], in_=pT[:, 256:384]), R=[pT], W=[gT])
        def loop(i, gen):
            x2, h3b, idxT, gT = TS[i % 2]
            ntok = 128 if i < 16 else 64
            Pacc = [B[0], B[1]]
            st_g, st_x, st_w = {}, {}, {}

            def stageA(t):
                guv = GUV.next()
                st_g[t] = guv
                self.dma(lambda e: e.indirect_dma_start(out=guv[:, :], out_offset=None, in_=self.uvtab.t,
                                                         in_offset=bass.IndirectOffsetOnAxis(ap=idxT[:, t:t + 1], axis=0)),
                         R=[idxT, self.uvtab], W=[guv], q="pool")

            st_a = {}

            def stageB1(t):
                guv = st_g[t]
                ac = act_r.next(); dj = djunk_r.next()
                st_a[t] = ac
                Px = A if order.index(t) % 2 == 0 else C
                tX = Px[0].t
                for n in range(2):
                    self.mm(Px[n], tX[:, n * 512:(n + 1) * 512], self.ident, self.ident[:, t:t + 1].to_broadcast([128, 128]),
                            h3b, h3b[:, n * 512:(n + 1) * 512], True, True)
                self.op("dve", lambda e: e.scalar_tensor_tensor(out=dj[:, :], in0=guv[:, 0:D], scalar=1.0, in1=tX[:, :], op0=ALU.mult, op1=ALU.mult,
                                                                accum_out=ac[:, 0:1]), R=[guv, Px[0], Px[1]], W=[dj, ac])

            def stageB2(t):
                ac = st_a.pop(t)
                g2 = gl_r.next()
                self.op("act", lambda e: e.activation(out=g2[:, 0:1], in_=ac[:, 0:1], func=AF.Gelu), R=[ac], W=[g2])
                self.op("act", lambda e: e.activation(out=g2[:, 1:2], in_=g2[:, 0:1], func=AF.Copy, scale=gT[:, t:t + 1]), R=[g2, gT], W=[g2])
                ws = wsel_r.next()
                st_w[t] = ws
                r32 = t % 32
                self.op("act", lambda e: e.activation(out=ws[:, 0:32], in_=cmask[:, 127 - r32:159 - r32], func=AF.Copy, scale=g2[:, 1:2]),
                        R=[cmask, g2], W=[ws])

            ngrp = ntok // 32
            order = [g * 32 + r for r in range(32) for g in range(ngrp)]

            def stageC2(ts_):
                items = [(t, st_w.pop(t), st_g.pop(t)) for t in ts_]
                for n in range(2):
                    for (t, ws, guv) in items:
                        j = t // 32
                        r = t % 32
                        self.op("pe", lambda e, n=n, ws=ws, guv=guv, j=j, r=r: e.matmul(
                            Pacc[n].t[32 * j:32 * j + 32, n * 512:(n + 1) * 512], lhsT=ws[:, 0:32], rhs=guv[:, D + n * 512:D + (n + 1) * 512],
                            start=(r == 0), stop=(r == 31), tile_position=(0, 32 * j)), R=[ws, guv], W=[Pacc[n]])
            LA = NB - 4
            nst = len(order)
            for step in range(nst + LA + 3):
                if step < nst:
                    stageA(order[step])
                if 0 <= step - LA < nst:
                    stageB1(order[step - LA])
                if 0 <= step - LA - 1 < nst:
                    stageB2(order[step - LA - 1])
                k = step - LA - 2
                if k >= 1 and k % 2 == 1 and k < nst:
                    stageC2([order[k - 1], order[k]])
                if gen is not None:
                    for _ in range(3):
                        next(gen, None)
            yo = self.X.next()
            for n in range(2):
                self.op("dve", lambda e, n=n, yo=yo: e.tensor_tensor(out=yo[:, n * 512:(n + 1) * 512], in0=x2[:, n * 512:(n + 1) * 512],
                                                                 in1=Pacc[n].t[:, n * 512:(n + 1) * 512], op=ALU.add), R=[x2, Pacc[n]], W=[yo])
            self.store(O["y_own"][i * 128:(i + 1) * 128, :], yo, yo[:, :])
            if gen is not None:
                for _ in gen:
                    pass

        NADV = self.dbg.get("NADV", 6)
        g0 = pre(0)
        self.npre = 0
        for _ in g0:
            self.npre += 1
        for i in range(NOWN):
            gen = pre(i + 1) if i + 1 < NOWN else None
            loop(i, gen)
        self.pop()


def _own_blocks(j):
    blks = []
    for s in range(16):
        m = s // 2
        if s % 2 == 0:
            blks.append(4 * m + (0 if j == 0 else 1))
        else:
            blks.append(4 * m + (3 if j == 0 else 2))
    return blks


def _consts(j):
    f = np.float32
    c = {}
    c["idn"] = np.eye(128, dtype=f)
    oq = np.zeros((96, 96), f)
    oq[0:64, 0:64] = 1.0 / 64
    oq[64:96, 64:96] = 1.0 / 32
    c["onesq"] = oq
    c["ones128"] = np.full((128, 128), 1.0 / 128, f)
    rm = np.zeros((96, 96), f)
    for d in range(64, 80):
        rm[d + 16, d] = -1.0
    for d in range(80, 96):
        rm[d - 16, d] = 1.0
    c["rotm"] = rm
    cm = np.zeros((128, 255), f)
    cm[:, 127] = 1.0
    c["cmask"] = cm
    c["trilm"] = np.tril(np.ones((128, 128), f))
    tms = np.zeros((128, 128), f)
    for bt in range(4):
        tms[bt * 16:(bt + 1) * 16, bt * 16:(bt + 1) * 16] = np.tril(np.ones((16, 16), f))
    c["trilms"] = tms
    inv = (np.float32(10000.0) ** (-np.arange(16, dtype=f) / np.float32(16))).astype(f)
    pos = np.zeros((128, 33), f)
    for i in range(32):
        pos[:, i] = i * 128 + np.arange(128)
    pos[0:64, 32] = 1024 + (np.arange(64) % 16)
    ang = (pos[:, :, None] * inv[None, None, :]).astype(f)
    c["ropek"] = np.concatenate([np.cos(ang), np.sin(ang)], axis=2).astype(f)
    blks = _own_blocks(j)
    posq = np.zeros((NTOK,), f)
    for s, b in enumerate(blks):
        posq[s * 128:(s + 1) * 128] = b * 128 + np.arange(128)
    posq[2048:2112] = 1024 + (np.arange(64) % 16)
    angq = (posq[None, :] * np.concatenate([inv, inv])[:, None]).astype(f)
    c["cosq"] = np.cos(angq).astype(f)
    c["sinq"] = np.sin(angq).astype(f)
    kk = np.arange(128)[:, None] // 64
    qq = np.arange(128)[None, :] // 64
    diag = (kk <= qq).astype(f)
    full = np.ones((128, 128), f)
    zero = np.zeros((128, 128), f)
    c["masks"] = np.stack([diag, zero, full, diag] if j == 0 else [full, diag, diag, zero]).astype(f)
    return c


_NC_CACHE = {}


def _get_nc(dbg=None):
    key = repr(sorted((dbg or {}).items()))
    if key not in _NC_CACHE:
        k = Kern(dbg)
        _NC_CACHE[key] = k.build()
    return _NC_CACHE[key]


def kernel(x_prompt, x_sample, cache_mla_ckv, cache_mla_krope, cache_mem_k, cache_mem_v, mem_prompt,
           g_mix, w_in, g_q_lat, w_uq, g_qn, g_qr, g_kv_lat, g_kr, w_uk, w_uv, g_kn, w_oa,
           g_gm, w_s, b_s, w_ob, w_o,
           g_xattn, g_mem, w_cq, g_cq, w_ck, g_ck, w_cv, w_co,
           g_ffn, w_pq, sub_keys, peer_u, peer_v, _dbg=None):
    f = np.float32
    A = lambda a: np.ascontiguousarray(np.asarray(a, dtype=f))
    x_prompt = A(x_prompt); x_sample = A(x_sample)
    shared = {
        "w_in": A(w_in)[0], "w_uq": A(w_uq)[0], "w_uk": A(w_uk)[0], "w_uv": A(w_uv)[0],
        "w_oa": A(w_oa)[0], "w_ob": A(w_ob)[0], "w_o": A(w_o)[0],
        "w_cq": A(w_cq)[0], "w_ck": A(w_ck)[0], "w_cv": A(w_cv)[0], "w_co": A(w_co)[0],
        "w_pq": A(w_pq)[0], "sub_keys": A(sub_keys)[0].reshape(2048, 128),
        "peer_u": A(peer_u)[0], "peer_v": A(peer_v)[0],
        "g_mix": A(g_mix), "g_q_lat": A(g_q_lat), "g_qn": A(g_qn), "g_qr": A(g_qr), "g_kv_lat": A(g_kv_lat),
        "g_kr": A(g_kr), "g_kn": A(g_kn), "g_gm": A(g_gm), "g_xattn": A(g_xattn), "g_mem": A(g_mem),
        "g_cq": A(g_cq), "g_ck": A(g_ck), "g_ffn": A(g_ffn),
        "w_s": A(w_s)[0].reshape(1024, 128), "b_s": A(b_s)[0],
    }
    ckv_c = A(cache_mla_ckv)[0]; kr_c = A(cache_mla_krope)[0]
    mk_c = A(cache_mem_k)[0]; mv_c = A(cache_mem_v)[0]; mem_p = A(mem_prompt)
    consts = [_consts(0), _consts(1)]
    in_maps = []
    for c in range(NCORES):
        b, j = c // 2, c % 2
        blks = _own_blocks(j)
        x_own = np.zeros((NTOK, D), f)
        for s, blk in enumerate(blks):
            x_own[s * 128:(s + 1) * 128] = x_prompt[b, blk * 128:(blk + 1) * 128]
        x_own[2048:2112] = x_sample[4 * c:4 * c + 4].reshape(64, D)
        m = dict(shared)
        m.update(consts[j])
        m["x_all"] = x_prompt[b]
        m["x_own"] = x_own
        m["ckv_c"] = np.ascontiguousarray(ckv_c[4 * c:4 * c + 4].reshape(4096, 256))
        m["kr_c"] = np.ascontiguousarray(kr_c[4 * c:4 * c + 4].reshape(4096, 32))
        m["memk_c"] = np.ascontiguousarray(mk_c[4 * c:4 * c + 4].reshape(1024, 512))
        m["memv_c"] = np.ascontiguousarray(mv_c[4 * c:4 * c + 4].reshape(1024, 512))
        m["mem_p"] = mem_p[b]
        in_maps.append(m)
    nc = _get_nc(_dbg)
    res = run_bass_kernel_spmd(nc, in_maps, core_ids=list(range(NCORES)))
    R = res.results
    B, S, DB, DS = 4, 4096, 32, 16
    y_p = np.zeros((B, S, D), f); y_s = np.zeros((DB, DS, D), f)
    ckv_p = np.zeros((1, B, S, 256), f); kr_p = np.zeros((1, B, S, 32), f)
    mk_p = np.zeros((1, B, 256, 4, 128), f); mv_p = np.zeros((1, B, 256, 4, 128), f)
    ckv_s = np.zeros((1, DB, DS, 256), f); kr_s = np.zeros((1, DB, DS, 32), f); vg_s = np.zeros((1, DB, DS, D), f)
    for c in range(NCORES):
        b, j = c // 2, c % 2
        r = R[c]
        for s, blk in enumerate(_own_blocks(j)):
            y_p[b, blk * 128:(blk + 1) * 128] = r["y_own"][s * 128:(s + 1) * 128]
        y_s[4 * c:4 * c + 4] = r["y_own"][2048:2112].reshape(4, 16, D)
        half = slice(j * 2048, (j + 1) * 2048)
        ckv_p[0, b, half] = r["ckv_p"][half]
        kr_p[0, b, half] = r["kr_p"][half]
        mk_p[0, b, j * 128:(j + 1) * 128] = r["mk_p"][j * 128:(j + 1) * 128].reshape(128, 4, 128)
        mv_p[0, b, j * 128:(j + 1) * 128] = r["mv_p"][j * 128:(j + 1) * 128].reshape(128, 4, 128)
        ckv_s[0, 4 * c:4 * c + 4] = r["ckv_s"].reshape(4, 16, 256)
        kr_s[0, 4 * c:4 * c + 4] = r["kr_s"].reshape(4, 16, 32)
        vg_s[0, 4 * c:4 * c + 4] = r["vg_s"].reshape(4, 16, D)
    return (y_p, y_s, ckv_p, kr_p, mk_p, mv_p, ckv_s, kr_s, vg_s)
```

```python
import numpy as np
from contextlib import ExitStack
import concourse.bass as bass
import concourse.mybir as mybir
from concourse.bass_utils import run_bass_kernel_spmd

F32 = mybir.dt.float32
BF16 = mybir.dt.bfloat16
I32 = mybir.dt.int32
U32 = mybir.dt.uint32
ALU = mybir.AluOpType
AF = mybir.ActivationFunctionType
AX = mybir.AxisListType

NCORES = 8
D = 1024
EPS = 1e-6
NOWN = 17
NTOK = NOWN * 128
NKP = 4096
NKS = 4 * 1040
NK = NKP + NKS
MLA_SCALE = 96 ** -0.5
MEM_SCALE = 128 ** -0.5
IN_Q, IN_KV, IN_Z = 0, 384, 672


class Buf:
    __slots__ = ("name", "lw", "rd")

    def __init__(self, name):
        self.name = name
        self.lw = None
        self.rd = {}


class T:
    __slots__ = ("t", "b")

    def __init__(self, t, b):
        self.t = t
        self.b = b

    def __getitem__(self, k):
        return self.t[k]


class _Eng:
    def __init__(self, name, selfsync):
        self.name = name
        self.sem = "e_" + name
        self.count = 0
        self.seen = {}
        self.prog = []
        self.selfsync = selfsync


class _Queue:
    def __init__(self, name, eng, nslots):
        self.name = name
        self.eng = eng
        self.slots = [["q_%s_%d" % (name, i), 0] for i in range(nslots)]
        self.next = 0


class Sched:
    def __init__(self, nc, st, selfsync=True):
        self.nc = nc
        self.eng = {
            "pe": _Eng("pe", False),
            "act": _Eng("act", selfsync),
            "dve": _Eng("dve", selfsync),
            "pool": _Eng("pool", selfsync),
            "sp": _Eng("sp", False),
        }
        self.queues = {
            "sp": _Queue("sp", "sp", 8),
            "pool": _Queue("pool", "pool", 8),
            "act": _Queue("act", "act", 4),
            "conv": _Queue("conv", "pool", 16),
        }
        self.final_tokens = []
        names = [E.sem for E in self.eng.values()]
        for Q in self.queues.values():
            names += [s[0] for s in Q.slots]
        self.sems = {n: st.enter_context(nc.semaphore(n)) for n in names}
        self.ninst = 0

    def _collect(self, E, reads, writes, extra=None):
        need = {}

        def add(tok):
            if tok is None:
                return
            s, v = tok
            if need.get(s, 0) < v:
                need[s] = v
        for b in reads:
            add(b.lw)
        for b in writes:
            add(b.lw)
            for s, v in b.rd.items():
                add((s, v))
        if extra:
            for t in extra:
                add(t)
        waits = []
        for s, v in need.items():
            if s == E.sem and not E.selfsync:
                continue
            if E.seen.get(s, 0) >= v:
                continue
            E.seen[s] = v
            waits.append((s, v))
        return waits

    def _commit(self, tok, reads, writes):
        for b in writes:
            b.lw = tok
            b.rd = {}
        s, v = tok
        for b in reads:
            if b in writes:
                continue
            if b.rd.get(s, 0) < v:
                b.rd[s] = v

    def op(self, eng, fn, R=(), W=()):
        E = self.eng[eng]
        reads = [x.b for x in R]
        writes = [x.b for x in W]
        waits = self._collect(E, reads, writes)
        E.count += 1
        tok = (E.sem, E.count)
        E.prog.append((waits, fn, (E.sem, 1)))
        self._commit(tok, reads, writes)
        return tok

    def dma(self, queue, fn, R=(), W=(), final=False):
        Q = self.queues[queue]
        E = self.eng[Q.eng]
        reads = [x.b for x in R]
        writes = [x.b for x in W]
        slot = Q.slots[Q.next]
        Q.next = (Q.next + 1) % len(Q.slots)
        extra = [(slot[0], slot[1] * 16)] if slot[1] > 0 else None
        waits = self._collect(E, reads, writes, extra)
        slot[1] += 1
        tok = (slot[0], slot[1] * 16)
        E.prog.append((waits, fn, (slot[0], 16)))
        self._commit(tok, reads, writes)
        if final:
            self.final_tokens.append(tok)
        return tok

    def barrier(self):
        toks = []
        for E in self.eng.values():
            if E.count > 0:
                toks.append((E.sem, E.count))
        for qn, Q in self.queues.items():
            if qn == "conv":
                continue
            for s in Q.slots:
                if s[1] > 0:
                    toks.append((s[0], s[1] * 16))
        for E in self.eng.values():
            waits = []
            for s, v in toks:
                if s == E.sem:
                    continue
                if E.seen.get(s, 0) >= v:
                    continue
                E.seen[s] = v
                waits.append((s, v))
            if waits:
                E.prog.append((waits, None, None))

    def flush(self, last=False):
        nc = self.nc
        sems = self.sems
        if last:
            fin = {}
            for s, v in self.final_tokens:
                if fin.get(s, 0) < v:
                    fin[s] = v
            self.eng["sp"].prog.append(([(s, v) for s, v in fin.items()], None, None))

        def run(E):
            prog = E.prog
            E.prog = []
            self.ninst += len(prog)

            def body(e):
                for waits, fn, inc in prog:
                    for s, v in waits:
                        e.wait_ge(sems[s], v)
                    if fn is not None:
                        ins = fn(e)
                        ins.then_inc(sems[inc[0]], inc[1])
            return body
        with nc.Block(no_gpsimd_drain=True) as block:
            block.sync(run(self.eng["sp"]))
            block.tensor(run(self.eng["pe"]))
            block.scalar(run(self.eng["act"]))
            block.vector(run(self.eng["dve"]))
            block.gpsimd(run(self.eng["pool"]))


class Ring:
    def __init__(self, tiles):
        self.tiles = tiles
        self.i = 0

    def next(self):
        t = self.tiles[self.i]
        self.i = (self.i + 1) % len(self.tiles)
        return t


class Kern:
    def __init__(self, dbg=None):
        self.nc = bass.Bass("TRN2", target_bir_lowering=False)
        self.st = ExitStack()
        self.S = Sched(self.nc, self.st)
        self.scopes = [self.st]
        self.dbg = dbg or {}
        self.uid = 0

    def din(self, name, shape, dt=F32):
        return self.nc.dram_tensor(name, list(shape), dt, kind="ExternalInput").ap()

    def dout(self, name, shape, dt=F32):
        return self.nc.dram_tensor(name, list(shape), dt, kind="ExternalOutput").ap()

    def sb(self, name, shape, dt):
        self.uid += 1
        nm = "%s_%d" % (name, self.uid)
        t = self.scopes[-1].enter_context(self.nc.sbuf_tensor(nm, list(shape), dt))
        return T(t, Buf(nm))

    def ring(self, name, shape, dt, n):
        return Ring([self.sb(name, shape, dt) for _ in range(n)])

    def push(self):
        s = ExitStack()
        self.scopes.append(s)
        return s

    def pop(self):
        self.S.barrier()
        self.S.flush()
        s = self.scopes.pop()
        s.close()

    def op(self, eng, fn, R=(), W=()):
        return self.S.op(eng, fn, R, W)

    def dma(self, fn, R=(), W=(), q="sp", final=False):
        return self.S.dma(q, fn, R, W, final)

    def load(self, dst, dst_ap, src_ap, q="sp", slow=False):
        if slow:
            self.dma(lambda e: e.dma_start(out=dst_ap, in_=src_ap, allow_slow_non_contiguous=True), W=[dst], q=q)
        else:
            self.dma(lambda e: e.dma_start(out=dst_ap, in_=src_ap), W=[dst], q=q)

    def store(self, dst_ap, src, src_ap):
        self.dma(lambda e: e.dma_start(out=dst_ap, in_=src_ap), R=[src], final=True)

    def wload(self, name, src, K, N, c0=0):
        kc = K // 128
        w = self.sb(name, [128, kc, N], BF16)
        for c in range(kc):
            self.load(w, w[:, c, :], src[c * 128:(c + 1) * 128, c0:c0 + N], q="pool")
        return w

    def gbload(self, name, src, n):
        g = self.sb(name, [128, n], F32)
        self.load(g, g[:, :], src[0:1, 0:n].partition_broadcast(128))
        return g

    def mm(self, ps, ps_ap, lhsT, lhsT_ap, rhs, rhs_ap, start, stop):
        self.op("pe", lambda e: e.matmul(ps_ap, lhsT=lhsT_ap, rhs=rhs_ap, start=start, stop=stop),
                R=[lhsT, rhs], W=[ps])

    def rstd(self, src, src_ap, n, Dn):
        sm = self.small.next()
        jk = self.junk
        self.op("act", lambda e: e.activation(out=jk[:, 0:n], in_=src_ap, func=AF.Square, accum_out=sm[:, 0:1]),
                R=[src], W=[jk, sm])
        self.op("act", lambda e: e.activation(out=sm[:, 1:2], in_=sm[:, 0:1], func=AF.Sqrt, bias=self.epsb[:, 0:1],
                                              scale=1.0 / Dn), R=[sm, self.epsb], W=[sm])
        self.op("dve", lambda e: e.reciprocal(out=sm[:, 2:3], in_=sm[:, 1:2]), R=[sm], W=[sm])
        return sm, sm[:, 2:3]

    def rms_tok(self, src, src_ap, n, gb, gb_ap, dst, dst_ap):
        sm, col = self.rstd(src, src_ap, n, n)
        self.op("dve", lambda e: e.scalar_tensor_tensor(out=dst_ap, in0=src_ap, scalar=col, in1=gb_ap,
                                                        op0=ALU.mult, op1=ALU.mult), R=[src, sm, gb], W=[dst])

    def transposes(self, src, src_ap_fn, n, dst, dst_ap, rows=128, eng="act"):
        pT = self.psT
        for j in range(n):
            ap = src_ap_fn(j)
            self.op("pe", lambda e, ap=ap, j=j: e.transpose(out=pT[:, j * 128:j * 128 + rows], in_=ap,
                                                             identity=self.ident[0:rows, 0:rows]),
                    R=[src, self.ident], W=[pT])
        view = pT[:, 0:n * 128].rearrange("p (c k) -> p c k", c=n)[:, :, 0:rows]
        if eng == "act":
            self.op("act", lambda e: e.copy(out=dst_ap, in_=view), R=[pT], W=[dst])
        else:
            self.op("dve", lambda e: e.tensor_copy(out=dst_ap, in_=view), R=[pT], W=[dst])

    def normT(self, xt, gb, hb, hT):
        self.rms_tok(xt, xt[:, :], D, gb, gb[:, :], hb, hb[:, :])
        self.transposes(hb, lambda j: hb[:, j * 128:(j + 1) * 128], 8, hT, hT[:, :, :])

    def proj(self, ps, ps_ap, hT, w, c0, n, kc=8):
        for c in range(kc):
            self.mm(ps, ps_ap, hT, hT[:, c, :], w, w[:, c, c0:c0 + n], c == 0, c == kc - 1)

    def build(self):
        nc = self.nc
        I = {}
        O = {}

        def di(name, shape, dt=F32):
            I[name] = self.din(name, shape, dt)

        def do(name, shape, dt=F32):
            O[name] = self.dout(name, shape, dt)
        di("x_all", [4096, D]); di("x_own", [NTOK, D])
        di("ckv_c", [4096, 256]); di("kr_c", [4096, 32])
        di("memk_c", [1024, 512]); di("memv_c", [1024, 512]); di("mem_p", [256, D])
        di("w_in", [D, 4768]); di("w_uq", [384, 1536]); di("w_uk", [256, 1024]); di("w_uv", [256, 1024])
        di("w_oa", [D, D]); di("w_ob", [D, D]); di("w_o", [D, D])
        di("w_cq", [D, 512]); di("w_ck", [D, 512]); di("w_cv", [D, 512]); di("w_co", [512, D])
        di("w_pq", [D, 2048]); di("sub_keys", [2048, 128]); di("peer_u", [16384, D]); di("peer_v", [16384, D])
        for g, n in [("g_mix", D), ("g_q_lat", 384), ("g_qn", 64), ("g_qr", 32), ("g_kv_lat", 256), ("g_kr", 32),
                     ("g_kn", 64), ("g_gm", D), ("g_xattn", D), ("g_mem", D), ("g_cq", 128), ("g_ck", 128),
                     ("g_ffn", D)]:
            di(g, [1, n])
        di("w_s", [1024, 128]); di("b_s", [8, 128])
        di("idn", [128, 128]); di("onesq", [96, 96]); di("ones128", [128, 128]); di("rotm", [96, 96])
        di("cmask", [128, 255]); di("trilm", [128, 128]); di("trilms", [128, 128])
        di("ropek", [128, 33, 32]); di("cosq", [32, NTOK]); di("sinq", [32, NTOK]); di("masks", [4, 128, 128])
        do("y_own", [NTOK, D]); do("ckv_p", [4096, 256]); do("kr_p", [4096, 32])
        do("mk_p", [256, 512]); do("mv_p", [256, 512])
        do("ckv_s", [64, 256]); do("kr_s", [64, 32]); do("vg_s", [64, D])
        self.I, self.O = I, O

        def ps(name, shape, dt):
            t = self.st.enter_context(nc.psum_tensor(name, shape, dt))
            return t
        self.psT = T(ps("psT", [128, 1024], BF16), Buf("psT"))
        tA = ps("psA", [128, 1024], F32); tB = ps("psB", [128, 1024], F32); tC = ps("psC", [128, 1024], F32)
        tD = ps("psD", [128, 512], F32)
        self.A = [T(tA, Buf("A0")), T(tA, Buf("A1"))]
        self.B = [T(tB, Buf("B0")), T(tB, Buf("B1"))]
        self.C = [T(tC, Buf("C0")), T(tC, Buf("C1"))]
        self.Dp = T(tD, Buf("D"))

        self.ident = self.sb("ident", [128, 128], BF16)
        self.load(self.ident, self.ident[:, :], I["idn"], q="pool")
        self.epsb = self.sb("epsb", [128, 1], F32)
        self.op("dve", lambda e: e.memset(self.epsb[:, :], EPS), W=[self.epsb])
        self.mscr = T(nc.dram_tensor("m_scr", [NTOK, D], BF16).ap(), Buf("mscr"))
        self.junk = self.sb("junk", [128, D], BF16)
        self.small = self.ring("small", [128, 8], F32, 4)
        self.X = self.ring("X", [128, D], F32, 2)
        self.HB = self.sb("HB", [128, D], BF16)
        self.HT = self.ring("HT", [128, 8, 128], BF16, 1)

        self.uvtab = T(nc.dram_tensor("uv_bf", [16384, 2 * D], BF16).ap(), Buf("uvtab"))
        self.hscr = T(nc.dram_tensor("h_scr", [NTOK, D], BF16).ap(), Buf("hscr"))
        self._conv_pending = True
        ph = self.dbg.get("phases", "1234")
        self.push()
        self.o_all = self.sb("o_all", [128, NOWN, D], BF16)
        self.op("pool", lambda e: e.memset(self.o_all[:, 16, :], 0.0), W=[self.o_all])
        self.gb_mix = self.gbload("gb_mix", I["g_mix"], D)
        self.push()
        ckvT = self.sb("ckvT", [128, 2, NK], BF16)
        KT = self.sb("KT", [96, NK], BF16)
        cqnT = self.sb("cqnT", [128, 3, NTOK], BF16)
        if "1" in ph:
            self.phase1(ckvT, KT, cqnT)
        if "2" in ph:
            self.phase2(ckvT, KT, cqnT)
        self.pop()
        if "3" in ph:
            self.phase3a()
        self.pop()
        self.emit_conv()
        if "4" in ph:
            self.phase3b()
        self.S.barrier()
        self.S.flush(last=True)
        self.st.close()
        return nc

    def emit_conv(self):
        if not self._conv_pending:
            return
        self._conv_pending = False
        I = self.I
        RCH = 1024
        for r in range(0, 16384, RCH):
            for which, nm in ((0, "peer_u"), (1, "peer_v")):
                self.dma(lambda e, r=r, which=which, nm=nm: e.dma_start(out=self.uvtab.t[r:r + RCH, which * D:(which + 1) * D],
                                                                         in_=I[nm][r:r + RCH, :]), W=[self.uvtab], q="conv")

    def phase1(self, ckvT, KT, cqnT):
        I, O = self.I, self.O
        self.push()
        Wkv = self.wload("Wkv", I["w_in"], D, 288, IN_KV)
        Wq = self.wload("Wq", I["w_in"], D, 384, IN_Q)
        gb_kv = self.sb("gb_kv", [128, 288], F32)
        self.load(gb_kv, gb_kv[:, 0:256], I["g_kv_lat"][0:1, 0:256].partition_broadcast(128))
        self.load(gb_kv, gb_kv[:, 256:288], I["g_kr"][0:1, 0:32].partition_broadcast(128))
        gb_ql = self.gbload("gb_ql", I["g_q_lat"], 384)
        ropek = self.sb("ropek", [128, 33, 32], F32)
        self.load(ropek, ropek[:, :, :], I["ropek"])
        kvo_r = self.ring("kvo", [128, 288], F32, 2)
        krn_r = self.ring("krn", [128, 64], F32, 2)
        kvb_r = self.ring("kvb", [128, 384], BF16, 2)
        for t in kvb_r.tiles:
            self.op("pool", lambda e, t=t: e.memset(t[:, :], 0.0), W=[t])
        cqb = self.sb("cqb", [128, 384], BF16)
        pT = self.psT

        def finish_kv(kvo, kind, idx, pT):
            kvb = kvb_r.next()
            self.op("act", lambda e: e.copy(out=kvb[:, 0:256], in_=kvo[:, 0:256]), R=[kvo], W=[kvb])
            self.op("act", lambda e: e.copy(out=kvb[:, 320:352], in_=kvo[:, 256:288]), R=[kvo], W=[kvb])
            for j in range(3):
                self.op("pe", lambda e, j=j: e.transpose(out=pT[:, j * 128:(j + 1) * 128], in_=kvb[:, j * 128:(j + 1) * 128],
                                                       identity=self.ident[:, :]), R=[kvb, self.ident], W=[pT])
            if kind == "snew":
                dst = ckvT[:, :, NKP:NK].rearrange("p c (b k) -> p c b k", b=4)[:, :, :, 1024:1040]
                src = pT[:, 0:256].rearrange("p (c b k) -> p c b k", c=2, b=8)[:, :, 0:4, :]
                self.op("act", lambda e: e.copy(out=dst, in_=src), R=[pT], W=[ckvT])
                dstk = KT[64:96, NKP:NK].rearrange("p (b k) -> p b k", b=4)[:, :, 1024:1040]
                srck = pT[64:96, 256:320].rearrange("p (b k) -> p b k", b=4)
                self.op("act", lambda e: e.copy(out=dstk, in_=srck), R=[pT], W=[KT])
            else:
                c0 = idx * 128 if kind == "p" else NKP + (idx // 8) * 1040 + (idx % 8) * 128
                src = pT[:, 0:256].rearrange("p (c k) -> p c k", c=2)
                if "ckv" not in self.dbg.get("fk_skip", ""):
                    self.op("act", lambda e: e.copy(out=ckvT[:, :, c0:c0 + 128], in_=src), R=[pT], W=[ckvT])
                if "kt" not in self.dbg.get("fk_skip", ""):
                    self.op("act", lambda e: e.copy(out=KT[64:96, c0:c0 + 128], in_=pT[64:96, 256:384]), R=[pT], W=[KT])

        class TV:
            def __init__(self, ap, b):
                self.t = ap
                self.b = b

            def __getitem__(self, k):
                return self.t[k]
        psTs = [self.psT, TV(self.Dp.t[:, :].bitcast(BF16), self.Dp.b)]
        HBs = [self.HB, self.sb("HB2", [128, D], BF16)]
        HTs = [self.sb("HTa", [128, 8, 128], BF16), self.sb("HTb", [128, 8, 128], BF16)]
        cqbs = [cqb, self.sb("cqb2", [128, 384], BF16)]
        glob_psT, glob_HB = self.psT, self.HB

        def use(slot):
            self.psT = psTs[slot]
            self.HB = HBs[slot]

        def kv_tile(i, slot):
            xt = self.X.next()
            src = I["x_all"][i * 128:(i + 1) * 128, :] if i < 32 else I["x_own"][16 * 128:17 * 128, :]
            self.load(xt, xt[:, :], src)
            yield
            use(slot)
            hT = HTs[slot]
            self.rms_tok(xt, xt[:, :], D, self.gb_mix, self.gb_mix[:, :], self.HB, self.HB[:, :])
            yield
            use(slot)
            hb = self.HB
            self.transposes(hb, lambda j: hb[:, j * 128:(j + 1) * 128], 8, hT, hT[:, :, :])
            yield
            zp = self.A[slot]
            zoff = slot * 512
            z = zp.t[:, zoff:zoff + 288]
            self.proj(zp, z, hT, Wkv, 0, 288)
            yield
            kvo = kvo_r.next()
            krn = krn_r.next()
            self.rms_tok(zp, zp.t[:, zoff:zoff + 256], 256, gb_kv, gb_kv[:, 0:256], kvo, kvo[:, 0:256])
            self.rms_tok(zp, zp.t[:, zoff + 256:zoff + 288], 32, gb_kv, gb_kv[:, 256:288], krn, krn[:, 0:32])
            yield
            cs = ropek[:, i, 0:16]
            sn = ropek[:, i, 16:32]
            self.op("dve", lambda e: e.tensor_tensor(out=krn[:, 32:48], in0=krn[:, 0:16], in1=cs, op=ALU.mult), R=[krn, ropek], W=[krn])
            self.op("dve", lambda e: e.tensor_tensor(out=krn[:, 48:64], in0=krn[:, 16:32], in1=sn, op=ALU.mult), R=[krn, ropek], W=[krn])
            self.op("dve", lambda e: e.tensor_tensor(out=kvo[:, 256:272], in0=krn[:, 32:48], in1=krn[:, 48:64], op=ALU.subtract), R=[krn], W=[kvo])
            self.op("dve", lambda e: e.tensor_tensor(out=krn[:, 32:48], in0=krn[:, 0:16], in1=sn, op=ALU.mult), R=[krn, ropek], W=[krn])
            self.op("dve", lambda e: e.tensor_tensor(out=krn[:, 48:64], in0=krn[:, 16:32], in1=cs, op=ALU.mult), R=[krn, ropek], W=[krn])
            self.op("dve", lambda e: e.tensor_tensor(out=kvo[:, 272:288], in0=krn[:, 32:48], in1=krn[:, 48:64], op=ALU.add), R=[krn], W=[kvo])
            if i < 32:
                self.store(O["ckv_p"][i * 128:(i + 1) * 128, :], kvo, kvo[:, 0:256])
                self.store(O["kr_p"][i * 128:(i + 1) * 128, :], kvo, kvo[:, 256:288])
            else:
                self.store(O["ckv_s"][:, :], kvo, kvo[0:64, 0:256])
                self.store(O["kr_s"][:, :], kvo, kvo[0:64, 256:288])
            yield
            use(slot)
            finish_kv(kvo, "p" if i < 32 else "snew", i if i < 32 else 0, self.psT)
            yield

        def cache_tile(idx, slot):
            kvo = kvo_r.next()
            self.load(kvo, kvo[:, 0:256], I["ckv_c"][idx * 128:(idx + 1) * 128, :])
            self.load(kvo, kvo[:, 256:288], I["kr_c"][idx * 128:(idx + 1) * 128, :])
            yield
            use(slot)
            finish_kv(kvo, "scache", idx, self.psT)
            yield

        def q_tile(i, slot):
            xt = self.X.next()
            self.load(xt, xt[:, :], I["x_own"][i * 128:(i + 1) * 128, :])
            yield
            use(slot)
            hT = HTs[slot]
            self.rms_tok(xt, xt[:, :], D, self.gb_mix, self.gb_mix[:, :], self.HB, self.HB[:, :])
            yield
            use(slot)
            hb = self.HB
            self.transposes(hb, lambda j: hb[:, j * 128:(j + 1) * 128], 8, hT, hT[:, :, :])
            yield
            zp = self.A[slot]
            zoff = slot * 512
            self.proj(zp, zp.t[:, zoff:zoff + 384], hT, Wq, 0, 384)
            yield
            cq_ = cqbs[slot]
            self.rms_tok(zp, zp.t[:, zoff:zoff + 384], 384, gb_ql, gb_ql[:, :], cq_, cq_[:, :])
            yield
            use(slot)
            self.transposes(cq_, lambda j: cq_[:, j * 128:(j + 1) * 128], 3, cqnT, cqnT[:, :, i * 128:(i + 1) * 128])
            yield

        def run2(makers):
            pending = list(makers)
            active = {}
            while pending or active:
                for slot in (0, 1):
                    if slot not in active and pending:
                        active[slot] = pending.pop(0)(slot)
                for slot in (0, 1):
                    g = active.get(slot)
                    if g is None:
                        continue
                    try:
                        next(g)
                    except StopIteration:
                        del active[slot]
        mk = [(lambda slot, i=i: kv_tile(i, slot)) for i in self.dbg.get("p1_new", list(range(33)))]
        mk += [(lambda slot, idx=idx: cache_tile(idx, slot)) for idx in range(self.dbg.get("p1_cache", 32))]
        mk += [(lambda slot, i=i: q_tile(i, slot)) for i in range(self.dbg.get("p1_q", NOWN))]
        run2(mk)
        self.psT, self.HB = glob_psT, glob_HB
        self.pop()

    def phase2(self, ckvT, KT, cqnT):
        I, O = self.I, self.O
        self.push()
        wuq = self.wload("wuq", I["w_uq"], 384, 1536)
        wuk = self.wload("wuk", I["w_uk"], 256, 1024)
        wuv = self.wload("wuv", I["w_uv"], 256, 1024)
        onesq = self.sb("onesq", [96, 96], BF16)
        self.load(onesq, onesq[:, :], I["onesq"], q="pool")
        rotm = self.sb("rotm", [96, 96], BF16)
        self.load(rotm, rotm[:, :], I["rotm"], q="pool")
        gq = self.sb("gq", [96, 2], F32)
        self.load(gq, gq[0:64, 0:1], I["g_qn"][0:1, 0:64].rearrange("o d -> d o"))
        self.load(gq, gq[64:96, 0:1], I["g_qr"][0:1, 0:32].rearrange("o d -> d o"))
        self.op("dve", lambda e: e.tensor_scalar(out=gq[:, 1:2], in0=gq[:, 0:1], scalar1=MLA_SCALE, scalar2=None, op0=ALU.mult),
                R=[gq], W=[gq])
        gkn = self.sb("gkn", [64, 1], F32)
        self.load(gkn, gkn[:, :], I["g_kn"][0:1, 0:64].rearrange("o d -> d o"))
        cosq = self.sb("cosq", [96, NTOK], F32)
        sinq = self.sb("sinq", [96, NTOK], F32)
        self.load(cosq, cosq[64:96, :], I["cosq"])
        self.load(sinq, sinq[64:96, :], I["sinq"])
        maskT = self.sb("maskT", [128, 4, 128], BF16)
        self.load(maskT, maskT[:, :, :], I["masks"].rearrange("m k q -> k m q"), q="pool")
        NVT = 32 + 36
        V = self.sb("V", [128, NVT, 72], BF16)
        self.op("pool", lambda e: e.memset(V[:, :, :], 0.0), W=[V])
        self.op("pool", lambda e: e.memset(V[:, :, 64:65], 1.0), W=[V])
        QT = self.sb("QT", [96, NTOK], BF16)
        sq_r = self.ring("sq", [96, 512], BF16, 2)
        rs_r = self.ring("rs", [96, 512], F32, 2)
        tq_r = self.ring("tq", [96, 512], F32, 2)
        t1 = self.sb("t1", [96, 512], F32)
        t2 = self.sb("t2", [96, 512], F32)
        pt_r = self.ring("pt", [128, 4, 128], BF16, 3)
        ptS = [self.sb("ptS", [128, 9, 64], BF16) for _ in range(4)]
        for t in ptS:
            self.op("pool", lambda e, t=t: e.memset(t[:, :, :], 0.0), W=[t])
        A, B, C, Dp = self.A, self.B, self.C, self.Dp
        eps96 = self.epsb

        def fm_norm(ps, ps_ap, rows, n, ones_ap, gcol, gcol_ap, dst, dst_ap):
            sq = sq_r.next()
            rs = rs_r.next()
            self.op("act", lambda e: e.activation(out=sq[0:rows, 0:n], in_=ps_ap, func=AF.Square), R=[ps], W=[sq])
            self.mm(B[0], B[0].t[0:rows, 0:n], onesq, ones_ap, sq, sq[0:rows, 0:n], True, True)
            self.op("act", lambda e: e.activation(out=rs[0:rows, 0:n], in_=B[0].t[0:rows, 0:n], func=AF.Sqrt,
                                                  bias=eps96[0:rows, 0:1], scale=1.0), R=[B[0], eps96], W=[rs])
            tq = tq_r.next()
            self.op("act", lambda e: e.activation(out=tq[0:rows, 0:n], in_=ps_ap, func=AF.Copy, scale=gcol_ap), R=[ps, gcol], W=[tq])
            self.op("dve", lambda e: e.reciprocal(out=rs[0:rows, 0:n], in_=rs[0:rows, 0:n]), R=[rs], W=[rs])
            self.op("pool", lambda e: e.tensor_tensor(out=dst_ap, in0=tq[0:rows, 0:n], in1=rs[0:rows, 0:n], op=ALU.mult), R=[tq, rs], W=[dst])

        kchunks = [(c0, min(512, NK - c0)) for c0 in range(0, NK, 512)]
        qchunks = [(c0, min(512, NTOK - c0)) for c0 in range(0, NTOK, 512)]
        vt = [(i, i * 128, 128) for i in range(32)]
        for bt in range(4):
            for j in range(9):
                vt.append((32 + bt * 9 + j, NKP + bt * 1040 + j * 128, 128 if j < 8 else 16))
        nh = self.dbg.get("nheads", 16)
        pi = 0
        for h in range(nh):
            for (c0, n) in kchunks:
                P = A[pi % 2]; po = (pi % 2) * 512; pi += 1
                for c in range(2):
                    self.mm(P, P.t[0:64, po:po + n], wuk, wuk[:, c, h * 64:(h + 1) * 64], ckvT, ckvT[:, c, c0:c0 + n], c == 0, c == 1)
                fm_norm(P, P.t[0:64, po:po + n], 64, n, onesq[0:64, 0:64], gkn, gkn[:, 0:1], KT, KT[0:64, c0:c0 + n])
            p2c = self.dbg.get("p2_cut", 9)
            if p2c < 2:
                continue
            for g0 in range(0, NVT, 8):
                P = A[pi % 2]; po = (pi % 2) * 512; pi += 1
                grp = vt[g0:g0 + 8]
                for (ti, c0, rows) in grp:
                    jj = ti - g0
                    for c in range(2):
                        self.mm(P, P.t[0:rows, po + jj * 64:po + jj * 64 + 64], ckvT, ckvT[:, c, c0:c0 + rows],
                                wuv, wuv[:, c, h * 64:(h + 1) * 64], c == 0, c == 1)
                ng = len(grp)
                src = P.t[:, po:po + ng * 64].rearrange("p (j d) -> p j d", j=ng)
                self.op("act", lambda e, src=src, g0=g0, ng=ng: e.copy(out=V[:, g0:g0 + ng, 0:64], in_=src), R=[P], W=[V])
            if p2c < 3:
                continue
            for (c0, n) in qchunks:
                P = A[pi % 2]; po = (pi % 2) * 512; pi += 1
                for c in range(3):
                    self.mm(P, P.t[0:96, po:po + n], wuq, wuq[:, c, h * 96:(h + 1) * 96], cqnT, cqnT[:, c, c0:c0 + n], c == 0, c == 2)
                fm_norm(P, P.t[0:96, po:po + n], 96, n, onesq[:, :], gq, gq[:, 1:2], QT, QT[0:96, c0:c0 + n])
                self.mm(B[1], B[1].t[0:96, 512:512 + n], rotm, rotm[:, :], QT, QT[0:96, c0:c0 + n], True, True)
                self.op("dve", lambda e, c0=c0, n=n: e.tensor_tensor(out=t1[64:96, 0:n], in0=QT[64:96, c0:c0 + n], in1=cosq[64:96, c0:c0 + n], op=ALU.mult),
                        R=[QT, cosq], W=[t1])
                self.op("dve", lambda e, c0=c0, n=n: e.tensor_tensor(out=t2[64:96, 0:n], in0=B[1].t[64:96, 512:512 + n], in1=sinq[64:96, c0:c0 + n], op=ALU.mult),
                        R=[B[1], sinq], W=[t2])
                self.op("dve", lambda e, c0=c0, n=n: e.tensor_tensor(out=QT[64:96, c0:c0 + n], in0=t1[64:96, 0:n], in1=t2[64:96, 0:n], op=ALU.add),
                        R=[t1, t2], W=[QT])
            if p2c < 4:
                continue
            si = 0
            groups = []
            for s in range(16):
                nkb = 4 * (s // 2) + (2 if s % 2 == 0 else 4)
                for g0 in range(0, nkb, 4):
                    groups.append((s, nkb, list(range(g0, min(g0 + 4, nkb)))))
            Obank = [(Dp, 0), (B[1], 512)]

            def qk(gi):
                s, nkb, blks = groups[gi]
                Sp = C[gi % 2]; so = (gi % 2) * 512
                for j, kb in enumerate(blks):
                    self.mm(Sp, Sp.t[:, so + j * 128:so + (j + 1) * 128], KT, KT[0:96, kb * 128:(kb + 1) * 128],
                            QT, QT[0:96, s * 128:(s + 1) * 128], True, True)
            qk(0)
            for gi, (s, nkb, blks) in enumerate(groups):
                if gi + 1 < len(groups):
                    qk(gi + 1)
                Sp = C[gi % 2]; so = (gi % 2) * 512
                Ops, oo = Obank[s % 2]
                pt = pt_r.next()
                nb = len(blks)
                self.op("act", lambda e, Sp=Sp, so=so, nb=nb, pt=pt: e.activation(
                    out=pt[:, 0:nb, :], in_=Sp.t[:, so:so + nb * 128].rearrange("p (j q) -> p j q", j=nb), func=AF.Exp),
                    R=[Sp], W=[pt])
                for j, kb in enumerate(blks):
                    if kb >= nkb - 2:
                        mi = (0 if s % 2 == 0 else 2) + (kb - (nkb - 2))
                        self.op("dve", lambda e, pt=pt, j=j, mi=mi: e.tensor_tensor(out=pt[:, j, :], in0=pt[:, j, :], in1=maskT[:, mi, :], op=ALU.mult),
                                R=[pt, maskT], W=[pt])
                for j, kb in enumerate(blks):
                    self.mm(Ops, Ops.t[:, oo:oo + 72], pt, pt[:, j, :], V, V[:, kb, :], kb == 0, kb == nkb - 1)
                if blks[-1] == nkb - 1:
                    sm = self.small.next()
                    self.op("dve", lambda e, sm=sm, Ops=Ops, oo=oo: e.reciprocal(out=sm[:, 0:1], in_=Ops.t[:, oo + 64:oo + 65]), R=[Ops], W=[sm])
                    self.op("dve", lambda e, sm=sm, Ops=Ops, oo=oo, s=s, h=h: e.tensor_scalar(out=self.o_all[:, s, h * 64:(h + 1) * 64], in0=Ops.t[:, oo:oo + 64],
                                                                                   scalar1=sm[:, 0:1], scalar2=None, op0=ALU.mult),
                            R=[Ops, sm], W=[self.o_all])
            si = len(groups)
            if p2c < 5:
                continue
            Ops = Dp
            for bt in range(4):
                base = NKP + bt * 1040
                Sp = C[si % 2]; so = (si % 2) * 512; si += 1
                qc = 2048 + bt * 16
                for j in range(8):
                    self.mm(Sp, Sp.t[:, so + j * 16:so + (j + 1) * 16], KT, KT[0:96, base + j * 128:base + (j + 1) * 128],
                            QT, QT[0:96, qc:qc + 16], True, True)
                self.mm(Sp, Sp.t[0:16, so + 128:so + 144], KT, KT[0:96, base + 1024:base + 1040], QT, QT[0:96, qc:qc + 16], True, True)
                p = ptS[bt]
                self.op("act", lambda e, Sp=Sp, so=so, p=p, bt=bt: e.activation(
                    out=p[:, 0:8, bt * 16:(bt + 1) * 16], in_=Sp.t[:, so:so + 128].rearrange("p (j q) -> p j q", j=8), func=AF.Exp),
                    R=[Sp], W=[p])
                self.op("act", lambda e, Sp=Sp, so=so, p=p, bt=bt: e.activation(
                    out=p[0:16, 8, bt * 16:(bt + 1) * 16], in_=Sp.t[0:16, so + 128:so + 144], func=AF.Exp), R=[Sp], W=[p])
                for j in range(9):
                    rows = 128 if j < 8 else 16
                    self.mm(Ops, Ops.t[0:64, 0:72], p, p[0:rows, j, :], V, V[0:rows, 32 + bt * 9 + j, :],
                            bt == 0 and j == 0, bt == 3 and j == 8)
            sm = self.small.next()
            self.op("dve", lambda e, sm=sm, Ops=Ops: e.reciprocal(out=sm[0:64, 0:1], in_=Ops.t[0:64, 64:65]), R=[Ops], W=[sm])
            self.op("dve", lambda e, sm=sm, Ops=Ops, h=h: e.tensor_scalar(out=self.o_all[0:64, 16, h * 64:(h + 1) * 64], in0=Ops.t[0:64, 0:64],
                                                                     scalar1=sm[0:64, 0:1], scalar2=None, op0=ALU.mult),
                    R=[Ops, sm], W=[self.o_all])
        self.pop()

    def phase3a(self):
        I, O = self.I, self.O
        self.push()
        Wz = self.wload("Wz", I["w_in"], D, 4096, IN_Z)
        woa = self.wload("woa", I["w_oa"], D, D)
        wob = self.wload("wob", I["w_ob"], D, D)
        gb_gm = self.gbload("gb_gm", I["g_gm"], D)
        self.emit_conv()
        trilm = self.sb("trilm", [128, 128], F32)
        trilms = self.sb("trilms", [128, 128], F32)
        self.load(trilm, trilm[:, :], I["trilm"])
        self.load(trilms, trilms[:, :], I["trilms"])
        wsf = self.sb("wsf", [128, 8, 128], F32)
        wsb = self.sb("wsb", [128, 8, 128], BF16)
        WmT = [self.sb("WmT", [128, 8, 128], BF16) for _ in range(2)]
        bT = [self.sb("bT", [128, 8], F32) for _ in range(2)]
        ws_tgs = I["w_s"].rearrange("(g t) s -> t g s", g=8)
        for k in range(2):
            if k == 0:
                self.load(wsf, wsf[:, :, :], ws_tgs)
                self.load(bT[0], bT[0][:, :], I["b_s"].rearrange("g t -> t g"), slow=True)
                tm = trilm
            else:
                self.op("pool", lambda e: e.memset(wsf[:, :, :], 0.0), W=[wsf])
                self.op("pool", lambda e: e.memset(bT[1][:, :], 0.0), W=[bT[1]])
                for bt in range(4):
                    self.load(wsf, wsf[bt * 16:(bt + 1) * 16, :, bt * 16:(bt + 1) * 16], ws_tgs[0:16, :, 0:16])
                    self.load(bT[1], bT[1][bt * 16:(bt + 1) * 16, :], I["b_s"][:, 0:16].rearrange("g t -> t g"), slow=True)
                tm = trilms
            self.op("dve", lambda e, tm=tm: e.tensor_tensor(out=wsb[:, :, :], in0=wsf[:, :, :],
                                                           in1=tm[:, :].unsqueeze(1).to_broadcast([128, 8, 128]), op=ALU.mult),
                    R=[wsf, tm], W=[wsb])
            self.transposes(wsb, lambda j: wsb[:, j, :], 8, WmT[k], WmT[k][:, :, :])
        u = self.sb("u", [128, D], F32)
        gv = self.sb("gv", [128, D], F32)
        vg = self.sb("vg", [128, D], F32)
        vgb = self.sb("vgb", [128, D], BF16)
        sig = self.sb("sig", [128, D], F32)
        um = self.sb("um", [128, D], BF16)
        umT = self.sb("umT", [128, 8, 128], BF16)
        oT = self.sb("oT", [128, 8, 128], BF16)
        tt = self.sb("tt", [128, D], F32)
        mo_r = self.ring("mo", [128, D], BF16, 2)
        A, B, C = self.A, self.B, self.C
        zi = 0
        for i in range(NOWN):
            k = 0 if i < 16 else 1
            xt = self.X.next()
            self.load(xt, xt[:, :], I["x_own"][i * 128:(i + 1) * 128, :])
            hT = self.HT.next()
            self.normT(xt, self.gb_mix, self.HB, hT)

            def zchunk(col, func, dst, dst_ap):
                nonlocal zi
                P = A[zi % 2]; po = (zi % 2) * 512; zi += 1
                self.proj(P, P.t[:, po:po + 512], hT, Wz, col, 512)
                self.op("act", lambda e: e.activation(out=dst_ap, in_=P.t[:, po:po + 512], func=func), R=[P], W=[dst])
            for n in range(2):
                zchunk(n * 512, AF.Gelu, u, u[:, n * 512:(n + 1) * 512])
            for n in range(2):
                zchunk(1024 + n * 512, AF.Gelu, gv, gv[:, n * 512:(n + 1) * 512])
            self.rms_tok(gv, gv[:, :], D, gb_gm, gb_gm[:, :], vg, vg[:, :])
            if i == 16:
                self.store(O["vg_s"][:, :], vg, vg[0:64, :])
            self.op("act", lambda e: e.copy(out=vgb[:, :], in_=vg[:, :]), R=[vg], W=[vgb])
            for g in range(8):
                Pm = B[g // 4]
                self.mm(Pm, Pm.t[:, g * 128:(g + 1) * 128], WmT[k], WmT[k][:, g, :], vgb, vgb[:, g * 128:(g + 1) * 128], True, True)
            for g in range(8):
                Pm = B[g // 4]
                self.op("dve", lambda e, g=g, Pm=Pm, k=k: e.scalar_tensor_tensor(
                    out=um[:, g * 128:(g + 1) * 128], in0=Pm.t[:, g * 128:(g + 1) * 128], scalar=bT[k][:, g:g + 1],
                    in1=u[:, g * 128:(g + 1) * 128], op0=ALU.add, op1=ALU.mult), R=[Pm, bT[k], u], W=[um])
            self.transposes(self.o_all, lambda j, i=i: self.o_all[:, i, j * 128:(j + 1) * 128], 8, oT, oT[:, :, :])
            for n in range(2):
                self.proj(C[n], C[n].t[:, n * 512:(n + 1) * 512], oT, woa, n * 512, 512)
            for n in range(2):
                zchunk(2048 + n * 512, AF.Sigmoid, sig, sig[:, n * 512:(n + 1) * 512])
            for n in range(2):
                self.op("dve", lambda e, n=n: e.tensor_tensor(out=tt[:, n * 512:(n + 1) * 512], in0=sig[:, n * 512:(n + 1) * 512],
                                                           in1=C[n].t[:, n * 512:(n + 1) * 512], op=ALU.mult), R=[sig, C[n]], W=[tt])
            self.transposes(um, lambda j: um[:, j * 128:(j + 1) * 128], 8, umT, umT[:, :, :])
            for n in range(2):
                self.proj(C[n], C[n].t[:, n * 512:(n + 1) * 512], umT, wob, n * 512, 512)
            for n in range(2):
                zchunk(3072 + n * 512, AF.Sigmoid, sig, sig[:, n * 512:(n + 1) * 512])
            for n in range(2):
                self.op("dve", lambda e, n=n: e.tensor_tensor(out=sig[:, n * 512:(n + 1) * 512], in0=sig[:, n * 512:(n + 1) * 512],
                                                           in1=C[n].t[:, n * 512:(n + 1) * 512], op=ALU.mult), R=[sig, C[n]], W=[sig])
            mo = mo_r.next()
            self.op("dve", lambda e, mo=mo: e.tensor_tensor(out=mo[:, :], in0=sig[:, :], in1=tt[:, :], op=ALU.add),
                    R=[sig, tt], W=[mo])
            self.dma(lambda e, mo=mo, i=i: e.dma_start(out=self.mscr.t[i * 128:(i + 1) * 128, :], in_=mo[:, :]), R=[mo], W=[self.mscr])
        self.pop()

    def phase3b(self):
        I, O = self.I, self.O
        A, B, C, Dp, pT = self.A, self.B, self.C, self.Dp, self.psT
        self.push()
        wo = self.wload("wo", I["w_o"], D, D)
        wcq = self.wload("wcq", I["w_cq"], D, 512)
        wco = self.wload("wco", I["w_co"], 512, D)
        wpq = self.wload("wpq", I["w_pq"], D, 2048)
        gb_xa = self.gbload("gb_xa", I["g_xattn"], D)
        gb_ff = self.gbload("gb_ff", I["g_ffn"], D)
        ones128 = self.sb("ones128", [128, 128], BF16)
        self.load(ones128, ones128[:, :], I["ones128"], q="pool")
        cmask = self.sb("cmask", [128, 255], BF16)
        self.load(cmask, cmask[:, :], I["cmask"], q="pool")
        gcq = self.sb("gcq", [128, 2], F32)
        self.load(gcq, gcq[:, 0:1], I["g_cq"][0:1, 0:128].rearrange("o d -> d o"))
        self.op("dve", lambda e: e.tensor_scalar(out=gcq[:, 1:2], in0=gcq[:, 0:1], scalar1=MEM_SCALE, scalar2=None, op0=ALU.mult),
                R=[gcq], W=[gcq])
        memKT = [self.sb("memKT", [128, 4, 256], BF16) for _ in range(4)]
        memV = [self.sb("memV", [128, 2, 4, 136], BF16) for _ in range(4)]
        for t in memV:
            self.op("pool", lambda e, t=t: e.memset(t[:, :, :, :], 0.0), W=[t])
            self.op("pool", lambda e, t=t: e.memset(t[:, :, :, 128:129], 1.0), W=[t])
        kf = self.sb("kf", [128, 512], F32)
        kb16 = self.sb("kb16", [128, 512], BF16)
        skT = self.sb("skT", [128, 16, 128], BF16)

        self.push()
        wck = self.wload("wck", I["w_ck"], D, 512)
        wcv = self.wload("wcv", I["w_cv"], D, 512)
        gb_mem = self.gbload("gb_mem", I["g_mem"], D)
        gb_ck = self.gbload("gb_ck", I["g_ck"], 128)
        sq4 = self.sb("sq4", [128, 512], F32)
        skb = self.sb("skb", [128, 16, 128], BF16)
        self.load(skb, skb[:, :, :], I["sub_keys"].rearrange("(a n) d -> n a d", a=16), q="pool")
        for half in range(2):
            self.transposes(skb, lambda j, half=half: skb[:, half * 8 + j, :], 8, skT, skT[:, half * 8:(half + 1) * 8, :])
        for mt in range(2):
            xt = self.X.next()
            self.load(xt, xt[:, :], I["mem_p"][mt * 128:(mt + 1) * 128, :])
            hT = self.HT.next()
            self.normT(xt, gb_mem, self.HB, hT)
            P = A[0]
            self.proj(P, P.t[:, 0:512], hT, wck, 0, 512)
            sm = self.small.next()
            self.op("act", lambda e: e.activation(out=sq4[:, :], in_=P.t[:, 0:512], func=AF.Square), R=[P], W=[sq4])
            self.op("dve", lambda e, sm=sm: e.tensor_reduce(out=sm[:, 0:4], in_=sq4[:, :].rearrange("p (h d) -> p h d", h=4),
                                                          axis=AX.X, op=ALU.add), R=[sq4], W=[sm])
            self.op("act", lambda e, sm=sm: e.activation(out=sm[:, 4:8], in_=sm[:, 0:4], func=AF.Sqrt, bias=self.epsb[:, 0:1],
                                                       scale=1.0 / 128), R=[sm, self.epsb], W=[sm])
            self.op("dve", lambda e, sm=sm: e.reciprocal(out=sm[:, 4:8], in_=sm[:, 4:8]), R=[sm], W=[sm])
            self.op("dve", lambda e, sm=sm: e.tensor_tensor(out=kf[:, :].rearrange("p (h d) -> p h d", h=4),
                                                          in0=P.t[:, 0:512].rearrange("p (h d) -> p h d", h=4),
                                                          in1=sm[:, 4:8].unsqueeze(2).to_broadcast([128, 4, 128]), op=ALU.mult),
                    R=[P, sm], W=[kf])
            self.op("dve", lambda e: e.tensor_tensor(out=kf[:, :].rearrange("p (h d) -> p h d", h=4),
                                                   in0=kf[:, :].rearrange("p (h d) -> p h d", h=4),
                                                   in1=gb_ck[:, :].unsqueeze(1).to_broadcast([128, 4, 128]), op=ALU.mult),
                    R=[kf, gb_ck], W=[kf])
            self.store(O["mk_p"][mt * 128:(mt + 1) * 128, :], kf, kf[:, :])
            self.op("act", lambda e: e.copy(out=kb16[:, :], in_=kf[:, :]), R=[kf], W=[kb16])
            self.transposes(kb16, lambda j: kb16[:, j * 128:(j + 1) * 128], 4, memKT[0], memKT[0][:, :, mt * 128:(mt + 1) * 128])
            P2 = A[1]
            self.proj(P2, P2.t[:, 512:1024], hT, wcv, 0, 512)
            self.op("act", lambda e: e.copy(out=sq4[:, :], in_=P2.t[:, 512:1024]), R=[P2], W=[sq4])
            self.store(O["mv_p"][mt * 128:(mt + 1) * 128, :], sq4, sq4[:, :])
            self.op("dve", lambda e, mt=mt: e.tensor_copy(out=memV[0][:, mt, :, 0:128], in_=sq4[:, :].rearrange("p (h d) -> p h d", h=4)),
                    R=[sq4], W=[memV[0]])
        self.pop()

        def load_sample_mem():
            for bt in range(4):
                for mt in range(2):
                    r0 = bt * 256 + mt * 128
                    self.load(kf, kf[:, :], I["memk_c"][r0:r0 + 128, :])
                    self.op("act", lambda e: e.copy(out=kb16[:, :], in_=kf[:, :]), R=[kf], W=[kb16])
                    self.transposes(kb16, lambda j: kb16[:, j * 128:(j + 1) * 128], 4, memKT[bt], memKT[bt][:, :, mt * 128:(mt + 1) * 128])
                    xt = self.X.next()
                    self.load(xt, xt[:, 0:512], I["memv_c"][r0:r0 + 128, :])
                    self.op("dve", lambda e, xt=xt, bt=bt, mt=mt: e.tensor_copy(out=memV[bt][:, mt, :, 0:128],
                                                                           in_=xt[:, 0:512].rearrange("p (h d) -> p h d", h=4)),
                            R=[xt], W=[memV[bt]])

        TS = [(self.sb("x2", [128, D], F32), self.sb("h3b", [128, D], BF16), self.sb("idxT", [128, 128], I32), self.sb("gT", [128, 128], F32))
              for _ in range(2)]
        mi_r = self.ring("mi", [128, D], BF16, 1)
        mT = self.sb("mT", [128, 8, 128], BF16)
        sqc = self.sb("sqc", [128, 512], BF16)
        rsc = self.sb("rsc", [128, 512], F32)
        qcn = self.sb("qcn", [128, 4, 128], BF16)
        ptc = self.sb("ptc", [128, 8, 128], BF16)
        ptcS = [self.sb("ptcS", [128, 8, 128], BF16) for _ in range(4)]
        for t in ptcS:
            self.op("pool", lambda e, t=t: e.memset(t[:, :, :], 0.0), W=[t])
        oc = self.sb("oc", [128, 4, 128], BF16)
        ocT = self.sb("ocT", [128, 4, 128], BF16)
        h3T = self.sb("h3T", [128, 8, 128], BF16)
        qTb = self.sb("qTb", [128, 16, 128], BF16)
        sc = self.sb("sc", [128, 2048], F32)
        wks = [self.sb("wk", [128, 256], F32) for _ in range(2)]
        sv = self.sb("sv", [128, 16, 16], F32)
        si = self.sb("si", [128, 16, 16], U32)
        sif = self.sb("sif", [128, 16, 16], F32)
        ts = self.sb("ts", [128, 8, 16], F32)
        sel = self.sb("sel", [128, 8, 16], U32)
        self_f = self.sb("self_f", [128, 8, 16], F32)
        k1i = self.sb("k1i", [128, 8, 16], I32)
        k1f = self.sb("k1f", [128, 8, 16], F32)
        k2f = self.sb("k2f", [128, 8, 16], F32)
        iota16 = self.sb("iota16", [128, 16], F32)
        for k in range(16):
            self.op("pool", lambda e, k=k: e.memset(iota16[:, k:k + 1], float(k)), W=[iota16])
        eq = self.sb("eq", [128, 4, 16, 16], BF16)
        i12 = self.sb("i12", [128, 2, 128], F32)
        tb = self.sb("tb", [128, 3, 128], BF16)
        ex = self.sb("ex", [128, 8, 16], F32)
        i1T = self.sb("i1T", [128, 128], F32)
        i2T = self.sb("i2T", [128, 128], F32)
        NB = self.dbg.get("NB", 8)
        GUV = self.ring("GUV", [128, 2 * D], BF16, NB)
        wsel_r = self.ring("wsel", [128, 128], BF16, 4)
        djunk_r = self.ring("djunk", [128, D], BF16, 1)
        act_r = self.ring("actc", [128, 1], F32, 6)
        gl_r = self.ring("glc", [128, 2], F32, 6)

        svb = [T(sv.t, Buf("svb")) for _ in range(16)]
        sib = [T(si.t, Buf("sib")) for _ in range(16)]
        tsb = [T(ts.t, Buf("tsb")) for _ in range(8)]
        selb = [T(sel.t, Buf("selb")) for _ in range(8)]

        def pre(i):
            x2, h3b, idxT, gT = TS[i % 2]
            x1 = x2
            if i == 16:
                load_sample_mem()
            xt = self.X.next()
            yield self.load(xt, xt[:, :], I["x_own"][i * 128:(i + 1) * 128, :])
            mi_ = mi_r.next()
            yield self.dma(lambda e, mi_=mi_, i=i: e.dma_start(out=mi_[:, :], in_=self.mscr.t[i * 128:(i + 1) * 128, :]), R=[self.mscr], W=[mi_])
            yield from (None for _ in range(2))
            yield self.transposes(mi_, lambda j, mi_=mi_: mi_[:, j * 128:(j + 1) * 128], 8, mT, mT[:, :, :])
            yield from (None for _ in range(2))
            for n in range(2):
                yield self.proj(Dp, Dp.t[:, 0:512], mT, wo, n * 512, 512)
                yield self.op("dve", lambda e, n=n, xt=xt: e.tensor_tensor(out=x1[:, n * 512:(n + 1) * 512], in0=xt[:, n * 512:(n + 1) * 512],
                                                                 in1=Dp.t[:, 0:512], op=ALU.add), R=[xt, Dp], W=[x1])
            hT = self.HT.next()
            yield self.rms_tok(x1, x1[:, :], D, gb_xa, gb_xa[:, :], self.HB, self.HB[:, :])
            yield from (None for _ in range(3))
            yield self.transposes(self.HB, lambda j: self.HB[:, j * 128:(j + 1) * 128], 8, hT, hT[:, :, :])
            yield from (None for _ in range(2))
            P = Dp
            for hd in range(4):
                for c in range(8):
                    yield self.mm(P, P.t[:, hd * 128:(hd + 1) * 128], wcq, wcq[:, c, hd * 128:(hd + 1) * 128], hT, hT[:, c, :], c == 0, c == 7)
            yield self.op("act", lambda e: e.activation(out=sqc[:, :], in_=P.t[:, 0:512], func=AF.Square), R=[P], W=[sqc])
            yield self.op("act", lambda e: e.copy(out=kf[:, :], in_=P.t[:, 0:512]), R=[P], W=[kf])
            yield from (None for _ in range(2))
            yield self.mm(P, P.t[:, 0:512], ones128, ones128[:, :], sqc, sqc[:, :], True, True)
            yield self.op("act", lambda e: e.activation(out=rsc[:, :], in_=P.t[:, 0:512], func=AF.Sqrt, bias=self.epsb[:, 0:1], scale=1.0),
                    R=[P, self.epsb], W=[rsc])
            yield self.op("dve", lambda e: e.reciprocal(out=rsc[:, :], in_=rsc[:, :]), R=[rsc], W=[rsc])
            yield self.op("dve", lambda e: e.scalar_tensor_tensor(out=qcn[:, :, :].rearrange("p h t -> p (h t)"), in0=kf[:, :], scalar=gcq[:, 1:2],
                                                            in1=rsc[:, :], op0=ALU.mult, op1=ALU.mult), R=[kf, gcq, rsc], W=[qcn])
            yield from (None for _ in range(3))
            if i < 16:
                batches = [(0, 0, 128, ptc)]
            else:
                batches = [(bt, bt * 16, 16, ptcS[bt]) for bt in range(4)]
            for (mb, c0, n, p) in batches:
                for pair in range(2):
                    for hd in (2 * pair, 2 * pair + 1):
                        for mt in range(2):
                            jj = (hd % 2) * 2 + mt
                            yield self.mm(P, P.t[:, jj * 128:jj * 128 + n], memKT[mb], memKT[mb][:, hd, mt * 128:(mt + 1) * 128],
                                          qcn, qcn[:, hd, c0:c0 + n], True, True)
                    yield self.op("act", lambda e, pair=pair, p=p, c0=c0, n=n: e.activation(
                        out=p[:, pair * 4:(pair + 1) * 4, c0:c0 + n],
                        in_=P.t[:, 0:512].rearrange("p (j q) -> p j q", j=4)[:, :, 0:n], func=AF.Exp),
                        R=[P], W=[p])
            yield from (None for _ in range(2))
            nq = 128
            for pair in range(2):
                for hd in (2 * pair, 2 * pair + 1):
                    col = (hd % 2) * 256
                    nacc = len(batches) * 2
                    k = 0
                    for (mb, c0, n, p) in batches:
                        for mt in range(2):
                            yield self.mm(P, P.t[0:nq, col:col + 136], p, p[:, hd * 2 + mt, 0:nq], memV[mb], memV[mb][:, mt, hd, :], k == 0, k == nacc - 1)
                            k += 1
                sm = self.small.next()
                ocv = P.t[:, 0:512].rearrange("p (h c) -> p h c", h=2)
                yield self.op("dve", lambda e, sm=sm, ocv=ocv: e.tensor_scalar(out=sm[:, 0:2], in0=ocv[:, :, 128:129].rearrange("p h o -> p (h o)"),
                                                                     scalar1=1e-30, scalar2=None, op0=ALU.max), R=[P], W=[sm])
                yield self.op("dve", lambda e, sm=sm: e.reciprocal(out=sm[:, 0:2], in_=sm[:, 0:2]), R=[sm], W=[sm])
                yield self.op("dve", lambda e, sm=sm, ocv=ocv, pair=pair: e.tensor_tensor(out=oc[:, 2 * pair:2 * pair + 2, :], in0=ocv[:, :, 0:128],
                                                                                in1=sm[:, 0:2].unsqueeze(2).to_broadcast([128, 2, 128]), op=ALU.mult),
                        R=[P, sm], W=[oc])
            yield from (None for _ in range(2))
            yield self.transposes(oc, lambda j: oc[:, j, :], 4, ocT, ocT[:, :, :])
            yield from (None for _ in range(2))
            for n in range(2):
                yield self.proj(P, P.t[:, 0:512], ocT, wco, n * 512, 512, kc=4)
                yield self.op("dve", lambda e, n=n: e.tensor_tensor(out=x2[:, n * 512:(n + 1) * 512], in0=x1[:, n * 512:(n + 1) * 512],
                                                           in1=P.t[:, 0:512], op=ALU.add), R=[x1, P], W=[x2])
            yield self.rms_tok(x2, x2[:, :], D, gb_ff, gb_ff[:, :], h3b, h3b[:, :])
            yield from (None for _ in range(3))
            yield self.transposes(h3b, lambda j: h3b[:, j * 128:(j + 1) * 128], 8, h3T, h3T[:, :, :])
            yield from (None for _ in range(2))
            for r in range(4):
                for hp in range(r * 4, r * 4 + 4):
                    off = (hp % 4) * 128
                    for c in range(8):
                        yield self.mm(P, P.t[:, off:off + 128], wpq, wpq[:, c, hp * 128:(hp + 1) * 128], h3T, h3T[:, c, :], c == 0, c == 7)
                yield self.op("act", lambda e, r=r: e.copy(out=qTb[:, r * 4:r * 4 + 4, :].rearrange("p a t -> p (a t)"), in_=P.t[:, 0:512]),
                              R=[P], W=[qTb])
            for r in range(4):
                for hp in range(r * 4, r * 4 + 4):
                    off = (hp % 4) * 128
                    yield self.mm(P, P.t[:, off:off + 128], qTb, qTb[:, hp, :], skT, skT[:, hp, :], True, True)
                yield self.op("act", lambda e, r=r: e.copy(out=sc[:, r * 512:(r + 1) * 512], in_=P.t[:, 0:512]), R=[P], W=[sc])

            def top16_pair(items):
                for stage in range(5):
                    for (src_ap, width, vout, iout, wkb, vb, ib) in items:
                        if stage == 0:
                            self.op("dve", lambda e, vout=vout, src_ap=src_ap: e.max(out=vout[:, 0:8], in_=src_ap), R=[sc], W=[vb])
                        elif stage == 1:
                            self.op("dve", lambda e, vout=vout, iout=iout, src_ap=src_ap: e.max_index(out=iout[:, 0:8], in_max=vout[:, 0:8], in_values=src_ap),
                                    R=[sc, vb], W=[ib])
                        elif stage == 2:
                            self.op("dve", lambda e, vout=vout, src_ap=src_ap, wkb=wkb, width=width: e.match_replace(
                                out=wkb[:, 0:width], in_to_replace=vout[:, 0:8], in_values=src_ap, imm_value=-1e30), R=[sc, vb], W=[wkb])
                        elif stage == 3:
                            self.op("dve", lambda e, vout=vout, wkb=wkb, width=width: e.max(out=vout[:, 8:16], in_=wkb[:, 0:width]), R=[wkb], W=[vb])
                        else:
                            self.op("dve", lambda e, vout=vout, iout=iout, wkb=wkb, width=width: e.max_index(
                                out=iout[:, 8:16], in_max=vout[:, 8:16], in_values=wkb[:, 0:width]), R=[wkb, vb], W=[ib])
                    yield None
            for hp in range(0, 16, 2):
                yield from top16_pair([(sc[:, q * 128:(q + 1) * 128], 128, sv[:, q, :], si[:, q, :], wks[q % 2], svb[q], sib[q]) for q in (hp, hp + 1)])
            yield self.op("dve", lambda e: e.tensor_copy(out=sif[:, :, :], in_=si[:, :, :]), R=sib, W=[sif])
            sv4 = sv[:, :, :].rearrange("p (h two) k -> p h two k", two=2)
            cand = sc[:, :].rearrange("p (h a b) -> p h a b", h=8, a=16)
            yield self.op("dve", lambda e: e.tensor_tensor(out=cand, in0=sv4[:, :, 0, :].unsqueeze(3).to_broadcast([128, 8, 16, 16]),
                                                   in1=sv4[:, :, 1, :].unsqueeze(2).to_broadcast([128, 8, 16, 16]), op=ALU.add),
                    R=svb, W=[sc])
            for hh in range(0, 8, 2):
                yield from top16_pair([(sc[:, q * 256:(q + 1) * 256], 256, ts[:, q, :], sel[:, q, :], wks[q % 2], tsb[q], selb[q]) for q in (hh, hh + 1)])
            yield self.op("dve", lambda e: e.tensor_copy(out=self_f[:, :, :], in_=sel[:, :, :]), R=selb, W=[self_f])
            yield self.op("dve", lambda e: e.tensor_scalar(out=k1i[:, :, :], in0=self_f[:, :, :], scalar1=-7.5, scalar2=0.0625, op0=ALU.add, op1=ALU.mult),
                    R=[self_f], W=[k1i])
            yield self.op("dve", lambda e: e.tensor_copy(out=k1f[:, :, :], in_=k1i[:, :, :]), R=[k1i], W=[k1f])
            yield self.op("dve", lambda e: e.scalar_tensor_tensor(out=k2f[:, :, :], in0=k1f[:, :, :], scalar=-16.0, in1=self_f[:, :, :],
                                                            op0=ALU.mult, op1=ALU.add), R=[k1f, self_f], W=[k2f])
            sif4 = sif[:, :, :].rearrange("p (h two) k -> p h two k", two=2)
            io4 = iota16[:, :].unsqueeze(1).unsqueeze(1).to_broadcast([128, 4, 16, 16])
            for which, kf_ in ((0, k1f), (1, k2f)):
                for h2 in range(2):
                    hs = slice(h2 * 4, h2 * 4 + 4)
                    yield self.op("dve", lambda e, kf_=kf_, hs=hs: e.tensor_tensor(out=eq[:, :, :, :], in0=io4,
                                                                          in1=kf_[:, hs, :].unsqueeze(3).to_broadcast([128, 4, 16, 16]), op=ALU.is_equal),
                            R=[iota16, kf_], W=[eq])
                    yield self.op("pool", lambda e, which=which, hs=hs: e.tensor_tensor(out=eq[:, :, :, :], in0=eq[:, :, :, :],
                                                                              in1=sif4[:, hs, which, :].unsqueeze(2).to_broadcast([128, 4, 16, 16]), op=ALU.mult),
                            R=[eq, sif], W=[eq])
                    yield self.op("dve", lambda e, which=which, h2=h2: e.tensor_reduce(out=i12[:, which, h2 * 64:(h2 + 1) * 64].rearrange("p (h k) -> p h k", h=4),
                                                                              in_=eq[:, :, :, :], axis=AX.X, op=ALU.add), R=[eq], W=[i12])
            yield self.op("act", lambda e: e.copy(out=tb[:, 0:2, :], in_=i12[:, :, :]), R=[i12], W=[tb])
            yield self.op("dve", lambda e: e.tensor_tensor(out=ex[:, :, :], in0=ts[:, :, :], in1=ts[:, :, 0:1].to_broadcast([128, 8, 16]), op=ALU.subtract),
                    R=tsb, W=[ex])
            yield self.op("act", lambda e: e.activation(out=ex[:, :, :], in_=ex[:, :, :], func=AF.Exp), R=[ex], W=[ex])
            sm = self.small.next()
            yield self.op("dve", lambda e, sm=sm: e.tensor_reduce(out=sm[:, 0:8], in_=ex[:, :, :], axis=AX.X, op=ALU.add), R=[ex], W=[sm])
            yield self.op("dve", lambda e, sm=sm: e.reciprocal(out=sm[:, 0:8], in_=sm[:, 0:8]), R=[sm], W=[sm])
            yield self.op("dve", lambda e, sm=sm: e.tensor_tensor(out=tb[:, 2, :].rearrange("p (h k) -> p h k", h=8), in0=ex[:, :, :],
                                                          in1=sm[:, 0:8].unsqueeze(2).to_broadcast([128, 8, 16]), op=ALU.mult),
                    R=[ex, sm], W=[tb])
            yield from (None for _ in range(3))
            for j in range(3):
                yield self.op("pe", lambda e, j=j: e.transpose(out=pT[:, j * 128:(j + 1) * 128], in_=tb[:, j, :], identity=self.ident[:, :]),
                        R=[tb, self.ident], W=[pT])
            yield self.op("act", lambda e: e.copy(out=i1T[:, :], in_=pT[:, 0:128]), R=[pT], W=[i1T])
            yield self.op("act", lambda e: e.copy(out=i2T[:, :], in_=pT[:, 128:256]), R=[pT], W=[i2T])
            yield self.op("dve", lambda e: e.scalar_tensor_tensor(out=idxT[:, :], in0=i1T[:, :], scalar=128.0, in1=i2T[:, :],
                                                            op0=ALU.mult, op1=ALU.add), R=[i1T, i2T], W=[idxT])
            yield self.op("act", lambda e: e.copy(out=gT[:, :], in_=pT[:, 256:384]), R=[pT], W=[gT])
        def loop(i, gen):
            x2, h3b, idxT, gT = TS[i % 2]
            ntok = 128 if i < 16 else 64
            Pacc = [B[0], B[1]]
            st_g, st_x, st_w = {}, {}, {}

            def stageA(t):
                guv = GUV.next()
                st_g[t] = guv
                self.dma(lambda e: e.indirect_dma_start(out=guv[:, :], out_offset=None, in_=self.uvtab.t,
                                                         in_offset=bass.IndirectOffsetOnAxis(ap=idxT[:, t:t + 1], axis=0)),
                         R=[idxT, self.uvtab], W=[guv], q="pool")

            st_a = {}

            def stageB1(t):
                guv = st_g[t]
                ac = act_r.next(); dj = djunk_r.next()
                st_a[t] = ac
                Px = A if order.index(t) % 2 == 0 else C
                tX = Px[0].t
                for n in range(2):
                    self.mm(Px[n], tX[:, n * 512:(n + 1) * 512], self.ident, self.ident[:, t:t + 1].to_broadcast([128, 128]),
                            h3b, h3b[:, n * 512:(n + 1) * 512], True, True)
                self.op("dve", lambda e: e.scalar_tensor_tensor(out=dj[:, :], in0=guv[:, 0:D], scalar=1.0, in1=tX[:, :], op0=ALU.mult, op1=ALU.mult,
                                                                accum_out=ac[:, 0:1]), R=[guv, Px[0], Px[1]], W=[dj, ac])

            def stageB2(t):
                ac = st_a.pop(t)
                g2 = gl_r.next()
                self.op("act", lambda e: e.activation(out=g2[:, 0:1], in_=ac[:, 0:1], func=AF.Gelu), R=[ac], W=[g2])
                self.op("act", lambda e: e.activation(out=g2[:, 1:2], in_=g2[:, 0:1], func=AF.Copy, scale=gT[:, t:t + 1]), R=[g2, gT], W=[g2])
                ws = wsel_r.next()
                st_w[t] = ws
                r32 = t % 32
                self.op("act", lambda e: e.activation(out=ws[:, 0:32], in_=cmask[:, 127 - r32:159 - r32], func=AF.Copy, scale=g2[:, 1:2]),
                        R=[cmask, g2], W=[ws])

            ngrp = ntok // 32
            order = [g * 32 + r for r in range(32) for g in range(ngrp)]

            def stageC2(ts_):
                items = [(t, st_w.pop(t), st_g.pop(t)) for t in ts_]
                for n in range(2):
                    for (t, ws, guv) in items:
                        j = t // 32
                        r = t % 32
                        self.op("pe", lambda e, n=n, ws=ws, guv=guv, j=j, r=r: e.matmul(
                            Pacc[n].t[32 * j:32 * j + 32, n * 512:(n + 1) * 512], lhsT=ws[:, 0:32], rhs=guv[:, D + n * 512:D + (n + 1) * 512],
                            start=(r == 0), stop=(r == 31), tile_position=(0, 32 * j)), R=[ws, guv], W=[Pacc[n]])
            LA = NB - 4
            nst = len(order)
            for step in range(nst + LA + 3):
                if step < nst:
                    stageA(order[step])
                if 0 <= step - LA < nst:
                    stageB1(order[step - LA])
                if 0 <= step - LA - 1 < nst:
                    stageB2(order[step - LA - 1])
                k = step - LA - 2
                if k >= 1 and k % 2 == 1 and k < nst:
                    stageC2([order[k - 1], order[k]])
                if gen is not None:
                    for _ in range(3):
                        next(gen, None)
            yo = self.X.next()
            for n in range(2):
                self.op("dve", lambda e, n=n, yo=yo: e.tensor_tensor(out=yo[:, n * 512:(n + 1) * 512], in0=x2[:, n * 512:(n + 1) * 512],
                                                                 in1=Pacc[n].t[:, n * 512:(n + 1) * 512], op=ALU.add), R=[x2, Pacc[n]], W=[yo])
            self.store(O["y_own"][i * 128:(i + 1) * 128, :], yo, yo[:, :])
            if gen is not None:
                for _ in gen:
                    pass

        NADV = self.dbg.get("NADV", 6)
        g0 = pre(0)
        self.npre = 0
        for _ in g0:
            self.npre += 1
        for i in range(NOWN):
            gen = pre(i + 1) if i + 1 < NOWN else None
            loop(i, gen)
        self.pop()


def _own_blocks(j):
    blks = []
    for s in range(16):
        m = s // 2
        if s % 2 == 0:
            blks.append(4 * m + (0 if j == 0 else 1))
        else:
            blks.append(4 * m + (3 if j == 0 else 2))
    return blks


def _consts(j):
    f = np.float32
    c = {}
    c["idn"] = np.eye(128, dtype=f)
    oq = np.zeros((96, 96), f)
    oq[0:64, 0:64] = 1.0 / 64
    oq[64:96, 64:96] = 1.0 / 32
    c["onesq"] = oq
    c["ones128"] = np.full((128, 128), 1.0 / 128, f)
    rm = np.zeros((96, 96), f)
    for d in range(64, 80):
        rm[d + 16, d] = -1.0
    for d in range(80, 96):
        rm[d - 16, d] = 1.0
    c["rotm"] = rm
    cm = np.zeros((128, 255), f)
    cm[:, 127] = 1.0
    c["cmask"] = cm
    c["trilm"] = np.tril(np.ones((128, 128), f))
    tms = np.zeros((128, 128), f)
    for bt in range(4):
        tms[bt * 16:(bt + 1) * 16, bt * 16:(bt + 1) * 16] = np.tril(np.ones((16, 16), f))
    c["trilms"] = tms
    inv = (np.float32(10000.0) ** (-np.arange(16, dtype=f) / np.float32(16))).astype(f)
    pos = np.zeros((128, 33), f)
    for i in range(32):
        pos[:, i] = i * 128 + np.arange(128)
    pos[0:64, 32] = 1024 + (np.arange(64) % 16)
    ang = (pos[:, :, None] * inv[None, None, :]).astype(f)
    c["ropek"] = np.concatenate([np.cos(ang), np.sin(ang)], axis=2).astype(f)
    blks = _own_blocks(j)
    posq = np.zeros((NTOK,), f)
    for s, b in enumerate(blks):
        posq[s * 128:(s + 1) * 128] = b * 128 + np.arange(128)
    posq[2048:2112] = 1024 + (np.arange(64) % 16)
    angq = (posq[None, :] * np.concatenate([inv, inv])[:, None]).astype(f)
    c["cosq"] = np.cos(angq).astype(f)
    c["sinq"] = np.sin(angq).astype(f)
    kk = np.arange(128)[:, None] // 64
    qq = np.arange(128)[None, :] // 64
    diag = (kk <= qq).astype(f)
    full = np.ones((128, 128), f)
    zero = np.zeros((128, 128), f)
    c["masks"] = np.stack([diag, zero, full, diag] if j == 0 else [full, diag, diag, zero]).astype(f)
    return c


_NC_CACHE = {}


def _get_nc(dbg=None):
    key = repr(sorted((dbg or {}).items()))
    if key not in _NC_CACHE:
        k = Kern(dbg)
        _NC_CACHE[key] = k.build()
    return _NC_CACHE[key]


def kernel(x_prompt, x_sample, cache_mla_ckv, cache_mla_krope, cache_mem_k, cache_mem_v, mem_prompt,
           g_mix, w_in, g_q_lat, w_uq, g_qn, g_qr, g_kv_lat, g_kr, w_uk, w_uv, g_kn, w_oa,
           g_gm, w_s, b_s, w_ob, w_o,
           g_xattn, g_mem, w_cq, g_cq, w_ck, g_ck, w_cv, w_co,
           g_ffn, w_pq, sub_keys, peer_u, peer_v, _dbg=None):
    f = np.float32
    A = lambda a: np.ascontiguousarray(np.asarray(a, dtype=f))
    x_prompt = A(x_prompt); x_sample = A(x_sample)
    shared = {
        "w_in": A(w_in)[0], "w_uq": A(w_uq)[0], "w_uk": A(w_uk)[0], "w_uv": A(w_uv)[0],
        "w_oa": A(w_oa)[0], "w_ob": A(w_ob)[0], "w_o": A(w_o)[0],
        "w_cq": A(w_cq)[0], "w_ck": A(w_ck)[0], "w_cv": A(w_cv)[0], "w_co": A(w_co)[0],
        "w_pq": A(w_pq)[0], "sub_keys": A(sub_keys)[0].reshape(2048, 128),
        "peer_u": A(peer_u)[0], "peer_v": A(peer_v)[0],
        "g_mix": A(g_mix), "g_q_lat": A(g_q_lat), "g_qn": A(g_qn), "g_qr": A(g_qr), "g_kv_lat": A(g_kv_lat),
        "g_kr": A(g_kr), "g_kn": A(g_kn), "g_gm": A(g_gm), "g_xattn": A(g_xattn), "g_mem": A(g_mem),
        "g_cq": A(g_cq), "g_ck": A(g_ck), "g_ffn": A(g_ffn),
        "w_s": A(w_s)[0].reshape(1024, 128), "b_s": A(b_s)[0],
    }
    ckv_c = A(cache_mla_ckv)[0]; kr_c = A(cache_mla_krope)[0]
    mk_c = A(cache_mem_k)[0]; mv_c = A(cache_mem_v)[0]; mem_p = A(mem_prompt)
    consts = [_consts(0), _consts(1)]
    in_maps = []
    for c in range(NCORES):
        b, j = c // 2, c % 2
        blks = _own_blocks(j)
        x_own = np.zeros((NTOK, D), f)
        for s, blk in enumerate(blks):
            x_own[s * 128:(s + 1) * 128] = x_prompt[b, blk * 128:(blk + 1) * 128]
        x_own[2048:2112] = x_sample[4 * c:4 * c + 4].reshape(64, D)
        m = dict(shared)
        m.update(consts[j])
        m["x_all"] = x_prompt[b]
        m["x_own"] = x_own
        m["ckv_c"] = np.ascontiguousarray(ckv_c[4 * c:4 * c + 4].reshape(4096, 256))
        m["kr_c"] = np.ascontiguousarray(kr_c[4 * c:4 * c + 4].reshape(4096, 32))
        m["memk_c"] = np.ascontiguousarray(mk_c[4 * c:4 * c + 4].reshape(1024, 512))
        m["memv_c"] = np.ascontiguousarray(mv_c[4 * c:4 * c + 4].reshape(1024, 512))
        m["mem_p"] = mem_p[b]
        in_maps.append(m)
    nc = _get_nc(_dbg)
    res = run_bass_kernel_spmd(nc, in_maps, core_ids=list(range(NCORES)))
    R = res.results
    B, S, DB, DS = 4, 4096, 32, 16
    y_p = np.zeros((B, S, D), f); y_s = np.zeros((DB, DS, D), f)
    ckv_p = np.zeros((1, B, S, 256), f); kr_p = np.zeros((1, B, S, 32), f)
    mk_p = np.zeros((1, B, 256, 4, 128), f); mv_p = np.zeros((1, B, 256, 4, 128), f)
    ckv_s = np.zeros((1, DB, DS, 256), f); kr_s = np.zeros((1, DB, DS, 32), f); vg_s = np.zeros((1, DB, DS, D), f)
    for c in range(NCORES):
        b, j = c // 2, c % 2
        r = R[c]
        for s, blk in enumerate(_own_blocks(j)):
            y_p[b, blk * 128:(blk + 1) * 128] = r["y_own"][s * 128:(s + 1) * 128]
        y_s[4 * c:4 * c + 4] = r["y_own"][2048:2112].reshape(4, 16, D)
        half = slice(j * 2048, (j + 1) * 2048)
        ckv_p[0, b, half] = r["ckv_p"][half]
        kr_p[0, b, half] = r["kr_p"][half]
        mk_p[0, b, j * 128:(j + 1) * 128] = r["mk_p"][j * 128:(j + 1) * 128].reshape(128, 4, 128)
        mv_p[0, b, j * 128:(j + 1) * 128] = r["mv_p"][j * 128:(j + 1) * 128].reshape(128, 4, 128)
        ckv_s[0, 4 * c:4 * c + 4] = r["ckv_s"].reshape(4, 16, 256)
        kr_s[0, 4 * c:4 * c + 4] = r["kr_s"].reshape(4, 16, 32)
        vg_s[0, 4 * c:4 * c + 4] = r["vg_s"].reshape(4, 16, D)
    return (y_p, y_s, ckv_p, kr_p, mk_p, mv_p, ckv_s, kr_s, vg_s)
```

```python
import numpy as np
from contextlib import ExitStack
import concourse.bass as bass
import concourse.mybir as mybir
from concourse.bass_utils import run_bass_kernel_spmd

F32 = mybir.dt.float32
BF16 = mybir.dt.bfloat16
I32 = mybir.dt.int32
U32 = mybir.dt.uint32
ALU = mybir.AluOpType
AF = mybir.ActivationFunctionType
AX = mybir.AxisListType

NCORES = 8
D = 1024
EPS = 1e-6
NOWN = 17
NTOK = NOWN * 128
NKP = 4096
NKS = 4 * 1040
NK = NKP + NKS
MLA_SCALE = 96 ** -0.5
MEM_SCALE = 128 ** -0.5
IN_Q, IN_KV, IN_Z = 0, 384, 672


class Buf:
    __slots__ = ("name", "lw", "rd")

    def __init__(self, name):
        self.name = name
        self.lw = None
        self.rd = {}


class T:
    __slots__ = ("t", "b")

    def __init__(self, t, b):
        self.t = t
        self.b = b

    def __getitem__(self, k):
        return self.t[k]


class _Eng:
    def __init__(self, name, selfsync):
        self.name = name
        self.sem = "e_" + name
        self.count = 0
        self.seen = {}
        self.prog = []
        self.selfsync = selfsync


class _Queue:
    def __init__(self, name, eng, nslots):
        self.name = name
        self.eng = eng
        self.slots = [["q_%s_%d" % (name, i), 0] for i in range(nslots)]
        self.next = 0


class Sched:
    def __init__(self, nc, st, selfsync=True):
        self.nc = nc
        self.eng = {
            "pe": _Eng("pe", False),
            "act": _Eng("act", selfsync),
            "dve": _Eng("dve", selfsync),
            "pool": _Eng("pool", selfsync),
            "sp": _Eng("sp", False),
        }
        self.queues = {
            "sp": _Queue("sp", "sp", 8),
            "pool": _Queue("pool", "pool", 8),
            "act": _Queue("act", "act", 4),
            "conv": _Queue("conv", "pool", 16),
        }
        self.final_tokens = []
        names = [E.sem for E in self.eng.values()]
        for Q in self.queues.values():
            names += [s[0] for s in Q.slots]
        self.sems = {n: st.enter_context(nc.semaphore(n)) for n in names}
        self.ninst = 0

    def _collect(self, E, reads, writes, extra=None):
        need = {}

        def add(tok):
            if tok is None:
                return
            s, v = tok
            if need.get(s, 0) < v:
                need[s] = v
        for b in reads:
            add(b.lw)
        for b in writes:
            add(b.lw)
            for s, v in b.rd.items():
                add((s, v))
        if extra:
            for t in extra:
                add(t)
        waits = []
        for s, v in need.items():
            if s == E.sem and not E.selfsync:
                continue
            if E.seen.get(s, 0) >= v:
                continue
            E.seen[s] = v
            waits.append((s, v))
        return waits

    def _commit(self, tok, reads, writes):
        for b in writes:
            b.lw = tok
            b.rd = {}
        s, v = tok
        for b in reads:
            if b in writes:
                continue
            if b.rd.get(s, 0) < v:
                b.rd[s] = v

    def op(self, eng, fn, R=(), W=()):
        E = self.eng[eng]
        reads = [x.b for x in R]
        writes = [x.b for x in W]
        waits = self._collect(E, reads, writes)
        E.count += 1
        tok = (E.sem, E.count)
        E.prog.append((waits, fn, (E.sem, 1)))
        self._commit(tok, reads, writes)
        return tok

    def dma(self, queue, fn, R=(), W=(), final=False):
        Q = self.queues[queue]
        E = self.eng[Q.eng]
        reads = [x.b for x in R]
        writes = [x.b for x in W]
        slot = Q.slots[Q.next]
        Q.next = (Q.next + 1) % len(Q.slots)
        extra = [(slot[0], slot[1] * 16)] if slot[1] > 0 else None
        waits = self._collect(E, reads, writes, extra)
        slot[1] += 1
        tok = (slot[0], slot[1] * 16)
        E.prog.append((waits, fn, (slot[0], 16)))
        self._commit(tok, reads, writes)
        if final:
            self.final_tokens.append(tok)
        return tok

    def barrier(self):
        toks = []
        for E in self.eng.values():
            if E.count > 0:
                toks.append((E.sem, E.count))
        for qn, Q in self.queues.items():
            if qn == "conv":
                continue
            for s in Q.slots:
                if s[1] > 0:
                    toks.append((s[0], s[1] * 16))
        for E in self.eng.values():
            waits = []
            for s, v in toks:
                if s == E.sem:
                    continue
                if E.seen.get(s, 0) >= v:
                    continue
                E.seen[s] = v
                waits.append((s, v))
            if waits:
                E.prog.append((waits, None, None))

    def flush(self, last=False):
        nc = self.nc
        sems = self.sems
        if last:
            fin = {}
            for s, v in self.final_tokens:
                if fin.get(s, 0) < v:
                    fin[s] = v
            self.eng["sp"].prog.append(([(s, v) for s, v in fin.items()], None, None))

        def run(E):
            prog = E.prog
            E.prog = []
            self.ninst += len(prog)

            def body(e):
                for waits, fn, inc in prog:
                    for s, v in waits:
                        e.wait_ge(sems[s], v)
                    if fn is not None:
                        ins = fn(e)
                        ins.then_inc(sems[inc[0]], inc[1])
            return body
        with nc.Block(no_gpsimd_drain=True) as block:
            block.sync(run(self.eng["sp"]))
            block.tensor(run(self.eng["pe"]))
            block.scalar(run(self.eng["act"]))
            block.vector(run(self.eng["dve"]))
            block.gpsimd(run(self.eng["pool"]))


class Ring:
    def __init__(self, tiles):
        self.tiles = tiles
        self.i = 0

    def next(self):
        t = self.tiles[self.i]
        self.i = (self.i + 1) % len(self.tiles)
        return t


class Kern:
    def __init__(self, dbg=None):
        self.nc = bass.Bass("TRN2", target_bir_lowering=False)
        self.st = ExitStack()
        self.S = Sched(self.nc, self.st)
        self.scopes = [self.st]
        self.dbg = dbg or {}
        self.uid = 0

    def din(self, name, shape, dt=F32):
        return self.nc.dram_tensor(name, list(shape), dt, kind="ExternalInput").ap()

    def dout(self, name, shape, dt=F32):
        return self.nc.dram_tensor(name, list(shape), dt, kind="ExternalOutput").ap()

    def sb(self, name, shape, dt):
        self.uid += 1
        nm = "%s_%d" % (name, self.uid)
        t = self.scopes[-1].enter_context(self.nc.sbuf_tensor(nm, list(shape), dt))
        return T(t, Buf(nm))

    def ring(self, name, shape, dt, n):
        return Ring([self.sb(name, shape, dt) for _ in range(n)])

    def push(self):
        s = ExitStack()
        self.scopes.append(s)
        return s

    def pop(self):
        self.S.barrier()
        self.S.flush()
        s = self.scopes.pop()
        s.close()

    def op(self, eng, fn, R=(), W=()):
        return self.S.op(eng, fn, R, W)

    def dma(self, fn, R=(), W=(), q="sp", final=False):
        return self.S.dma(q, fn, R, W, final)

    def load(self, dst, dst_ap, src_ap, q="sp", slow=False):
        if slow:
            self.dma(lambda e: e.dma_start(out=dst_ap, in_=src_ap, allow_slow_non_contiguous=True), W=[dst], q=q)
        else:
            self.dma(lambda e: e.dma_start(out=dst_ap, in_=src_ap), W=[dst], q=q)

    def store(self, dst_ap, src, src_ap):
        self.dma(lambda e: e.dma_start(out=dst_ap, in_=src_ap), R=[src], final=True)

    def wload(self, name, src, K, N, c0=0):
        kc = K // 128
        w = self.sb(name, [128, kc, N], BF16)
        for c in range(kc):
            self.load(w, w[:, c, :], src[c * 128:(c + 1) * 128, c0:c0 + N], q="pool")
        return w

    def gbload(self, name, src, n):
        g = self.sb(name, [128, n], F32)
        self.load(g, g[:, :], src[0:1, 0:n].partition_broadcast(128))
        return g

    def mm(self, ps, ps_ap, lhsT, lhsT_ap, rhs, rhs_ap, start, stop):
        self.op("pe", lambda e: e.matmul(ps_ap, lhsT=lhsT_ap, rhs=rhs_ap, start=start, stop=stop),
                R=[lhsT, rhs], W=[ps])

    def rstd(self, src, src_ap, n, Dn):
        sm = self.small.next()
        jk = self.junk
        self.op("act", lambda e: e.activation(out=jk[:, 0:n], in_=src_ap, func=AF.Square, accum_out=sm[:, 0:1]),
                R=[src], W=[jk, sm])
        self.op("act", lambda e: e.activation(out=sm[:, 1:2], in_=sm[:, 0:1], func=AF.Sqrt, bias=self.epsb[:, 0:1],
                                              scale=1.0 / Dn), R=[sm, self.epsb], W=[sm])
        self.op("dve", lambda e: e.reciprocal(out=sm[:, 2:3], in_=sm[:, 1:2]), R=[sm], W=[sm])
        return sm, sm[:, 2:3]

    def rms_tok(self, src, src_ap, n, gb, gb_ap, dst, dst_ap):
        sm, col = self.rstd(src, src_ap, n, n)
        self.op("dve", lambda e: e.scalar_tensor_tensor(out=dst_ap, in0=src_ap, scalar=col, in1=gb_ap,
                                                        op0=ALU.mult, op1=ALU.mult), R=[src, sm, gb], W=[dst])

    def transposes(self, src, src_ap_fn, n, dst, dst_ap, rows=128, eng="act"):
        pT = self.psT
        for j in range(n):
            ap = src_ap_fn(j)
            self.op("pe", lambda e, ap=ap, j=j: e.transpose(out=pT[:, j * 128:j * 128 + rows], in_=ap,
                                                             identity=self.ident[0:rows, 0:rows]),
                    R=[src, self.ident], W=[pT])
        view = pT[:, 0:n * 128].rearrange("p (c k) -> p c k", c=n)[:, :, 0:rows]
        if eng == "act":
            self.op("act", lambda e: e.copy(out=dst_ap, in_=view), R=[pT], W=[dst])
        else:
            self.op("dve", lambda e: e.tensor_copy(out=dst_ap, in_=view), R=[pT], W=[dst])

    def normT(self, xt, gb, hb, hT):
        self.rms_tok(xt, xt[:, :], D, gb, gb[:, :], hb, hb[:, :])
        self.transposes(hb, lambda j: hb[:, j * 128:(j + 1) * 128], 8, hT, hT[:, :, :])

    def proj(self, ps, ps_ap, hT, w, c0, n, kc=8):
        for c in range(kc):
            self.mm(ps, ps_ap, hT, hT[:, c, :], w, w[:, c, c0:c0 + n], c == 0, c == kc - 1)

    def build(self):
        nc = self.nc
        I = {}
        O = {}

        def di(name, shape, dt=F32):
            I[name] = self.din(name, shape, dt)

        def do(name, shape, dt=F32):
            O[name] = self.dout(name, shape, dt)
        di("x_all", [4096, D]); di("x_own", [NTOK, D])
        di("ckv_c", [4096, 256]); di("kr_c", [4096, 32])
        di("memk_c", [1024, 512]); di("memv_c", [1024, 512]); di("mem_p", [256, D])
        di("w_in", [D, 4768]); di("w_uq", [384, 1536]); di("w_uk", [256, 1024]); di("w_uv", [256, 1024])
        di("w_oa", [D, D]); di("w_ob", [D, D]); di("w_o", [D, D])
        di("w_cq", [D, 512]); di("w_ck", [D, 512]); di("w_cv", [D, 512]); di("w_co", [512, D])
        di("w_pq", [D, 2048]); di("sub_keys", [2048, 128]); di("peer_u", [16384, D]); di("peer_v", [16384, D])
        for g, n in [("g_mix", D), ("g_q_lat", 384), ("g_qn", 64), ("g_qr", 32), ("g_kv_lat", 256), ("g_kr", 32),
                     ("g_kn", 64), ("g_gm", D), ("g_xattn", D), ("g_mem", D), ("g_cq", 128), ("g_ck", 128),
                     ("g_ffn", D)]:
            di(g, [1, n])
        di("w_s", [1024, 128]); di("b_s", [8, 128])
        di("idn", [128, 128]); di("onesq", [96, 96]); di("ones128", [128, 128]); di("rotm", [96, 96])
        di("cmask", [128, 255]); di("trilm", [128, 128]); di("trilms", [128, 128])
        di("ropek", [128, 33, 32]); di("cosq", [32, NTOK]); di("sinq", [32, NTOK]); di("masks", [4, 128, 128])
        do("y_own", [NTOK, D]); do("ckv_p", [4096, 256]); do("kr_p", [4096, 32])
        do("mk_p", [256, 512]); do("mv_p", [256, 512])
        do("ckv_s", [64, 256]); do("kr_s", [64, 32]); do("vg_s", [64, D])
        self.I, self.O = I, O

        def ps(name, shape, dt):
            t = self.st.enter_context(nc.psum_tensor(name, shape, dt))
            return t
        self.psT = T(ps("psT", [128, 1024], BF16), Buf("psT"))
        tA = ps("psA", [128, 1024], F32); tB = ps("psB", [128, 1024], F32); tC = ps("psC", [128, 1024], F32)
        tD = ps("psD", [128, 512], F32)
        self.A = [T(tA, Buf("A0")), T(tA, Buf("A1"))]
        self.B = [T(tB, Buf("B0")), T(tB, Buf("B1"))]
        self.C = [T(tC, Buf("C0")), T(tC, Buf("C1"))]
        self.Dp = T(tD, Buf("D"))

        self.ident = self.sb("ident", [128, 128], BF16)
        self.load(self.ident, self.ident[:, :], I["idn"], q="pool")
        self.epsb = self.sb("epsb", [128, 1], F32)
        self.op("dve", lambda e: e.memset(self.epsb[:, :], EPS), W=[self.epsb])
        self.mscr = T(nc.dram_tensor("m_scr", [NTOK, D], BF16).ap(), Buf("mscr"))
        self.junk = self.sb("junk", [128, D], BF16)
        self.small = self.ring("small", [128, 8], F32, 4)
        self.X = self.ring("X", [128, D], F32, 2)
        self.HB = self.sb("HB", [128, D], BF16)
        self.HT = self.ring("HT", [128, 8, 128], BF16, 1)

        self.uvtab = T(nc.dram_tensor("uv_bf", [16384, 2 * D], BF16).ap(), Buf("uvtab"))
        self.hscr = T(nc.dram_tensor("h_scr", [NTOK, D], BF16).ap(), Buf("hscr"))
        self._conv_pending = True
        ph = self.dbg.get("phases", "1234")
        self.push()
        self.o_all = self.sb("o_all", [128, NOWN, D], BF16)
        self.op("pool", lambda e: e.memset(self.o_all[:, 16, :], 0.0), W=[self.o_all])
        self.gb_mix = self.gbload("gb_mix", I["g_mix"], D)
        self.push()
        ckvT = self.sb("ckvT", [128, 2, NK], BF16)
        KT = self.sb("KT", [96, NK], BF16)
        cqnT = self.sb("cqnT", [128, 3, NTOK], BF16)
        if "1" in ph:
            self.phase1(ckvT, KT, cqnT)
        if "2" in ph:
            self.phase2(ckvT, KT, cqnT)
        self.pop()
        if "3" in ph:
            self.phase3a()
        self.pop()
        self.emit_conv()
        if "4" in ph:
            self.phase3b()
        self.S.barrier()
        self.S.flush(last=True)
        self.st.close()
        return nc

    def emit_conv(self):
        if not self._conv_pending:
            return
        self._conv_pending = False
        I = self.I
        RCH = 1024
        for r in range(0, 16384, RCH):
            for which, nm in ((0, "peer_u"), (1, "peer_v")):
                self.dma(lambda e, r=r, which=which, nm=nm: e.dma_start(out=self.uvtab.t[r:r + RCH, which * D:(which + 1) * D],
                                                                         in_=I[nm][r:r + RCH, :]), W=[self.uvtab], q="conv")

    def phase1(self, ckvT, KT, cqnT):
        I, O = self.I, self.O
        self.push()
        Wkv = self.wload("Wkv", I["w_in"], D, 288, IN_KV)
        Wq = self.wload("Wq", I["w_in"], D, 384, IN_Q)
        gb_kv = self.sb("gb_kv", [128, 288], F32)
        self.load(gb_kv, gb_kv[:, 0:256], I["g_kv_lat"][0:1, 0:256].partition_broadcast(128))
        self.load(gb_kv, gb_kv[:, 256:288], I["g_kr"][0:1, 0:32].partition_broadcast(128))
        gb_ql = self.gbload("gb_ql", I["g_q_lat"], 384)
        ropek = self.sb("ropek", [128, 33, 32], F32)
        self.load(ropek, ropek[:, :, :], I["ropek"])
        kvo_r = self.ring("kvo", [128, 288], F32, 2)
        krn_r = self.ring("krn", [128, 64], F32, 2)
        kvb_r = self.ring("kvb", [128, 384], BF16, 2)
        for t in kvb_r.tiles:
            self.op("pool", lambda e, t=t: e.memset(t[:, :], 0.0), W=[t])
        cqb = self.sb("cqb", [128, 384], BF16)
        pT = self.psT

        def finish_kv(kvo, kind, idx, pT):
            kvb = kvb_r.next()
            self.op("act", lambda e: e.copy(out=kvb[:, 0:256], in_=kvo[:, 0:256]), R=[kvo], W=[kvb])
            self.op("act", lambda e: e.copy(out=kvb[:, 320:352], in_=kvo[:, 256:288]), R=[kvo], W=[kvb])
            for j in range(3):
                self.op("pe", lambda e, j=j: e.transpose(out=pT[:, j * 128:(j + 1) * 128], in_=kvb[:, j * 128:(j + 1) * 128],
                                                       identity=self.ident[:, :]), R=[kvb, self.ident], W=[pT])
            if kind == "snew":
                dst = ckvT[:, :, NKP:NK].rearrange("p c (b k) -> p c b k", b=4)[:, :, :, 1024:1040]
                src = pT[:, 0:256].rearrange("p (c b k) -> p c b k", c=2, b=8)[:, :, 0:4, :]
                self.op("act", lambda e: e.copy(out=dst, in_=src), R=[pT], W=[ckvT])
                dstk = KT[64:96, NKP:NK].rearrange("p (b k) -> p b k", b=4)[:, :, 1024:1040]
                srck = pT[64:96, 256:320].rearrange("p (b k) -> p b k", b=4)
                self.op("act", lambda e: e.copy(out=dstk, in_=srck), R=[pT], W=[KT])
            else:
                c0 = idx * 128 if kind == "p" else NKP + (idx // 8) * 1040 + (idx % 8) * 128
                src = pT[:, 0:256].rearrange("p (c k) -> p c k", c=2)
                if "ckv" not in self.dbg.get("fk_skip", ""):
                    self.op("act", lambda e: e.copy(out=ckvT[:, :, c0:c0 + 128], in_=src), R=[pT], W=[ckvT])
                if "kt" not in self.dbg.get("fk_skip", ""):
                    self.op("act", lambda e: e.copy(out=KT[64:96, c0:c0 + 128], in_=pT[64:96, 256:384]), R=[pT], W=[KT])

        class TV:
            def __init__(self, ap, b):
                self.t = ap
                self.b = b

            def __getitem__(self, k):
                return self.t[k]
        psTs = [self.psT, TV(self.Dp.t[:, :].bitcast(BF16), self.Dp.b)]
        HBs = [self.HB, self.sb("HB2", [128, D], BF16)]
        HTs = [self.sb("HTa", [128, 8, 128], BF16), self.sb("HTb", [128, 8, 128], BF16)]
        cqbs = [cqb, self.sb("cqb2", [128, 384], BF16)]
        glob_psT, glob_HB = self.psT, self.HB

        def use(slot):
            self.psT = psTs[slot]
            self.HB = HBs[slot]

        def kv_tile(i, slot):
            xt = self.X.next()
            src = I["x_all"][i * 128:(i + 1) * 128, :] if i < 32 else I["x_own"][16 * 128:17 * 128, :]
            self.load(xt, xt[:, :], src)
            yield
            use(slot)
            hT = HTs[slot]
            self.rms_tok(xt, xt[:, :], D, self.gb_mix, self.gb_mix[:, :], self.HB, self.HB[:, :])
            yield
            use(slot)
            hb = self.HB
            self.transposes(hb, lambda j: hb[:, j * 128:(j + 1) * 128], 8, hT, hT[:, :, :])
            yield
            zp = self.A[slot]
            zoff = slot * 512
            z = zp.t[:, zoff:zoff + 288]
            self.proj(zp, z, hT, Wkv, 0, 288)
            yield
            kvo = kvo_r.next()
            krn = krn_r.next()
            self.rms_tok(zp, zp.t[:, zoff:zoff + 256], 256, gb_kv, gb_kv[:, 0:256], kvo, kvo[:, 0:256])
            self.rms_tok(zp, zp.t[:, zoff + 256:zoff + 288], 32, gb_kv, gb_kv[:, 256:288], krn, krn[:, 0:32])
            yield
            cs = ropek[:, i, 0:16]
            sn = ropek[:, i, 16:32]
            self.op("dve", lambda e: e.tensor_tensor(out=krn[:, 32:48], in0=krn[:, 0:16], in1=cs, op=ALU.mult), R=[krn, ropek], W=[krn])
            self.op("dve", lambda e: e.tensor_tensor(out=krn[:, 48:64], in0=krn[:, 16:32], in1=sn, op=ALU.mult), R=[krn, ropek], W=[krn])
            self.op("dve", lambda e: e.tensor_tensor(out=kvo[:, 256:272], in0=krn[:, 32:48], in1=krn[:, 48:64], op=ALU.subtract), R=[krn], W=[kvo])
            self.op("dve", lambda e: e.tensor_tensor(out=krn[:, 32:48], in0=krn[:, 0:16], in1=sn, op=ALU.mult), R=[krn, ropek], W=[krn])
            self.op("dve", lambda e: e.tensor_tensor(out=krn[:, 48:64], in0=krn[:, 16:32], in1=cs, op=ALU.mult), R=[krn, ropek], W=[krn])
            self.op("dve", lambda e: e.tensor_tensor(out=kvo[:, 272:288], in0=krn[:, 32:48], in1=krn[:, 48:64], op=ALU.add), R=[krn], W=[kvo])
            if i < 32:
                self.store(O["ckv_p"][i * 128:(i + 1) * 128, :], kvo, kvo[:, 0:256])
                self.store(O["kr_p"][i * 128:(i + 1) * 128, :], kvo, kvo[:, 256:288])
            else:
                self.store(O["ckv_s"][:, :], kvo, kvo[0:64, 0:256])
                self.store(O["kr_s"][:, :], kvo, kvo[0:64, 256:288])
            yield
            use(slot)
            finish_kv(kvo, "p" if i < 32 else "snew", i if i < 32 else 0, self.psT)
            yield

        def cache_tile(idx, slot):
            kvo = kvo_r.next()
            self.load(kvo, kvo[:, 0:256], I["ckv_c"][idx * 128:(idx + 1) * 128, :])
            self.load(kvo, kvo[:, 256:288], I["kr_c"][idx * 128:(idx + 1) * 128, :])
            yield
            use(slot)
            finish_kv(kvo, "scache", idx, self.psT)
            yield

        def q_tile(i, slot):
            xt = self.X.next()
            self.load(xt, xt[:, :], I["x_own"][i * 128:(i + 1) * 128, :])
            yield
            use(slot)
            hT = HTs[slot]
            self.rms_tok(xt, xt[:, :], D, self.gb_mix, self.gb_mix[:, :], self.HB, self.HB[:, :])
            yield
            use(slot)
            hb = self.HB
            self.transposes(hb, lambda j: hb[:, j * 128:(j + 1) * 128], 8, hT, hT[:, :, :])
            yield
            zp = self.A[slot]
            zoff = slot * 512
            self.proj(zp, zp.t[:, zoff:zoff + 384], hT, Wq, 0, 384)
            yield
            cq_ = cqbs[slot]
            self.rms_tok(zp, zp.t[:, zoff:zoff + 384], 384, gb_ql, gb_ql[:, :], cq_, cq_[:, :])
            yield
            use(slot)
            self.transposes(cq_, lambda j: cq_[:, j * 128:(j + 1) * 128], 3, cqnT, cqnT[:, :, i * 128:(i + 1) * 128])
            yield

        def run2(makers):
            pending = list(makers)
            active = {}
            while pending or active:
                for slot in (0, 1):
                    if slot not in active and pending:
                        active[slot] = pending.pop(0)(slot)
                for slot in (0, 1):
                    g = active.get(slot)
                    if g is None:
                        continue
                    try:
                        next(g)
                    except StopIteration:
                        del active[slot]
        mk = [(lambda slot, i=i: kv_tile(i, slot)) for i in self.dbg.get("p1_new", list(range(33)))]
        mk += [(lambda slot, idx=idx: cache_tile(idx, slot)) for idx in range(self.dbg.get("p1_cache", 32))]
        mk += [(lambda slot, i=i: q_tile(i, slot)) for i in range(self.dbg.get("p1_q", NOWN))]
        run2(mk)
        self.psT, self.HB = glob_psT, glob_HB
        self.pop()

    def phase2(self, ckvT, KT, cqnT):
        I, O = self.I, self.O
        self.push()
        wuq = self.wload("wuq", I["w_uq"], 384, 1536)
        wuk = self.wload("wuk", I["w_uk"], 256, 1024)
        wuv = self.wload("wuv", I["w_uv"], 256, 1024)
        onesq = self.sb("onesq", [96, 96], BF16)
        self.load(onesq, onesq[:, :], I["onesq"], q="pool")
        rotm = self.sb("rotm", [96, 96], BF16)
        self.load(rotm, rotm[:, :], I["rotm"], q="pool")
        gq = self.sb("gq", [96, 2], F32)
        self.load(gq, gq[0:64, 0:1], I["g_qn"][0:1, 0:64].rearrange("o d -> d o"))
        self.load(gq, gq[64:96, 0:1], I["g_qr"][0:1, 0:32].rearrange("o d -> d o"))
        self.op("dve", lambda e: e.tensor_scalar(out=gq[:, 1:2], in0=gq[:, 0:1], scalar1=MLA_SCALE, scalar2=None, op0=ALU.mult),
                R=[gq], W=[gq])
        gkn = self.sb("gkn", [64, 1], F32)
        self.load(gkn, gkn[:, :], I["g_kn"][0:1, 0:64].rearrange("o d -> d o"))
        cosq = self.sb("cosq", [96, NTOK], F32)
        sinq = self.sb("sinq", [96, NTOK], F32)
        self.load(cosq, cosq[64:96, :], I["cosq"])
        self.load(sinq, sinq[64:96, :], I["sinq"])
        maskT = self.sb("maskT", [128, 4, 128], BF16)
        self.load(maskT, maskT[:, :, :], I["masks"].rearrange("m k q -> k m q"), q="pool")
        NVT = 32 + 36
        V = self.sb("V", [128, NVT, 72], BF16)
        self.op("pool", lambda e: e.memset(V[:, :, :], 0.0), W=[V])
        self.op("pool", lambda e: e.memset(V[:, :, 64:65], 1.0), W=[V])
        QT = self.sb("QT", [96, NTOK], BF16)
        sq_r = self.ring("sq", [96, 512], BF16, 2)
        rs_r = self.ring("rs", [96, 512], F32, 2)
        tq_r = self.ring("tq", [96, 512], F32, 2)
        t1 = self.sb("t1", [96, 512], F32)
        t2 = self.sb("t2", [96, 512], F32)
        pt_r = self.ring("pt", [128, 4, 128], BF16, 3)
        ptS = [self.sb("ptS", [128, 9, 64], BF16) for _ in range(4)]
        for t in ptS:
            self.op("pool", lambda e, t=t: e.memset(t[:, :, :], 0.0), W=[t])
        A, B, C, Dp = self.A, self.B, self.C, self.Dp
        eps96 = self.epsb

        def fm_norm(ps, ps_ap, rows, n, ones_ap, gcol, gcol_ap, dst, dst_ap):
            sq = sq_r.next()
            rs = rs_r.next()
            self.op("act", lambda e: e.activation(out=sq[0:rows, 0:n], in_=ps_ap, func=AF.Square), R=[ps], W=[sq])
            self.mm(B[0], B[0].t[0:rows, 0:n], onesq, ones_ap, sq, sq[0:rows, 0:n], True, True)
            self.op("act", lambda e: e.activation(out=rs[0:rows, 0:n], in_=B[0].t[0:rows, 0:n], func=AF.Sqrt,
                                                  bias=eps96[0:rows, 0:1], scale=1.0), R=[B[0], eps96], W=[rs])
            tq = tq_r.next()
            self.op("act", lambda e: e.activation(out=tq[0:rows, 0:n], in_=ps_ap, func=AF.Copy, scale=gcol_ap), R=[ps, gcol], W=[tq])
            self.op("dve", lambda e: e.reciprocal(out=rs[0:rows, 0:n], in_=rs[0:rows, 0:n]), R=[rs], W=[rs])
            self.op("pool", lambda e: e.tensor_tensor(out=dst_ap, in0=tq[0:rows, 0:n], in1=rs[0:rows, 0:n], op=ALU.mult), R=[tq, rs], W=[dst])

        kchunks = [(c0, min(512, NK - c0)) for c0 in range(0, NK, 512)]
        qchunks = [(c0, min(512, NTOK - c0)) for c0 in range(0, NTOK, 512)]
        vt = [(i, i * 128, 128) for i in range(32)]
        for bt in range(4):
            for j in range(9):
                vt.append((32 + bt * 9 + j, NKP + bt * 1040 + j * 128, 128 if j < 8 else 16))
        nh = self.dbg.get("nheads", 16)
        pi = 0
        for h in range(nh):
            for (c0, n) in kchunks:
                P = A[pi % 2]; po = (pi % 2) * 512; pi += 1
                for c in range(2):
                    self.mm(P, P.t[0:64, po:po + n], wuk, wuk[:, c, h * 64:(h + 1) * 64], ckvT, ckvT[:, c, c0:c0 + n], c == 0, c == 1)
                fm_norm(P, P.t[0:64, po:po + n], 64, n, onesq[0:64, 0:64], gkn, gkn[:, 0:1], KT, KT[0:64, c0:c0 + n])
            p2c = self.dbg.get("p2_cut", 9)
            if p2c < 2:
                continue
            for g0 in range(0, NVT, 8):
                P = A[pi % 2]; po = (pi % 2) * 512; pi += 1
                grp = vt[g0:g0 + 8]
                for (ti, c0, rows) in grp:
                    jj = ti - g0
                    for c in range(2):
                        self.mm(P, P.t[0:rows, po + jj * 64:po + jj * 64 + 64], ckvT, ckvT[:, c, c0:c0 + rows],
                                wuv, wuv[:, c, h * 64:(h + 1) * 64], c == 0, c == 1)
                ng = len(grp)
                src = P.t[:, po:po + ng * 64].rearrange("p (j d) -> p j d", j=ng)
                self.op("act", lambda e, src=src, g0=g0, ng=ng: e.copy(out=V[:, g0:g0 + ng, 0:64], in_=src), R=[P], W=[V])
            if p2c < 3:
                continue
            for (c0, n) in qchunks:
                P = A[pi % 2]; po = (pi % 2) * 512; pi += 1
                for c in range(3):
                    self.mm(P, P.t[0:96, po:po + n], wuq, wuq[:, c, h * 96:(h + 1) * 96], cqnT, cqnT[:, c, c0:c0 + n], c == 0, c == 2)
                fm_norm(P, P.t[0:96, po:po + n], 96, n, onesq[:, :], gq, gq[:, 1:2], QT, QT[0:96, c0:c0 + n])
                self.mm(B[1], B[1].t[0:96, 512:512 + n], rotm, rotm[:, :], QT, QT[0:96, c0:c0 + n], True, True)
                self.op("dve", lambda e, c0=c0, n=n: e.tensor_tensor(out=t1[64:96, 0:n], in0=QT[64:96, c0:c0 + n], in1=cosq[64:96, c0:c0 + n], op=ALU.mult),
                        R=[QT, cosq], W=[t1])
                self.op("dve", lambda e, c0=c0, n=n: e.tensor_tensor(out=t2[64:96, 0:n], in0=B[1].t[64:96, 512:512 + n], in1=sinq[64:96, c0:c0 + n], op=ALU.mult),
                        R=[B[1], sinq], W=[t2])
                self.op("dve", lambda e, c0=c0, n=n: e.tensor_tensor(out=QT[64:96, c0:c0 + n], in0=t1[64:96, 0:n], in1=t2[64:96, 0:n], op=ALU.add),
                        R=[t1, t2], W=[QT])
            if p2c < 4:
                continue
            si = 0
            groups = []
            for s in range(16):
                nkb = 4 * (s // 2) + (2 if s % 2 == 0 else 4)
                for g0 in range(0, nkb, 4):
                    groups.append((s, nkb, list(range(g0, min(g0 + 4, nkb)))))
            Obank = [(Dp, 0), (B[1], 512)]

            def qk(gi):
                s, nkb, blks = groups[gi]
                Sp = C[gi % 2]; so = (gi % 2) * 512
                for j, kb in enumerate(blks):
                    self.mm(Sp, Sp.t[:, so + j * 128:so + (j + 1) * 128], KT, KT[0:96, kb * 128:(kb + 1) * 128],
                            QT, QT[0:96, s * 128:(s + 1) * 128], True, True)
            qk(0)
            for gi, (s, nkb, blks) in enumerate(groups):
                if gi + 1 < len(groups):
                    qk(gi + 1)
                Sp = C[gi % 2]; so = (gi % 2) * 512
                Ops, oo = Obank[s % 2]
                pt = pt_r.next()
                nb = len(blks)
                self.op("act", lambda e, Sp=Sp, so=so, nb=nb, pt=pt: e.activation(
                    out=pt[:, 0:nb, :], in_=Sp.t[:, so:so + nb * 128].rearrange("p (j q) -> p j q", j=nb), func=AF.Exp),
                    R=[Sp], W=[pt])
                for j, kb in enumerate(blks):
                    if kb >= nkb - 2:
                        mi = (0 if s % 2 == 0 else 2) + (kb - (nkb - 2))
                        self.op("dve", lambda e, pt=pt, j=j, mi=mi: e.tensor_tensor(out=pt[:, j, :], in0=pt[:, j, :], in1=maskT[:, mi, :], op=ALU.mult),
                                R=[pt, maskT], W=[pt])
                for j, kb in enumerate(blks):
                    self.mm(Ops, Ops.t[:, oo:oo + 72], pt, pt[:, j, :], V, V[:, kb, :], kb == 0, kb == nkb - 1)
                if blks[-1] == nkb - 1:
                    sm = self.small.next()
                    self.op("dve", lambda e, sm=sm, Ops=Ops, oo=oo: e.reciprocal(out=sm[:, 0:1], in_=Ops.t[:, oo + 64:oo + 65]), R=[Ops], W=[sm])
                    self.op("dve", lambda e, sm=sm, Ops=Ops, oo=oo, s=s, h=h: e.tensor_scalar(out=self.o_all[:, s, h * 64:(h + 1) * 64], in0=Ops.t[:, oo:oo + 64],
                                                                                   scalar1=sm[:, 0:1], scalar2=None, op0=ALU.mult),
                            R=[Ops, sm], W=[self.o_all])
            si = len(groups)
            if p2c < 5:
                continue
            Ops = Dp
            for bt in range(4):
                base = NKP + bt * 1040
                Sp = C[si % 2]; so = (si % 2) * 512; si += 1
                qc = 2048 + bt * 16
                for j in range(8):
                    self.mm(Sp, Sp.t[:, so + j * 16:so + (j + 1) * 16], KT, KT[0:96, base + j * 128:base + (j + 1) * 128],
                            QT, QT[0:96, qc:qc + 16], True, True)
                self.mm(Sp, Sp.t[0:16, so + 128:so + 144], KT, KT[0:96, base + 1024:base + 1040], QT, QT[0:96, qc:qc + 16], True, True)
                p = ptS[bt]
                self.op("act", lambda e, Sp=Sp, so=so, p=p, bt=bt: e.activation(
                    out=p[:, 0:8, bt * 16:(bt + 1) * 16], in_=Sp.t[:, so:so + 128].rearrange("p (j q) -> p j q", j=8), func=AF.Exp),
                    R=[Sp], W=[p])
                self.op("act", lambda e, Sp=Sp, so=so, p=p, bt=bt: e.activation(
                    out=p[0:16, 8, bt * 16:(bt + 1) * 16], in_=Sp.t[0:16, so + 128:so + 144], func=AF.Exp), R=[Sp], W=[p])
                for j in range(9):
                    rows = 128 if j < 8 else 16
                    self.mm(Ops, Ops.t[0:64, 0:72], p, p[0:rows, j, :], V, V[0:rows, 32 + bt * 9 + j, :],
                            bt == 0 and j == 0, bt == 3 and j == 8)
            sm = self.small.next()
            self.op("dve", lambda e, sm=sm, Ops=Ops: e.reciprocal(out=sm[0:64, 0:1], in_=Ops.t[0:64, 64:65]), R=[Ops], W=[sm])
            self.op("dve", lambda e, sm=sm, Ops=Ops, h=h: e.tensor_scalar(out=self.o_all[0:64, 16, h * 64:(h + 1) * 64], in0=Ops.t[0:64, 0:64],
                                                                     scalar1=sm[0:64, 0:1], scalar2=None, op0=ALU.mult),
                    R=[Ops, sm], W=[self.o_all])
        self.pop()

    def phase3a(self):
        I, O = self.I, self.O
        self.push()
        Wz = self.wload("Wz", I["w_in"], D, 4096, IN_Z)
        woa = self.wload("woa", I["w_oa"], D, D)
        wob = self.wload("wob", I["w_ob"], D, D)
        gb_gm = self.gbload("gb_gm", I["g_gm"], D)
        self.emit_conv()
        trilm = self.sb("trilm", [128, 128], F32)
        trilms = self.sb("trilms", [128, 128], F32)
        self.load(trilm, trilm[:, :], I["trilm"])
        self.load(trilms, trilms[:, :], I["trilms"])
        wsf = self.sb("wsf", [128, 8, 128], F32)
        wsb = self.sb("wsb", [128, 8, 128], BF16)
        WmT = [self.sb("WmT", [128, 8, 128], BF16) for _ in range(2)]
        bT = [self.sb("bT", [128, 8], F32) for _ in range(2)]
        ws_tgs = I["w_s"].rearrange("(g t) s -> t g s", g=8)
        for k in range(2):
            if k == 0:
                self.load(wsf, wsf[:, :, :], ws_tgs)
                self.load(bT[0], bT[0][:, :], I["b_s"].rearrange("g t -> t g"), slow=True)
                tm = trilm
            else:
                self.op("pool", lambda e: e.memset(wsf[:, :, :], 0.0), W=[wsf])
                self.op("pool", lambda e: e.memset(bT[1][:, :], 0.0), W=[bT[1]])
                for bt in range(4):
                    self.load(wsf, wsf[bt * 16:(bt + 1) * 16, :, bt * 16:(bt + 1) * 16], ws_tgs[0:16, :, 0:16])
                    self.load(bT[1], bT[1][bt * 16:(bt + 1) * 16, :], I["b_s"][:, 0:16].rearrange("g t -> t g"), slow=True)
                tm = trilms
            self.op("dve", lambda e, tm=tm: e.tensor_tensor(out=wsb[:, :, :], in0=wsf[:, :, :],
                                                           in1=tm[:, :].unsqueeze(1).to_broadcast([128, 8, 128]), op=ALU.mult),
                    R=[wsf, tm], W=[wsb])
            self.transposes(wsb, lambda j: wsb[:, j, :], 8, WmT[k], WmT[k][:, :, :])
        u = self.sb("u", [128, D], F32)
        gv = self.sb("gv", [128, D], F32)
        vg = self.sb("vg", [128, D], F32)
        vgb = self.sb("vgb", [128, D], BF16)
        sig = self.sb("sig", [128, D], F32)
        um = self.sb("um", [128, D], BF16)
        umT = self.sb("umT", [128, 8, 128], BF16)
        oT = self.sb("oT", [128, 8, 128], BF16)
        tt = self.sb("tt", [128, D], F32)
        mo_r = self.ring("mo", [128, D], BF16, 2)
        A, B, C = self.A, self.B, self.C
        zi = 0
        for i in range(NOWN):
            k = 0 if i < 16 else 1
            xt = self.X.next()
            self.load(xt, xt[:, :], I["x_own"][i * 128:(i + 1) * 128, :])
            hT = self.HT.next()
            self.normT(xt, self.gb_mix, self.HB, hT)

            def zchunk(col, func, dst, dst_ap):
                nonlocal zi
                P = A[zi % 2]; po = (zi % 2) * 512; zi += 1
                self.proj(P, P.t[:, po:po + 512], hT, Wz, col, 512)
                self.op("act", lambda e: e.activation(out=dst_ap, in_=P.t[:, po:po + 512], func=func), R=[P], W=[dst])
            for n in range(2):
                zchunk(n * 512, AF.Gelu, u, u[:, n * 512:(n + 1) * 512])
            for n in range(2):
                zchunk(1024 + n * 512, AF.Gelu, gv, gv[:, n * 512:(n + 1) * 512])
            self.rms_tok(gv, gv[:, :], D, gb_gm, gb_gm[:, :], vg, vg[:, :])
            if i == 16:
                self.store(O["vg_s"][:, :], vg, vg[0:64, :])
            self.op("act", lambda e: e.copy(out=vgb[:, :], in_=vg[:, :]), R=[vg], W=[vgb])
            for g in range(8):
                Pm = B[g // 4]
                self.mm(Pm, Pm.t[:, g * 128:(g + 1) * 128], WmT[k], WmT[k][:, g, :], vgb, vgb[:, g * 128:(g + 1) * 128], True, True)
            for g in range(8):
                Pm = B[g // 4]
                self.op("dve", lambda e, g=g, Pm=Pm, k=k: e.scalar_tensor_tensor(
                    out=um[:, g * 128:(g + 1) * 128], in0=Pm.t[:, g * 128:(g + 1) * 128], scalar=bT[k][:, g:g + 1],
                    in1=u[:, g * 128:(g + 1) * 128], op0=ALU.add, op1=ALU.mult), R=[Pm, bT[k], u], W=[um])
            self.transposes(self.o_all, lambda j, i=i: self.o_all[:, i, j * 128:(j + 1) * 128], 8, oT, oT[:, :, :])
            for n in range(2):
                self.proj(C[n], C[n].t[:, n * 512:(n + 1) * 512], oT, woa, n * 512, 512)
            for n in range(2):
                zchunk(2048 + n * 512, AF.Sigmoid, sig, sig[:, n * 512:(n + 1) * 512])
            for n in range(2):
                self.op("dve", lambda e, n=n: e.tensor_tensor(out=tt[:, n * 512:(n + 1) * 512], in0=sig[:, n * 512:(n + 1) * 512],
                                                           in1=C[n].t[:, n * 512:(n + 1) * 512], op=ALU.mult), R=[sig, C[n]], W=[tt])
            self.transposes(um, lambda j: um[:, j * 128:(j + 1) * 128], 8, umT, umT[:, :, :])
            for n in range(2):
                self.proj(C[n], C[n].t[:, n * 512:(n + 1) * 512], umT, wob, n * 512, 512)
            for n in range(2):
                zchunk(3072 + n * 512, AF.Sigmoid, sig, sig[:, n * 512:(n + 1) * 512])
            for n in range(2):
                self.op("dve", lambda e, n=n: e.tensor_tensor(out=sig[:, n * 512:(n + 1) * 512], in0=sig[:, n * 512:(n + 1) * 512],
                                                           in1=C[n].t[:, n * 512:(n + 1) * 512], op=ALU.mult), R=[sig, C[n]], W=[sig])
            mo = mo_r.next()
            self.op("dve", lambda e, mo=mo: e.tensor_tensor(out=mo[:, :], in0=sig[:, :], in1=tt[:, :], op=ALU.add),
                    R=[sig, tt], W=[mo])
            self.dma(lambda e, mo=mo, i=i: e.dma_start(out=self.mscr.t[i * 128:(i + 1) * 128, :], in_=mo[:, :]), R=[mo], W=[self.mscr])
        self.pop()

    def phase3b(self):
        I, O = self.I, self.O
        A, B, C, Dp, pT = self.A, self.B, self.C, self.Dp, self.psT
        self.push()
        wo = self.wload("wo", I["w_o"], D, D)
        wcq = self.wload("wcq", I["w_cq"], D, 512)
        wco = self.wload("wco", I["w_co"], 512, D)
        wpq = self.wload("wpq", I["w_pq"], D, 2048)
        gb_xa = self.gbload("gb_xa", I["g_xattn"], D)
        gb_ff = self.gbload("gb_ff", I["g_ffn"], D)
        ones128 = self.sb("ones128", [128, 128], BF16)
        self.load(ones128, ones128[:, :], I["ones128"], q="pool")
        cmask = self.sb("cmask", [128, 255], BF16)
        self.load(cmask, cmask[:, :], I["cmask"], q="pool")
        gcq = self.sb("gcq", [128, 2], F32)
        self.load(gcq, gcq[:, 0:1], I["g_cq"][0:1, 0:128].rearrange("o d -> d o"))
        self.op("dve", lambda e: e.tensor_scalar(out=gcq[:, 1:2], in0=gcq[:, 0:1], scalar1=MEM_SCALE, scalar2=None, op0=ALU.mult),
                R=[gcq], W=[gcq])
        memKT = [self.sb("memKT", [128, 4, 256], BF16) for _ in range(4)]
        memV = [self.sb("memV", [128, 2, 4, 136], BF16) for _ in range(4)]
        for t in memV:
            self.op("pool", lambda e, t=t: e.memset(t[:, :, :, :], 0.0), W=[t])
            self.op("pool", lambda e, t=t: e.memset(t[:, :, :, 128:129], 1.0), W=[t])
        kf = self.sb("kf", [128, 512], F32)
        kb16 = self.sb("kb16", [128, 512], BF16)
        skT = self.sb("skT", [128, 16, 128], BF16)

        self.push()
        wck = self.wload("wck", I["w_ck"], D, 512)
        wcv = self.wload("wcv", I["w_cv"], D, 512)
        gb_mem = self.gbload("gb_mem", I["g_mem"], D)
        gb_ck = self.gbload("gb_ck", I["g_ck"], 128)
        sq4 = self.sb("sq4", [128, 512], F32)
        skb = self.sb("skb", [128, 16, 128], BF16)
        self.load(skb, skb[:, :, :], I["sub_keys"].rearrange("(a n) d -> n a d", a=16), q="pool")
        for half in range(2):
            self.transposes(skb, lambda j, half=half: skb[:, half * 8 + j, :], 8, skT, skT[:, half * 8:(half + 1) * 8, :])
        for mt in range(2):
            xt = self.X.next()
            self.load(xt, xt[:, :], I["mem_p"][mt * 128:(mt + 1) * 128, :])
            hT = self.HT.next()
            self.normT(xt, gb_mem, self.HB, hT)
            P = A[0]
            self.proj(P, P.t[:, 0:512], hT, wck, 0, 512)
            sm = self.small.next()
            self.op("act", lambda e: e.activation(out=sq4[:, :], in_=P.t[:, 0:512], func=AF.Square), R=[P], W=[sq4])
            self.op("dve", lambda e, sm=sm: e.tensor_reduce(out=sm[:, 0:4], in_=sq4[:, :].rearrange("p (h d) -> p h d", h=4),
                                                          axis=AX.X, op=ALU.add), R=[sq4], W=[sm])
            self.op("act", lambda e, sm=sm: e.activation(out=sm[:, 4:8], in_=sm[:, 0:4], func=AF.Sqrt, bias=self.epsb[:, 0:1],
                                                       scale=1.0 / 128), R=[sm, self.epsb], W=[sm])
            self.op("dve", lambda e, sm=sm: e.reciprocal(out=sm[:, 4:8], in_=sm[:, 4:8]), R=[sm], W=[sm])
            self.op("dve", lambda e, sm=sm: e.tensor_tensor(out=kf[:, :].rearrange("p (h d) -> p h d", h=4),
                                                          in0=P.t[:, 0:512].rearrange("p (h d) -> p h d", h=4),
                                                          in1=sm[:, 4:8].unsqueeze(2).to_broadcast([128, 4, 128]), op=ALU.mult),
                    R=[P, sm], W=[kf])
            self.op("dve", lambda e: e.tensor_tensor(out=kf[:, :].rearrange("p (h d) -> p h d", h=4),
                                                   in0=kf[:, :].rearrange("p (h d) -> p h d", h=4),
                                                   in1=gb_ck[:, :].unsqueeze(1).to_broadcast([128, 4, 128]), op=ALU.mult),
                    R=[kf, gb_ck], W=[kf])
            self.store(O["mk_p"][mt * 128:(mt + 1) * 128, :], kf, kf[:, :])
            self.op("act", lambda e: e.copy(out=kb16[:, :], in_=kf[:, :]), R=[kf], W=[kb16])
            self.transposes(kb16, lambda j: kb16[:, j * 128:(j + 1) * 128], 4, memKT[0], memKT[0][:, :, mt * 128:(mt + 1) * 128])
            P2 = A[1]
            self.proj(P2, P2.t[:, 512:1024], hT, wcv, 0, 512)
            self.op("act", lambda e: e.copy(out=sq4[:, :], in_=P2.t[:, 512:1024]), R=[P2], W=[sq4])
            self.store(O["mv_p"][mt * 128:(mt + 1) * 128, :], sq4, sq4[:, :])
            self.op("dve", lambda e, mt=mt: e.tensor_copy(out=memV[0][:, mt, :, 0:128], in_=sq4[:, :].rearrange("p (h d) -> p h d", h=4)),
                    R=[sq4], W=[memV[0]])
        self.pop()

        def load_sample_mem():
            for bt in range(4):
                for mt in range(2):
                    r0 = bt * 256 + mt * 128
                    self.load(kf, kf[:, :], I["memk_c"][r0:r0 + 128, :])
                    self.op("act", lambda e: e.copy(out=kb16[:, :], in_=kf[:, :]), R=[kf], W=[kb16])
                    self.transposes(kb16, lambda j: kb16[:, j * 128:(j + 1) * 128], 4, memKT[bt], memKT[bt][:, :, mt * 128:(mt + 1) * 128])
                    xt = self.X.next()
                    self.load(xt, xt[:, 0:512], I["memv_c"][r0:r0 + 128, :])
                    self.op("dve", lambda e, xt=xt, bt=bt, mt=mt: e.tensor_copy(out=memV[bt][:, mt, :, 0:128],
                                                                           in_=xt[:, 0:512].rearrange("p (h d) -> p h d", h=4)),
                            R=[xt], W=[memV[bt]])

        TS = [(self.sb("x2", [128, D], F32), self.sb("h3b", [128, D], BF16), self.sb("idxT", [128, 128], I32), self.sb("gT", [128, 128], F32))
              for _ in range(2)]
        mi_r = self.ring("mi", [128, D], BF16, 1)
        mT = self.sb("mT", [128, 8, 128], BF16)
        sqc = self.sb("sqc", [128, 512], BF16)
        rsc = self.sb("rsc", [128, 512], F32)
        qcn = self.sb("qcn", [128, 4, 128], BF16)
        ptc = self.sb("ptc", [128, 8, 128], BF16)
        ptcS = [self.sb("ptcS", [128, 8, 128], BF16) for _ in range(4)]
        for t in ptcS:
            self.op("pool", lambda e, t=t: e.memset(t[:, :, :], 0.0), W=[t])
        oc = self.sb("oc", [128, 4, 128], BF16)
        ocT = self.sb("ocT", [128, 4, 128], BF16)
        h3T = self.sb("h3T", [128, 8, 128], BF16)
        qTb = self.sb("qTb", [128, 16, 128], BF16)
        qtok = kb16
        sc = self.sb("sc", [128, 2048], F32)
        wks = [self.sb("wk", [128, 256], F32) for _ in range(2)]
        sv = self.sb("sv", [128, 16, 16], F32)
        si = self.sb("si", [128, 16, 16], U32)
        sif = self.sb("sif", [128, 16, 16], F32)
        ts = self.sb("ts", [128, 8, 16], F32)
        sel = self.sb("sel", [128, 8, 16], U32)
        self_f = self.sb("self_f", [128, 8, 16], F32)
        k1i = self.sb("k1i", [128, 8, 16], I32)
        k1f = self.sb("k1f", [128, 8, 16], F32)
        k2f = self.sb("k2f", [128, 8, 16], F32)
        iota16 = self.sb("iota16", [128, 16], F32)
        for k in range(16):
            self.op("pool", lambda e, k=k: e.memset(iota16[:, k:k + 1], float(k)), W=[iota16])
        eq = self.sb("eq", [128, 4, 16, 16], BF16)
        i12 = self.sb("i12", [128, 2, 128], F32)
        tb = self.sb("tb", [128, 3, 128], BF16)
        ex = self.sb("ex", [128, 8, 16], F32)
        i1T = self.sb("i1T", [128, 128], F32)
        i2T = self.sb("i2T", [128, 128], F32)
        NB = self.dbg.get("NB", 8)
        GUV = self.ring("GUV", [128, 2 * D], BF16, NB)
        wsel_r = self.ring("wsel", [128, 128], BF16, 4)
        djunk_r = self.ring("djunk", [128, D], BF16, 1)
        act_r = self.ring("actc", [128, 1], F32, 6)
        gl_r = self.ring("glc", [128, 2], F32, 6)

        svb = [T(sv.t, Buf("svb")) for _ in range(16)]
        sib = [T(si.t, Buf("sib")) for _ in range(16)]
        tsb = [T(ts.t, Buf("tsb")) for _ in range(8)]
        selb = [T(sel.t, Buf("selb")) for _ in range(8)]

        def pre(i):
            x2, h3b, idxT, gT = TS[i % 2]
            x1 = x2
            if i == 16:
                load_sample_mem()
            xt = self.X.next()
            yield self.load(xt, xt[:, :], I["x_own"][i * 128:(i + 1) * 128, :])
            mi_ = mi_r.next()
            yield self.dma(lambda e, mi_=mi_, i=i: e.dma_start(out=mi_[:, :], in_=self.mscr.t[i * 128:(i + 1) * 128, :]), R=[self.mscr], W=[mi_])
            yield from (None for _ in range(2))
            yield self.transposes(mi_, lambda j, mi_=mi_: mi_[:, j * 128:(j + 1) * 128], 8, mT, mT[:, :, :])
            yield from (None for _ in range(2))
            for n in range(2):
                yield self.proj(Dp, Dp.t[:, 0:512], mT, wo, n * 512, 512)
                yield self.op("dve", lambda e, n=n, xt=xt: e.tensor_tensor(out=x1[:, n * 512:(n + 1) * 512], in0=xt[:, n * 512:(n + 1) * 512],
                                                                 in1=Dp.t[:, 0:512], op=ALU.add), R=[xt, Dp], W=[x1])
            hT = self.HT.next()
            yield self.rms_tok(x1, x1[:, :], D, gb_xa, gb_xa[:, :], self.HB, self.HB[:, :])
            yield from (None for _ in range(3))
            yield self.transposes(self.HB, lambda j: self.HB[:, j * 128:(j + 1) * 128], 8, hT, hT[:, :, :])
            yield from (None for _ in range(2))
            P = Dp
            for hd in range(4):
                for c in range(8):
                    yield self.mm(P, P.t[:, hd * 128:(hd + 1) * 128], wcq, wcq[:, c, hd * 128:(hd + 1) * 128], hT, hT[:, c, :], c == 0, c == 7)
            yield self.op("act", lambda e: e.activation(out=sqc[:, :], in_=P.t[:, 0:512], func=AF.Square), R=[P], W=[sqc])
            yield self.op("act", lambda e: e.copy(out=kf[:, :], in_=P.t[:, 0:512]), R=[P], W=[kf])
            yield from (None for _ in range(2))
            yield self.mm(P, P.t[:, 0:512], ones128, ones128[:, :], sqc, sqc[:, :], True, True)
            yield self.op("act", lambda e: e.activation(out=rsc[:, :], in_=P.t[:, 0:512], func=AF.Sqrt, bias=self.epsb[:, 0:1], scale=1.0),
                    R=[P, self.epsb], W=[rsc])
            yield self.op("dve", lambda e: e.reciprocal(out=rsc[:, :], in_=rsc[:, :]), R=[rsc], W=[rsc])
            yield self.op("dve", lambda e: e.scalar_tensor_tensor(out=qcn[:, :, :].rearrange("p h t -> p (h t)"), in0=kf[:, :], scalar=gcq[:, 1:2],
                                                            in1=rsc[:, :], op0=ALU.mult, op1=ALU.mult), R=[kf, gcq, rsc], W=[qcn])
            yield from (None for _ in range(3))
            if i < 16:
                batches = [(0, 0, 128, ptc)]
            else:
                batches = [(bt, bt * 16, 16, ptcS[bt]) for bt in range(4)]
            for (mb, c0, n, p) in batches:
                for pair in range(2):
                    for hd in (2 * pair, 2 * pair + 1):
                        for mt in range(2):
                            jj = (hd % 2) * 2 + mt
                            yield self.mm(P, P.t[:, jj * 128:jj * 128 + n], memKT[mb], memKT[mb][:, hd, mt * 128:(mt + 1) * 128],
                                          qcn, qcn[:, hd, c0:c0 + n], True, True)
                    yield self.op("act", lambda e, pair=pair, p=p, c0=c0, n=n: e.activation(
                        out=p[:, pair * 4:(pair + 1) * 4, c0:c0 + n],
                        in_=P.t[:, 0:512].rearrange("p (j q) -> p j q", j=4)[:, :, 0:n], func=AF.Exp),
                        R=[P], W=[p])
            yield from (None for _ in range(2))
            nq = 128
            for pair in range(2):
                for hd in (2 * pair, 2 * pair + 1):
                    col = (hd % 2) * 256
                    nacc = len(batches) * 2
                    k = 0
                    for (mb, c0, n, p) in batches:
                        for mt in range(2):
                            yield self.mm(P, P.t[0:nq, col:col + 136], p, p[:, hd * 2 + mt, 0:nq], memV[mb], memV[mb][:, mt, hd, :], k == 0, k == nacc - 1)
                            k += 1
                sm = self.small.next()
                ocv = P.t[:, 0:512].rearrange("p (h c) -> p h c", h=2)
                yield self.op("dve", lambda e, sm=sm, ocv=ocv: e.tensor_scalar(out=sm[:, 0:2], in0=ocv[:, :, 128:129].rearrange("p h o -> p (h o)"),
                                                                     scalar1=1e-30, scalar2=None, op0=ALU.max), R=[P], W=[sm])
                yield self.op("dve", lambda e, sm=sm: e.reciprocal(out=sm[:, 0:2], in_=sm[:, 0:2]), R=[sm], W=[sm])
                yield self.op("dve", lambda e, sm=sm, ocv=ocv, pair=pair: e.tensor_tensor(out=oc[:, 2 * pair:2 * pair + 2, :], in0=ocv[:, :, 0:128],
                                                                                in1=sm[:, 0:2].unsqueeze(2).to_broadcast([128, 2, 128]), op=ALU.mult),
                        R=[P, sm], W=[oc])
            yield from (None for _ in range(2))
            yield self.transposes(oc, lambda j: oc[:, j, :], 4, ocT, ocT[:, :, :])
            yield from (None for _ in range(2))
            for n in range(2):
                yield self.proj(P, P.t[:, 0:512], ocT, wco, n * 512, 512, kc=4)
                yield self.op("dve", lambda e, n=n: e.tensor_tensor(out=x2[:, n * 512:(n + 1) * 512], in0=x1[:, n * 512:(n + 1) * 512],
                                                           in1=P.t[:, 0:512], op=ALU.add), R=[x1, P], W=[x2])
            yield self.rms_tok(x2, x2[:, :], D, gb_ff, gb_ff[:, :], h3b, h3b[:, :])
            yield from (None for _ in range(3))
            yield self.transposes(h3b, lambda j: h3b[:, j * 128:(j + 1) * 128], 8, h3T, h3T[:, :, :])
            yield from (None for _ in range(2))
            for r in range(4):
                for c in range(8):
                    yield self.mm(P, P.t[:, 0:512], h3T, h3T[:, c, :], wpq, wpq[:, c, r * 512:(r + 1) * 512], c == 0, c == 7)
                yield self.op("act", lambda e: e.copy(out=qtok[:, :], in_=P.t[:, 0:512]), R=[P], W=[qtok])
                yield from (None for _ in range(2))
                yield self.transposes(qtok, lambda j: qtok[:, j * 128:(j + 1) * 128], 4, qTb, qTb[:, r * 4:r * 4 + 4, :])
                yield from (None for _ in range(1))
            for r in range(4):
                for hp in range(r * 4, r * 4 + 4):
                    off = (hp % 4) * 128
                    yield self.mm(P, P.t[:, off:off + 128], qTb, qTb[:, hp, :], skT, skT[:, hp, :], True, True)
                yield self.op("act", lambda e, r=r: e.copy(out=sc[:, r * 512:(r + 1) * 512], in_=P.t[:, 0:512]), R=[P], W=[sc])

            def top16_pair(items):
                for stage in range(5):
                    for (src_ap, width, vout, iout, wkb, vb, ib) in items:
                        if stage == 0:
                            self.op("dve", lambda e, vout=vout, src_ap=src_ap: e.max(out=vout[:, 0:8], in_=src_ap), R=[sc], W=[vb])
                        elif stage == 1:
                            self.op("dve", lambda e, vout=vout, iout=iout, src_ap=src_ap: e.max_index(out=iout[:, 0:8], in_max=vout[:, 0:8], in_values=src_ap),
                                    R=[sc, vb], W=[ib])
                        elif stage == 2:
                            self.op("dve", lambda e, vout=vout, src_ap=src_ap, wkb=wkb, width=width: e.match_replace(
                                out=wkb[:, 0:width], in_to_replace=vout[:, 0:8], in_values=src_ap, imm_value=-1e30), R=[sc, vb], W=[wkb])
                        elif stage == 3:
                            self.op("dve", lambda e, vout=vout, wkb=wkb, width=width: e.max(out=vout[:, 8:16], in_=wkb[:, 0:width]), R=[wkb], W=[vb])
                        else:
                            self.op("dve", lambda e, vout=vout, iout=iout, wkb=wkb, width=width: e.max_index(
                                out=iout[:, 8:16], in_max=vout[:, 8:16], in_values=wkb[:, 0:width]), R=[wkb, vb], W=[ib])
                    yield None
            for hp in range(0, 16, 2):
                yield from top16_pair([(sc[:, q * 128:(q + 1) * 128], 128, sv[:, q, :], si[:, q, :], wks[q % 2], svb[q], sib[q]) for q in (hp, hp + 1)])
            yield self.op("dve", lambda e: e.tensor_copy(out=sif[:, :, :], in_=si[:, :, :]), R=sib, W=[sif])
            sv4 = sv[:, :, :].rearrange("p (h two) k -> p h two k", two=2)
            cand = sc[:, :].rearrange("p (h a b) -> p h a b", h=8, a=16)
            yield self.op("dve", lambda e: e.tensor_tensor(out=cand, in0=sv4[:, :, 0, :].unsqueeze(3).to_broadcast([128, 8, 16, 16]),
                                                   in1=sv4[:, :, 1, :].unsqueeze(2).to_broadcast([128, 8, 16, 16]), op=ALU.add),
                    R=svb, W=[sc])
            for hh in range(0, 8, 2):
                yield from top16_pair([(sc[:, q * 256:(q + 1) * 256], 256, ts[:, q, :], sel[:, q, :], wks[q % 2], tsb[q], selb[q]) for q in (hh, hh + 1)])
            yield self.op("dve", lambda e: e.tensor_copy(out=self_f[:, :, :], in_=sel[:, :, :]), R=selb, W=[self_f])
            yield self.op("dve", lambda e: e.tensor_scalar(out=k1i[:, :, :], in0=self_f[:, :, :], scalar1=-7.5, scalar2=0.0625, op0=ALU.add, op1=ALU.mult),
                    R=[self_f], W=[k1i])
            yield self.op("dve", lambda e: e.tensor_copy(out=k1f[:, :, :], in_=k1i[:, :, :]), R=[k1i], W=[k1f])
            yield self.op("dve", lambda e: e.scalar_tensor_tensor(out=k2f[:, :, :], in0=k1f[:, :, :], scalar=-16.0, in1=self_f[:, :, :],
                                                            op0=ALU.mult, op1=ALU.add), R=[k1f, self_f], W=[k2f])
            sif4 = sif[:, :, :].rearrange("p (h two) k -> p h two k", two=2)
            io4 = iota16[:, :].unsqueeze(1).unsqueeze(1).to_broadcast([128, 4, 16, 16])
            for which, kf_ in ((0, k1f), (1, k2f)):
                for h2 in range(2):
                    hs = slice(h2 * 4, h2 * 4 + 4)
                    yield self.op("dve", lambda e, kf_=kf_, hs=hs: e.tensor_tensor(out=eq[:, :, :, :], in0=io4,
                                                                          in1=kf_[:, hs, :].unsqueeze(3).to_broadcast([128, 4, 16, 16]), op=ALU.is_equal),
                            R=[iota16, kf_], W=[eq])
                    yield self.op("pool", lambda e, which=which, hs=hs: e.tensor_tensor(out=eq[:, :, :, :], in0=eq[:, :, :, :],
                                                                              in1=sif4[:, hs, which, :].unsqueeze(2).to_broadcast([128, 4, 16, 16]), op=ALU.mult),
                            R=[eq, sif], W=[eq])
                    yield self.op("dve", lambda e, which=which, h2=h2: e.tensor_reduce(out=i12[:, which, h2 * 64:(h2 + 1) * 64].rearrange("p (h k) -> p h k", h=4),
                                                                              in_=eq[:, :, :, :], axis=AX.X, op=ALU.add), R=[eq], W=[i12])
            yield self.op("act", lambda e: e.copy(out=tb[:, 0:2, :], in_=i12[:, :, :]), R=[i12], W=[tb])
            yield self.op("dve", lambda e: e.tensor_tensor(out=ex[:, :, :], in0=ts[:, :, :], in1=ts[:, :, 0:1].to_broadcast([128, 8, 16]), op=ALU.subtract),
                    R=tsb, W=[ex])
            yield self.op("act", lambda e: e.activation(out=ex[:, :, :], in_=ex[:, :, :], func=AF.Exp), R=[ex], W=[ex])
            sm = self.small.next()
            yield self.op("dve", lambda e, sm=sm: e.tensor_reduce(out=sm[:, 0:8], in_=ex[:, :, :], axis=AX.X, op=ALU.add), R=[ex], W=[sm])
            yield self.op("dve", lambda e, sm=sm: e.reciprocal(out=sm[:, 0:8], in_=sm[:, 0:8]), R=[sm], W=[sm])
            yield self.op("dve", lambda e, sm=sm: e.tensor_tensor(out=tb[:, 2, :].rearrange("p (h k) -> p h k", h=8), in0=ex[:, :, :],
                                                          in1=sm[:, 0:8].unsqueeze(2).to_broadcast([128, 8, 16]), op=ALU.mult),
                    R=[ex, sm], W=[tb])
            yield from (None for _ in range(3))
            for j in range(3):
                yield self.op("pe", lambda e, j=j: e.transpose(out=pT[:, j * 128:(j + 1) * 128], in_=tb[:, j, :], identity=self.ident[:, :]),
                        R=[tb, self.ident], W=[pT])
            yield self.op("act", lambda e: e.copy(out=i1T[:, :], in_=pT[:, 0:128]), R=[pT], W=[i1T])
            yield self.op("act", lambda e: e.copy(out=i2T[:, :], in_=pT[:, 128:256]), R=[pT], W=[i2T])
            yield self.op("dve", lambda e: e.scalar_tensor_tensor(out=idxT[:, :], in0=i1T[:, :], scalar=128.0, in1=i2T[:, :],
                                                            op0=ALU.mult, op1=ALU.add), R=[i1T, i2T], W=[idxT])
            yield self.op("act", lambda e: e.copy(out=gT[:, :], in_=pT[:, 256:384]), R=[pT], W=[gT])
        def loop(i, gen):
            x2, h3b, idxT, gT = TS[i % 2]
            ntok = 128 if i < 16 else 64
            Pacc = [B[0], B[1]]
            st_g, st_x, st_w = {}, {}, {}

            def stageA(t):
                guv = GUV.next()
                st_g[t] = guv
                self.dma(lambda e: e.indirect_dma_start(out=guv[:, :], out_offset=None, in_=self.uvtab.t,
                                                         in_offset=bass.IndirectOffsetOnAxis(ap=idxT[:, t:t + 1], axis=0)),
                         R=[idxT, self.uvtab], W=[guv], q="pool")

            st_a = {}

            def stageB1(t):
                guv = st_g[t]
                ac = act_r.next(); dj = djunk_r.next()
                st_a[t] = ac
                Px = A if order.index(t) % 2 == 0 else C
                tX = Px[0].t
                for n in range(2):
                    self.mm(Px[n], tX[:, n * 512:(n + 1) * 512], self.ident, self.ident[:, t:t + 1].to_broadcast([128, 128]),
                            h3b, h3b[:, n * 512:(n + 1) * 512], True, True)
                self.op("dve", lambda e: e.scalar_tensor_tensor(out=dj[:, :], in0=guv[:, 0:D], scalar=1.0, in1=tX[:, :], op0=ALU.mult, op1=ALU.mult,
                                                                accum_out=ac[:, 0:1]), R=[guv, Px[0], Px[1]], W=[dj, ac])

            def stageB2(t):
                ac = st_a.pop(t)
                g2 = gl_r.next()
                self.op("act", lambda e: e.activation(out=g2[:, 0:1], in_=ac[:, 0:1], func=AF.Gelu), R=[ac], W=[g2])
                self.op("act", lambda e: e.activation(out=g2[:, 1:2], in_=g2[:, 0:1], func=AF.Copy, scale=gT[:, t:t + 1]), R=[g2, gT], W=[g2])
                ws = wsel_r.next()
                st_w[t] = ws
                r32 = t % 32
                self.op("act", lambda e: e.activation(out=ws[:, 0:32], in_=cmask[:, 127 - r32:159 - r32], func=AF.Copy, scale=g2[:, 1:2]),
                        R=[cmask, g2], W=[ws])

            ngrp = ntok // 32
            order = [g * 32 + r for r in range(32) for g in range(ngrp)]

            def stageC2(ts_):
                items = [(t, st_w.pop(t), st_g.pop(t)) for t in ts_]
                for n in range(2):
                    for (t, ws, guv) in items:
                        j = t // 32
                        r = t % 32
                        self.op("pe", lambda e, n=n, ws=ws, guv=guv, j=j, r=r: e.matmul(
                            Pacc[n].t[32 * j:32 * j + 32, n * 512:(n + 1) * 512], lhsT=ws[:, 0:32], rhs=guv[:, D + n * 512:D + (n + 1) * 512],
                            start=(r == 0), stop=(r == 31), tile_position=(0, 32 * j)), R=[ws, guv], W=[Pacc[n]])
            LA = NB - 4
            nst = len(order)
            for step in range(nst + LA + 3):
                if step < nst:
                    stageA(order[step])
                if 0 <= step - LA < nst:
                    stageB1(order[step - LA])
                if 0 <= step - LA - 1 < nst:
                    stageB2(order[step - LA - 1])
                k = step - LA - 2
                if k >= 1 and k % 2 == 1 and k < nst:
                    stageC2([order[k - 1], order[k]])
                if gen is not None:
                    for _ in range(3):
                        next(gen, None)
            yo = self.X.next()
            for n in range(2):
                self.op("dve", lambda e, n=n, yo=yo: e.tensor_tensor(out=yo[:, n * 512:(n + 1) * 512], in0=x2[:, n * 512:(n + 1) * 512],
                                                                 in1=Pacc[n].t[:, n * 512:(n + 1) * 512], op=ALU.add), R=[x2, Pacc[n]], W=[yo])
            self.store(O["y_own"][i * 128:(i + 1) * 128, :], yo, yo[:, :])
            if gen is not None:
                for _ in gen:
                    pass

        NADV = self.dbg.get("NADV", 6)
        g0 = pre(0)
        self.npre = 0
        for _ in g0:
            self.npre += 1
        for i in range(NOWN):
            gen = pre(i + 1) if i + 1 < NOWN else None
            loop(i, gen)
        self.pop()


def _own_blocks(j):
    blks = []
    for s in range(16):
        m = s // 2
        if s % 2 == 0:
            blks.append(4 * m + (0 if j == 0 else 1))
        else:
            blks.append(4 * m + (3 if j == 0 else 2))
    return blks


def _consts(j):
    f = np.float32
    c = {}
    c["idn"] = np.eye(128, dtype=f)
    oq = np.zeros((96, 96), f)
    oq[0:64, 0:64] = 1.0 / 64
    oq[64:96, 64:96] = 1.0 / 32
    c["onesq"] = oq
    c["ones128"] = np.full((128, 128), 1.0 / 128, f)
    rm = np.zeros((96, 96), f)
    for d in range(64, 80):
        rm[d + 16, d] = -1.0
    for d in range(80, 96):
        rm[d - 16, d] = 1.0
    c["rotm"] = rm
    cm = np.zeros((128, 255), f)
    cm[:, 127] = 1.0
    c["cmask"] = cm
    c["trilm"] = np.tril(np.ones((128, 128), f))
    tms = np.zeros((128, 128), f)
    for bt in range(4):
        tms[bt * 16:(bt + 1) * 16, bt * 16:(bt + 1) * 16] = np.tril(np.ones((16, 16), f))
    c["trilms"] = tms
    inv = (np.float32(10000.0) ** (-np.arange(16, dtype=f) / np.float32(16))).astype(f)
    pos = np.zeros((128, 33), f)
    for i in range(32):
        pos[:, i] = i * 128 + np.arange(128)
    pos[0:64, 32] = 1024 + (np.arange(64) % 16)
    ang = (pos[:, :, None] * inv[None, None, :]).astype(f)
    c["ropek"] = np.concatenate([np.cos(ang), np.sin(ang)], axis=2).astype(f)
    blks = _own_blocks(j)
    posq = np.zeros((NTOK,), f)
    for s, b in enumerate(blks):
        posq[s * 128:(s + 1) * 128] = b * 128 + np.arange(128)
    posq[2048:2112] = 1024 + (np.arange(64) % 16)
    angq = (posq[None, :] * np.concatenate([inv, inv])[:, None]).astype(f)
    c["cosq"] = np.cos(angq).astype(f)
    c["sinq"] = np.sin(angq).astype(f)
    kk = np.arange(128)[:, None] // 64
    qq = np.arange(128)[None, :] // 64
    diag = (kk <= qq).astype(f)
    full = np.ones((128, 128), f)
    zero = np.zeros((128, 128), f)
    c["masks"] = np.stack([diag, zero, full, diag] if j == 0 else [full, diag, diag, zero]).astype(f)
    return c


_NC_CACHE = {}


def _get_nc(dbg=None):
    key = repr(sorted((dbg or {}).items()))
    if key not in _NC_CACHE:
        k = Kern(dbg)
        _NC_CACHE[key] = k.build()
    return _NC_CACHE[key]


def kernel(x_prompt, x_sample, cache_mla_ckv, cache_mla_krope, cache_mem_k, cache_mem_v, mem_prompt,
           g_mix, w_in, g_q_lat, w_uq, g_qn, g_qr, g_kv_lat, g_kr, w_uk, w_uv, g_kn, w_oa,
           g_gm, w_s, b_s, w_ob, w_o,
           g_xattn, g_mem, w_cq, g_cq, w_ck, g_ck, w_cv, w_co,
           g_ffn, w_pq, sub_keys, peer_u, peer_v, _dbg=None):
    f = np.float32
    A = lambda a: np.ascontiguousarray(np.asarray(a, dtype=f))
    x_prompt = A(x_prompt); x_sample = A(x_sample)
    shared = {
        "w_in": A(w_in)[0], "w_uq": A(w_uq)[0], "w_uk": A(w_uk)[0], "w_uv": A(w_uv)[0],
        "w_oa": A(w_oa)[0], "w_ob": A(w_ob)[0], "w_o": A(w_o)[0],
        "w_cq": A(w_cq)[0], "w_ck": A(w_ck)[0], "w_cv": A(w_cv)[0], "w_co": A(w_co)[0],
        "w_pq": A(w_pq)[0], "sub_keys": A(sub_keys)[0].reshape(2048, 128),
        "peer_u": A(peer_u)[0], "peer_v": A(peer_v)[0],
        "g_mix": A(g_mix), "g_q_lat": A(g_q_lat), "g_qn": A(g_qn), "g_qr": A(g_qr), "g_kv_lat": A(g_kv_lat),
        "g_kr": A(g_kr), "g_kn": A(g_kn), "g_gm": A(g_gm), "g_xattn": A(g_xattn), "g_mem": A(g_mem),
        "g_cq": A(g_cq), "g_ck": A(g_ck), "g_ffn": A(g_ffn),
        "w_s": A(w_s)[0].reshape(1024, 128), "b_s": A(b_s)[0],
    }
    ckv_c = A(cache_mla_ckv)[0]; kr_c = A(cache_mla_krope)[0]
    mk_c = A(cache_mem_k)[0]; mv_c = A(cache_mem_v)[0]; mem_p = A(mem_prompt)
    consts = [_consts(0), _consts(1)]
    in_maps = []
    for c in range(NCORES):
        b, j = c // 2, c % 2
        blks = _own_blocks(j)
        x_own = np.zeros((NTOK, D), f)
        for s, blk in enumerate(blks):
            x_own[s * 128:(s + 1) * 128] = x_prompt[b, blk * 128:(blk + 1) * 128]
        x_own[2048:2112] = x_sample[4 * c:4 * c + 4].reshape(64, D)
        m = dict(shared)
        m.update(consts[j])
        m["x_all"] = x_prompt[b]
        m["x_own"] = x_own
        m["ckv_c"] = np.ascontiguousarray(ckv_c[4 * c:4 * c + 4].reshape(4096, 256))
        m["kr_c"] = np.ascontiguousarray(kr_c[4 * c:4 * c + 4].reshape(4096, 32))
        m["memk_c"] = np.ascontiguousarray(mk_c[4 * c:4 * c + 4].reshape(1024, 512))
        m["memv_c"] = np.ascontiguousarray(mv_c[4 * c:4 * c + 4].reshape(1024, 512))
        m["mem_p"] = mem_p[b]
        in_maps.append(m)
    nc = _get_nc(_dbg)
    res = run_bass_kernel_spmd(nc, in_maps, core_ids=list(range(NCORES)))
    R = res.results
    B, S, DB, DS = 4, 4096, 32, 16
    y_p = np.zeros((B, S, D), f); y_s = np.zeros((DB, DS, D), f)
    ckv_p = np.zeros((1, B, S, 256), f); kr_p = np.zeros((1, B, S, 32), f)
    mk_p = np.zeros((1, B, 256, 4, 128), f); mv_p = np.zeros((1, B, 256, 4, 128), f)
    ckv_s = np.zeros((1, DB, DS, 256), f); kr_s = np.zeros((1, DB, DS, 32), f); vg_s = np.zeros((1, DB, DS, D), f)
    for c in range(NCORES):
        b, j = c // 2, c % 2
        r = R[c]
        for s, blk in enumerate(_own_blocks(j)):
            y_p[b, blk * 128:(blk + 1) * 128] = r["y_own"][s * 128:(s + 1) * 128]
        y_s[4 * c:4 * c + 4] = r["y_own"][2048:2112].reshape(4, 16, D)
        half = slice(j * 2048, (j + 1) * 2048)
        ckv_p[0, b, half] = r["ckv_p"][half]
        kr_p[0, b, half] = r["kr_p"][half]
        mk_p[0, b, j * 128:(j + 1) * 128] = r["mk_p"][j * 128:(j + 1) * 128].reshape(128, 4, 128)
        mv_p[0, b, j * 128:(j + 1) * 128] = r["mv_p"][j * 128:(j + 1) * 128].reshape(128, 4, 128)
        ckv_s[0, 4 * c:4 * c + 4] = r["ckv_s"].reshape(4, 16, 256)
        kr_s[0, 4 * c:4 * c + 4] = r["kr_s"].reshape(4, 16, 32)
        vg_s[0, 4 * c:4 * c + 4] = r["vg_s"].reshape(4, 16, D)
    return (y_p, y_s, ckv_p, kr_p, mk_p, mv_p, ckv_s, kr_s, vg_s)
```

```python
import numpy as np
from contextlib import ExitStack
import concourse.bass as bass
import concourse.mybir as mybir
from concourse.bass_utils import run_bass_kernel_spmd

F32 = mybir.dt.float32
BF16 = mybir.dt.bfloat16
I32 = mybir.dt.int32
U32 = mybir.dt.uint32
ALU = mybir.AluOpType
AF = mybir.ActivationFunctionType
AX = mybir.AxisListType

NCORES = 8
D = 1024
EPS = 1e-6
NOWN = 17
NTOK = NOWN * 128
NKP = 4096
NKS = 4 * 1040
NK = NKP + NKS
MLA_SCALE = 96 ** -0.5
MEM_SCALE = 128 ** -0.5
IN_Q, IN_KV, IN_Z = 0, 384, 672


class Buf:
    __slots__ = ("name", "lw", "rd")

    def __init__(self, name):
        self.name = name
        self.lw = None
        self.rd = {}


class T:
    __slots__ = ("t", "b")

    def __init__(self, t, b):
        self.t = t
        self.b = b

    def __getitem__(self, k):
        return self.t[k]


class _Eng:
    def __init__(self, name, selfsync):
        self.name = name
        self.sem = "e_" + name
        self.count = 0
        self.seen = {}
        self.prog = []
        self.selfsync = selfsync


class _Queue:
    def __init__(self, name, eng, nslots):
        self.name = name
        self.eng = eng
        self.slots = [["q_%s_%d" % (name, i), 0] for i in range(nslots)]
        self.next = 0


class Sched:
    def __init__(self, nc, st, selfsync=True):
        self.nc = nc
        self.eng = {
            "pe": _Eng("pe", False),
            "act": _Eng("act", selfsync),
            "dve": _Eng("dve", selfsync),
            "pool": _Eng("pool", selfsync),
            "sp": _Eng("sp", False),
        }
        self.queues = {
            "sp": _Queue("sp", "sp", 8),
            "pool": _Queue("pool", "pool", 8),
            "act": _Queue("act", "act", 4),
            "conv": _Queue("conv", "pool", 16),
        }
        self.final_tokens = []
        names = [E.sem for E in self.eng.values()]
        for Q in self.queues.values():
            names += [s[0] for s in Q.slots]
        self.sems = {n: st.enter_context(nc.semaphore(n)) for n in names}
        self.ninst = 0

    def _collect(self, E, reads, writes, extra=None):
        need = {}

        def add(tok):
            if tok is None:
                return
            s, v = tok
            if need.get(s, 0) < v:
                need[s] = v
        for b in reads:
            add(b.lw)
        for b in writes:
            add(b.lw)
            for s, v in b.rd.items():
                add((s, v))
        if extra:
            for t in extra:
                add(t)
        waits = []
        for s, v in need.items():
            if s == E.sem and not E.selfsync:
                continue
            if E.seen.get(s, 0) >= v:
                continue
            E.seen[s] = v
            waits.append((s, v))
        return waits

    def _commit(self, tok, reads, writes):
        for b in writes:
            b.lw = tok
            b.rd = {}
        s, v = tok
        for b in reads:
            if b in writes:
                continue
            if b.rd.get(s, 0) < v:
                b.rd[s] = v

    def op(self, eng, fn, R=(), W=()):
        E = self.eng[eng]
        reads = [x.b for x in R]
        writes = [x.b for x in W]
        waits = self._collect(E, reads, writes)
        E.count += 1
        tok = (E.sem, E.count)
        E.prog.append((waits, fn, (E.sem, 1)))
        self._commit(tok, reads, writes)
        return tok

    def dma(self, queue, fn, R=(), W=(), final=False):
        Q = self.queues[queue]
        E = self.eng[Q.eng]
        reads = [x.b for x in R]
        writes = [x.b for x in W]
        slot = Q.slots[Q.next]
        Q.next = (Q.next + 1) % len(Q.slots)
        extra = [(slot[0], slot[1] * 16)] if slot[1] > 0 else None
        waits = self._collect(E, reads, writes, extra)
        slot[1] += 1
        tok = (slot[0], slot[1] * 16)
        E.prog.append((waits, fn, (slot[0], 16)))
        self._commit(tok, reads, writes)
        if final:
            self.final_tokens.append(tok)
        return tok

    def barrier(self):
        toks = []
        for E in self.eng.values():
            if E.count > 0:
                toks.append((E.sem, E.count))
        for qn, Q in self.queues.items():
            if qn == "conv":
                continue
            for s in Q.slots:
                if s[1] > 0:
                    toks.append((s[0], s[1] * 16))
        for E in self.eng.values():
            waits = []
            for s, v in toks:
                if s == E.sem:
                    continue
                if E.seen.get(s, 0) >= v:
                    continue
                E.seen[s] = v
                waits.append((s, v))
            if waits:
                E.prog.append((waits, None, None))

    def flush(self, last=False):
        nc = self.nc
        sems = self.sems
        if last:
            fin = {}
            for s, v in self.final_tokens:
                if fin.get(s, 0) < v:
                    fin[s] = v
            self.eng["sp"].prog.append(([(s, v) for s, v in fin.items()], None, None))

        def run(E):
            prog = E.prog
            E.prog = []
            self.ninst += len(prog)

            def body(e):
                for waits, fn, inc in prog:
                    for s, v in waits:
                        e.wait_ge(sems[s], v)
                    if fn is not None:
                        ins = fn(e)
                        ins.then_inc(sems[inc[0]], inc[1])
            return body
        with nc.Block(no_gpsimd_drain=True) as block:
            block.sync(run(self.eng["sp"]))
            block.tensor(run(self.eng["pe"]))
            block.scalar(run(self.eng["act"]))
            block.vector(run(self.eng["dve"]))
            block.gpsimd(run(self.eng["pool"]))


class Ring:
    def __init__(self, tiles):
        self.tiles = tiles
        self.i = 0

    def next(self):
        t = self.tiles[self.i]
        self.i = (self.i + 1) % len(self.tiles)
        return t


class Kern:
    def __init__(self, dbg=None):
        self.nc = bass.Bass("TRN2", target_bir_lowering=False)
        self.st = ExitStack()
        self.S = Sched(self.nc, self.st)
        self.scopes = [self.st]
        self.dbg = dbg or {}
        self.uid = 0

    def din(self, name, shape, dt=F32):
        return self.nc.dram_tensor(name, list(shape), dt, kind="ExternalInput").ap()

    def dout(self, name, shape, dt=F32):
        return self.nc.dram_tensor(name, list(shape), dt, kind="ExternalOutput").ap()

    def sb(self, name, shape, dt):
        self.uid += 1
        nm = "%s_%d" % (name, self.uid)
        t = self.scopes[-1].enter_context(self.nc.sbuf_tensor(nm, list(shape), dt))
        return T(t, Buf(nm))

    def ring(self, name, shape, dt, n):
        return Ring([self.sb(name, shape, dt) for _ in range(n)])

    def push(self):
        s = ExitStack()
        self.scopes.append(s)
        return s

    def pop(self):
        self.S.barrier()
        self.S.flush()
        s = self.scopes.pop()
        s.close()

    def op(self, eng, fn, R=(), W=()):
        return self.S.op(eng, fn, R, W)

    def dma(self, fn, R=(), W=(), q="sp", final=False):
        return self.S.dma(q, fn, R, W, final)

    def load(self, dst, dst_ap, src_ap, q="sp", slow=False):
        if slow:
            self.dma(lambda e: e.dma_start(out=dst_ap, in_=src_ap, allow_slow_non_contiguous=True), W=[dst], q=q)
        else:
            self.dma(lambda e: e.dma_start(out=dst_ap, in_=src_ap), W=[dst], q=q)

    def store(self, dst_ap, src, src_ap):
        self.dma(lambda e: e.dma_start(out=dst_ap, in_=src_ap), R=[src], final=True)

    def wload(self, name, src, K, N, c0=0):
        kc = K // 128
        w = self.sb(name, [128, kc, N], BF16)
        for c in range(kc):
            self.load(w, w[:, c, :], src[c * 128:(c + 1) * 128, c0:c0 + N], q="pool")
        return w

    def gbload(self, name, src, n):
        g = self.sb(name, [128, n], F32)
        self.load(g, g[:, :], src[0:1, 0:n].partition_broadcast(128))
        return g

    def mm(self, ps, ps_ap, lhsT, lhsT_ap, rhs, rhs_ap, start, stop):
        self.op("pe", lambda e: e.matmul(ps_ap, lhsT=lhsT_ap, rhs=rhs_ap, start=start, stop=stop),
                R=[lhsT, rhs], W=[ps])

    def rstd(self, src, src_ap, n, Dn):
        sm = self.small.next()
        jk = self.junk
        self.op("act", lambda e: e.activation(out=jk[:, 0:n], in_=src_ap, func=AF.Square, accum_out=sm[:, 0:1]),
                R=[src], W=[jk, sm])
        self.op("act", lambda e: e.activation(out=sm[:, 1:2], in_=sm[:, 0:1], func=AF.Sqrt, bias=self.epsb[:, 0:1],
                                              scale=1.0 / Dn), R=[sm, self.epsb], W=[sm])
        self.op("dve", lambda e: e.reciprocal(out=sm[:, 2:3], in_=sm[:, 1:2]), R=[sm], W=[sm])
        return sm, sm[:, 2:3]

    def rms_tok(self, src, src_ap, n, gb, gb_ap, dst, dst_ap):
        sm, col = self.rstd(src, src_ap, n, n)
        self.op("dve", lambda e: e.scalar_tensor_tensor(out=dst_ap, in0=src_ap, scalar=col, in1=gb_ap,
                                                        op0=ALU.mult, op1=ALU.mult), R=[src, sm, gb], W=[dst])

    def transposes(self, src, src_ap_fn, n, dst, dst_ap, rows=128, eng="act"):
        pT = self.psT
        for j in range(n):
            ap = src_ap_fn(j)
            self.op("pe", lambda e, ap=ap, j=j: e.transpose(out=pT[:, j * 128:j * 128 + rows], in_=ap,
                                                             identity=self.ident[0:rows, 0:rows]),
                    R=[src, self.ident], W=[pT])
        view = pT[:, 0:n * 128].rearrange("p (c k) -> p c k", c=n)[:, :, 0:rows]
        if eng == "act":
            self.op("act", lambda e: e.copy(out=dst_ap, in_=view), R=[pT], W=[dst])
        else:
            self.op("dve", lambda e: e.tensor_copy(out=dst_ap, in_=view), R=[pT], W=[dst])

    def normT(self, xt, gb, hb, hT):
        self.rms_tok(xt, xt[:, :], D, gb, gb[:, :], hb, hb[:, :])
        self.transposes(hb, lambda j: hb[:, j * 128:(j + 1) * 128], 8, hT, hT[:, :, :])

    def proj(self, ps, ps_ap, hT, w, c0, n, kc=8):
        for c in range(kc):
            self.mm(ps, ps_ap, hT, hT[:, c, :], w, w[:, c, c0:c0 + n], c == 0, c == kc - 1)

    def build(self):
        nc = self.nc
        I = {}
        O = {}

        def di(name, shape, dt=F32):
            I[name] = self.din(name, shape, dt)

        def do(name, shape, dt=F32):
            O[name] = self.dout(name, shape, dt)
        di("x_all", [4096, D]); di("x_own", [NTOK, D])
        di("ckv_c", [4096, 256]); di("kr_c", [4096, 32])
        di("memk_c", [1024, 512]); di("memv_c", [1024, 512]); di("mem_p", [256, D])
        di("w_in", [D, 4768]); di("w_uq", [384, 1536]); di("w_uk", [256, 1024]); di("w_uv", [256, 1024])
        di("w_oa", [D, D]); di("w_ob", [D, D]); di("w_o", [D, D])
        di("w_cq", [D, 512]); di("w_ck", [D, 512]); di("w_cv", [D, 512]); di("w_co", [512, D])
        di("w_pq", [D, 2048]); di("sub_keys", [2048, 128]); di("peer_u", [16384, D]); di("peer_v", [16384, D])
        for g, n in [("g_mix", D), ("g_q_lat", 384), ("g_qn", 64), ("g_qr", 32), ("g_kv_lat", 256), ("g_kr", 32),
                     ("g_kn", 64), ("g_gm", D), ("g_xattn", D), ("g_mem", D), ("g_cq", 128), ("g_ck", 128),
                     ("g_ffn", D)]:
            di(g, [1, n])
        di("w_s", [1024, 128]); di("b_s", [8, 128])
        di("idn", [128, 128]); di("onesq", [96, 96]); di("ones128", [128, 128]); di("rotm", [96, 96])
        di("cmask", [128, 255]); di("trilm", [128, 128]); di("trilms", [128, 128])
        di("ropek", [128, 33, 32]); di("cosq", [32, NTOK]); di("sinq", [32, NTOK]); di("masks", [4, 128, 128])
        do("y_own", [NTOK, D]); do("ckv_p", [4096, 256]); do("kr_p", [4096, 32])
        do("mk_p", [256, 512]); do("mv_p", [256, 512])
        do("ckv_s", [64, 256]); do("kr_s", [64, 32]); do("vg_s", [64, D])
        self.I, self.O = I, O

        def ps(name, shape, dt):
            t = self.st.enter_context(nc.psum_tensor(name, shape, dt))
            return t
        self.psT = T(ps("psT", [128, 1024], BF16), Buf("psT"))
        tA = ps("psA", [128, 1024], F32); tB = ps("psB", [128, 1024], F32); tC = ps("psC", [128, 1024], F32)
        tD = ps("psD", [128, 512], F32)
        self.A = [T(tA, Buf("A0")), T(tA, Buf("A1"))]
        self.B = [T(tB, Buf("B0")), T(tB, Buf("B1"))]
        self.C = [T(tC, Buf("C0")), T(tC, Buf("C1"))]
        self.Dp = T(tD, Buf("D"))

        self.ident = self.sb("ident", [128, 128], BF16)
        self.load(self.ident, self.ident[:, :], I["idn"], q="pool")
        self.epsb = self.sb("epsb", [128, 1], F32)
        self.op("dve", lambda e: e.memset(self.epsb[:, :], EPS), W=[self.epsb])
        self.mscr = T(nc.dram_tensor("m_scr", [NTOK, D], BF16).ap(), Buf("mscr"))
        self.junk = self.sb("junk", [128, D], BF16)
        self.small = self.ring("small", [128, 8], F32, 4)
        self.X = self.ring("X", [128, D], F32, 2)
        self.HB = self.sb("HB", [128, D], BF16)
        self.HT = self.ring("HT", [128, 8, 128], BF16, 1)

        self.uvtab = T(nc.dram_tensor("uv_bf", [16384, 2 * D], BF16).ap(), Buf("uvtab"))
        self.hscr = T(nc.dram_tensor("h_scr", [NTOK, D], BF16).ap(), Buf("hscr"))
        self._conv_pending = True
        ph = self.dbg.get("phases", "1234")
        self.push()
        self.o_all = self.sb("o_all", [128, NOWN, D], BF16)
        self.op("pool", lambda e: e.memset(self.o_all[:, 16, :], 0.0), W=[self.o_all])
        self.gb_mix = self.gbload("gb_mix", I["g_mix"], D)
        self.push()
        ckvT = self.sb("ckvT", [128, 2, NK], BF16)
        KT = self.sb("KT", [96, NK], BF16)
        cqnT = self.sb("cqnT", [128, 3, NTOK], BF16)
        if "1" in ph:
            self.phase1(ckvT, KT, cqnT)
        if "2" in ph:
            self.phase2(ckvT, KT, cqnT)
        self.pop()
        if "3" in ph:
            self.phase3a()
        self.pop()
        self.emit_conv()
        if "4" in ph:
            self.phase3b()
        self.S.barrier()
        self.S.flush(last=True)
        self.st.close()
        return nc

    def emit_conv(self):
        if not self._conv_pending:
            return
        self._conv_pending = False
        I = self.I
        RCH = 1024
        for r in range(0, 16384, RCH):
            for which, nm in ((0, "peer_u"), (1, "peer_v")):
                self.dma(lambda e, r=r, which=which, nm=nm: e.dma_start(out=self.uvtab.t[r:r + RCH, which * D:(which + 1) * D],
                                                                         in_=I[nm][r:r + RCH, :]), W=[self.uvtab], q="conv")

    def phase1(self, ckvT, KT, cqnT):
        I, O = self.I, self.O
        self.push()
        Wkv = self.wload("Wkv", I["w_in"], D, 288, IN_KV)
        Wq = self.wload("Wq", I["w_in"], D, 384, IN_Q)
        gb_kv = self.sb("gb_kv", [128, 288], F32)
        self.load(gb_kv, gb_kv[:, 0:256], I["g_kv_lat"][0:1, 0:256].partition_broadcast(128))
        self.load(gb_kv, gb_kv[:, 256:288], I["g_kr"][0:1, 0:32].partition_broadcast(128))
        gb_ql = self.gbload("gb_ql", I["g_q_lat"], 384)
        ropek = self.sb("ropek", [128, 33, 32], F32)
        self.load(ropek, ropek[:, :, :], I["ropek"])
        kvo_r = self.ring("kvo", [128, 288], F32, 2)
        krn_r = self.ring("krn", [128, 64], F32, 2)
        kvb_r = self.ring("kvb", [128, 384], BF16, 2)
        for t in kvb_r.tiles:
            self.op("pool", lambda e, t=t: e.memset(t[:, :], 0.0), W=[t])
        cqb = self.sb("cqb", [128, 384], BF16)
        pT = self.psT

        def finish_kv(kvo, kind, idx, pT):
            kvb = kvb_r.next()
            self.op("act", lambda e: e.copy(out=kvb[:, 0:256], in_=kvo[:, 0:256]), R=[kvo], W=[kvb])
            self.op("act", lambda e: e.copy(out=kvb[:, 320:352], in_=kvo[:, 256:288]), R=[kvo], W=[kvb])
            for j in range(3):
                self.op("pe", lambda e, j=j: e.transpose(out=pT[:, j * 128:(j + 1) * 128], in_=kvb[:, j * 128:(j + 1) * 128],
                                                       identity=self.ident[:, :]), R=[kvb, self.ident], W=[pT])
            if kind == "snew":
                dst = ckvT[:, :, NKP:NK].rearrange("p c (b k) -> p c b k", b=4)[:, :, :, 1024:1040]
                src = pT[:, 0:256].rearrange("p (c b k) -> p c b k", c=2, b=8)[:, :, 0:4, :]
                self.op("act", lambda e: e.copy(out=dst, in_=src), R=[pT], W=[ckvT])
                dstk = KT[64:96, NKP:NK].rearrange("p (b k) -> p b k", b=4)[:, :, 1024:1040]
                srck = pT[64:96, 256:320].rearrange("p (b k) -> p b k", b=4)
                self.op("act", lambda e: e.copy(out=dstk, in_=srck), R=[pT], W=[KT])
            else:
                c0 = idx * 128 if kind == "p" else NKP + (idx // 8) * 1040 + (idx % 8) * 128
                src = pT[:, 0:256].rearrange("p (c k) -> p c k", c=2)
                if "ckv" not in self.dbg.get("fk_skip", ""):
                    self.op("act", lambda e: e.copy(out=ckvT[:, :, c0:c0 + 128], in_=src), R=[pT], W=[ckvT])
                if "kt" not in self.dbg.get("fk_skip", ""):
                    self.op("act", lambda e: e.copy(out=KT[64:96, c0:c0 + 128], in_=pT[64:96, 256:384]), R=[pT], W=[KT])

        class TV:
            def __init__(self, ap, b):
                self.t = ap
                self.b = b

            def __getitem__(self, k):
                return self.t[k]
        psTs = [self.psT, TV(self.Dp.t[:, :].bitcast(BF16), self.Dp.b)]
        HBs = [self.HB, self.sb("HB2", [128, D], BF16)]
        HTs = [self.sb("HTa", [128, 8, 128], BF16), self.sb("HTb", [128, 8, 128], BF16)]
        cqbs = [cqb, self.sb("cqb2", [128, 384], BF16)]
        glob_psT, glob_HB = self.psT, self.HB

        def use(slot):
            self.psT = psTs[slot]
            self.HB = HBs[slot]

        def kv_tile(i, slot):
            xt = self.X.next()
            src = I["x_all"][i * 128:(i + 1) * 128, :] if i < 32 else I["x_own"][16 * 128:17 * 128, :]
            self.load(xt, xt[:, :], src)
            yield
            use(slot)
            hT = HTs[slot]
            self.rms_tok(xt, xt[:, :], D, self.gb_mix, self.gb_mix[:, :], self.HB, self.HB[:, :])
            yield
            use(slot)
            hb = self.HB
            self.transposes(hb, lambda j: hb[:, j * 128:(j + 1) * 128], 8, hT, hT[:, :, :])
            yield
            zp = self.A[slot]
            zoff = slot * 512
            z = zp.t[:, zoff:zoff + 288]
            self.proj(zp, z, hT, Wkv, 0, 288)
            yield
            kvo = kvo_r.next()
            krn = krn_r.next()
            self.rms_tok(zp, zp.t[:, zoff:zoff + 256], 256, gb_kv, gb_kv[:, 0:256], kvo, kvo[:, 0:256])
            self.rms_tok(zp, zp.t[:, zoff + 256:zoff + 288], 32, gb_kv, gb_kv[:, 256:288], krn, krn[:, 0:32])
            yield
            cs = ropek[:, i, 0:16]
            sn = ropek[:, i, 16:32]
            self.op("dve", lambda e: e.tensor_tensor(out=krn[:, 32:48], in0=krn[:, 0:16], in1=cs, op=ALU.mult), R=[krn, ropek], W=[krn])
            self.op("dve", lambda e: e.tensor_tensor(out=krn[:, 48:64], in0=krn[:, 16:32], in1=sn, op=ALU.mult), R=[krn, ropek], W=[krn])
            self.op("dve", lambda e: e.tensor_tensor(out=kvo[:, 256:272], in0=krn[:, 32:48], in1=krn[:, 48:64], op=ALU.subtract), R=[krn], W=[kvo])
            self.op("dve", lambda e: e.tensor_tensor(out=krn[:, 32:48], in0=krn[:, 0:16], in1=sn, op=ALU.mult), R=[krn, ropek], W=[krn])
            self.op("dve", lambda e: e.tensor_tensor(out=krn[:, 48:64], in0=krn[:, 16:32], in1=cs, op=ALU.mult), R=[krn, ropek], W=[krn])
            self.op("dve", lambda e: e.tensor_tensor(out=kvo[:, 272:288], in0=krn[:, 32:48], in1=krn[:, 48:64], op=ALU.add), R=[krn], W=[kvo])
            if i < 32:
                self.store(O["ckv_p"][i * 128:(i + 1) * 128, :], kvo, kvo[:, 0:256])
                self.store(O["kr_p"][i * 128:(i + 1) * 128, :], kvo, kvo[:, 256:288])
            else:
                self.store(O["ckv_s"][:, :], kvo, kvo[0:64, 0:256])
                self.store(O["kr_s"][:, :], kvo, kvo[0:64, 256:288])
            yield
            use(slot)
            finish_kv(kvo, "p" if i < 32 else "snew", i if i < 32 else 0, self.psT)
            yield

        def cache_tile(idx, slot):
            kvo = kvo_r.next()
            self.load(kvo, kvo[:, 0:256], I["ckv_c"][idx * 128:(idx + 1) * 128, :])
            self.load(kvo, kvo[:, 256:288], I["kr_c"][idx * 128:(idx + 1) * 128, :])
            yield
            use(slot)
            finish_kv(kvo, "scache", idx, self.psT)
            yield

        def q_tile(i, slot):
            xt = self.X.next()
            self.load(xt, xt[:, :], I["x_own"][i * 128:(i + 1) * 128, :])
            yield
            use(slot)
            hT = HTs[slot]
            self.rms_tok(xt, xt[:, :], D, self.gb_mix, self.gb_mix[:, :], self.HB, self.HB[:, :])
            yield
            use(slot)
            hb = self.HB
            self.transposes(hb, lambda j: hb[:, j * 128:(j + 1) * 128], 8, hT, hT[:, :, :])
            yield
            zp = self.A[slot]
            zoff = slot * 512
            self.proj(zp, zp.t[:, zoff:zoff + 384], hT, Wq, 0, 384)
            yield
            cq_ = cqbs[slot]
            self.rms_tok(zp, zp.t[:, zoff:zoff + 384], 384, gb_ql, gb_ql[:, :], cq_, cq_[:, :])
            yield
            use(slot)
            self.transposes(cq_, lambda j: cq_[:, j * 128:(j + 1) * 128], 3, cqnT, cqnT[:, :, i * 128:(i + 1) * 128])
            yield

        def run2(makers):
            pending = list(makers)
            active = {}
            while pending or active:
                for slot in (0, 1):
                    if slot not in active and pending:
                        active[slot] = pending.pop(0)(slot)
                for slot in (0, 1):
                    g = active.get(slot)
                    if g is None:
                        continue
                    try:
                        next(g)
                    except StopIteration:
                        del active[slot]
        mk = [(lambda slot, i=i: kv_tile(i, slot)) for i in self.dbg.get("p1_new", list(range(33)))]
        mk += [(lambda slot, idx=idx: cache_tile(idx, slot)) for idx in range(self.dbg.get("p1_cache", 32))]
        mk += [(lambda slot, i=i: q_tile(i, slot)) for i in range(self.dbg.get("p1_q", NOWN))]
        run2(mk)
        self.psT, self.HB = glob_psT, glob_HB
        self.pop()

    def phase2(self, ckvT, KT, cqnT):
        I, O = self.I, self.O
        self.push()
        wuq = self.wload("wuq", I["w_uq"], 384, 1536)
        wuk = self.wload("wuk", I["w_uk"], 256, 1024)
        wuv = self.wload("wuv", I["w_uv"], 256, 1024)
        onesq = self.sb("onesq", [96, 96], BF16)
        self.load(onesq, onesq[:, :], I["onesq"], q="pool")
        rotm = self.sb("rotm", [96, 96], BF16)
        self.load(rotm, rotm[:, :], I["rotm"], q="pool")
        gq = self.sb("gq", [96, 2], F32)
        self.load(gq, gq[0:64, 0:1], I["g_qn"][0:1, 0:64].rearrange("o d -> d o"))
        self.load(gq, gq[64:96, 0:1], I["g_qr"][0:1, 0:32].rearrange("o d -> d o"))
        self.op("dve", lambda e: e.tensor_scalar(out=gq[:, 1:2], in0=gq[:, 0:1], scalar1=MLA_SCALE, scalar2=None, op0=ALU.mult),
                R=[gq], W=[gq])
        gkn = self.sb("gkn", [64, 1], F32)
        self.load(gkn, gkn[:, :], I["g_kn"][0:1, 0:64].rearrange("o d -> d o"))
        cosq = self.sb("cosq", [96, NTOK], F32)
        sinq = self.sb("sinq", [96, NTOK], F32)
        self.load(cosq, cosq[64:96, :], I["cosq"])
        self.load(sinq, sinq[64:96, :], I["sinq"])
        maskT = self.sb("maskT", [128, 4, 128], BF16)
        self.load(maskT, maskT[:, :, :], I["masks"].rearrange("m k q -> k m q"), q="pool")
        NVT = 32 + 36
        V = self.sb("V", [128, NVT, 2, 72], BF16)
        self.op("pool", lambda e: e.memset(V[:, :, :, :], 0.0), W=[V])
        self.op("pool", lambda e: e.memset(V[:, :, :, 64:65], 1.0), W=[V])
        QT = self.sb("QT", [96, NTOK], BF16)
        sq_r = self.ring("sq", [96, 512], BF16, 2)
        rs_r = self.ring("rs", [96, 512], F32, 2)
        tq_r = self.ring("tq", [96, 512], F32, 2)
        t1 = self.sb("t1", [96, 512], F32)
        t2 = self.sb("t2", [96, 512], F32)
        pt_r = self.ring("pt", [128, 4, 128], BF16, 3)
        ptS = [self.sb("ptS", [128, 9, 64], BF16) for _ in range(4)]
        for t in ptS:
            self.op("pool", lambda e, t=t: e.memset(t[:, :, :], 0.0), W=[t])
        A, B, C, Dp = self.A, self.B, self.C, self.Dp
        eps96 = self.epsb

        def fm_norm(ps, ps_ap, rows, n, ones_ap, gcol, gcol_ap, dst, dst_ap):
            sq = sq_r.next()
            rs = rs_r.next()
            self.op("act", lambda e: e.activation(out=sq[0:rows, 0:n], in_=ps_ap, func=AF.Square), R=[ps], W=[sq])
            self.mm(B[0], B[0].t[0:rows, 0:n], onesq, ones_ap, sq, sq[0:rows, 0:n], True, True)
            self.op("act", lambda e: e.activation(out=rs[0:rows, 0:n], in_=B[0].t[0:rows, 0:n], func=AF.Sqrt,
                                                  bias=eps96[0:rows, 0:1], scale=1.0), R=[B[0], eps96], W=[rs])
            tq = tq_r.next()
            self.op("act", lambda e: e.activation(out=tq[0:rows, 0:n], in_=ps_ap, func=AF.Copy, scale=gcol_ap), R=[ps, gcol], W=[tq])
            self.op("dve", lambda e: e.reciprocal(out=rs[0:rows, 0:n], in_=rs[0:rows, 0:n]), R=[rs], W=[rs])
            self.op("pool", lambda e: e.tensor_tensor(out=dst_ap, in0=tq[0:rows, 0:n], in1=rs[0:rows, 0:n], op=ALU.mult), R=[tq, rs], W=[dst])

        kchunks = [(c0, min(512, NK - c0)) for c0 in range(0, NK, 512)]
        qchunks = [(c0, min(512, NTOK - c0)) for c0 in range(0, NTOK, 512)]
        vt = [(i, i * 128, 128) for i in range(32)]
        for bt in range(4):
            for j in range(9):
                vt.append((32 + bt * 9 + j, NKP + bt * 1040 + j * 128, 128 if j < 8 else 16))
        nh = self.dbg.get("nheads", 16)
        pi = 0
        for h in range(nh):
            for (c0, n) in kchunks:
                P = A[pi % 2]; po = (pi % 2) * 512; pi += 1
                for c in range(2):
                    self.mm(P, P.t[0:64, po:po + n], wuk, wuk[:, c, h * 64:(h + 1) * 64], ckvT, ckvT[:, c, c0:c0 + n], c == 0, c == 1)
                fm_norm(P, P.t[0:64, po:po + n], 64, n, onesq[0:64, 0:64], gkn, gkn[:, 0:1], KT, KT[0:64, c0:c0 + n])
            p2c = self.dbg.get("p2_cut", 9)
            if p2c < 2:
                continue
            if h % 2 == 0:
                for g0 in range(0, NVT, 4):
                    P = A[pi % 2]; po = (pi % 2) * 512; pi += 1
                    grp = vt[g0:g0 + 4]
                    for (ti, c0, rows) in grp:
                        jj = ti - g0
                        for c in range(2):
                            self.mm(P, P.t[0:rows, po + jj * 128:po + jj * 128 + 128], ckvT, ckvT[:, c, c0:c0 + rows],
                                    wuv, wuv[:, c, h * 64:(h + 2) * 64], c == 0, c == 1)
                    ng = len(grp)
                    src = P.t[:, po:po + ng * 128].rearrange("p (j a d) -> p j a d", j=ng, a=2)
                    self.op("act", lambda e, src=src, g0=g0, ng=ng: e.copy(out=V[:, g0:g0 + ng, :, 0:64], in_=src), R=[P], W=[V])
            if p2c < 3:
                continue
            for (c0, n) in qchunks:
                P = A[pi % 2]; po = (pi % 2) * 512; pi += 1
                for c in range(3):
                    self.mm(P, P.t[0:96, po:po + n], wuq, wuq[:, c, h * 96:(h + 1) * 96], cqnT, cqnT[:, c, c0:c0 + n], c == 0, c == 2)
                fm_norm(P, P.t[0:96, po:po + n], 96, n, onesq[:, :], gq, gq[:, 1:2], QT, QT[0:96, c0:c0 + n])
                self.mm(B[1], B[1].t[0:96, 512:512 + n], rotm, rotm[:, :], QT, QT[0:96, c0:c0 + n], True, True)
                self.op("dve", lambda e, c0=c0, n=n: e.tensor_tensor(out=t1[64:96, 0:n], in0=QT[64:96, c0:c0 + n], in1=cosq[64:96, c0:c0 + n], op=ALU.mult),
                        R=[QT, cosq], W=[t1])
                self.op("dve", lambda e, c0=c0, n=n: e.tensor_tensor(out=t2[64:96, 0:n], in0=B[1].t[64:96, 512:512 + n], in1=sinq[64:96, c0:c0 + n], op=ALU.mult),
                        R=[B[1], sinq], W=[t2])
                self.op("dve", lambda e, c0=c0, n=n: e.tensor_tensor(out=QT[64:96, c0:c0 + n], in0=t1[64:96, 0:n], in1=t2[64:96, 0:n], op=ALU.add),
                        R=[t1, t2], W=[QT])
            if p2c < 4:
                continue
            si = 0
            groups = []
            for s in range(16):
                nkb = 4 * (s // 2) + (2 if s % 2 == 0 else 4)
                for g0 in range(0, nkb, 4):
                    groups.append((s, nkb, list(range(g0, min(g0 + 4, nkb)))))
            Obank = [(Dp, 0), (B[1], 512)]

            def qk(gi):
                s, nkb, blks = groups[gi]
                Sp = C[gi % 2]; so = (gi % 2) * 512
                for j, kb in enumerate(blks):
                    self.mm(Sp, Sp.t[:, so + j * 128:so + (j + 1) * 128], KT, KT[0:96, kb * 128:(kb + 1) * 128],
                            QT, QT[0:96, s * 128:(s + 1) * 128], True, True)
            qk(0)
            for gi, (s, nkb, blks) in enumerate(groups):
                if gi + 1 < len(groups):
                    qk(gi + 1)
                Sp = C[gi % 2]; so = (gi % 2) * 512
                Ops, oo = Obank[s % 2]
                pt = pt_r.next()
                nb = len(blks)
                self.op("act", lambda e, Sp=Sp, so=so, nb=nb, pt=pt: e.activation(
                    out=pt[:, 0:nb, :], in_=Sp.t[:, so:so + nb * 128].rearrange("p (j q) -> p j q", j=nb), func=AF.Exp),
                    R=[Sp], W=[pt])
                for j, kb in enumerate(blks):
                    if kb >= nkb - 2:
                        mi = (0 if s % 2 == 0 else 2) + (kb - (nkb - 2))
                        self.op("dve", lambda e, pt=pt, j=j, mi=mi: e.tensor_tensor(out=pt[:, j, :], in0=pt[:, j, :], in1=maskT[:, mi, :], op=ALU.mult),
                                R=[pt, maskT], W=[pt])
                for j, kb in enumerate(blks):
                    self.mm(Ops, Ops.t[:, oo:oo + 72], pt, pt[:, j, :], V, V[:, kb, h % 2, :], kb == 0, kb == nkb - 1)
                if blks[-1] == nkb - 1:
                    sm = self.small.next()
                    self.op("dve", lambda e, sm=sm, Ops=Ops, oo=oo: e.reciprocal(out=sm[:, 0:1], in_=Ops.t[:, oo + 64:oo + 65]), R=[Ops], W=[sm])
                    self.op("dve", lambda e, sm=sm, Ops=Ops, oo=oo, s=s, h=h: e.tensor_scalar(out=self.o_all[:, s, h * 64:(h + 1) * 64], in0=Ops.t[:, oo:oo + 64],
                                                                                   scalar1=sm[:, 0:1], scalar2=None, op0=ALU.mult),
                            R=[Ops, sm], W=[self.o_all])
            si = len(groups)
            if p2c < 5:
                continue
            Ops = Dp
            for bt in range(4):
                base = NKP + bt * 1040
                Sp = C[si % 2]; so = (si % 2) * 512; si += 1
                qc = 2048 + bt * 16
                for j in range(8):
                    self.mm(Sp, Sp.t[:, so + j * 16:so + (j + 1) * 16], KT, KT[0:96, base + j * 128:base + (j + 1) * 128],
                            QT, QT[0:96, qc:qc + 16], True, True)
                self.mm(Sp, Sp.t[0:16, so + 128:so + 144], KT, KT[0:96, base + 1024:base + 1040], QT, QT[0:96, qc:qc + 16], True, True)
                p = ptS[bt]
                self.op("act", lambda e, Sp=Sp, so=so, p=p, bt=bt: e.activation(
                    out=p[:, 0:8, bt * 16:(bt + 1) * 16], in_=Sp.t[:, so:so + 128].rearrange("p (j q) -> p j q", j=8), func=AF.Exp),
                    R=[Sp], W=[p])
                self.op("act", lambda e, Sp=Sp, so=so, p=p, bt=bt: e.activation(
                    out=p[0:16, 8, bt * 16:(bt + 1) * 16], in_=Sp.t[0:16, so + 128:so + 144], func=AF.Exp), R=[Sp], W=[p])
                for j in range(9):
                    rows = 128 if j < 8 else 16
                    self.mm(Ops, Ops.t[0:64, 0:72], p, p[0:rows, j, :], V, V[0:rows, 32 + bt * 9 + j, h % 2, :],
                            bt == 0 and j == 0, bt == 3 and j == 8)
            sm = self.small.next()
            self.op("dve", lambda e, sm=sm, Ops=Ops: e.reciprocal(out=sm[0:64, 0:1], in_=Ops.t[0:64, 64:65]), R=[Ops], W=[sm])
            self.op("dve", lambda e, sm=sm, Ops=Ops, h=h: e.tensor_scalar(out=self.o_all[0:64, 16, h * 64:(h + 1) * 64], in0=Ops.t[0:64, 0:64],
                                                                     scalar1=sm[0:64, 0:1], scalar2=None, op0=ALU.mult),
                    R=[Ops, sm], W=[self.o_all])
        self.pop()

    def phase3a(self):
        I, O = self.I, self.O
        self.push()
        Wz = self.wload("Wz", I["w_in"], D, 4096, IN_Z)
        woa = self.wload("woa", I["w_oa"], D, D)
        wob = self.wload("wob", I["w_ob"], D, D)
        gb_gm = self.gbload("gb_gm", I["g_gm"], D)
        self.emit_conv()
        trilm = self.sb("trilm", [128, 128], F32)
        trilms = self.sb("trilms", [128, 128], F32)
        self.load(trilm, trilm[:, :], I["trilm"])
        self.load(trilms, trilms[:, :], I["trilms"])
        wsf = self.sb("wsf", [128, 8, 128], F32)
        wsb = self.sb("wsb", [128, 8, 128], BF16)
        WmT = [self.sb("WmT", [128, 8, 128], BF16) for _ in range(2)]
        bT = [self.sb("bT", [128, 8], F32) for _ in range(2)]
        ws_tgs = I["w_s"].rearrange("(g t) s -> t g s", g=8)
        for k in range(2):
            if k == 0:
                self.load(wsf, wsf[:, :, :], ws_tgs)
                self.load(bT[0], bT[0][:, :], I["b_s"].rearrange("g t -> t g"), slow=True)
                tm = trilm
            else:
                self.op("pool", lambda e: e.memset(wsf[:, :, :], 0.0), W=[wsf])
                self.op("pool", lambda e: e.memset(bT[1][:, :], 0.0), W=[bT[1]])
                for bt in range(4):
                    self.load(wsf, wsf[bt * 16:(bt + 1) * 16, :, bt * 16:(bt + 1) * 16], ws_tgs[0:16, :, 0:16])
                    self.load(bT[1], bT[1][bt * 16:(bt + 1) * 16, :], I["b_s"][:, 0:16].rearrange("g t -> t g"), slow=True)
                tm = trilms
            self.op("dve", lambda e, tm=tm: e.tensor_tensor(out=wsb[:, :, :], in0=wsf[:, :, :],
                                                           in1=tm[:, :].unsqueeze(1).to_broadcast([128, 8, 128]), op=ALU.mult),
                    R=[wsf, tm], W=[wsb])
            self.transposes(wsb, lambda j: wsb[:, j, :], 8, WmT[k], WmT[k][:, :, :])
        u = self.sb("u", [128, D], F32)
        gv = self.sb("gv", [128, D], F32)
        vg = self.sb("vg", [128, D], F32)
        vgb = self.sb("vgb", [128, D], BF16)
        sig = self.sb("sig", [128, D], F32)
        um = self.sb("um", [128, D], BF16)
        umT = self.sb("umT", [128, 8, 128], BF16)
        oT = self.sb("oT", [128, 8, 128], BF16)
        tt = self.sb("tt", [128, D], F32)
        mo_r = self.ring("mo", [128, D], BF16, 2)
        A, B, C = self.A, self.B, self.C
        zi = 0
        for i in range(NOWN):
            k = 0 if i < 16 else 1
            xt = self.X.next()
            self.load(xt, xt[:, :], I["x_own"][i * 128:(i + 1) * 128, :])
            hT = self.HT.next()
            self.normT(xt, self.gb_mix, self.HB, hT)

            def zchunk(col, func, dst, dst_ap):
                nonlocal zi
                P = A[zi % 2]; po = (zi % 2) * 512; zi += 1
                self.proj(P, P.t[:, po:po + 512], hT, Wz, col, 512)
                self.op("act", lambda e: e.activation(out=dst_ap, in_=P.t[:, po:po + 512], func=func), R=[P], W=[dst])
            for n in range(2):
                zchunk(n * 512, AF.Gelu, u, u[:, n * 512:(n + 1) * 512])
            for n in range(2):
                zchunk(1024 + n * 512, AF.Gelu, gv, gv[:, n * 512:(n + 1) * 512])
            self.rms_tok(gv, gv[:, :], D, gb_gm, gb_gm[:, :], vg, vg[:, :])
            if i == 16:
                self.store(O["vg_s"][:, :], vg, vg[0:64, :])
            self.op("act", lambda e: e.copy(out=vgb[:, :], in_=vg[:, :]), R=[vg], W=[vgb])
            for g in range(8):
                Pm = B[g // 4]
                self.mm(Pm, Pm.t[:, g * 128:(g + 1) * 128], WmT[k], WmT[k][:, g, :], vgb, vgb[:, g * 128:(g + 1) * 128], True, True)
            for g in range(8):
                Pm = B[g // 4]
                self.op("dve", lambda e, g=g, Pm=Pm, k=k: e.scalar_tensor_tensor(
                    out=um[:, g * 128:(g + 1) * 128], in0=Pm.t[:, g * 128:(g + 1) * 128], scalar=bT[k][:, g:g + 1],
                    in1=u[:, g * 128:(g + 1) * 128], op0=ALU.add, op1=ALU.mult), R=[Pm, bT[k], u], W=[um])
            self.transposes(self.o_all, lambda j, i=i: self.o_all[:, i, j * 128:(j + 1) * 128], 8, oT, oT[:, :, :])
            for n in range(2):
                self.proj(C[n], C[n].t[:, n * 512:(n + 1) * 512], oT, woa, n * 512, 512)
            for n in range(2):
                zchunk(2048 + n * 512, AF.Sigmoid, sig, sig[:, n * 512:(n + 1) * 512])
            for n in range(2):
                self.op("dve", lambda e, n=n: e.tensor_tensor(out=tt[:, n * 512:(n + 1) * 512], in0=sig[:, n * 512:(n + 1) * 512],
                                                           in1=C[n].t[:, n * 512:(n + 1) * 512], op=ALU.mult), R=[sig, C[n]], W=[tt])
            self.transposes(um, lambda j: um[:, j * 128:(j + 1) * 128], 8, umT, umT[:, :, :])
            for n in range(2):
                self.proj(C[n], C[n].t[:, n * 512:(n + 1) * 512], umT, wob, n * 512, 512)
            for n in range(2):
                zchunk(3072 + n * 512, AF.Sigmoid, sig, sig[:, n * 512:(n + 1) * 512])
            for n in range(2):
                self.op("dve", lambda e, n=n: e.tensor_tensor(out=sig[:, n * 512:(n + 1) * 512], in0=sig[:, n * 512:(n + 1) * 512],
                                                           in1=C[n].t[:, n * 512:(n + 1) * 512], op=ALU.mult), R=[sig, C[n]], W=[sig])
            mo = mo_r.next()
            self.op("dve", lambda e, mo=mo: e.tensor_tensor(out=mo[:, :], in0=sig[:, :], in1=tt[:, :], op=ALU.add),
                    R=[sig, tt], W=[mo])
            self.dma(lambda e, mo=mo, i=i: e.dma_start(out=self.mscr.t[i * 128:(i + 1) * 128, :], in_=mo[:, :]), R=[mo], W=[self.mscr])
        self.pop()

    def phase3b(self):
        I, O = self.I, self.O
        A, B, C, Dp, pT = self.A, self.B, self.C, self.Dp, self.psT
        self.push()
        wo = self.wload("wo", I["w_o"], D, D)
        wcq = self.wload("wcq", I["w_cq"], D, 512)
        wco = self.wload("wco", I["w_co"], 512, D)
        wpq = self.wload("wpq", I["w_pq"], D, 2048)
        gb_xa = self.gbload("gb_xa", I["g_xattn"], D)
        gb_ff = self.gbload("gb_ff", I["g_ffn"], D)
        ones128 = self.sb("ones128", [128, 128], BF16)
        self.load(ones128, ones128[:, :], I["ones128"], q="pool")
        cmask = self.sb("cmask", [128, 255], BF16)
        self.load(cmask, cmask[:, :], I["cmask"], q="pool")
        gcq = self.sb("gcq", [128, 2], F32)
        self.load(gcq, gcq[:, 0:1], I["g_cq"][0:1, 0:128].rearrange("o d -> d o"))
        self.op("dve", lambda e: e.tensor_scalar(out=gcq[:, 1:2], in0=gcq[:, 0:1], scalar1=MEM_SCALE, scalar2=None, op0=ALU.mult),
                R=[gcq], W=[gcq])
        memKT = [self.sb("memKT", [128, 4, 256], BF16) for _ in range(4)]
        memV = [self.sb("memV", [128, 2, 4, 136], BF16) for _ in range(4)]
        for t in memV:
            self.op("pool", lambda e, t=t: e.memset(t[:, :, :, :], 0.0), W=[t])
            self.op("pool", lambda e, t=t: e.memset(t[:, :, :, 128:129], 1.0), W=[t])
        kf = self.sb("kf", [128, 512], F32)
        kb16 = self.sb("kb16", [128, 512], BF16)
        skT = self.sb("skT", [128, 16, 128], BF16)

        self.push()
        wck = self.wload("wck", I["w_ck"], D, 512)
        wcv = self.wload("wcv", I["w_cv"], D, 512)
        gb_mem = self.gbload("gb_mem", I["g_mem"], D)
        gb_ck = self.gbload("gb_ck", I["g_ck"], 128)
        sq4 = self.sb("sq4", [128, 512], F32)
        skb = self.sb("skb", [128, 16, 128], BF16)
        self.load(skb, skb[:, :, :], I["sub_keys"].rearrange("(a n) d -> n a d", a=16), q="pool")
        for half in range(2):
            self.transposes(skb, lambda j, half=half: skb[:, half * 8 + j, :], 8, skT, skT[:, half * 8:(half + 1) * 8, :])
        for mt in range(2):
            xt = self.X.next()
            self.load(xt, xt[:, :], I["mem_p"][mt * 128:(mt + 1) * 128, :])
            hT = self.HT.next()
            self.normT(xt, gb_mem, self.HB, hT)
            P = A[0]
            self.proj(P, P.t[:, 0:512], hT, wck, 0, 512)
            sm = self.small.next()
            self.op("act", lambda e: e.activation(out=sq4[:, :], in_=P.t[:, 0:512], func=AF.Square), R=[P], W=[sq4])
            self.op("dve", lambda e, sm=sm: e.tensor_reduce(out=sm[:, 0:4], in_=sq4[:, :].rearrange("p (h d) -> p h d", h=4),
                                                          axis=AX.X, op=ALU.add), R=[sq4], W=[sm])
            self.op("act", lambda e, sm=sm: e.activation(out=sm[:, 4:8], in_=sm[:, 0:4], func=AF.Sqrt, bias=self.epsb[:, 0:1],
                                                       scale=1.0 / 128), R=[sm, self.epsb], W=[sm])
            self.op("dve", lambda e, sm=sm: e.reciprocal(out=sm[:, 4:8], in_=sm[:, 4:8]), R=[sm], W=[sm])
            self.op("dve", lambda e, sm=sm: e.tensor_tensor(out=kf[:, :].rearrange("p (h d) -> p h d", h=4),
                                                          in0=P.t[:, 0:512].rearrange("p (h d) -> p h d", h=4),
                                                          in1=sm[:, 4:8].unsqueeze(2).to_broadcast([128, 4, 128]), op=ALU.mult),
                    R=[P, sm], W=[kf])
            self.op("dve", lambda e: e.tensor_tensor(out=kf[:, :].rearrange("p (h d) -> p h d", h=4),
                                                   in0=kf[:, :].rearrange("p (h d) -> p h d", h=4),
                                                   in1=gb_ck[:, :].unsqueeze(1).to_broadcast([128, 4, 128]), op=ALU.mult),
                    R=[kf, gb_ck], W=[kf])
            self.store(O["mk_p"][mt * 128:(mt + 1) * 128, :], kf, kf[:, :])
            self.op("act", lambda e: e.copy(out=kb16[:, :], in_=kf[:, :]), R=[kf], W=[kb16])
            self.transposes(kb16, lambda j: kb16[:, j * 128:(j + 1) * 128], 4, memKT[0], memKT[0][:, :, mt * 128:(mt + 1) * 128])
            P2 = A[1]
            self.proj(P2, P2.t[:, 512:1024], hT, wcv, 0, 512)
            self.op("act", lambda e: e.copy(out=sq4[:, :], in_=P2.t[:, 512:1024]), R=[P2], W=[sq4])
            self.store(O["mv_p"][mt * 128:(mt + 1) * 128, :], sq4, sq4[:, :])
            self.op("dve", lambda e, mt=mt: e.tensor_copy(out=memV[0][:, mt, :, 0:128], in_=sq4[:, :].rearrange("p (h d) -> p h d", h=4)),
                    R=[sq4], W=[memV[0]])
        self.pop()

        def load_sample_mem():
            for bt in range(4):
                for mt in range(2):
                    r0 = bt * 256 + mt * 128
                    self.load(kf, kf[:, :], I["memk_c"][r0:r0 + 128, :])
                    self.op("act", lambda e: e.copy(out=kb16[:, :], in_=kf[:, :]), R=[kf], W=[kb16])
                    self.transposes(kb16, lambda j: kb16[:, j * 128:(j + 1) * 128], 4, memKT[bt], memKT[bt][:, :, mt * 128:(mt + 1) * 128])
                    xt = self.X.next()
                    self.load(xt, xt[:, 0:512], I["memv_c"][r0:r0 + 128, :])
                    self.op("dve", lambda e, xt=xt, bt=bt, mt=mt: e.tensor_copy(out=memV[bt][:, mt, :, 0:128],
                                                                           in_=xt[:, 0:512].rearrange("p (h d) -> p h d", h=4)),
                            R=[xt], W=[memV[bt]])

        TS = [(self.sb("x2", [128, D], F32), self.sb("h3b", [128, D], BF16), self.sb("idxT", [128, 128], I32), self.sb("gT", [128, 128], F32))
              for _ in range(2)]
        mi_r = self.ring("mi", [128, D], BF16, 1)
        mT = self.sb("mT", [128, 8, 128], BF16)
        sqc = self.sb("sqc", [128, 512], BF16)
        rsc = self.sb("rsc", [128, 512], F32)
        qcn = self.sb("qcn", [128, 4, 128], BF16)
        ptc = self.sb("ptc", [128, 8, 128], BF16)
        ptcS = [self.sb("ptcS", [128, 8, 128], BF16) for _ in range(4)]
        for t in ptcS:
            self.op("pool", lambda e, t=t: e.memset(t[:, :, :], 0.0), W=[t])
        oc = self.sb("oc", [128, 4, 128], BF16)
        ocT = self.sb("ocT", [128, 4, 128], BF16)
        h3T = self.sb("h3T", [128, 8, 128], BF16)
        qTb = self.sb("qTb", [128, 16, 128], BF16)
        qtok = kb16
        sc = self.sb("sc", [128, 2048], F32)
        wks = [self.sb("wk", [128, 256], F32) for _ in range(2)]
        sv = self.sb("sv", [128, 16, 16], F32)
        si = self.sb("si", [128, 16, 16], U32)
        sif = self.sb("sif", [128, 16, 16], F32)
        ts = self.sb("ts", [128, 8, 16], F32)
        sel = self.sb("sel", [128, 8, 16], U32)
        self_f = self.sb("self_f", [128, 8, 16], F32)
        k1i = self.sb("k1i", [128, 8, 16], I32)
        k1f = self.sb("k1f", [128, 8, 16], F32)
        k2f = self.sb("k2f", [128, 8, 16], F32)
        iota16 = self.sb("iota16", [128, 16], F32)
        for k in range(16):
            self.op("pool", lambda e, k=k: e.memset(iota16[:, k:k + 1], float(k)), W=[iota16])
        eq = self.sb("eq", [128, 4, 16, 16], BF16)
        i12 = self.sb("i12", [128, 2, 128], F32)
        tb = self.sb("tb", [128, 3, 128], BF16)
        ex = self.sb("ex", [128, 8, 16], F32)
        i1T = self.sb("i1T", [128, 128], F32)
        i2T = self.sb("i2T", [128, 128], F32)
        NB = self.dbg.get("NB", 8)
        GUV = self.ring("GUV", [128, 2 * D], BF16, NB)
        wsel_r = self.ring("wsel", [128, 128], BF16, 4)
        djunk_r = self.ring("djunk", [128, D], BF16, 1)
        act_r = self.ring("actc", [128, 1], F32, 6)
        gl_r = self.ring("glc", [128, 2], F32, 6)

        svb = [T(sv.t, Buf("svb")) for _ in range(16)]
        sib = [T(si.t, Buf("sib")) for _ in range(16)]
        tsb = [T(ts.t, Buf("tsb")) for _ in range(8)]
        selb = [T(sel.t, Buf("selb")) for _ in range(8)]

        def pre(i):
            x2, h3b, idxT, gT = TS[i % 2]
            x1 = x2
            if i == 16:
                load_sample_mem()
            xt = self.X.next()
            yield self.load(xt, xt[:, :], I["x_own"][i * 128:(i + 1) * 128, :])
            mi_ = mi_r.next()
            yield self.dma(lambda e, mi_=mi_, i=i: e.dma_start(out=mi_[:, :], in_=self.mscr.t[i * 128:(i + 1) * 128, :]), R=[self.mscr], W=[mi_])
            yield from (None for _ in range(2))
            yield self.transposes(mi_, lambda j, mi_=mi_: mi_[:, j * 128:(j + 1) * 128], 8, mT, mT[:, :, :])
            yield from (None for _ in range(2))
            for n in range(2):
                yield self.proj(Dp, Dp.t[:, 0:512], mT, wo, n * 512, 512)
                yield self.op("dve", lambda e, n=n, xt=xt: e.tensor_tensor(out=x1[:, n * 512:(n + 1) * 512], in0=xt[:, n * 512:(n + 1) * 512],
                                                                 in1=Dp.t[:, 0:512], op=ALU.add), R=[xt, Dp], W=[x1])
            hT = self.HT.next()
            yield self.rms_tok(x1, x1[:, :], D, gb_xa, gb_xa[:, :], self.HB, self.HB[:, :])
            yield from (None for _ in range(3))
            yield self.transposes(self.HB, lambda j: self.HB[:, j * 128:(j + 1) * 128], 8, hT, hT[:, :, :])
            yield from (None for _ in range(2))
            P = Dp
            for hd in range(4):
                for c in range(8):
                    yield self.mm(P, P.t[:, hd * 128:(hd + 1) * 128], wcq, wcq[:, c, hd * 128:(hd + 1) * 128], hT, hT[:, c, :], c == 0, c == 7)
            yield self.op("act", lambda e: e.activation(out=sqc[:, :], in_=P.t[:, 0:512], func=AF.Square), R=[P], W=[sqc])
            yield self.op("act", lambda e: e.copy(out=kf[:, :], in_=P.t[:, 0:512]), R=[P], W=[kf])
            yield from (None for _ in range(2))
            yield self.mm(P, P.t[:, 0:512], ones128, ones128[:, :], sqc, sqc[:, :], True, True)
            yield self.op("act", lambda e: e.activation(out=rsc[:, :], in_=P.t[:, 0:512], func=AF.Sqrt, bias=self.epsb[:, 0:1], scale=1.0),
                    R=[P, self.epsb], W=[rsc])
            yield self.op("dve", lambda e: e.reciprocal(out=rsc[:, :], in_=rsc[:, :]), R=[rsc], W=[rsc])
            yield self.op("dve", lambda e: e.scalar_tensor_tensor(out=qcn[:, :, :].rearrange("p h t -> p (h t)"), in0=kf[:, :], scalar=gcq[:, 1:2],
                                                            in1=rsc[:, :], op0=ALU.mult, op1=ALU.mult), R=[kf, gcq, rsc], W=[qcn])
            yield from (None for _ in range(3))
            if i < 16:
                batches = [(0, 0, 128, ptc)]
            else:
                batches = [(bt, bt * 16, 16, ptcS[bt]) for bt in range(4)]
            for (mb, c0, n, p) in batches:
                for pair in range(2):
                    for hd in (2 * pair, 2 * pair + 1):
                        for mt in range(2):
                            jj = (hd % 2) * 2 + mt
                            yield self.mm(P, P.t[:, jj * 128:jj * 128 + n], memKT[mb], memKT[mb][:, hd, mt * 128:(mt + 1) * 128],
                                          qcn, qcn[:, hd, c0:c0 + n], True, True)
                    yield self.op("act", lambda e, pair=pair, p=p, c0=c0, n=n: e.activation(
                        out=p[:, pair * 4:(pair + 1) * 4, c0:c0 + n],
                        in_=P.t[:, 0:512].rearrange("p (j q) -> p j q", j=4)[:, :, 0:n], func=AF.Exp),
                        R=[P], W=[p])
            yield from (None for _ in range(2))
            nq = 128
            for pair in range(2):
                for hd in (2 * pair, 2 * pair + 1):
                    col = (hd % 2) * 256
                    nacc = len(batches) * 2
                    k = 0
                    for (mb, c0, n, p) in batches:
                        for mt in range(2):
                            yield self.mm(P, P.t[0:nq, col:col + 136], p, p[:, hd * 2 + mt, 0:nq], memV[mb], memV[mb][:, mt, hd, :], k == 0, k == nacc - 1)
                            k += 1
                sm = self.small.next()
                ocv = P.t[:, 0:512].rearrange("p (h c) -> p h c", h=2)
                yield self.op("dve", lambda e, sm=sm, ocv=ocv: e.tensor_scalar(out=sm[:, 0:2], in0=ocv[:, :, 128:129].rearrange("p h o -> p (h o)"),
                                                                     scalar1=1e-30, scalar2=None, op0=ALU.max), R=[P], W=[sm])
                yield self.op("dve", lambda e, sm=sm: e.reciprocal(out=sm[:, 0:2], in_=sm[:, 0:2]), R=[sm], W=[sm])
                yield self.op("dve", lambda e, sm=sm, ocv=ocv, pair=pair: e.tensor_tensor(out=oc[:, 2 * pair:2 * pair + 2, :], in0=ocv[:, :, 0:128],
                                                                                in1=sm[:, 0:2].unsqueeze(2).to_broadcast([128, 2, 128]), op=ALU.mult),
                        R=[P, sm], W=[oc])
            yield from (None for _ in range(2))
            yield self.transposes(oc, lambda j: oc[:, j, :], 4, ocT, ocT[:, :, :])
            yield from (None for _ in range(2))
            for n in range(2):
                yield self.proj(P, P.t[:, 0:512], ocT, wco, n * 512, 512, kc=4)
                yield self.op("dve", lambda e, n=n: e.tensor_tensor(out=x2[:, n * 512:(n + 1) * 512], in0=x1[:, n * 512:(n + 1) * 512],
                                                           in1=P.t[:, 0:512], op=ALU.add), R=[x1, P], W=[x2])
            yield self.rms_tok(x2, x2[:, :], D, gb_ff, gb_ff[:, :], h3b, h3b[:, :])
            yield from (None for _ in range(3))
            yield self.transposes(h3b, lambda j: h3b[:, j * 128:(j + 1) * 128], 8, h3T, h3T[:, :, :])
            yield from (None for _ in range(2))
            for r in range(4):
                for c in range(8):
                    yield self.mm(P, P.t[:, 0:512], h3T, h3T[:, c, :], wpq, wpq[:, c, r * 512:(r + 1) * 512], c == 0, c == 7)
                yield self.op("act", lambda e: e.copy(out=qtok[:, :], in_=P.t[:, 0:512]), R=[P], W=[qtok])
                yield from (None for _ in range(2))
                yield self.transposes(qtok, lambda j: qtok[:, j * 128:(j + 1) * 128], 4, qTb, qTb[:, r * 4:r * 4 + 4, :])
                yield from (None for _ in range(1))
            for r in range(4):
                for hp in range(r * 4, r * 4 + 4):
                    off = (hp % 4) * 128
                    yield self.mm(P, P.t[:, off:off + 128], qTb, qTb[:, hp, :], skT, skT[:, hp, :], True, True)
                yield self.op("act", lambda e, r=r: e.copy(out=sc[:, r * 512:(r + 1) * 512], in_=P.t[:, 0:512]), R=[P], W=[sc])

            def top16_pair(items):
                for stage in range(5):
                    for (src_ap, width, vout, iout, wkb, vb, ib) in items:
                        if stage == 0:
                            self.op("dve", lambda e, vout=vout, src_ap=src_ap: e.max(out=vout[:, 0:8], in_=src_ap), R=[sc], W=[vb])
                        elif stage == 1:
                            self.op("dve", lambda e, vout=vout, iout=iout, src_ap=src_ap: e.max_index(out=iout[:, 0:8], in_max=vout[:, 0:8], in_values=src_ap),
                                    R=[sc, vb], W=[ib])
                        elif stage == 2:
                            self.op("dve", lambda e, vout=vout, src_ap=src_ap, wkb=wkb, width=width: e.match_replace(
                                out=wkb[:, 0:width], in_to_replace=vout[:, 0:8], in_values=src_ap, imm_value=-1e30), R=[sc, vb], W=[wkb])
                        elif stage == 3:
                            self.op("dve", lambda e, vout=vout, wkb=wkb, width=width: e.max(out=vout[:, 8:16], in_=wkb[:, 0:width]), R=[wkb], W=[vb])
                        else:
                            self.op("dve", lambda e, vout=vout, iout=iout, wkb=wkb, width=width: e.max_index(
                                out=iout[:, 8:16], in_max=vout[:, 8:16], in_values=wkb[:, 0:width]), R=[wkb, vb], W=[ib])
                    yield None
            for hp in range(0, 16, 2):
                yield from top16_pair([(sc[:, q * 128:(q + 1) * 128], 128, sv[:, q, :], si[:, q, :], wks[q % 2], svb[q], sib[q]) for q in (hp, hp + 1)])
            yield self.op("dve", lambda e: e.tensor_copy(out=sif[:, :, :], in_=si[:, :, :]), R=sib, W=[sif])
            sv4 = sv[:, :, :].rearrange("p (h two) k -> p h two k", two=2)
            cand = sc[:, :].rearrange("p (h a b) -> p h a b", h=8, a=16)
            yield self.op("dve", lambda e: e.tensor_tensor(out=cand, in0=sv4[:, :, 0, :].unsqueeze(3).to_broadcast([128, 8, 16, 16]),
                                                   in1=sv4[:, :, 1, :].unsqueeze(2).to_broadcast([128, 8, 16, 16]), op=ALU.add),
                    R=svb, W=[sc])
            for hh in range(0, 8, 2):
                yield from top16_pair([(sc[:, q * 256:(q + 1) * 256], 256, ts[:, q, :], sel[:, q, :], wks[q % 2], tsb[q], selb[q]) for q in (hh, hh + 1)])
            yield self.op("dve", lambda e: e.tensor_copy(out=self_f[:, :, :], in_=sel[:, :, :]), R=selb, W=[self_f])
            yield self.op("dve", lambda e: e.tensor_scalar(out=k1i[:, :, :], in0=self_f[:, :, :], scalar1=-7.5, scalar2=0.0625, op0=ALU.add, op1=ALU.mult),
                    R=[self_f], W=[k1i])
            yield self.op("dve", lambda e: e.tensor_copy(out=k1f[:, :, :], in_=k1i[:, :, :]), R=[k1i], W=[k1f])
            yield self.op("dve", lambda e: e.scalar_tensor_tensor(out=k2f[:, :, :], in0=k1f[:, :, :], scalar=-16.0, in1=self_f[:, :, :],
                                                            op0=ALU.mult, op1=ALU.add), R=[k1f, self_f], W=[k2f])
            sif4 = sif[:, :, :].rearrange("p (h two) k -> p h two k", two=2)
            io4 = iota16[:, :].unsqueeze(1).unsqueeze(1).to_broadcast([128, 4, 16, 16])
            for which, kf_ in ((0, k1f), (1, k2f)):
                for h2 in range(2):
                    hs = slice(h2 * 4, h2 * 4 + 4)
                    yield self.op("dve", lambda e, kf_=kf_, hs=hs: e.tensor_tensor(out=eq[:, :, :, :], in0=io4,
                                                                          in1=kf_[:, hs, :].unsqueeze(3).to_broadcast([128, 4, 16, 16]), op=ALU.is_equal),
                            R=[iota16, kf_], W=[eq])
                    yield self.op("pool", lambda e, which=which, hs=hs: e.tensor_tensor(out=eq[:, :, :, :], in0=eq[:, :, :, :],
                                                                              in1=sif4[:, hs, which, :].unsqueeze(2).to_broadcast([128, 4, 16, 16]), op=ALU.mult),
                            R=[eq, sif], W=[eq])
                    yield self.op("dve", lambda e, which=which, h2=h2: e.tensor_reduce(out=i12[:, which, h2 * 64:(h2 + 1) * 64].rearrange("p (h k) -> p h k", h=4),
                                                                              in_=eq[:, :, :, :], axis=AX.X, op=ALU.add), R=[eq], W=[i12])
            yield self.op("act", lambda e: e.copy(out=tb[:, 0:2, :], in_=i12[:, :, :]), R=[i12], W=[tb])
            yield self.op("dve", lambda e: e.tensor_tensor(out=ex[:, :, :], in0=ts[:, :, :], in1=ts[:, :, 0:1].to_broadcast([128, 8, 16]), op=ALU.subtract),
                    R=tsb, W=[ex])
            yield self.op("act", lambda e: e.activation(out=ex[:, :, :], in_=ex[:, :, :], func=AF.Exp), R=[ex], W=[ex])
            sm = self.small.next()
            yield self.op("dve", lambda e, sm=sm: e.tensor_reduce(out=sm[:, 0:8], in_=ex[:, :, :], axis=AX.X, op=ALU.add), R=[ex], W=[sm])
            yield self.op("dve", lambda e, sm=sm: e.reciprocal(out=sm[:, 0:8], in_=sm[:, 0:8]), R=[sm], W=[sm])
            yield self.op("dve", lambda e, sm=sm: e.tensor_tensor(out=tb[:, 2, :].rearrange("p (h k) -> p h k", h=8), in0=ex[:, :, :],
                                                          in1=sm[:, 0:8].unsqueeze(2).to_broadcast([128, 8, 16]), op=ALU.mult),
                    R=[ex, sm], W=[tb])
            yield from (None for _ in range(3))
            for j in range(3):
                yield self.op("pe", lambda e, j=j: e.transpose(out=pT[:, j * 128:(j + 1) * 128], in_=tb[:, j, :], identity=self.ident[:, :]),
                        R=[tb, self.ident], W=[pT])
            yield self.op("act", lambda e: e.copy(out=i1T[:, :], in_=pT[:, 0:128]), R=[pT], W=[i1T])
            yield self.op("act", lambda e: e.copy(out=i2T[:, :], in_=pT[:, 128:256]), R=[pT], W=[i2T])
            yield self.op("dve", lambda e: e.scalar_tensor_tensor(out=idxT[:, :], in0=i1T[:, :], scalar=128.0, in1=i2T[:, :],
                                                            op0=ALU.mult, op1=ALU.add), R=[i1T, i2T], W=[idxT])
            yield self.op("act", lambda e: e.copy(out=gT[:, :], in_=pT[:, 256:384]), R=[pT], W=[gT])
        def loop(i, gen):
            x2, h3b, idxT, gT = TS[i % 2]
            ntok = 128 if i < 16 else 64
            Pacc = [B[0], B[1]]
            st_g, st_x, st_w = {}, {}, {}

            def stageA(t):
                guv = GUV.next()
                st_g[t] = guv
                self.dma(lambda e: e.indirect_dma_start(out=guv[:, :], out_offset=None, in_=self.uvtab.t,
                                                         in_offset=bass.IndirectOffsetOnAxis(ap=idxT[:, t:t + 1], axis=0)),
                         R=[idxT, self.uvtab], W=[guv], q="pool")

            st_a = {}

            def stageB1(t):
                guv = st_g[t]
                ac = act_r.next(); dj = djunk_r.next()
                st_a[t] = ac
                Px = A if order.index(t) % 2 == 0 else C
                tX = Px[0].t
                for n in range(2):
                    self.mm(Px[n], tX[:, n * 512:(n + 1) * 512], self.ident, self.ident[:, t:t + 1].to_broadcast([128, 128]),
                            h3b, h3b[:, n * 512:(n + 1) * 512], True, True)
                self.op("dve", lambda e: e.scalar_tensor_tensor(out=dj[:, :], in0=guv[:, 0:D], scalar=1.0, in1=tX[:, :], op0=ALU.mult, op1=ALU.mult,
                                                                accum_out=ac[:, 0:1]), R=[guv, Px[0], Px[1]], W=[dj, ac])

            def stageB2(t):
                ac = st_a.pop(t)
                g2 = gl_r.next()
                self.op("act", lambda e: e.activation(out=g2[:, 0:1], in_=ac[:, 0:1], func=AF.Gelu), R=[ac], W=[g2])
                self.op("act", lambda e: e.activation(out=g2[:, 1:2], in_=g2[:, 0:1], func=AF.Copy, scale=gT[:, t:t + 1]), R=[g2, gT], W=[g2])
                ws = wsel_r.next()
                st_w[t] = ws
                r32 = t % 32
                self.op("act", lambda e: e.activation(out=ws[:, 0:32], in_=cmask[:, 127 - r32:159 - r32], func=AF.Copy, scale=g2[:, 1:2]),
                        R=[cmask, g2], W=[ws])

            ngrp = ntok // 32
            order = [g * 32 + r for r in range(32) for g in range(ngrp)]

            def stageC2(ts_):
                items = [(t, st_w.pop(t), st_g.pop(t)) for t in ts_]
                for n in range(2):
                    for (t, ws, guv) in items:
                        j = t // 32
                        r = t % 32
                        self.op("pe", lambda e, n=n, ws=ws, guv=guv, j=j, r=r: e.matmul(
                            Pacc[n].t[32 * j:32 * j + 32, n * 512:(n + 1) * 512], lhsT=ws[:, 0:32], rhs=guv[:, D + n * 512:D + (n + 1) * 512],
                            start=(r == 0), stop=(r == 31), tile_position=(0, 32 * j)), R=[ws, guv], W=[Pacc[n]])
            LA = NB - 4
            nst = len(order)
            for step in range(nst + LA + 3):
                if step < nst:
                    stageA(order[step])
                if 0 <= step - LA < nst:
                    stageB1(order[step - LA])
                if 0 <= step - LA - 1 < nst:
                    stageB2(order[step - LA - 1])
                k = step - LA - 2
                if k >= 1 and k % 2 == 1 and k < nst:
                    stageC2([order[k - 1], order[k]])
                if gen is not None:
                    for _ in range(3 if step % 3 == 0 else 2):
                        next(gen, None)
            yo = self.X.next()
            for n in range(2):
                self.op("dve", lambda e, n=n, yo=yo: e.tensor_tensor(out=yo[:, n * 512:(n + 1) * 512], in0=x2[:, n * 512:(n + 1) * 512],
                                                                 in1=Pacc[n].t[:, n * 512:(n + 1) * 512], op=ALU.add), R=[x2, Pacc[n]], W=[yo])
            self.store(O["y_own"][i * 128:(i + 1) * 128, :], yo, yo[:, :])
            if gen is not None:
                for _ in gen:
                    pass

        NADV = self.dbg.get("NADV", 6)
        g0 = pre(0)
        self.npre = 0
        for _ in g0:
            self.npre += 1
        for i in range(NOWN):
            gen = pre(i + 1) if i + 1 < NOWN else None
            loop(i, gen)
        self.pop()


def _own_blocks(j):
    blks = []
    for s in range(16):
        m = s // 2
        if s % 2 == 0:
            blks.append(4 * m + (0 if j == 0 else 1))
        else:
            blks.append(4 * m + (3 if j == 0 else 2))
    return blks


def _consts(j):
    f = np.float32
    c = {}
    c["idn"] = np.eye(128, dtype=f)
    oq = np.zeros((96, 96), f)
    oq[0:64, 0:64] = 1.0 / 64
    oq[64:96, 64:96] = 1.0 / 32
    c["onesq"] = oq
    c["ones128"] = np.full((128, 128), 1.0 / 128, f)
    rm = np.zeros((96, 96), f)
    for d in range(64, 80):
        rm[d + 16, d] = -1.0
    for d in range(80, 96):
        rm[d - 16, d] = 1.0
    c["rotm"] = rm
    cm = np.zeros((128, 255), f)
    cm[:, 127] = 1.0
    c["cmask"] = cm
    c["trilm"] = np.tril(np.ones((128, 128), f))
    tms = np.zeros((128, 128), f)
    for bt in range(4):
        tms[bt * 16:(bt + 1) * 16, bt * 16:(bt + 1) * 16] = np.tril(np.ones((16, 16), f))
    c["trilms"] = tms
    inv = (np.float32(10000.0) ** (-np.arange(16, dtype=f) / np.float32(16))).astype(f)
    pos = np.zeros((128, 33), f)
    for i in range(32):
        pos[:, i] = i * 128 + np.arange(128)
    pos[0:64, 32] = 1024 + (np.arange(64) % 16)
    ang = (pos[:, :, None] * inv[None, None, :]).astype(f)
    c["ropek"] = np.concatenate([np.cos(ang), np.sin(ang)], axis=2).astype(f)
    blks = _own_blocks(j)
    posq = np.zeros((NTOK,), f)
    for s, b in enumerate(blks):
        posq[s * 128:(s + 1) * 128] = b * 128 + np.arange(128)
    posq[2048:2112] = 1024 + (np.arange(64) % 16)
    angq = (posq[None, :] * np.concatenate([inv, inv])[:, None]).astype(f)
    c["cosq"] = np.cos(angq).astype(f)
    c["sinq"] = np.sin(angq).astype(f)
    kk = np.arange(128)[:, None] // 64
    qq = np.arange(128)[None, :] // 64
    diag = (kk <= qq).astype(f)
    full = np.ones((128, 128), f)
    zero = np.zeros((128, 128), f)
    c["masks"] = np.stack([diag, zero, full, diag] if j == 0 else [full, diag, diag, zero]).astype(f)
    return c


_NC_CACHE = {}


def _get_nc(dbg=None):
    key = repr(sorted((dbg or {}).items()))
    if key not in _NC_CACHE:
        k = Kern(dbg)
        _NC_CACHE[key] = k.build()
    return _NC_CACHE[key]


def kernel(x_prompt, x_sample, cache_mla_ckv, cache_mla_krope, cache_mem_k, cache_mem_v, mem_prompt,
           g_mix, w_in, g_q_lat, w_uq, g_qn, g_qr, g_kv_lat, g_kr, w_uk, w_uv, g_kn, w_oa,
           g_gm, w_s, b_s, w_ob, w_o,
           g_xattn, g_mem, w_cq, g_cq, w_ck, g_ck, w_cv, w_co,
           g_ffn, w_pq, sub_keys, peer_u, peer_v, _dbg=None):
    f = np.float32
    A = lambda a: np.ascontiguousarray(np.asarray(a, dtype=f))
    x_prompt = A(x_prompt); x_sample = A(x_sample)
    shared = {
        "w_in": A(w_in)[0], "w_uq": A(w_uq)[0], "w_uk": A(w_uk)[0], "w_uv": A(w_uv)[0],
        "w_oa": A(w_oa)[0], "w_ob": A(w_ob)[0], "w_o": A(w_o)[0],
        "w_cq": A(w_cq)[0], "w_ck": A(w_ck)[0], "w_cv": A(w_cv)[0], "w_co": A(w_co)[0],
        "w_pq": A(w_pq)[0], "sub_keys": A(sub_keys)[0].reshape(2048, 128),
        "peer_u": A(peer_u)[0], "peer_v": A(peer_v)[0],
        "g_mix": A(g_mix), "g_q_lat": A(g_q_lat), "g_qn": A(g_qn), "g_qr": A(g_qr), "g_kv_lat": A(g_kv_lat),
        "g_kr": A(g_kr), "g_kn": A(g_kn), "g_gm": A(g_gm), "g_xattn": A(g_xattn), "g_mem": A(g_mem),
        "g_cq": A(g_cq), "g_ck": A(g_ck), "g_ffn": A(g_ffn),
        "w_s": A(w_s)[0].reshape(1024, 128), "b_s": A(b_s)[0],
    }
    ckv_c = A(cache_mla_ckv)[0]; kr_c = A(cache_mla_krope)[0]
    mk_c = A(cache_mem_k)[0]; mv_c = A(cache_mem_v)[0]; mem_p = A(mem_prompt)
    consts = [_consts(0), _consts(1)]
    in_maps = []
    for c in range(NCORES):
        b, j = c // 2, c % 2
        blks = _own_blocks(j)
        x_own = np.zeros((NTOK, D), f)
        for s, blk in enumerate(blks):
            x_own[s * 128:(s + 1) * 128] = x_prompt[b, blk * 128:(blk + 1) * 128]
        x_own[2048:2112] = x_sample[4 * c:4 * c + 4].reshape(64, D)
        m = dict(shared)
        m.update(consts[j])
        m["x_all"] = x_prompt[b]
        m["x_own"] = x_own
        m["ckv_c"] = np.ascontiguousarray(ckv_c[4 * c:4 * c + 4].reshape(4096, 256))
        m["kr_c"] = np.ascontiguousarray(kr_c[4 * c:4 * c + 4].reshape(4096, 32))
        m["memk_c"] = np.ascontiguousarray(mk_c[4 * c:4 * c + 4].reshape(1024, 512))
        m["memv_c"] = np.ascontiguousarray(mv_c[4 * c:4 * c + 4].reshape(1024, 512))
        m["mem_p"] = mem_p[b]
        in_maps.append(m)
    nc = _get_nc(_dbg)
    res = run_bass_kernel_spmd(nc, in_maps, core_ids=list(range(NCORES)))
    R = res.results
    B, S, DB, DS = 4, 4096, 32, 16
    y_p = np.zeros((B, S, D), f); y_s = np.zeros((DB, DS, D), f)
    ckv_p = np.zeros((1, B, S, 256), f); kr_p = np.zeros((1, B, S, 32), f)
    mk_p = np.zeros((1, B, 256, 4, 128), f); mv_p = np.zeros((1, B, 256, 4, 128), f)
    ckv_s = np.zeros((1, DB, DS, 256), f); kr_s = np.zeros((1, DB, DS, 32), f); vg_s = np.zeros((1, DB, DS, D), f)
    for c in range(NCORES):
        b, j = c // 2, c % 2
        r = R[c]
        for s, blk in enumerate(_own_blocks(j)):
            y_p[b, blk * 128:(blk + 1) * 128] = r["y_own"][s * 128:(s + 1) * 128]
        y_s[4 * c:4 * c + 4] = r["y_own"][2048:2112].reshape(4, 16, D)
        half = slice(j * 2048, (j + 1) * 2048)
        ckv_p[0, b, half] = r["ckv_p"][half]
        kr_p[0, b, half] = r["kr_p"][half]
        mk_p[0, b, j * 128:(j + 1) * 128] = r["mk_p"][j * 128:(j + 1) * 128].reshape(128, 4, 128)
        mv_p[0, b, j * 128:(j + 1) * 128] = r["mv_p"][j * 128:(j + 1) * 128].reshape(128, 4, 128)
        ckv_s[0, 4 * c:4 * c + 4] = r["ckv_s"].reshape(4, 16, 256)
        kr_s[0, 4 * c:4 * c + 4] = r["kr_s"].reshape(4, 16, 32)
        vg_s[0, 4 * c:4 * c + 4] = r["vg_s"].reshape(4, 16, D)
    return (y_p, y_s, ckv_p, kr_p, mk_p, mv_p, ckv_s, kr_s, vg_s)
```

```python
import numpy as np
from contextlib import ExitStack
import concourse.bass as bass
import concourse.mybir as mybir
from concourse.bass_utils import run_bass_kernel_spmd

F32 = mybir.dt.float32
BF16 = mybir.dt.bfloat16
I32 = mybir.dt.int32
U32 = mybir.dt.uint32
ALU = mybir.AluOpType
AF = mybir.ActivationFunctionType
AX = mybir.AxisListType

NCORES = 8
D = 1024
EPS = 1e-6
NOWN = 17
NTOK = NOWN * 128
NKP = 4096
NKS = 4 * 1040
NK = NKP + NKS
MLA_SCALE = 96 ** -0.5
MEM_SCALE = 128 ** -0.5
IN_Q, IN_KV, IN_Z = 0, 384, 672


class Buf:
    __slots__ = ("name", "lw", "rd")

    def __init__(self, name):
        self.name = name
        self.lw = None
        self.rd = {}


class T:
    __slots__ = ("t", "b")

    def __init__(self, t, b):
        self.t = t
        self.b = b

    def __getitem__(self, k):
        return self.t[k]


class _Eng:
    def __init__(self, name, selfsync):
        self.name = name
        self.sem = "e_" + name
        self.count = 0
        self.seen = {}
        self.prog = []
        self.selfsync = selfsync


class _Queue:
    def __init__(self, name, eng, nslots):
        self.name = name
        self.eng = eng
        self.slots = [["q_%s_%d" % (name, i), 0] for i in range(nslots)]
        self.next = 0


class Sched:
    def __init__(self, nc, st, selfsync=True):
        self.nc = nc
        self.eng = {
            "pe": _Eng("pe", False),
            "act": _Eng("act", selfsync),
            "dve": _Eng("dve", selfsync),
            "pool": _Eng("pool", selfsync),
            "sp": _Eng("sp", False),
        }
        self.queues = {
            "sp": _Queue("sp", "sp", 8),
            "pool": _Queue("pool", "pool", 8),
            "act": _Queue("act", "act", 4),
            "conv": _Queue("conv", "pool", 32),
        }
        self.final_tokens = []
        names = [E.sem for E in self.eng.values()]
        for Q in self.queues.values():
            names += [s[0] for s in Q.slots]
        self.sems = {n: st.enter_context(nc.semaphore(n)) for n in names}
        self.ninst = 0

    def _collect(self, E, reads, writes, extra=None):
        need = {}

        def add(tok):
            if tok is None:
                return
            s, v = tok
            if need.get(s, 0) < v:
                need[s] = v
        for b in reads:
            add(b.lw)
        for b in writes:
            add(b.lw)
            for s, v in b.rd.items():
                add((s, v))
        if extra:
            for t in extra:
                add(t)
        waits = []
        for s, v in need.items():
            if s == E.sem and not E.selfsync:
                continue
            if E.seen.get(s, 0) >= v:
                continue
            E.seen[s] = v
            waits.append((s, v))
        return waits

    def _commit(self, tok, reads, writes):
        for b in writes:
            b.lw = tok
            b.rd = {}
        s, v = tok
        for b in reads:
            if b in writes:
                continue
            if b.rd.get(s, 0) < v:
                b.rd[s] = v

    def op(self, eng, fn, R=(), W=()):
        E = self.eng[eng]
        reads = [x.b for x in R]
        writes = [x.b for x in W]
        waits = self._collect(E, reads, writes)
        E.count += 1
        tok = (E.sem, E.count)
        E.prog.append((waits, fn, (E.sem, 1)))
        self._commit(tok, reads, writes)
        return tok

    def dma(self, queue, fn, R=(), W=(), final=False):
        Q = self.queues[queue]
        E = self.eng[Q.eng]
        reads = [x.b for x in R]
        writes = [x.b for x in W]
        slot = Q.slots[Q.next]
        Q.next = (Q.next + 1) % len(Q.slots)
        extra = [(slot[0], slot[1] * 16)] if slot[1] > 0 else None
        waits = self._collect(E, reads, writes, extra)
        slot[1] += 1
        tok = (slot[0], slot[1] * 16)
        E.prog.append((waits, fn, (slot[0], 16)))
        self._commit(tok, reads, writes)
        if final:
            self.final_tokens.append(tok)
        return tok

    def barrier(self):
        toks = []
        for E in self.eng.values():
            if E.count > 0:
                toks.append((E.sem, E.count))
        for qn, Q in self.queues.items():
            if qn == "conv":
                continue
            for s in Q.slots:
                if s[1] > 0:
                    toks.append((s[0], s[1] * 16))
        for E in self.eng.values():
            waits = []
            for s, v in toks:
                if s == E.sem:
                    continue
                if E.seen.get(s, 0) >= v:
                    continue
                E.seen[s] = v
                waits.append((s, v))
            if waits:
                E.prog.append((waits, None, None))

    def flush(self, last=False):
        nc = self.nc
        sems = self.sems
        if last:
            fin = {}
            for s, v in self.final_tokens:
                if fin.get(s, 0) < v:
                    fin[s] = v
            self.eng["sp"].prog.append(([(s, v) for s, v in fin.items()], None, None))

        def run(E):
            prog = E.prog
            E.prog = []
            self.ninst += len(prog)

            def body(e):
                for waits, fn, inc in prog:
                    for s, v in waits:
                        e.wait_ge(sems[s], v)
                    if fn is not None:
                        ins = fn(e)
                        ins.then_inc(sems[inc[0]], inc[1])
            return body
        with nc.Block(no_gpsimd_drain=True) as block:
            block.sync(run(self.eng["sp"]))
            block.tensor(run(self.eng["pe"]))
            block.scalar(run(self.eng["act"]))
            block.vector(run(self.eng["dve"]))
            block.gpsimd(run(self.eng["pool"]))


class Ring:
    def __init__(self, tiles):
        self.tiles = tiles
        self.i = 0

    def next(self):
        t = self.tiles[self.i]
        self.i = (self.i + 1) % len(self.tiles)
        return t


class Kern:
    def __init__(self, dbg=None):
        self.nc = bass.Bass("TRN2", target_bir_lowering=False)
        self.st = ExitStack()
        self.S = Sched(self.nc, self.st)
        self.scopes = [self.st]
        self.dbg = dbg or {}
        self.uid = 0

    def din(self, name, shape, dt=F32):
        return self.nc.dram_tensor(name, list(shape), dt, kind="ExternalInput").ap()

    def dout(self, name, shape, dt=F32):
        return self.nc.dram_tensor(name, list(shape), dt, kind="ExternalOutput").ap()

    def sb(self, name, shape, dt):
        self.uid += 1
        nm = "%s_%d" % (name, self.uid)
        t = self.scopes[-1].enter_context(self.nc.sbuf_tensor(nm, list(shape), dt))
        return T(t, Buf(nm))

    def ring(self, name, shape, dt, n):
        return Ring([self.sb(name, shape, dt) for _ in range(n)])

    def push(self):
        s = ExitStack()
        self.scopes.append(s)
        return s

    def pop(self):
        self.S.barrier()
        self.S.flush()
        s = self.scopes.pop()
        s.close()

    def op(self, eng, fn, R=(), W=()):
        return self.S.op(eng, fn, R, W)

    def dma(self, fn, R=(), W=(), q="sp", final=False):
        return self.S.dma(q, fn, R, W, final)

    def load(self, dst, dst_ap, src_ap, q="sp", slow=False):
        if slow:
            self.dma(lambda e: e.dma_start(out=dst_ap, in_=src_ap, allow_slow_non_contiguous=True), W=[dst], q=q)
        else:
            self.dma(lambda e: e.dma_start(out=dst_ap, in_=src_ap), W=[dst], q=q)

    def store(self, dst_ap, src, src_ap):
        self.dma(lambda e: e.dma_start(out=dst_ap, in_=src_ap), R=[src], final=True)

    def wload(self, name, src, K, N, c0=0):
        kc = K // 128
        w = self.sb(name, [128, kc, N], BF16)
        for c in range(kc):
            self.load(w, w[:, c, :], src[c * 128:(c + 1) * 128, c0:c0 + N], q="pool")
        return w

    def gbload(self, name, src, n):
        g = self.sb(name, [128, n], F32)
        self.load(g, g[:, :], src[0:1, 0:n].partition_broadcast(128))
        return g

    def mm(self, ps, ps_ap, lhsT, lhsT_ap, rhs, rhs_ap, start, stop):
        self.op("pe", lambda e: e.matmul(ps_ap, lhsT=lhsT_ap, rhs=rhs_ap, start=start, stop=stop),
                R=[lhsT, rhs], W=[ps])

    def rstd(self, src, src_ap, n, Dn):
        sm = self.small.next()
        jk = self.junk
        self.op("act", lambda e: e.activation(out=jk[:, 0:n], in_=src_ap, func=AF.Square, accum_out=sm[:, 0:1]),
                R=[src], W=[jk, sm])
        self.op("act", lambda e: e.activation(out=sm[:, 1:2], in_=sm[:, 0:1], func=AF.Sqrt, bias=self.epsb[:, 0:1],
                                              scale=1.0 / Dn), R=[sm, self.epsb], W=[sm])
        self.op("dve", lambda e: e.reciprocal(out=sm[:, 2:3], in_=sm[:, 1:2]), R=[sm], W=[sm])
        return sm, sm[:, 2:3]

    def rms_tok(self, src, src_ap, n, gb, gb_ap, dst, dst_ap):
        sm, col = self.rstd(src, src_ap, n, n)
        self.op("dve", lambda e: e.scalar_tensor_tensor(out=dst_ap, in0=src_ap, scalar=col, in1=gb_ap,
                                                        op0=ALU.mult, op1=ALU.mult), R=[src, sm, gb], W=[dst])

    def transposes(self, src, src_ap_fn, n, dst, dst_ap, rows=128, eng="act"):
        pT = self.psT
        for j in range(n):
            ap = src_ap_fn(j)
            self.op("pe", lambda e, ap=ap, j=j: e.transpose(out=pT[:, j * 128:j * 128 + rows], in_=ap,
                                                             identity=self.ident[0:rows, 0:rows]),
                    R=[src, self.ident], W=[pT])
        view = pT[:, 0:n * 128].rearrange("p (c k) -> p c k", c=n)[:, :, 0:rows]
        if eng == "act":
            self.op("act", lambda e: e.copy(out=dst_ap, in_=view), R=[pT], W=[dst])
        else:
            self.op("dve", lambda e: e.tensor_copy(out=dst_ap, in_=view), R=[pT], W=[dst])

    def normT(self, xt, gb, hb, hT):
        self.rms_tok(xt, xt[:, :], D, gb, gb[:, :], hb, hb[:, :])
        self.transposes(hb, lambda j: hb[:, j * 128:(j + 1) * 128], 8, hT, hT[:, :, :])

    def proj(self, ps, ps_ap, hT, w, c0, n, kc=8):
        for c in range(kc):
            self.mm(ps, ps_ap, hT, hT[:, c, :], w, w[:, c, c0:c0 + n], c == 0, c == kc - 1)

    def build(self):
        nc = self.nc
        I = {}
        O = {}

        def di(name, shape, dt=F32):
            I[name] = self.din(name, shape, dt)

        def do(name, shape, dt=F32):
            O[name] = self.dout(name, shape, dt)
        di("x_all", [4096, D]); di("x_own", [NTOK, D])
        di("ckv_c", [4096, 256]); di("kr_c", [4096, 32])
        di("memk_c", [1024, 512]); di("memv_c", [1024, 512]); di("mem_p", [256, D])
        di("w_in", [D, 4768]); di("w_uq", [384, 1536]); di("w_uk", [256, 1024]); di("w_uv", [256, 1024])
        di("w_oa", [D, D]); di("w_ob", [D, D]); di("w_o", [D, D])
        di("w_cq", [D, 512]); di("w_ck", [D, 512]); di("w_cv", [D, 512]); di("w_co", [512, D])
        di("w_pq", [D, 2048]); di("sub_keys", [2048, 128]); di("peer_u", [16384, D]); di("peer_v", [16384, D])
        for g, n in [("g_mix", D), ("g_q_lat", 384), ("g_qn", 64), ("g_qr", 32), ("g_kv_lat", 256), ("g_kr", 32),
                     ("g_kn", 64), ("g_gm", D), ("g_xattn", D), ("g_mem", D), ("g_cq", 128), ("g_ck", 128),
                     ("g_ffn", D)]:
            di(g, [1, n])
        di("w_s", [1024, 128]); di("b_s", [8, 128])
        di("idn", [128, 128]); di("onesq", [96, 96]); di("ones128", [128, 128]); di("rotm", [96, 96])
        di("cmask", [128, 255]); di("trilm", [128, 128]); di("trilms", [128, 128])
        di("ropek", [128, 33, 32]); di("cosq", [32, NTOK]); di("sinq", [32, NTOK]); di("masks", [4, 128, 128])
        do("y_own", [NTOK, D]); do("ckv_p", [4096, 256]); do("kr_p", [4096, 32])
        do("mk_p", [256, 512]); do("mv_p", [256, 512])
        do("ckv_s", [64, 256]); do("kr_s", [64, 32]); do("vg_s", [64, D])
        self.I, self.O = I, O

        def ps(name, shape, dt):
            t = self.st.enter_context(nc.psum_tensor(name, shape, dt))
            return t
        self.psT = T(ps("psT", [128, 1024], BF16), Buf("psT"))
        tA = ps("psA", [128, 1024], F32); tB = ps("psB", [128, 1024], F32); tC = ps("psC", [128, 1024], F32)
        tD = ps("psD", [128, 512], F32)
        self.A = [T(tA, Buf("A0")), T(tA, Buf("A1"))]
        self.B = [T(tB, Buf("B0")), T(tB, Buf("B1"))]
        self.C = [T(tC, Buf("C0")), T(tC, Buf("C1"))]
        self.Dp = T(tD, Buf("D"))

        self.ident = self.sb("ident", [128, 128], BF16)
        self.load(self.ident, self.ident[:, :], I["idn"], q="pool")
        self.epsb = self.sb("epsb", [128, 1], F32)
        self.op("dve", lambda e: e.memset(self.epsb[:, :], EPS), W=[self.epsb])
        self.mscr = T(nc.dram_tensor("m_scr", [NTOK, D], BF16).ap(), Buf("mscr"))
        self.junk = self.sb("junk", [128, D], BF16)
        self.small = self.ring("small", [128, 8], F32, 4)
        self.X = self.ring("X", [128, D], F32, 2)
        self.HB = self.sb("HB", [128, D], BF16)
        self.HT = self.ring("HT", [128, 8, 128], BF16, 1)

        self.uvtab = T(nc.dram_tensor("uv_bf", [16384, 2 * D], BF16).ap(), Buf("uvtab"))
        self.hscr = T(nc.dram_tensor("h_scr", [NTOK, D], BF16).ap(), Buf("hscr"))
        self._conv_pending = True
        ph = self.dbg.get("phases", "1234")
        self.push()
        self.o_all = self.sb("o_all", [128, NOWN, D], BF16)
        self.op("pool", lambda e: e.memset(self.o_all[:, 16, :], 0.0), W=[self.o_all])
        self.gb_mix = self.gbload("gb_mix", I["g_mix"], D)
        self.push()
        ckvT = self.sb("ckvT", [128, 2, NK], BF16)
        KT = self.sb("KT", [96, NK], BF16)
        cqnT = self.sb("cqnT", [128, 3, NTOK], BF16)
        if "1" in ph:
            self.phase1(ckvT, KT, cqnT)
        if "2" in ph:
            self.phase2(ckvT, KT, cqnT)
        self.pop()
        if "3" in ph:
            self.phase3a()
        self.pop()
        self.emit_conv()
        if "4" in ph:
            self.phase3b()
        self.S.barrier()
        self.S.flush(last=True)
        self.st.close()
        return nc

    def emit_conv(self):
        if not self._conv_pending:
            return
        self._conv_pending = False
        I = self.I
        RCH = 1024
        self.uvparts = []
        for r in range(0, 16384, RCH):
            for which, nm in ((0, "peer_u"), (1, "peer_v")):
                part = T(self.uvtab.t, Buf("uvpart"))
                self.uvparts.append(part)
                self.dma(lambda e, r=r, which=which, nm=nm: e.dma_start(out=self.uvtab.t[r:r + RCH, which * D:(which + 1) * D],
                                                                         in_=I[nm][r:r + RCH, :]), W=[part], q="conv")

    def phase1(self, ckvT, KT, cqnT):
        I, O = self.I, self.O
        self.push()
        Wkv = self.wload("Wkv", I["w_in"], D, 288, IN_KV)
        Wq = self.wload("Wq", I["w_in"], D, 384, IN_Q)
        gb_kv = self.sb("gb_kv", [128, 288], F32)
        self.load(gb_kv, gb_kv[:, 0:256], I["g_kv_lat"][0:1, 0:256].partition_broadcast(128))
        self.load(gb_kv, gb_kv[:, 256:288], I["g_kr"][0:1, 0:32].partition_broadcast(128))
        gb_ql = self.gbload("gb_ql", I["g_q_lat"], 384)
        self.emit_conv()
        ropek = self.sb("ropek", [128, 33, 32], F32)
        self.load(ropek, ropek[:, :, :], I["ropek"])
        kvo_r = self.ring("kvo", [128, 288], F32, 2)
        krn_r = self.ring("krn", [128, 64], F32, 2)
        kvb_r = self.ring("kvb", [128, 384], BF16, 2)
        for t in kvb_r.tiles:
            self.op("pool", lambda e, t=t: e.memset(t[:, :], 0.0), W=[t])
        cqb = self.sb("cqb", [128, 384], BF16)
        pT = self.psT

        def finish_kv(kvo, kind, idx, pT):
            kvb = kvb_r.next()
            self.op("act", lambda e: e.copy(out=kvb[:, 0:256], in_=kvo[:, 0:256]), R=[kvo], W=[kvb])
            self.op("act", lambda e: e.copy(out=kvb[:, 320:352], in_=kvo[:, 256:288]), R=[kvo], W=[kvb])
            for j in range(3):
                self.op("pe", lambda e, j=j: e.transpose(out=pT[:, j * 128:(j + 1) * 128], in_=kvb[:, j * 128:(j + 1) * 128],
                                                       identity=self.ident[:, :]), R=[kvb, self.ident], W=[pT])
            if kind == "snew":
                dst = ckvT[:, :, NKP:NK].rearrange("p c (b k) -> p c b k", b=4)[:, :, :, 1024:1040]
                src = pT[:, 0:256].rearrange("p (c b k) -> p c b k", c=2, b=8)[:, :, 0:4, :]
                self.op("act", lambda e: e.copy(out=dst, in_=src), R=[pT], W=[ckvT])
                dstk = KT[64:96, NKP:NK].rearrange("p (b k) -> p b k", b=4)[:, :, 1024:1040]
                srck = pT[64:96, 256:320].rearrange("p (b k) -> p b k", b=4)
                self.op("act", lambda e: e.copy(out=dstk, in_=srck), R=[pT], W=[KT])
            else:
                c0 = idx * 128 if kind == "p" else NKP + (idx // 8) * 1040 + (idx % 8) * 128
                src = pT[:, 0:256].rearrange("p (c k) -> p c k", c=2)
                if "ckv" not in self.dbg.get("fk_skip", ""):
                    self.op("act", lambda e: e.copy(out=ckvT[:, :, c0:c0 + 128], in_=src), R=[pT], W=[ckvT])
                if "kt" not in self.dbg.get("fk_skip", ""):
                    self.op("act", lambda e: e.copy(out=KT[64:96, c0:c0 + 128], in_=pT[64:96, 256:384]), R=[pT], W=[KT])

        class TV:
            def __init__(self, ap, b):
                self.t = ap
                self.b = b

            def __getitem__(self, k):
                return self.t[k]
        psTs = [self.psT, TV(self.Dp.t[:, :].bitcast(BF16), self.Dp.b)]
        HBs = [self.HB, self.sb("HB2", [128, D], BF16)]
        HTs = [self.sb("HTa", [128, 8, 128], BF16), self.sb("HTb", [128, 8, 128], BF16)]
        cqbs = [cqb, self.sb("cqb2", [128, 384], BF16)]
        glob_psT, glob_HB = self.psT, self.HB

        def use(slot):
            self.psT = psTs[slot]
            self.HB = HBs[slot]

        def kv_tile(i, slot):
            xt = self.X.next()
            src = I["x_all"][i * 128:(i + 1) * 128, :] if i < 32 else I["x_own"][16 * 128:17 * 128, :]
            self.load(xt, xt[:, :], src)
            yield
            use(slot)
            hT = HTs[slot]
            self.rms_tok(xt, xt[:, :], D, self.gb_mix, self.gb_mix[:, :], self.HB, self.HB[:, :])
            yield
            use(slot)
            hb = self.HB
            self.transposes(hb, lambda j: hb[:, j * 128:(j + 1) * 128], 8, hT, hT[:, :, :])
            yield
            zp = self.A[slot]
            zoff = slot * 512
            z = zp.t[:, zoff:zoff + 288]
            self.proj(zp, z, hT, Wkv, 0, 288)
            yield
            kvo = kvo_r.next()
            krn = krn_r.next()
            self.rms_tok(zp, zp.t[:, zoff:zoff + 256], 256, gb_kv, gb_kv[:, 0:256], kvo, kvo[:, 0:256])
            self.rms_tok(zp, zp.t[:, zoff + 256:zoff + 288], 32, gb_kv, gb_kv[:, 256:288], krn, krn[:, 0:32])
            yield
            cs = ropek[:, i, 0:16]
            sn = ropek[:, i, 16:32]
            self.op("dve", lambda e: e.tensor_tensor(out=krn[:, 32:48], in0=krn[:, 0:16], in1=cs, op=ALU.mult), R=[krn, ropek], W=[krn])
            self.op("dve", lambda e: e.tensor_tensor(out=krn[:, 48:64], in0=krn[:, 16:32], in1=sn, op=ALU.mult), R=[krn, ropek], W=[krn])
            self.op("dve", lambda e: e.tensor_tensor(out=kvo[:, 256:272], in0=krn[:, 32:48], in1=krn[:, 48:64], op=ALU.subtract), R=[krn], W=[kvo])
            self.op("dve", lambda e: e.tensor_tensor(out=krn[:, 32:48], in0=krn[:, 0:16], in1=sn, op=ALU.mult), R=[krn, ropek], W=[krn])
            self.op("dve", lambda e: e.tensor_tensor(out=krn[:, 48:64], in0=krn[:, 16:32], in1=cs, op=ALU.mult), R=[krn, ropek], W=[krn])
            self.op("dve", lambda e: e.tensor_tensor(out=kvo[:, 272:288], in0=krn[:, 32:48], in1=krn[:, 48:64], op=ALU.add), R=[krn], W=[kvo])
            if i < 32:
                self.store(O["ckv_p"][i * 128:(i + 1) * 128, :], kvo, kvo[:, 0:256])
                self.store(O["kr_p"][i * 128:(i + 1) * 128, :], kvo, kvo[:, 256:288])
            else:
                self.store(O["ckv_s"][:, :], kvo, kvo[0:64, 0:256])
                self.store(O["kr_s"][:, :], kvo, kvo[0:64, 256:288])
            yield
            use(slot)
            finish_kv(kvo, "p" if i < 32 else "snew", i if i < 32 else 0, self.psT)
            yield

        def cache_tile(idx, slot):
            kvo = kvo_r.next()
            self.load(kvo, kvo[:, 0:256], I["ckv_c"][idx * 128:(idx + 1) * 128, :])
            self.load(kvo, kvo[:, 256:288], I["kr_c"][idx * 128:(idx + 1) * 128, :])
            yield
            use(slot)
            finish_kv(kvo, "scache", idx, self.psT)
            yield

        def q_tile(i, slot):
            xt = self.X.next()
            self.load(xt, xt[:, :], I["x_own"][i * 128:(i + 1) * 128, :])
            yield
            use(slot)
            hT = HTs[slot]
            self.rms_tok(xt, xt[:, :], D, self.gb_mix, self.gb_mix[:, :], self.HB, self.HB[:, :])
            yield
            use(slot)
            hb = self.HB
            self.transposes(hb, lambda j: hb[:, j * 128:(j + 1) * 128], 8, hT, hT[:, :, :])
            yield
            zp = self.A[slot]
            zoff = slot * 512
            self.proj(zp, zp.t[:, zoff:zoff + 384], hT, Wq, 0, 384)
            yield
            cq_ = cqbs[slot]
            self.rms_tok(zp, zp.t[:, zoff:zoff + 384], 384, gb_ql, gb_ql[:, :], cq_, cq_[:, :])
            yield
            use(slot)
            self.transposes(cq_, lambda j: cq_[:, j * 128:(j + 1) * 128], 3, cqnT, cqnT[:, :, i * 128:(i + 1) * 128])
            yield

        def run2(makers):
            pending = list(makers)
            active = {}
            while pending or active:
                for slot in (0, 1):
                    if slot not in active and pending:
                        active[slot] = pending.pop(0)(slot)
                for slot in (0, 1):
                    g = active.get(slot)
                    if g is None:
                        continue
                    try:
                        next(g)
                    except StopIteration:
                        del active[slot]
        mk = [(lambda slot, i=i: kv_tile(i, slot)) for i in self.dbg.get("p1_new", list(range(33)))]
        mk += [(lambda slot, idx=idx: cache_tile(idx, slot)) for idx in range(self.dbg.get("p1_cache", 32))]
        mk += [(lambda slot, i=i: q_tile(i, slot)) for i in range(self.dbg.get("p1_q", NOWN))]
        run2(mk)
        self.psT, self.HB = glob_psT, glob_HB
        self.pop()

    def phase2(self, ckvT, KT, cqnT):
        I, O = self.I, self.O
        self.push()
        wuq = self.wload("wuq", I["w_uq"], 384, 1536)
        wuk = self.wload("wuk", I["w_uk"], 256, 1024)
        wuv = self.wload("wuv", I["w_uv"], 256, 1024)
        onesq = self.sb("onesq", [96, 96], BF16)
        self.load(onesq, onesq[:, :], I["onesq"], q="pool")
        rotm = self.sb("rotm", [96, 96], BF16)
        self.load(rotm, rotm[:, :], I["rotm"], q="pool")
        gq = self.sb("gq", [96, 2], F32)
        self.load(gq, gq[0:64, 0:1], I["g_qn"][0:1, 0:64].rearrange("o d -> d o"))
        self.load(gq, gq[64:96, 0:1], I["g_qr"][0:1, 0:32].rearrange("o d -> d o"))
        self.op("dve", lambda e: e.tensor_scalar(out=gq[:, 1:2], in0=gq[:, 0:1], scalar1=MLA_SCALE, scalar2=None, op0=ALU.mult),
                R=[gq], W=[gq])
        gkn = self.sb("gkn", [64, 1], F32)
        self.load(gkn, gkn[:, :], I["g_kn"][0:1, 0:64].rearrange("o d -> d o"))
        cosq = self.sb("cosq", [96, NTOK], F32)
        sinq = self.sb("sinq", [96, NTOK], F32)
        self.load(cosq, cosq[64:96, :], I["cosq"])
        self.load(sinq, sinq[64:96, :], I["sinq"])
        maskT = self.sb("maskT", [128, 4, 128], BF16)
        self.load(maskT, maskT[:, :, :], I["masks"].rearrange("m k q -> k m q"), q="pool")
        NVT = 32 + 36
        V = self.sb("V", [128, NVT, 2, 72], BF16)
        self.op("pool", lambda e: e.memset(V[:, :, :, :], 0.0), W=[V])
        self.op("pool", lambda e: e.memset(V[:, :, :, 64:65], 1.0), W=[V])
        QT = self.sb("QT", [96, NTOK], BF16)
        sq_r = self.ring("sq", [96, 512], BF16, 2)
        rs_r = self.ring("rs", [96, 512], F32, 2)
        tq_r = self.ring("tq", [96, 512], F32, 2)
        t1 = self.sb("t1", [96, 512], F32)
        t2 = self.sb("t2", [96, 512], F32)
        pt_r = self.ring("pt", [128, 4, 128], BF16, 3)
        ptS = [self.sb("ptS", [128, 9, 64], BF16) for _ in range(4)]
        for t in ptS:
            self.op("pool", lambda e, t=t: e.memset(t[:, :, :], 0.0), W=[t])
        A, B, C, Dp = self.A, self.B, self.C, self.Dp
        eps96 = self.epsb

        def fm_norm(ps, ps_ap, rows, n, ones_ap, gcol, gcol_ap, dst, dst_ap):
            sq = sq_r.next()
            rs = rs_r.next()
            self.op("act", lambda e: e.activation(out=sq[0:rows, 0:n], in_=ps_ap, func=AF.Square), R=[ps], W=[sq])
            self.mm(B[0], B[0].t[0:rows, 0:n], onesq, ones_ap, sq, sq[0:rows, 0:n], True, True)
            self.op("act", lambda e: e.activation(out=rs[0:rows, 0:n], in_=B[0].t[0:rows, 0:n], func=AF.Sqrt,
                                                  bias=eps96[0:rows, 0:1], scale=1.0), R=[B[0], eps96], W=[rs])
            tq = tq_r.next()
            self.op("act", lambda e: e.activation(out=tq[0:rows, 0:n], in_=ps_ap, func=AF.Copy, scale=gcol_ap), R=[ps, gcol], W=[tq])
            self.op("dve", lambda e: e.reciprocal(out=rs[0:rows, 0:n], in_=rs[0:rows, 0:n]), R=[rs], W=[rs])
            self.op("pool", lambda e: e.tensor_tensor(out=dst_ap, in0=tq[0:rows, 0:n], in1=rs[0:rows, 0:n], op=ALU.mult), R=[tq, rs], W=[dst])

        kchunks = [(c0, min(512, NK - c0)) for c0 in range(0, NK, 512)]
        qchunks = [(c0, min(512, NTOK - c0)) for c0 in range(0, NTOK, 512)]
        vt = [(i, i * 128, 128) for i in range(32)]
        for bt in range(4):
            for j in range(9):
                vt.append((32 + bt * 9 + j, NKP + bt * 1040 + j * 128, 128 if j < 8 else 16))
        nh = self.dbg.get("nheads", 16)
        pi = 0
        for h in range(nh):
            for (c0, n) in kchunks:
                P = A[pi % 2]; po = (pi % 2) * 512; pi += 1
                for c in range(2):
                    self.mm(P, P.t[0:64, po:po + n], wuk, wuk[:, c, h * 64:(h + 1) * 64], ckvT, ckvT[:, c, c0:c0 + n], c == 0, c == 1)
                fm_norm(P, P.t[0:64, po:po + n], 64, n, onesq[0:64, 0:64], gkn, gkn[:, 0:1], KT, KT[0:64, c0:c0 + n])
            p2c = self.dbg.get("p2_cut", 9)
            if p2c < 2:
                continue
            if h % 2 == 0:
                for g0 in range(0, NVT, 4):
                    P = A[pi % 2]; po = (pi % 2) * 512; pi += 1
                    grp = vt[g0:g0 + 4]
                    for (ti, c0, rows) in grp:
                        jj = ti - g0
                        for c in range(2):
                            self.mm(P, P.t[0:rows, po + jj * 128:po + jj * 128 + 128], ckvT, ckvT[:, c, c0:c0 + rows],
                                    wuv, wuv[:, c, h * 64:(h + 2) * 64], c == 0, c == 1)
                    ng = len(grp)
                    src = P.t[:, po:po + ng * 128].rearrange("p (j a d) -> p j a d", j=ng, a=2)
                    self.op("act", lambda e, src=src, g0=g0, ng=ng: e.copy(out=V[:, g0:g0 + ng, :, 0:64], in_=src), R=[P], W=[V])
            if p2c < 3:
                continue
            for (c0, n) in qchunks:
                P = A[pi % 2]; po = (pi % 2) * 512; pi += 1
                for c in range(3):
                    self.mm(P, P.t[0:96, po:po + n], wuq, wuq[:, c, h * 96:(h + 1) * 96], cqnT, cqnT[:, c, c0:c0 + n], c == 0, c == 2)
                fm_norm(P, P.t[0:96, po:po + n], 96, n, onesq[:, :], gq, gq[:, 1:2], QT, QT[0:96, c0:c0 + n])
                self.mm(B[1], B[1].t[0:96, 512:512 + n], rotm, rotm[:, :], QT, QT[0:96, c0:c0 + n], True, True)
                self.op("dve", lambda e, c0=c0, n=n: e.tensor_tensor(out=t1[64:96, 0:n], in0=QT[64:96, c0:c0 + n], in1=cosq[64:96, c0:c0 + n], op=ALU.mult),
                        R=[QT, cosq], W=[t1])
                self.op("dve", lambda e, c0=c0, n=n: e.tensor_tensor(out=t2[64:96, 0:n], in0=B[1].t[64:96, 512:512 + n], in1=sinq[64:96, c0:c0 + n], op=ALU.mult),
                        R=[B[1], sinq], W=[t2])
                self.op("dve", lambda e, c0=c0, n=n: e.tensor_tensor(out=QT[64:96, c0:c0 + n], in0=t1[64:96, 0:n], in1=t2[64:96, 0:n], op=ALU.add),
                        R=[t1, t2], W=[QT])
            if p2c < 4:
                continue
            si = 0
            groups = []
            for s in range(16):
                nkb = 4 * (s // 2) + (2 if s % 2 == 0 else 4)
                for g0 in range(0, nkb, 4):
                    groups.append((s, nkb, list(range(g0, min(g0 + 4, nkb)))))
            Obank = [(Dp, 0), (B[1], 512)]

            def qk(gi):
                s, nkb, blks = groups[gi]
                Sp = C[gi % 2]; so = (gi % 2) * 512
                for j, kb in enumerate(blks):
                    self.mm(Sp, Sp.t[:, so + j * 128:so + (j + 1) * 128], KT, KT[0:96, kb * 128:(kb + 1) * 128],
                            QT, QT[0:96, s * 128:(s + 1) * 128], True, True)
            qk(0)
            for gi, (s, nkb, blks) in enumerate(groups):
                if gi + 1 < len(groups):
                    qk(gi + 1)
                Sp = C[gi % 2]; so = (gi % 2) * 512
                Ops, oo = Obank[s % 2]
                pt = pt_r.next()
                nb = len(blks)
                self.op("act", lambda e, Sp=Sp, so=so, nb=nb, pt=pt: e.activation(
                    out=pt[:, 0:nb, :], in_=Sp.t[:, so:so + nb * 128].rearrange("p (j q) -> p j q", j=nb), func=AF.Exp),
                    R=[Sp], W=[pt])
                for j, kb in enumerate(blks):
                    if kb >= nkb - 2:
                        mi = (0 if s % 2 == 0 else 2) + (kb - (nkb - 2))
                        self.op("dve", lambda e, pt=pt, j=j, mi=mi: e.tensor_tensor(out=pt[:, j, :], in0=pt[:, j, :], in1=maskT[:, mi, :], op=ALU.mult),
                                R=[pt, maskT], W=[pt])
                for j, kb in enumerate(blks):
                    self.mm(Ops, Ops.t[:, oo:oo + 72], pt, pt[:, j, :], V, V[:, kb, h % 2, :], kb == 0, kb == nkb - 1)
                if blks[-1] == nkb - 1:
                    sm = self.small.next()
                    self.op("dve", lambda e, sm=sm, Ops=Ops, oo=oo: e.reciprocal(out=sm[:, 0:1], in_=Ops.t[:, oo + 64:oo + 65]), R=[Ops], W=[sm])
                    self.op("dve", lambda e, sm=sm, Ops=Ops, oo=oo, s=s, h=h: e.tensor_scalar(out=self.o_all[:, s, h * 64:(h + 1) * 64], in0=Ops.t[:, oo:oo + 64],
                                                                                   scalar1=sm[:, 0:1], scalar2=None, op0=ALU.mult),
                            R=[Ops, sm], W=[self.o_all])
            si = len(groups)
            if p2c < 5:
                continue
            Ops = Dp
            for bt in range(4):
                base = NKP + bt * 1040
                Sp = C[si % 2]; so = (si % 2) * 512; si += 1
                qc = 2048 + bt * 16
                for j in range(8):
                    self.mm(Sp, Sp.t[:, so + j * 16:so + (j + 1) * 16], KT, KT[0:96, base + j * 128:base + (j + 1) * 128],
                            QT, QT[0:96, qc:qc + 16], True, True)
                self.mm(Sp, Sp.t[0:16, so + 128:so + 144], KT, KT[0:96, base + 1024:base + 1040], QT, QT[0:96, qc:qc + 16], True, True)
                p = ptS[bt]
                self.op("act", lambda e, Sp=Sp, so=so, p=p, bt=bt: e.activation(
                    out=p[:, 0:8, bt * 16:(bt + 1) * 16], in_=Sp.t[:, so:so + 128].rearrange("p (j q) -> p j q", j=8), func=AF.Exp),
                    R=[Sp], W=[p])
                self.op("act", lambda e, Sp=Sp, so=so, p=p, bt=bt: e.activation(
                    out=p[0:16, 8, bt * 16:(bt + 1) * 16], in_=Sp.t[0:16, so + 128:so + 144], func=AF.Exp), R=[Sp], W=[p])
                for j in range(9):
                    rows = 128 if j < 8 else 16
                    self.mm(Ops, Ops.t[0:64, 0:72], p, p[0:rows, j, :], V, V[0:rows, 32 + bt * 9 + j, h % 2, :],
                            bt == 0 and j == 0, bt == 3 and j == 8)
            sm = self.small.next()
            self.op("dve", lambda e, sm=sm, Ops=Ops: e.reciprocal(out=sm[0:64, 0:1], in_=Ops.t[0:64, 64:65]), R=[Ops], W=[sm])
            self.op("dve", lambda e, sm=sm, Ops=Ops, h=h: e.tensor_scalar(out=self.o_all[0:64, 16, h * 64:(h + 1) * 64], in0=Ops.t[0:64, 0:64],
                                                                     scalar1=sm[0:64, 0:1], scalar2=None, op0=ALU.mult),
                    R=[Ops, sm], W=[self.o_all])
        self.pop()

    def phase3a(self):
        I, O = self.I, self.O
        self.push()
        Wz = self.wload("Wz", I["w_in"], D, 4096, IN_Z)
        woa = self.wload("woa", I["w_oa"], D, D)
        wob = self.wload("wob", I["w_ob"], D, D)
        gb_gm = self.gbload("gb_gm", I["g_gm"], D)
        trilm = self.sb("trilm", [128, 128], F32)
        trilms = self.sb("trilms", [128, 128], F32)
        self.load(trilm, trilm[:, :], I["trilm"])
        self.load(trilms, trilms[:, :], I["trilms"])
        wsf = self.sb("wsf", [128, 8, 128], F32)
        wsb = self.sb("wsb", [128, 8, 128], BF16)
        WmT = [self.sb("WmT", [128, 8, 128], BF16) for _ in range(2)]
        bT = [self.sb("bT", [128, 8], F32) for _ in range(2)]
        ws_tgs = I["w_s"].rearrange("(g t) s -> t g s", g=8)
        for k in range(2):
            if k == 0:
                self.load(wsf, wsf[:, :, :], ws_tgs)
                self.load(bT[0], bT[0][:, :], I["b_s"].rearrange("g t -> t g"), slow=True)
                tm = trilm
            else:
                self.op("pool", lambda e: e.memset(wsf[:, :, :], 0.0), W=[wsf])
                self.op("pool", lambda e: e.memset(bT[1][:, :], 0.0), W=[bT[1]])
                for bt in range(4):
                    self.load(wsf, wsf[bt * 16:(bt + 1) * 16, :, bt * 16:(bt + 1) * 16], ws_tgs[0:16, :, 0:16])
                    self.load(bT[1], bT[1][bt * 16:(bt + 1) * 16, :], I["b_s"][:, 0:16].rearrange("g t -> t g"), slow=True)
                tm = trilms
            self.op("dve", lambda e, tm=tm: e.tensor_tensor(out=wsb[:, :, :], in0=wsf[:, :, :],
                                                           in1=tm[:, :].unsqueeze(1).to_broadcast([128, 8, 128]), op=ALU.mult),
                    R=[wsf, tm], W=[wsb])
            self.transposes(wsb, lambda j: wsb[:, j, :], 8, WmT[k], WmT[k][:, :, :])
        u = self.sb("u", [128, D], F32)
        gv = self.sb("gv", [128, D], F32)
        vg = self.sb("vg", [128, D], F32)
        vgb = self.sb("vgb", [128, D], BF16)
        sig = self.sb("sig", [128, D], F32)
        um = self.sb("um", [128, D], BF16)
        umT = self.sb("umT", [128, 8, 128], BF16)
        oT = self.sb("oT", [128, 8, 128], BF16)
        tt = self.sb("tt", [128, D], F32)
        mo_r = self.ring("mo", [128, D], BF16, 2)
        A, B, C = self.A, self.B, self.C
        zi = 0
        for i in range(NOWN):
            k = 0 if i < 16 else 1
            xt = self.X.next()
            self.load(xt, xt[:, :], I["x_own"][i * 128:(i + 1) * 128, :])
            hT = self.HT.next()
            self.normT(xt, self.gb_mix, self.HB, hT)

            def zchunk(col, func, dst, dst_ap):
                nonlocal zi
                P = A[zi % 2]; po = (zi % 2) * 512; zi += 1
                self.proj(P, P.t[:, po:po + 512], hT, Wz, col, 512)
                self.op("act", lambda e: e.activation(out=dst_ap, in_=P.t[:, po:po + 512], func=func), R=[P], W=[dst])
            for n in range(2):
                zchunk(n * 512, AF.Gelu, u, u[:, n * 512:(n + 1) * 512])
            for n in range(2):
                zchunk(1024 + n * 512, AF.Gelu, gv, gv[:, n * 512:(n + 1) * 512])
            self.rms_tok(gv, gv[:, :], D, gb_gm, gb_gm[:, :], vg, vg[:, :])
            if i == 16:
                self.store(O["vg_s"][:, :], vg, vg[0:64, :])
            self.op("act", lambda e: e.copy(out=vgb[:, :], in_=vg[:, :]), R=[vg], W=[vgb])
            for g in range(8):
                Pm = B[g // 4]
                self.mm(Pm, Pm.t[:, g * 128:(g + 1) * 128], WmT[k], WmT[k][:, g, :], vgb, vgb[:, g * 128:(g + 1) * 128], True, True)
            for g in range(8):
                Pm = B[g // 4]
                self.op("dve", lambda e, g=g, Pm=Pm, k=k: e.scalar_tensor_tensor(
                    out=um[:, g * 128:(g + 1) * 128], in0=Pm.t[:, g * 128:(g + 1) * 128], scalar=bT[k][:, g:g + 1],
                    in1=u[:, g * 128:(g + 1) * 128], op0=ALU.add, op1=ALU.mult), R=[Pm, bT[k], u], W=[um])
            self.transposes(self.o_all, lambda j, i=i: self.o_all[:, i, j * 128:(j + 1) * 128], 8, oT, oT[:, :, :])
            for n in range(2):
                self.proj(C[n], C[n].t[:, n * 512:(n + 1) * 512], oT, woa, n * 512, 512)
            for n in range(2):
                zchunk(2048 + n * 512, AF.Sigmoid, sig, sig[:, n * 512:(n + 1) * 512])
            for n in range(2):
                self.op("dve", lambda e, n=n: e.tensor_tensor(out=tt[:, n * 512:(n + 1) * 512], in0=sig[:, n * 512:(n + 1) * 512],
                                                           in1=C[n].t[:, n * 512:(n + 1) * 512], op=ALU.mult), R=[sig, C[n]], W=[tt])
            self.transposes(um, lambda j: um[:, j * 128:(j + 1) * 128], 8, umT, umT[:, :, :])
            for n in range(2):
                self.proj(C[n], C[n].t[:, n * 512:(n + 1) * 512], umT, wob, n * 512, 512)
            for n in range(2):
                zchunk(3072 + n * 512, AF.Sigmoid, sig, sig[:, n * 512:(n + 1) * 512])
            for n in range(2):
                self.op("dve", lambda e, n=n: e.tensor_tensor(out=sig[:, n * 512:(n + 1) * 512], in0=sig[:, n * 512:(n + 1) * 512],
                                                           in1=C[n].t[:, n * 512:(n + 1) * 512], op=ALU.mult), R=[sig, C[n]], W=[sig])
            mo = mo_r.next()
            self.op("dve", lambda e, mo=mo: e.tensor_tensor(out=mo[:, :], in0=sig[:, :], in1=tt[:, :], op=ALU.add),
                    R=[sig, tt], W=[mo])
            self.dma(lambda e, mo=mo, i=i: e.dma_start(out=self.mscr.t[i * 128:(i + 1) * 128, :], in_=mo[:, :]), R=[mo], W=[self.mscr])
        self.pop()

    def phase3b(self):
        I, O = self.I, self.O
        A, B, C, Dp, pT = self.A, self.B, self.C, self.Dp, self.psT
        self.push()
        wo = self.wload("wo", I["w_o"], D, D)
        wcq = self.wload("wcq", I["w_cq"], D, 512)
        wco = self.wload("wco", I["w_co"], 512, D)
        wpq = self.wload("wpq", I["w_pq"], D, 2048)
        gb_xa = self.gbload("gb_xa", I["g_xattn"], D)
        gb_ff = self.gbload("gb_ff", I["g_ffn"], D)
        ones128 = self.sb("ones128", [128, 128], BF16)
        self.load(ones128, ones128[:, :], I["ones128"], q="pool")
        cmask = self.sb("cmask", [128, 255], BF16)
        self.load(cmask, cmask[:, :], I["cmask"], q="pool")
        gcq = self.sb("gcq", [128, 2], F32)
        self.load(gcq, gcq[:, 0:1], I["g_cq"][0:1, 0:128].rearrange("o d -> d o"))
        self.op("dve", lambda e: e.tensor_scalar(out=gcq[:, 1:2], in0=gcq[:, 0:1], scalar1=MEM_SCALE, scalar2=None, op0=ALU.mult),
                R=[gcq], W=[gcq])
        memKT = [self.sb("memKT", [128, 4, 256], BF16) for _ in range(4)]
        memV = [self.sb("memV", [128, 2, 4, 136], BF16) for _ in range(4)]
        for t in memV:
            self.op("pool", lambda e, t=t: e.memset(t[:, :, :, :], 0.0), W=[t])
            self.op("pool", lambda e, t=t: e.memset(t[:, :, :, 128:129], 1.0), W=[t])
        kf = self.sb("kf", [128, 512], F32)
        kb16 = self.sb("kb16", [128, 512], BF16)
        skT = self.sb("skT", [128, 16, 128], BF16)

        self.push()
        wck = self.wload("wck", I["w_ck"], D, 512)
        wcv = self.wload("wcv", I["w_cv"], D, 512)
        gb_mem = self.gbload("gb_mem", I["g_mem"], D)
        gb_ck = self.gbload("gb_ck", I["g_ck"], 128)
        sq4 = self.sb("sq4", [128, 512], F32)
        skb = self.sb("skb", [128, 16, 128], BF16)
        self.load(skb, skb[:, :, :], I["sub_keys"].rearrange("(a n) d -> n a d", a=16), q="pool")
        for half in range(2):
            self.transposes(skb, lambda j, half=half: skb[:, half * 8 + j, :], 8, skT, skT[:, half * 8:(half + 1) * 8, :])
        for mt in range(2):
            xt = self.X.next()
            self.load(xt, xt[:, :], I["mem_p"][mt * 128:(mt + 1) * 128, :])
            hT = self.HT.next()
            self.normT(xt, gb_mem, self.HB, hT)
            P = A[0]
            self.proj(P, P.t[:, 0:512], hT, wck, 0, 512)
            sm = self.small.next()
            self.op("act", lambda e: e.activation(out=sq4[:, :], in_=P.t[:, 0:512], func=AF.Square), R=[P], W=[sq4])
            self.op("dve", lambda e, sm=sm: e.tensor_reduce(out=sm[:, 0:4], in_=sq4[:, :].rearrange("p (h d) -> p h d", h=4),
                                                          axis=AX.X, op=ALU.add), R=[sq4], W=[sm])
            self.op("act", lambda e, sm=sm: e.activation(out=sm[:, 4:8], in_=sm[:, 0:4], func=AF.Sqrt, bias=self.epsb[:, 0:1],
                                                       scale=1.0 / 128), R=[sm, self.epsb], W=[sm])
            self.op("dve", lambda e, sm=sm: e.reciprocal(out=sm[:, 4:8], in_=sm[:, 4:8]), R=[sm], W=[sm])
            self.op("dve", lambda e, sm=sm: e.tensor_tensor(out=kf[:, :].rearrange("p (h d) -> p h d", h=4),
                                                          in0=P.t[:, 0:512].rearrange("p (h d) -> p h d", h=4),
                                                          in1=sm[:, 4:8].unsqueeze(2).to_broadcast([128, 4, 128]), op=ALU.mult),
                    R=[P, sm], W=[kf])
            self.op("dve", lambda e: e.tensor_tensor(out=kf[:, :].rearrange("p (h d) -> p h d", h=4),
                                                   in0=kf[:, :].rearrange("p (h d) -> p h d", h=4),
                                                   in1=gb_ck[:, :].unsqueeze(1).to_broadcast([128, 4, 128]), op=ALU.mult),
                    R=[kf, gb_ck], W=[kf])
            self.store(O["mk_p"][mt * 128:(mt + 1) * 128, :], kf, kf[:, :])
            self.op("act", lambda e: e.copy(out=kb16[:, :], in_=kf[:, :]), R=[kf], W=[kb16])
            self.transposes(kb16, lambda j: kb16[:, j * 128:(j + 1) * 128], 4, memKT[0], memKT[0][:, :, mt * 128:(mt + 1) * 128])
            P2 = A[1]
            self.proj(P2, P2.t[:, 512:1024], hT, wcv, 0, 512)
            self.op("act", lambda e: e.copy(out=sq4[:, :], in_=P2.t[:, 512:1024]), R=[P2], W=[sq4])
            self.store(O["mv_p"][mt * 128:(mt + 1) * 128, :], sq4, sq4[:, :])
            self.op("dve", lambda e, mt=mt: e.tensor_copy(out=memV[0][:, mt, :, 0:128], in_=sq4[:, :].rearrange("p (h d) -> p h d", h=4)),
                    R=[sq4], W=[memV[0]])
        self.pop()

        def load_sample_mem():
            for bt in range(4):
                for mt in range(2):
                    r0 = bt * 256 + mt * 128
                    self.load(kf, kf[:, :], I["memk_c"][r0:r0 + 128, :])
                    self.op("act", lambda e: e.copy(out=kb16[:, :], in_=kf[:, :]), R=[kf], W=[kb16])
                    self.transposes(kb16, lambda j: kb16[:, j * 128:(j + 1) * 128], 4, memKT[bt], memKT[bt][:, :, mt * 128:(mt + 1) * 128])
                    xt = self.X.next()
                    self.load(xt, xt[:, 0:512], I["memv_c"][r0:r0 + 128, :])
                    self.op("dve", lambda e, xt=xt, bt=bt, mt=mt: e.tensor_copy(out=memV[bt][:, mt, :, 0:128],
                                                                           in_=xt[:, 0:512].rearrange("p (h d) -> p h d", h=4)),
                            R=[xt], W=[memV[bt]])

        TS = [(self.sb("x2", [128, D], F32), self.sb("h3b", [128, D], BF16), self.sb("idxT", [128, 128], I32), self.sb("gT", [128, 128], F32))
              for _ in range(2)]
        mi_r = self.ring("mi", [128, D], BF16, 1)
        mT = self.sb("mT", [128, 8, 128], BF16)
        sqc = self.sb("sqc", [128, 512], BF16)
        rsc = self.sb("rsc", [128, 512], F32)
        qcn = self.sb("qcn", [128, 4, 128], BF16)
        ptc = self.sb("ptc", [128, 8, 128], BF16)
        ptcS = [self.sb("ptcS", [128, 8, 128], BF16) for _ in range(4)]
        for t in ptcS:
            self.op("pool", lambda e, t=t: e.memset(t[:, :, :], 0.0), W=[t])
        oc = self.sb("oc", [128, 4, 128], BF16)
        ocT = self.sb("ocT", [128, 4, 128], BF16)
        h3T = self.sb("h3T", [128, 8, 128], BF16)
        qTb = self.sb("qTb", [128, 16, 128], BF16)
        qtok = kb16
        sc = self.sb("sc", [128, 2048], F32)
        wks = [self.sb("wk", [128, 256], F32) for _ in range(2)]
        sv = self.sb("sv", [128, 16, 16], F32)
        si = self.sb("si", [128, 16, 16], U32)
        sif = self.sb("sif", [128, 16, 16], F32)
        ts = self.sb("ts", [128, 8, 16], F32)
        sel = self.sb("sel", [128, 8, 16], U32)
        self_f = self.sb("self_f", [128, 8, 16], F32)
        k1i = self.sb("k1i", [128, 8, 16], I32)
        k1f = self.sb("k1f", [128, 8, 16], F32)
        k2f = self.sb("k2f", [128, 8, 16], F32)
        iota16 = self.sb("iota16", [128, 16], F32)
        for k in range(16):
            self.op("pool", lambda e, k=k: e.memset(iota16[:, k:k + 1], float(k)), W=[iota16])
        eq = self.sb("eq", [128, 4, 16, 16], BF16)
        i12 = self.sb("i12", [128, 2, 128], F32)
        tb = self.sb("tb", [128, 3, 128], BF16)
        ex = self.sb("ex", [128, 8, 16], F32)
        i1T = self.sb("i1T", [128, 128], F32)
        i2T = self.sb("i2T", [128, 128], F32)
        NB = self.dbg.get("NB", 8)
        GUV = self.ring("GUV", [128, 2 * D], BF16, NB)
        wsel_r = self.ring("wsel", [128, 128], BF16, 4)
        djunk_r = self.ring("djunk", [128, D], BF16, 1)
        act_r = self.ring("actc", [128, 1], F32, 6)
        gl_r = self.ring("glc", [128, 2], F32, 6)

        svb = [T(sv.t, Buf("svb")) for _ in range(16)]
        sib = [T(si.t, Buf("sib")) for _ in range(16)]
        tsb = [T(ts.t, Buf("tsb")) for _ in range(8)]
        selb = [T(sel.t, Buf("selb")) for _ in range(8)]

        def pre(i):
            x2, h3b, idxT, gT = TS[i % 2]
            x1 = x2
            if i == 16:
                load_sample_mem()
            xt = self.X.next()
            yield self.load(xt, xt[:, :], I["x_own"][i * 128:(i + 1) * 128, :])
            mi_ = mi_r.next()
            yield self.dma(lambda e, mi_=mi_, i=i: e.dma_start(out=mi_[:, :], in_=self.mscr.t[i * 128:(i + 1) * 128, :]), R=[self.mscr], W=[mi_])
            yield from (None for _ in range(2))
            yield self.transposes(mi_, lambda j, mi_=mi_: mi_[:, j * 128:(j + 1) * 128], 8, mT, mT[:, :, :])
            yield from (None for _ in range(2))
            for n in range(2):
                yield self.proj(Dp, Dp.t[:, 0:512], mT, wo, n * 512, 512)
                yield self.op("dve", lambda e, n=n, xt=xt: e.tensor_tensor(out=x1[:, n * 512:(n + 1) * 512], in0=xt[:, n * 512:(n + 1) * 512],
                                                                 in1=Dp.t[:, 0:512], op=ALU.add), R=[xt, Dp], W=[x1])
            hT = self.HT.next()
            yield self.rms_tok(x1, x1[:, :], D, gb_xa, gb_xa[:, :], self.HB, self.HB[:, :])
            yield from (None for _ in range(3))
            yield self.transposes(self.HB, lambda j: self.HB[:, j * 128:(j + 1) * 128], 8, hT, hT[:, :, :])
            yield from (None for _ in range(2))
            P = Dp
            for hd in range(4):
                for c in range(8):
                    yield self.mm(P, P.t[:, hd * 128:(hd + 1) * 128], wcq, wcq[:, c, hd * 128:(hd + 1) * 128], hT, hT[:, c, :], c == 0, c == 7)
            yield self.op("act", lambda e: e.activation(out=sqc[:, :], in_=P.t[:, 0:512], func=AF.Square), R=[P], W=[sqc])
            yield self.op("act", lambda e: e.copy(out=kf[:, :], in_=P.t[:, 0:512]), R=[P], W=[kf])
            yield from (None for _ in range(2))
            yield self.mm(P, P.t[:, 0:512], ones128, ones128[:, :], sqc, sqc[:, :], True, True)
            yield self.op("act", lambda e: e.activation(out=rsc[:, :], in_=P.t[:, 0:512], func=AF.Sqrt, bias=self.epsb[:, 0:1], scale=1.0),
                    R=[P, self.epsb], W=[rsc])
            yield self.op("dve", lambda e: e.reciprocal(out=rsc[:, :], in_=rsc[:, :]), R=[rsc], W=[rsc])
            yield self.op("dve", lambda e: e.scalar_tensor_tensor(out=qcn[:, :, :].rearrange("p h t -> p (h t)"), in0=kf[:, :], scalar=gcq[:, 1:2],
                                                            in1=rsc[:, :], op0=ALU.mult, op1=ALU.mult), R=[kf, gcq, rsc], W=[qcn])
            yield from (None for _ in range(3))
            if i < 16:
                batches = [(0, 0, 128, ptc)]
            else:
                batches = [(bt, bt * 16, 16, ptcS[bt]) for bt in range(4)]
            for (mb, c0, n, p) in batches:
                for pair in range(2):
                    for hd in (2 * pair, 2 * pair + 1):
                        for mt in range(2):
                            jj = (hd % 2) * 2 + mt
                            yield self.mm(P, P.t[:, jj * 128:jj * 128 + n], memKT[mb], memKT[mb][:, hd, mt * 128:(mt + 1) * 128],
                                          qcn, qcn[:, hd, c0:c0 + n], True, True)
                    yield self.op("act", lambda e, pair=pair, p=p, c0=c0, n=n: e.activation(
                        out=p[:, pair * 4:(pair + 1) * 4, c0:c0 + n],
                        in_=P.t[:, 0:512].rearrange("p (j q) -> p j q", j=4)[:, :, 0:n], func=AF.Exp),
                        R=[P], W=[p])
            yield from (None for _ in range(2))
            nq = 128
            for pair in range(2):
                for hd in (2 * pair, 2 * pair + 1):
                    col = (hd % 2) * 256
                    nacc = len(batches) * 2
                    k = 0
                    for (mb, c0, n, p) in batches:
                        for mt in range(2):
                            yield self.mm(P, P.t[0:nq, col:col + 136], p, p[:, hd * 2 + mt, 0:nq], memV[mb], memV[mb][:, mt, hd, :], k == 0, k == nacc - 1)
                            k += 1
                sm = self.small.next()
                ocv = P.t[:, 0:512].rearrange("p (h c) -> p h c", h=2)
                yield self.op("dve", lambda e, sm=sm, ocv=ocv: e.tensor_scalar(out=sm[:, 0:2], in0=ocv[:, :, 128:129].rearrange("p h o -> p (h o)"),
                                                                     scalar1=1e-30, scalar2=None, op0=ALU.max), R=[P], W=[sm])
                yield self.op("dve", lambda e, sm=sm: e.reciprocal(out=sm[:, 0:2], in_=sm[:, 0:2]), R=[sm], W=[sm])
                yield self.op("dve", lambda e, sm=sm, ocv=ocv, pair=pair: e.tensor_tensor(out=oc[:, 2 * pair:2 * pair + 2, :], in0=ocv[:, :, 0:128],
                                                                                in1=sm[:, 0:2].unsqueeze(2).to_broadcast([128, 2, 128]), op=ALU.mult),
                        R=[P, sm], W=[oc])
            yield from (None for _ in range(2))
            yield self.transposes(oc, lambda j: oc[:, j, :], 4, ocT, ocT[:, :, :])
            yield from (None for _ in range(2))
            for n in range(2):
                yield self.proj(P, P.t[:, 0:512], ocT, wco, n * 512, 512, kc=4)
                yield self.op("dve", lambda e, n=n: e.tensor_tensor(out=x2[:, n * 512:(n + 1) * 512], in0=x1[:, n * 512:(n + 1) * 512],
                                                           in1=P.t[:, 0:512], op=ALU.add), R=[x1, P], W=[x2])
            yield self.rms_tok(x2, x2[:, :], D, gb_ff, gb_ff[:, :], h3b, h3b[:, :])
            yield from (None for _ in range(3))
            yield self.transposes(h3b, lambda j: h3b[:, j * 128:(j + 1) * 128], 8, h3T, h3T[:, :, :])
            yield from (None for _ in range(2))
            for r in range(4):
                for c in range(8):
                    yield self.mm(P, P.t[:, 0:512], h3T, h3T[:, c, :], wpq, wpq[:, c, r * 512:(r + 1) * 512], c == 0, c == 7)
                yield self.op("act", lambda e: e.copy(out=qtok[:, :], in_=P.t[:, 0:512]), R=[P], W=[qtok])
                yield from (None for _ in range(2))
                yield self.transposes(qtok, lambda j: qtok[:, j * 128:(j + 1) * 128], 4, qTb, qTb[:, r * 4:r * 4 + 4, :])
                yield from (None for _ in range(1))
            for r in range(4):
                for hp in range(r * 4, r * 4 + 4):
                    off = (hp % 4) * 128
                    yield self.mm(P, P.t[:, off:off + 128], qTb, qTb[:, hp, :], skT, skT[:, hp, :], True, True)
                yield self.op("act", lambda e, r=r: e.copy(out=sc[:, r * 512:(r + 1) * 512], in_=P.t[:, 0:512]), R=[P], W=[sc])

            def top16_pair(items):
                for stage in range(5):
                    for (src_ap, width, vout, iout, wkb, vb, ib) in items:
                        if stage == 0:
                            self.op("dve", lambda e, vout=vout, src_ap=src_ap: e.max(out=vout[:, 0:8], in_=src_ap), R=[sc], W=[vb])
                        elif stage == 1:
                            self.op("dve", lambda e, vout=vout, iout=iout, src_ap=src_ap: e.max_index(out=iout[:, 0:8], in_max=vout[:, 0:8], in_values=src_ap),
                                    R=[sc, vb], W=[ib])
                        elif stage == 2:
                            self.op("dve", lambda e, vout=vout, src_ap=src_ap, wkb=wkb, width=width: e.match_replace(
                                out=wkb[:, 0:width], in_to_replace=vout[:, 0:8], in_values=src_ap, imm_value=-1e30), R=[sc, vb], W=[wkb])
                        elif stage == 3:
                            self.op("dve", lambda e, vout=vout, wkb=wkb, width=width: e.max(out=vout[:, 8:16], in_=wkb[:, 0:width]), R=[wkb], W=[vb])
                        else:
                            self.op("dve", lambda e, vout=vout, iout=iout, wkb=wkb, width=width: e.max_index(
                                out=iout[:, 8:16], in_max=vout[:, 8:16], in_values=wkb[:, 0:width]), R=[wkb, vb], W=[ib])
                    yield None
            for hp in range(0, 16, 2):
                yield from top16_pair([(sc[:, q * 128:(q + 1) * 128], 128, sv[:, q, :], si[:, q, :], wks[q % 2], svb[q], sib[q]) for q in (hp, hp + 1)])
            yield self.op("dve", lambda e: e.tensor_copy(out=sif[:, :, :], in_=si[:, :, :]), R=sib, W=[sif])
            sv4 = sv[:, :, :].rearrange("p (h two) k -> p h two k", two=2)
            cand = sc[:, :].rearrange("p (h a b) -> p h a b", h=8, a=16)
            yield self.op("dve", lambda e: e.tensor_tensor(out=cand, in0=sv4[:, :, 0, :].unsqueeze(3).to_broadcast([128, 8, 16, 16]),
                                                   in1=sv4[:, :, 1, :].unsqueeze(2).to_broadcast([128, 8, 16, 16]), op=ALU.add),
                    R=svb, W=[sc])
            for hh in range(0, 8, 2):
                yield from top16_pair([(sc[:, q * 256:(q + 1) * 256], 256, ts[:, q, :], sel[:, q, :], wks[q % 2], tsb[q], selb[q]) for q in (hh, hh + 1)])
            yield self.op("dve", lambda e: e.tensor_copy(out=self_f[:, :, :], in_=sel[:, :, :]), R=selb, W=[self_f])
            yield self.op("dve", lambda e: e.tensor_scalar(out=k1i[:, :, :], in0=self_f[:, :, :], scalar1=-7.5, scalar2=0.0625, op0=ALU.add, op1=ALU.mult),
                    R=[self_f], W=[k1i])
            yield self.op("dve", lambda e: e.tensor_copy(out=k1f[:, :, :], in_=k1i[:, :, :]), R=[k1i], W=[k1f])
            yield self.op("dve", lambda e: e.scalar_tensor_tensor(out=k2f[:, :, :], in0=k1f[:, :, :], scalar=-16.0, in1=self_f[:, :, :],
                                                            op0=ALU.mult, op1=ALU.add), R=[k1f, self_f], W=[k2f])
            sif4 = sif[:, :, :].rearrange("p (h two) k -> p h two k", two=2)
            io4 = iota16[:, :].unsqueeze(1).unsqueeze(1).to_broadcast([128, 4, 16, 16])
            for which, kf_ in ((0, k1f), (1, k2f)):
                for h2 in range(2):
                    hs = slice(h2 * 4, h2 * 4 + 4)
                    yield self.op("dve", lambda e, kf_=kf_, hs=hs: e.tensor_tensor(out=eq[:, :, :, :], in0=io4,
                                                                          in1=kf_[:, hs, :].unsqueeze(3).to_broadcast([128, 4, 16, 16]), op=ALU.is_equal),
                            R=[iota16, kf_], W=[eq])
                    yield self.op("pool", lambda e, which=which, hs=hs: e.tensor_tensor(out=eq[:, :, :, :], in0=eq[:, :, :, :],
                                                                              in1=sif4[:, hs, which, :].unsqueeze(2).to_broadcast([128, 4, 16, 16]), op=ALU.mult),
                            R=[eq, sif], W=[eq])
                    yield self.op("dve", lambda e, which=which, h2=h2: e.tensor_reduce(out=i12[:, which, h2 * 64:(h2 + 1) * 64].rearrange("p (h k) -> p h k", h=4),
                                                                              in_=eq[:, :, :, :], axis=AX.X, op=ALU.add), R=[eq], W=[i12])
            yield self.op("act", lambda e: e.copy(out=tb[:, 0:2, :], in_=i12[:, :, :]), R=[i12], W=[tb])
            yield self.op("dve", lambda e: e.tensor_tensor(out=ex[:, :, :], in0=ts[:, :, :], in1=ts[:, :, 0:1].to_broadcast([128, 8, 16]), op=ALU.subtract),
                    R=tsb, W=[ex])
            yield self.op("act", lambda e: e.activation(out=ex[:, :, :], in_=ex[:, :, :], func=AF.Exp), R=[ex], W=[ex])
            sm = self.small.next()
            yield self.op("dve", lambda e, sm=sm: e.tensor_reduce(out=sm[:, 0:8], in_=ex[:, :, :], axis=AX.X, op=ALU.add), R=[ex], W=[sm])
            yield self.op("dve", lambda e, sm=sm: e.reciprocal(out=sm[:, 0:8], in_=sm[:, 0:8]), R=[sm], W=[sm])
            yield self.op("dve", lambda e, sm=sm: e.tensor_tensor(out=tb[:, 2, :].rearrange("p (h k) -> p h k", h=8), in0=ex[:, :, :],
                                                          in1=sm[:, 0:8].unsqueeze(2).to_broadcast([128, 8, 16]), op=ALU.mult),
                    R=[ex, sm], W=[tb])
            yield from (None for _ in range(3))
            for j in range(3):
                yield self.op("pe", lambda e, j=j: e.transpose(out=pT[:, j * 128:(j + 1) * 128], in_=tb[:, j, :], identity=self.ident[:, :]),
                        R=[tb, self.ident], W=[pT])
            yield self.op("act", lambda e: e.copy(out=i1T[:, :], in_=pT[:, 0:128]), R=[pT], W=[i1T])
            yield self.op("act", lambda e: e.copy(out=i2T[:, :], in_=pT[:, 128:256]), R=[pT], W=[i2T])
            yield self.op("dve", lambda e: e.scalar_tensor_tensor(out=idxT[:, :], in0=i1T[:, :], scalar=128.0, in1=i2T[:, :],
                                                            op0=ALU.mult, op1=ALU.add), R=[i1T, i2T], W=[idxT])
            yield self.op("act", lambda e: e.copy(out=gT[:, :], in_=pT[:, 256:384]), R=[pT], W=[gT])
        def loop(i, gen):
            x2, h3b, idxT, gT = TS[i % 2]
            ntok = 128 if i < 16 else 64
            Pacc = [B[0], B[1]]
            st_g, st_x, st_w = {}, {}, {}

            def stageA(t):
                guv = GUV.next()
                st_g[t] = guv
                self.dma(lambda e: e.indirect_dma_start(out=guv[:, :], out_offset=None, in_=self.uvtab.t,
                                                         in_offset=bass.IndirectOffsetOnAxis(ap=idxT[:, t:t + 1], axis=0)),
                         R=[idxT] + self.uvparts, W=[guv], q="pool")

            st_a = {}

            def stageB1(t):
                guv = st_g[t]
                ac = act_r.next(); dj = djunk_r.next()
                st_a[t] = ac
                Px = A if order.index(t) % 2 == 0 else C
                tX = Px[0].t
                for n in range(2):
                    self.mm(Px[n], tX[:, n * 512:(n + 1) * 512], self.ident, self.ident[:, t:t + 1].to_broadcast([128, 128]),
                            h3b, h3b[:, n * 512:(n + 1) * 512], True, True)
                self.op("dve", lambda e: e.scalar_tensor_tensor(out=dj[:, :], in0=guv[:, 0:D], scalar=1.0, in1=tX[:, :], op0=ALU.mult, op1=ALU.mult,
                                                                accum_out=ac[:, 0:1]), R=[guv, Px[0], Px[1]], W=[dj, ac])

            def stageB2(t):
                ac = st_a.pop(t)
                g2 = gl_r.next()
                self.op("act", lambda e: e.activation(out=g2[:, 0:1], in_=ac[:, 0:1], func=AF.Gelu), R=[ac], W=[g2])
                self.op("act", lambda e: e.activation(out=g2[:, 1:2], in_=g2[:, 0:1], func=AF.Copy, scale=gT[:, t:t + 1]), R=[g2, gT], W=[g2])
                ws = wsel_r.next()
                st_w[t] = ws
                r32 = t % 32
                self.op("act", lambda e: e.activation(out=ws[:, 0:32], in_=cmask[:, 127 - r32:159 - r32], func=AF.Copy, scale=g2[:, 1:2]),
                        R=[cmask, g2], W=[ws])

            ngrp = ntok // 32
            order = [g * 32 + r for r in range(32) for g in range(ngrp)]

            def stageC2(ts_):
                items = [(t, st_w.pop(t), st_g.pop(t)) for t in ts_]
                for n in range(2):
                    for (t, ws, guv) in items:
                        j = t // 32
                        r = t % 32
                        self.op("pe", lambda e, n=n, ws=ws, guv=guv, j=j, r=r: e.matmul(
                            Pacc[n].t[32 * j:32 * j + 32, n * 512:(n + 1) * 512], lhsT=ws[:, 0:32], rhs=guv[:, D + n * 512:D + (n + 1) * 512],
                            start=(r == 0), stop=(r == 31), tile_position=(0, 32 * j)), R=[ws, guv], W=[Pacc[n]])
            LA = NB - 4
            nst = len(order)
            for step in range(nst + LA + 3):
                if step < nst:
                    stageA(order[step])
                if 0 <= step - LA < nst:
                    stageB1(order[step - LA])
                if 0 <= step - LA - 1 < nst:
                    stageB2(order[step - LA - 1])
                k = step - LA - 2
                if k >= 1 and k % 2 == 1 and k < nst:
                    stageC2([order[k - 1], order[k]])
                if gen is not None:
                    for _ in range(3 if step % 3 == 0 else 2):
                        next(gen, None)
            yo = self.X.next()
            for n in range(2):
                self.op("dve", lambda e, n=n, yo=yo: e.tensor_tensor(out=yo[:, n * 512:(n + 1) * 512], in0=x2[:, n * 512:(n + 1) * 512],
                                                                 in1=Pacc[n].t[:, n * 512:(n + 1) * 512], op=ALU.add), R=[x2, Pacc[n]], W=[yo])
            self.store(O["y_own"][i * 128:(i + 1) * 128, :], yo, yo[:, :])
            if gen is not None:
                for _ in gen:
                    pass

        NADV = self.dbg.get("NADV", 6)
        g0 = pre(0)
        self.npre = 0
        for _ in g0:
            self.npre += 1
        for i in range(NOWN):
            gen = pre(i + 1) if i + 1 < NOWN else None
            loop(i, gen)
        self.pop()


def _own_blocks(j):
    blks = []
    for s in range(16):
        m = s // 2
        if s % 2 == 0:
            blks.append(4 * m + (0 if j == 0 else 1))
        else:
            blks.append(4 * m + (3 if j == 0 else 2))
    return blks


def _consts(j):
    f = np.float32
    c = {}
    c["idn"] = np.eye(128, dtype=f)
    oq = np.zeros((96, 96), f)
    oq[0:64, 0:64] = 1.0 / 64
    oq[64:96, 64:96] = 1.0 / 32
    c["onesq"] = oq
    c["ones128"] = np.full((128, 128), 1.0 / 128, f)
    rm = np.zeros((96, 96), f)
    for d in range(64, 80):
        rm[d + 16, d] = -1.0
    for d in range(80, 96):
        rm[d - 16, d] = 1.0
    c["rotm"] = rm
    cm = np.zeros((128, 255), f)
    cm[:, 127] = 1.0
    c["cmask"] = cm
    c["trilm"] = np.tril(np.ones((128, 128), f))
    tms = np.zeros((128, 128), f)
    for bt in range(4):
        tms[bt * 16:(bt + 1) * 16, bt * 16:(bt + 1) * 16] = np.tril(np.ones((16, 16), f))
    c["trilms"] = tms
    inv = (np.float32(10000.0) ** (-np.arange(16, dtype=f) / np.float32(16))).astype(f)
    pos = np.zeros((128, 33), f)
    for i in range(32):
        pos[:, i] = i * 128 + np.arange(128)
    pos[0:64, 32] = 1024 + (np.arange(64) % 16)
    ang = (pos[:, :, None] * inv[None, None, :]).astype(f)
    c["ropek"] = np.concatenate([np.cos(ang), np.sin(ang)], axis=2).astype(f)
    blks = _own_blocks(j)
    posq = np.zeros((NTOK,), f)
    for s, b in enumerate(blks):
        posq[s * 128:(s + 1) * 128] = b * 128 + np.arange(128)
    posq[2048:2112] = 1024 + (np.arange(64) % 16)
    angq = (posq[None, :] * np.concatenate([inv, inv])[:, None]).astype(f)
    c["cosq"] = np.cos(angq).astype(f)
    c["sinq"] = np.sin(angq).astype(f)
    kk = np.arange(128)[:, None] // 64
    qq = np.arange(128)[None, :] // 64
    diag = (kk <= qq).astype(f)
    full = np.ones((128, 128), f)
    zero = np.zeros((128, 128), f)
    c["masks"] = np.stack([diag, zero, full, diag] if j == 0 else [full, diag, diag, zero]).astype(f)
    return c


_NC_CACHE = {}


def _get_nc(dbg=None):
    key = repr(sorted((dbg or {}).items()))
    if key not in _NC_CACHE:
        k = Kern(dbg)
        _NC_CACHE[key] = k.build()
    return _NC_CACHE[key]


def kernel(x_prompt, x_sample, cache_mla_ckv, cache_mla_krope, cache_mem_k, cache_mem_v, mem_prompt,
           g_mix, w_in, g_q_lat, w_uq, g_qn, g_qr, g_kv_lat, g_kr, w_uk, w_uv, g_kn, w_oa,
           g_gm, w_s, b_s, w_ob, w_o,
           g_xattn, g_mem, w_cq, g_cq, w_ck, g_ck, w_cv, w_co,
           g_ffn, w_pq, sub_keys, peer_u, peer_v, _dbg=None):
    f = np.float32
    A = lambda a: np.ascontiguousarray(np.asarray(a, dtype=f))
    x_prompt = A(x_prompt); x_sample = A(x_sample)
    shared = {
        "w_in": A(w_in)[0], "w_uq": A(w_uq)[0], "w_uk": A(w_uk)[0], "w_uv": A(w_uv)[0],
        "w_oa": A(w_oa)[0], "w_ob": A(w_ob)[0], "w_o": A(w_o)[0],
        "w_cq": A(w_cq)[0], "w_ck": A(w_ck)[0], "w_cv": A(w_cv)[0], "w_co": A(w_co)[0],
        "w_pq": A(w_pq)[0], "sub_keys": A(sub_keys)[0].reshape(2048, 128),
        "peer_u": A(peer_u)[0], "peer_v": A(peer_v)[0],
        "g_mix": A(g_mix), "g_q_lat": A(g_q_lat), "g_qn": A(g_qn), "g_qr": A(g_qr), "g_kv_lat": A(g_kv_lat),
        "g_kr": A(g_kr), "g_kn": A(g_kn), "g_gm": A(g_gm), "g_xattn": A(g_xattn), "g_mem": A(g_mem),
        "g_cq": A(g_cq), "g_ck": A(g_ck), "g_ffn": A(g_ffn),
        "w_s": A(w_s)[0].reshape(1024, 128), "b_s": A(b_s)[0],
    }
    ckv_c = A(cache_mla_ckv)[0]; kr_c = A(cache_mla_krope)[0]
    mk_c = A(cache_mem_k)[0]; mv_c = A(cache_mem_v)[0]; mem_p = A(mem_prompt)
    consts = [_consts(0), _consts(1)]
    in_maps = []
    for c in range(NCORES):
        b, j = c // 2, c % 2
        blks = _own_blocks(j)
        x_own = np.zeros((NTOK, D), f)
        for s, blk in enumerate(blks):
            x_own[s * 128:(s + 1) * 128] = x_prompt[b, blk * 128:(blk + 1) * 128]
        x_own[2048:2112] = x_sample[4 * c:4 * c + 4].reshape(64, D)
        m = dict(shared)
        m.update(consts[j])
        m["x_all"] = x_prompt[b]
        m["x_own"] = x_own
        m["ckv_c"] = np.ascontiguousarray(ckv_c[4 * c:4 * c + 4].reshape(4096, 256))
        m["kr_c"] = np.ascontiguousarray(kr_c[4 * c:4 * c + 4].reshape(4096, 32))
        m["memk_c"] = np.ascontiguousarray(mk_c[4 * c:4 * c + 4].reshape(1024, 512))
        m["memv_c"] = np.ascontiguousarray(mv_c[4 * c:4 * c + 4].reshape(1024, 512))
        m["mem_p"] = mem_p[b]
        in_maps.append(m)
    nc = _get_nc(_dbg)
    res = run_bass_kernel_spmd(nc, in_maps, core_ids=list(range(NCORES)))
    R = res.results
    B, S, DB, DS = 4, 4096, 32, 16
    y_p = np.zeros((B, S, D), f); y_s = np.zeros((DB, DS, D), f)
    ckv_p = np.zeros((1, B, S, 256), f); kr_p = np.zeros((1, B, S, 32), f)
    mk_p = np.zeros((1, B, 256, 4, 128), f); mv_p = np.zeros((1, B, 256, 4, 128), f)
    ckv_s = np.zeros((1, DB, DS, 256), f); kr_s = np.zeros((1, DB, DS, 32), f); vg_s = np.zeros((1, DB, DS, D), f)
    for c in range(NCORES):
        b, j = c // 2, c % 2
        r = R[c]
        for s, blk in enumerate(_own_blocks(j)):
            y_p[b, blk * 128:(blk + 1) * 128] = r["y_own"][s * 128:(s + 1) * 128]
        y_s[4 * c:4 * c + 4] = r["y_own"][2048:2112].reshape(4, 16, D)
        half = slice(j * 2048, (j + 1) * 2048)
        ckv_p[0, b, half] = r["ckv_p"][half]
        kr_p[0, b, half] = r["kr_p"][half]
        mk_p[0, b, j * 128:(j + 1) * 128] = r["mk_p"][j * 128:(j + 1) * 128].reshape(128, 4, 128)
        mv_p[0, b, j * 128:(j + 1) * 128] = r["mv_p"][j * 128:(j + 1) * 128].reshape(128, 4, 128)
        ckv_s[0, 4 * c:4 * c + 4] = r["ckv_s"].reshape(4, 16, 256)
        kr_s[0, 4 * c:4 * c + 4] = r["kr_s"].reshape(4, 16, 32)
        vg_s[0, 4 * c:4 * c + 4] = r["vg_s"].reshape(4, 16, D)
    return (y_p, y_s, ckv_p, kr_p, mk_p, mv_p, ckv_s, kr_s, vg_s)
```
